# Optimizing a Trainium2 kernel written in Bass

```python
import math
import jax, jax.numpy as jnp
from jax import lax
import numpy as np

D_MODEL = 1024
BATCH = 4
SEQ = 4096
DEPTH = 2
DEC_BATCH = 128
DEC_SEQ = 4
PAST_LEN = 2048
PAGE_SIZE = 128

N_MIXERS = 2
N_ATTN = (DEPTH + 1) // 2
N_CONV = DEPTH // 2
N_HEADS = 16
HEAD_DIM = D_MODEL // N_HEADS
N_KV_HEADS = 4
GROUP = N_HEADS // N_KV_HEADS
ROT_DIM = HEAD_DIM // 4
ROPE_THETA = 500000.0
IDX_HEADS = 16
IDX_DIM = HEAD_DIM
TOPK_MAX = 256
Q_BLOCK = 128
CONV_W = 3
D_FF = 2816
ALPHA = (2 * DEPTH) ** 0.25
BETA = (8 * DEPTH) ** -0.25
LN_EPS = 1e-5
Q_W = N_HEADS * HEAD_DIM
KV_W = N_KV_HEADS * HEAD_DIM
QI_W = IDX_HEADS * IDX_DIM
ATTN_IN_W = Q_W + 2 * KV_W + QI_W + IDX_DIM + IDX_HEADS
ATTN_SPLITS = (Q_W, Q_W + KV_W, Q_W + 2 * KV_W, Q_W + 2 * KV_W + QI_W, Q_W + 2 * KV_W + QI_W + IDX_DIM)
IDX_SCALE = (IDX_HEADS ** -0.5) * (IDX_DIM ** -0.5)

kernel_name = "dsa_shortconv_deepnorm_step"


def layer_norm(x, g, b):
    x32 = x.astype(jnp.float32)
    mu = jnp.mean(x32, -1, keepdims=True)
    var = jnp.mean(jnp.square(x32 - mu), -1, keepdims=True)
    y = (x32 - mu) * lax.rsqrt(var + LN_EPS)
    return (y * g.astype(jnp.float32) + b.astype(jnp.float32)).astype(x.dtype)


def partial_rope(x, pos):
    half = ROT_DIM // 2
    inv = ROPE_THETA ** (-jnp.arange(half, dtype=jnp.float32) * (2.0 / ROT_DIM))
    ang = pos.astype(jnp.float32)[:, None] * inv[None, :]
    cos = jnp.cos(ang)[:, None, :].astype(x.dtype)
    sin = jnp.sin(ang)[:, None, :].astype(x.dtype)
    x1 = x[..., :half]
    x2 = x[..., half:ROT_DIM]
    return jnp.concatenate([x1 * cos - x2 * sin, x2 * cos + x1 * sin, x[..., ROT_DIM:]], -1)


def attn_projections(x, w_in, pos):
    B, T, _ = x.shape
    q, k, v, qi, ki, wi = jnp.split(x @ w_in, ATTN_SPLITS, axis=-1)
    q = partial_rope(q.reshape(B, T, N_HEADS, HEAD_DIM), pos)
    k = partial_rope(k.reshape(B, T, N_KV_HEADS, HEAD_DIM), pos)
    v = v.reshape(B, T, N_KV_HEADS, HEAD_DIM)
    qi = partial_rope(qi.reshape(B, T, IDX_HEADS, IDX_DIM), pos)
    ki = partial_rope(ki[:, :, None, :], pos)[:, :, 0, :]
    return q, k, v, qi, ki, wi * IDX_SCALE


def indexer_scores(qi, wi, ki):
    s = jnp.einsum('bqhd,bsd->bqhs', qi, ki, preferred_element_type=jnp.float32)
    return jnp.einsum('bqhs,bqh->bqs', jax.nn.relu(s), wi.astype(jnp.float32))


def sparse_attend(q, k_sel, v_sel, valid):
    B, Tq = q.shape[:2]
    qg = q.reshape(B, Tq, N_KV_HEADS, GROUP, HEAD_DIM)
    s = jnp.einsum('bqkgd,bqjkd->bqkgj', qg, k_sel, preferred_element_type=jnp.float32) * (HEAD_DIM ** -0.5)
    s = jnp.where(valid[:, :, None, None, :], s, -jnp.inf)
    p = jax.nn.softmax(s, axis=-1).astype(v_sel.dtype)
    o = jnp.einsum('bqkgj,bqjkd->bqkgd', p, v_sel)
    return o.reshape(B, Tq, Q_W)


def gather_rows(a, idx):
    return jax.vmap(lambda aa, ii: aa[ii])(a, idx)


def attn_prompt(x, w_in, w_out):
    B, T, _ = x.shape
    pos = jnp.arange(T)
    q, k, v, qi, ki, wi = attn_projections(x, w_in, pos)
    k_top = min(TOPK_MAX, T // 4)
    nb = T // Q_BLOCK

    def to_blocks(a):
        return jnp.moveaxis(a.reshape(B, nb, Q_BLOCK, *a.shape[2:]), 1, 0)

    def block(args):
        qb, qib, wib, start = args
        qpos = start + jnp.arange(Q_BLOCK)
        sc = indexer_scores(qib, wib, ki)
        sc = jnp.where((pos[None, :] <= qpos[:, None])[None], sc, -jnp.inf)
        _, idx = lax.top_k(sc, k_top)
        valid = idx <= qpos[None, :, None]
        return sparse_attend(qb, gather_rows(k, idx), gather_rows(v, idx), valid)

    starts = jnp.arange(nb) * Q_BLOCK
    o = lax.map(block, (to_blocks(q), to_blocks(qi), to_blocks(wi), starts))
    o = jnp.moveaxis(o, 0, 1).reshape(B, T, Q_W)
    return o @ w_out, k, v, ki


def attn_sample(x, w_in, w_out, cache_k, cache_v, cache_ik, page_table):
    Bd, Tn, _ = x.shape
    past = page_table.shape[1] * PAGE_SIZE
    pos = past + jnp.arange(Tn)
    q, k, v, qi, ki, wi = attn_projections(x, w_in, pos)
    L = past + Tn
    k_top = min(TOPK_MAX, L // 4)
    past_ik = cache_ik[page_table].reshape(Bd, past, IDX_DIM)
    all_ik = jnp.concatenate([past_ik, ki], axis=1)
    sc = indexer_scores(qi, wi, all_ik)
    sc = jnp.where((jnp.arange(L)[None, :] <= pos[:, None])[None], sc, -jnp.inf)
    _, idx = lax.top_k(sc, k_top)
    in_past = idx < past
    pidx = jnp.minimum(idx, past - 1)
    phys_page = gather_rows(page_table, pidx // PAGE_SIZE)
    flat_row = phys_page * PAGE_SIZE + pidx % PAGE_SIZE
    flat_k = cache_k.reshape(-1, N_KV_HEADS, HEAD_DIM)
    flat_v = cache_v.reshape(-1, N_KV_HEADS, HEAD_DIM)
    nidx = jnp.clip(idx - past, 0, Tn - 1)
    sel = in_past[..., None, None]
    k_sel = jnp.where(sel, flat_k[flat_row], gather_rows(k, nidx))
    v_sel = jnp.where(sel, flat_v[flat_row], gather_rows(v, nidx))
    valid = idx <= pos[None, :, None]
    o = sparse_attend(q, k_sel, v_sel, valid)
    return o @ w_out, k, v, ki


def causal_dwconv(z, prev, w):
    ext = jnp.concatenate([prev, z], axis=1)
    T = z.shape[1]
    y = ext[:, 0:T] * w[0]
    for j in range(1, CONV_W):
        y = y + ext[:, j:j + T] * w[j]
    return y, ext[:, -(CONV_W - 1):]


def conv_mixer(x, prev, w_in, conv_w, w_out):
    b, c, u = jnp.split(x @ w_in, 3, axis=-1)
    y, new_prev = causal_dwconv(c * u, prev, conv_w)
    return (b * y) @ w_out, new_prev


def conv_ffn(x, prev, w_up, conv_w, conv_b, w_down):
    a, g = jnp.split(x @ w_up, 2, axis=-1)
    ac, new_prev = causal_dwconv(a, prev, conv_w)
    h = jax.nn.silu(ac + conv_b) * g
    return h @ w_down, new_prev


def setup_inputs(seed: int = 0) -> dict:
    key = jax.random.key(seed)
    ks = jax.random.split(key, 24)
    n_pages = PAST_LEN // PAGE_SIZE
    used = DEC_BATCH * n_pages
    n_phys = used + max(1, used // 4)
    f32 = jnp.float32
    nrm = lambda k, s, sc: jax.random.normal(k, s, f32) * sc
    page_table = jax.random.permutation(ks[0], n_phys)[:used].reshape(DEC_BATCH, n_pages).astype(jnp.int32)
    return {
        "x_prompt": nrm(ks[1], (BATCH, SEQ, D_MODEL), 1.0),
        "x_sample": nrm(ks[2], (DEC_BATCH, DEC_SEQ, D_MODEL), 1.0),
        "cache_k": nrm(ks[3], (N_ATTN, n_phys, PAGE_SIZE, N_KV_HEADS, HEAD_DIM), 1.0),
        "cache_v": nrm(ks[4], (N_ATTN, n_phys, PAGE_SIZE, N_KV_HEADS, HEAD_DIM), 1.0),
        "cache_idx_k": nrm(ks[5], (N_ATTN, n_phys, PAGE_SIZE, IDX_DIM), 1.0),
        "state_conv": nrm(ks[6], (N_CONV, DEC_BATCH, CONV_W - 1, D_MODEL), 1.0),
        "state_ffn": nrm(ks[7], (DEPTH, DEC_BATCH, CONV_W - 1, D_FF), 1.0),
        "page_table": page_table,
        "w_attn_in": nrm(ks[8], (N_ATTN, D_MODEL, ATTN_IN_W), D_MODEL ** -0.5),
        "w_attn_out": nrm(ks[9], (N_ATTN, Q_W, D_MODEL), (Q_W ** -0.5) * BETA),
        "w_conv_in": nrm(ks[10], (N_CONV, D_MODEL, 3 * D_MODEL), D_MODEL ** -0.5),
        "conv_w": nrm(ks[11], (N_CONV, CONV_W, D_MODEL), CONV_W ** -0.5),
        "w_conv_out": nrm(ks[12], (N_CONV, D_MODEL, D_MODEL), (D_MODEL ** -0.5) * BETA),
        "w_ffn_up": nrm(ks[13], (DEPTH, D_MODEL, 2 * D_FF), D_MODEL ** -0.5),
        "ffn_conv_w": nrm(ks[14], (DEPTH, CONV_W, D_FF), CONV_W ** -0.5),
        "ffn_conv_b": nrm(ks[15], (DEPTH, D_FF), 0.02),
        "w_ffn_down": nrm(ks[16], (DEPTH, D_FF, D_MODEL), (D_FF ** -0.5) * BETA),
        "ln_g": 1.0 + nrm(ks[17], (DEPTH, 2, D_MODEL), 0.02),
        "ln_b": nrm(ks[18], (DEPTH, 2, D_MODEL), 0.02),
    }


def reference(x_prompt, x_sample, cache_k, cache_v, cache_idx_k, state_conv, state_ffn, page_table,
              w_attn_in, w_attn_out, w_conv_in, conv_w, w_conv_out, w_ffn_up, ffn_conv_w, ffn_conv_b,
              w_ffn_down, ln_g, ln_b):
    yp, ys = x_prompt, x_sample
    Bp = x_prompt.shape[0]
    kp_l, vp_l, ikp_l, ks_l, vs_l, iks_l = [], [], [], [], [], []
    cp_l, cs_l, fp_l, fs_l = [], [], [], []
    for i in range(DEPTH):
        j = i // N_MIXERS
        if i % N_MIXERS == 0:
            mp, kp, vp, ikp = attn_prompt(yp, w_attn_in[j], w_attn_out[j])
            ms, k_s, v_s, ik_s = attn_sample(ys, w_attn_in[j], w_attn_out[j], cache_k[j], cache_v[j],
                                             cache_idx_k[j], page_table)
            kp_l.append(kp); vp_l.append(vp); ikp_l.append(ikp)
            ks_l.append(k_s); vs_l.append(v_s); iks_l.append(ik_s)
        else:
            zeros_c = jnp.zeros((Bp, CONV_W - 1, D_MODEL), yp.dtype)
            mp, cp = conv_mixer(yp, zeros_c, w_conv_in[j], conv_w[j], w_conv_out[j])
            ms, cs = conv_mixer(ys, state_conv[j], w_conv_in[j], conv_w[j], w_conv_out[j])
            cp_l.append(cp); cs_l.append(cs)
        yp = layer_norm(ALPHA * yp + mp, ln_g[i, 0], ln_b[i, 0])
        ys = layer_norm(ALPHA * ys + ms, ln_g[i, 0], ln_b[i, 0])
        zeros_f = jnp.zeros((Bp, CONV_W - 1, D_FF), yp.dtype)
        fp, fsp = conv_ffn(yp, zeros_f, w_ffn_up[i], ffn_conv_w[i], ffn_conv_b[i], w_ffn_down[i])
        fs, fss = conv_ffn(ys, state_ffn[i], w_ffn_up[i], ffn_conv_w[i], ffn_conv_b[i], w_ffn_down[i])
        fp_l.append(fsp); fs_l.append(fss)
        yp = layer_norm(ALPHA * yp + fp, ln_g[i, 1], ln_b[i, 1])
        ys = layer_norm(ALPHA * ys + fs, ln_g[i, 1], ln_b[i, 1])
    return (yp, ys,
            jnp.stack(kp_l), jnp.stack(vp_l), jnp.stack(ikp_l),
            jnp.stack(ks_l), jnp.stack(vs_l), jnp.stack(iks_l),
            jnp.stack(cp_l), jnp.stack(cs_l),
            jnp.stack(fp_l), jnp.stack(fs_l))
```

```python
import contextlib
import numpy as np
import concourse.bass as bass
import concourse.mybir as mybir
from concourse.bass_utils import run_bass_kernel_spmd

F32 = mybir.dt.float32
BF16 = mybir.dt.bfloat16
I32 = mybir.dt.int32
AF = mybir.ActivationFunctionType
ALU = mybir.AluOpType
AX = mybir.AxisListType

P = 128
D = 1024
DFF = 2816
NCB = 11
SEQ = 4096
NKT = 32
NQT = 17
TQ = NQT * P
POS0 = 1920
NSQ = 16
TS = 64
PAST = 2048
NKS = 17
LS = PAST + 4
TOPK = 256
NIT = 20
ALPHA = 4.0 ** 0.25
IDX_SCALE = 1.0 / 32.0
EPS = 1e-5
NEG = -1.0e30
GROUPS = [(0, 1), (1, 4), (5, 4), (9, 4), (13, 4)]


class Buf:
    __slots__ = ("name", "w", "r", "dsem", "dtot")

    def __init__(self, name):
        self.name = name
        self.w = None
        self.r = {}
        self.dsem = None
        self.dtot = 0


def bc_mid(ap, n):
    a = [list(x) for x in ap.ap]
    return bass.AP(tensor=ap.tensor, offset=ap.offset, ap=[a[0], [0, n]] + a[1:])


def bc_last(ap, n):
    a = [list(x) for x in ap.ap]
    return bass.AP(tensor=ap.tensor, offset=ap.offset, ap=a + [[0, n]])


class Ctx:
    CH = 16000

    def __init__(self, nc, es):
        self.nc = nc
        self.es = es
        self.engs = {"pe": nc.tensor, "dve": nc.vector, "act": nc.scalar, "pool": nc.gpsimd, "sp": nc.sync}
        self.cnt = {e: 0 for e in self.engs}
        self.sems = {e: [] for e in self.engs}
        self.waited = {e: {} for e in self.engs}
        self.dma_bufs = []
        self.nsem = 0
        self.banks = []
        self.bank_i = 0
        self.wbufs = []
        self.wb_i = 0

    def new_sem(self, name):
        self.nsem += 1
        return self.es.enter_context(self.nc.semaphore(name))

    def sb(self, name, shape, dt, es=None):
        return (es or self.es).enter_context(self.nc.sbuf_tensor("sb_" + name, list(shape), dt))

    def ps(self, name, shape, dt):
        return self.es.enter_context(self.nc.psum_tensor("ps_" + name, list(shape), dt))

    def _esem(self, e, tick):
        ch = (tick - 1) // self.CH
        while len(self.sems[e]) <= ch:
            self.sems[e].append(self.new_sem("s_%s_%d" % (e, len(self.sems[e]))))
        return self.sems[e][ch], (tick - 1) % self.CH + 1

    def _wait(self, e, tok):
        if tok[0] == "eng":
            _, f, tick = tok
            key = f
            if self.waited[e].get(key, 0) >= tick:
                return
            sem, val = self._esem(f, tick)
        else:
            _, buf, val = tok
            key = ("d", id(buf))
            tick = val
            if self.waited[e].get(key, 0) >= tick:
                return
            sem = buf.dsem
        self.engs[e].wait_ge(sem, val)
        self.waited[e][key] = tick

    def _deps(self, e, reads, writes, nowaw=False):
        for b in reads:
            t = b.w
            if t is not None:
                if t[0] == "eng" and t[1] == e and e in ("pe", "sp"):
                    continue
                self._wait(e, t)
        if nowaw:
            return
        for b in writes:
            t = b.w
            if t is not None:
                if not (t[0] == "eng" and t[1] == e and e in ("pe", "sp", "dve", "act")):
                    self._wait(e, t)
            for t in b.r.values():
                if t[0] == "eng" and t[1] == e and e != "pool":
                    continue
                self._wait(e, t)

    def _commit(self, tok, reads, writes):
        key = tok[1] if tok[0] == "eng" else ("d", id(tok[1]))
        for b in reads:
            b.r[key] = tok
        for b in writes:
            b.w = tok
            b.r = {}

    def op(self, e, fn, reads=(), writes=()):
        self._deps(e, reads, writes)
        ins = fn(self.engs[e])
        self.cnt[e] += 1
        tick = self.cnt[e]
        sem, _ = self._esem(e, tick)
        ins.then_inc(sem, 1)
        self._commit(("eng", e, tick), reads, writes)
        return ins

    def dma(self, q, out, in_, sbuf, reads=(), writes=(), nowaw=False, indirect=None, slow=False):
        self._deps(q, reads, writes, nowaw)
        if sbuf.dsem is None:
            sbuf.dsem = self.new_sem("d_" + sbuf.name)
            self.dma_bufs.append(sbuf)
        sbuf.dtot += 16
        if indirect is not None:
            ins = self.nc.gpsimd.indirect_dma_start(out=out, out_offset=None, in_=in_, in_offset=indirect)
        else:
            ins = self.engs[q].dma_start(out=out, in_=in_, allow_slow_non_contiguous=True) if slow else self.engs[q].dma_start(out=out, in_=in_)
        ins.then_inc(sbuf.dsem, 16)
        self._commit(("dma", sbuf, sbuf.dtot), reads, writes)

    def barrier(self):
        for e in self.engs:
            for f in self.engs:
                if f != e and self.cnt[f] > 0:
                    self._wait(e, ("eng", f, self.cnt[f]))
            for b in self.dma_bufs:
                if b.dtot > 0:
                    self._wait(e, ("dma", b, b.dtot))

    def finish(self):
        for b in self.dma_bufs:
            self.engs["sp"].wait_ge(b.dsem, b.dtot)

    def bank(self, subset=None):
        if subset is None:
            subset = range(len(self.banks))
        self.bank_i += 1
        return self.banks[subset[self.bank_i % len(subset)]]

    def wbuf(self):
        i = self.wb_i % len(self.wbufs)
        self.wb_i += 1
        return self.wbufs[i]


def build(nphys, stage=99):
    nc = bass.Bass("TRN2", target_bir_lowering=False)

    def din(name, shape, dt=F32):
        return nc.dram_tensor(name, list(shape), dt, kind="ExternalInput").ap()

    def dout(name, shape, dt=F32):
        return nc.dram_tensor(name, list(shape), dt, kind="ExternalOutput").ap()

    def dscr(name, shape, dt):
        return nc.dram_tensor(name, list(shape), dt, kind="Internal").ap()

    NROW = nphys * P
    xk = din("xk", [SEQ, D])
    xq = din("xq", [TQ, D])
    xs = din("xs", [TS, D])
    ropek = din("ropek", [SEQ, 24])
    ropeq = din("ropeq", [TQ, 24])
    ropes = din("ropes", [TS, 24])
    qpos_d = din("qpos", [P, NQT + 1])
    consts = din("consts", [P, 512 + 128 + NIT + 1])
    msel_d = din("msel", [64, NSQ * 64])
    pt_d = din("pt", [NSQ, 16], I32)
    ck_d = din("cache_k", [NROW, 256])
    cv_d = din("cache_v", [NROW, 256])
    cik_d = din("cache_ik", [NROW, 64])
    stc_d = din("st_conv", [NSQ * 2, D])
    stf_d = din("st_ffn", [2, NSQ * 2, DFF])
    w_ai = din("w_attn_in", [D, 2640])
    w_ao = din("w_attn_out", [D, D])
    w_ci = din("w_conv_in", [D, 3 * D])
    cw_d = din("conv_w", [3, D])
    w_co = din("w_conv_out", [D, D])
    w_up = din("w_ffn_up", [2, D, 2 * DFF])
    fcw_d = din("ffn_conv_w", [2, 3, DFF])
    fcb_d = din("ffn_conv_b", [2, DFF])
    w_dn = din("w_ffn_down", [2, DFF, D])
    lng_d = din("ln_g", [4, D])
    lnb_d = din("ln_b", [4, D])

    o_y = dout("o_y", [TQ, D])
    o_ys = dout("o_ys", [TS, D])
    o_k = dout("o_k", [SEQ, 256])
    o_v = dout("o_v", [SEQ, 256])
    o_ik = dout("o_ik", [SEQ, 64])
    o_ks = dout("o_ks", [TS, 256])
    o_vs = dout("o_vs", [TS, 256])
    o_iks = dout("o_iks", [TS, 64])
    o_cp = dout("o_cp", [2, D])
    o_cs = dout("o_cs", [NSQ * 2, D])
    o_fp = dout("o_fp", [2, 2, DFF])
    o_fs = dout("o_fs", [2, NSQ * 2, DFF])

    wq_s = dscr("wq_s", [D, 2064], BF16)
    wkv_s = dscr("wkv_s", [D, 576], BF16)
    wo_s = dscr("wo_s", [D, D], BF16)
    wup_s = dscr("wup_s", [2, NCB, P, 8, 2, 256], BF16)
    wdn_s = dscr("wdn_s", [2, DFF, D], BF16)
    wci_s = dscr("wci_s", [8, P, 8, 3, 128], BF16)
    wco_s = dscr("wco_s", [D, D], BF16)
    y1_s = dscr("y1_s", [TQ + TS, D], F32)
    wsd_s = dscr("wsd_s", [TS, 16], F32)

    es = contextlib.ExitStack()
    with es:
        c = Ctx(nc, es)
        for i in range(6):
            c.banks.append((c.ps("pm%d" % i, [P, 512], F32), Buf("pm%d" % i)))
        ptb = [(c.ps("pt%d" % i, [P, 1024], BF16), Buf("pt%d" % i)) for i in range(2)]
        pt_i = [0]

        def ptbank():
            i = pt_i[0] % 2
            pt_i[0] += 1
            return ptb[i]

        WBN = 4608
        for i in range(4):
            c.wbufs.append((c.sb("wb%d" % i, [P, WBN], BF16), Buf("wb%d" % i)))
        cst = c.sb("cst", [P, 512 + 128 + NIT + 1], F32)
        Bcst = Buf("cst")
        identb = c.sb("identb", [P, P], BF16)
        Bid = Buf("identb")
        zerob = c.sb("zerob", [P, 512], BF16); Bzero = Buf("zerob")
        c.op("pool", lambda e: e.memset(zerob[:, :], 0.0), writes=[Bzero])
        c.dma("sp", cst[:], consts[:, :], Bcst, writes=[Bcst])
        c.dma("pool", identb[:], consts[:, 512:640], Bid, writes=[Bid])
        iota = cst[:, 0:512]
        identf = cst[:, 512:640]
        pow2 = cst[:, 640:640 + NIT]
        pidx = cst[:, 640 + NIT:641 + NIT]

        Bwq, Bwkv, Bwo, Bwup, Bwdn, Bwci, Bwco = (Buf(n) for n in ("wq", "wkv", "wo", "wup", "wdn", "wci", "wco"))

        pending = []

        def conv_now(dst, src, B):
            c.dma("pool", dst, src, B, writes=[B], nowaw=True)

        def conv(dst, src, B):
            if B in (Bwkv, Bwq, Bwo):
                conv_now(dst, src, B)
            else:
                pending.append((dst, src, B))

        def drain(n):
            for _ in range(n):
                if pending:
                    conv_now(*pending.pop(0))

        for r0 in range(0, D, 256):
            rs = slice(r0, r0 + 256)
            conv(wkv_s[rs, 0:512], w_ai[rs, 1024:1536], Bwkv)
            conv(wkv_s[rs, 512:576], w_ai[rs, 2560:2624], Bwkv)
        for r0 in range(0, D, 256):
            rs = slice(r0, r0 + 256)
            conv(wq_s[rs, 0:1024], w_ai[rs, 0:1024], Bwq)
            conv(wq_s[rs, 1024:2048], w_ai[rs, 1536:2560], Bwq)
            conv(wq_s[rs, 2048:2064], w_ai[rs, 2624:2640], Bwq)
            conv(wo_s[rs, :], w_ao[rs, :], Bwo)
        for i in range(2):
            wv = w_up[i].rearrange("(k p) (s c n) -> c k p s n", p=P, s=2, c=NCB, n=256)
            for cb in range(NCB):
                for k in range(8):
                    conv(wup_s[i, cb, :, k, :, :], wv[cb, k], Bwup)
            for r0 in range(0, DFF, 704):
                conv(wdn_s[i, r0:r0 + 704, :], w_dn[i, r0:r0 + 704, :], Bwdn)
            if i == 0:
                wv = w_ci.rearrange("(k p) (s c n) -> c k p s n", p=P, s=3, c=8, n=128)
                for ch in range(8):
                    for k in range(8):
                        conv(wci_s[ch, :, k, :, :], wv[ch, k], Bwci)
                for r0 in range(0, D, 256):
                    conv(wco_s[r0:r0 + 256, :], w_co[r0:r0 + 256, :], Bwco)

        def wload(parts, reads):
            wb, Bw = c.wbuf()
            for (off, a, b, src) in parts:
                dst = wb[:, off:off + a * b].rearrange("p (a b) -> p a b", a=a)
                c.dma("sp", dst, src, Bw, reads=reads, writes=[Bw])
            return wb, Bw

        es1 = contextlib.ExitStack()
        es1.__enter__()
        KT2 = c.sb("KT2", [P, 2, SEQ], BF16, es1); BKT = Buf("KT2")
        kiT = c.sb("kiT", [64, SEQ], BF16, es1); BkiT = Buf("kiT")
        Vaug = c.sb("Vaug", [P, NKT, 4, 65], BF16, es1); BV = Buf("Vaug")
        Isb = c.sb("Isb", [P, SEQ], F32, es1); BI = Buf("Isb")
        msk = c.sb("msk", [P, SEQ], BF16, es1); Bmsk = Buf("msk")
        mskT = c.sb("mskT", [P, NKT, P], BF16, es1); BmT = Buf("mskT")
        QT2 = c.sb("QT2", [P, 8, P], BF16, es1); BQT = Buf("QT2")
        qiT = c.sb("qiT", [64, 16, P], BF16, es1); BqiT = Buf("qiT")
        diag = c.sb("diag", [P, 16, P], BF16, es1); Bdiag = Buf("diag")
        Rb = [(c.sb("R%d" % i, [P, 512], BF16, es1), Buf("R%d" % i)) for i in range(4)]
        Eb = [(c.sb("E%d" % i, [P, 512], BF16, es1), Buf("E%d" % i)) for i in range(2)]
        Pb = [(c.sb("Pm%d" % i, [P, 512], BF16, es1), Buf("Pm%d" % i)) for i in range(2)]
        Yf = c.sb("Yf", [P, 2640], F32, es1); BYf = Buf("Yf")
        Yb = c.sb("Yb", [P, 2640], BF16, es1); BYb = Buf("Yb")
        xt = c.sb("xt", [P, D], F32, es1); Bx = Buf("xt")
        xb = c.sb("xb", [P, D], BF16, es1); Bxb = Buf("xb")
        xT = c.sb("xT", [P, 8, P], BF16, es1); BxT = Buf("xT")
        Obf = c.sb("Obf", [P, D], BF16, es1); BO = Buf("Obf")
        rr = c.sb("rr", [P, D], F32, es1); Brr = Buf("rr")
        yy = rr; Byy = Brr
        gb0 = c.sb("gb0", [P, 2 * D], F32, es1); Bgb0 = Buf("gb0")
        rk = c.sb("rk", [P, NKT, 24], F32, es1); Brk = Buf("rk")
        rq = c.sb("rq", [P, NQT + 1, 24], F32, es1); Brq = Buf("rq")
        qpos = c.sb("qpos", [P, NQT + 1], F32, es1); Bqp = Buf("qpos")
        small = c.sb("small", [P, 256], F32, es1); Bsm = Buf("small")
        tmpr = c.sb("tmpr", [P, 33, 16], F32, es1); Btr = Buf("tmpr")
        wsc = c.sb("wsc", [P, 16], F32, es1); Bwsc = Buf("wsc")
        biasb = c.sb("biasb", [P, 512], F32, es1); Bbias = Buf("biasb")
        ikp = c.sb("ikp", [P, 16, 64], BF16, es1); Bikp = Buf("ikp")
        Kp = c.sb("Kp", [P, 16, 256], BF16, es1); BKp = Buf("Kp")
        Vp = c.sb("Vp", [P, 16, 256], BF16, es1); BVp = Buf("Vp")
        ptall = c.sb("ptall", [P, NSQ * 16], I32, es1); Bptall = Buf("ptall")
        idxall = c.sb("idxall", [P, NSQ * 16], I32, es1); Bidx = Buf("idxall")
        qisb = c.sb("qisb", [64, 64], BF16, es1); Bqisb = Buf("qisb")
        Qsb = c.sb("Qsb", [P, 8, 4], BF16, es1); BQsb = Buf("Qsb")
        KnT2 = c.sb("KnT2", [P, 2, 64], BF16, es1); BKn = Buf("KnT2")
        kinT = c.sb("kinT", [64, 64], BF16, es1); Bkin = Buf("kinT")
        Wht = c.sb("Wht", [64, 16], F32, es1); BWht = Buf("Wht")
        Wsel = c.sb("Wsel", [64, NSQ, 64], BF16, es1); BWsel = Buf("Wsel")
        mselb = c.sb("mselb", [64, NSQ * 64], BF16, es1); Bmsel = Buf("mselb")
        Onb = c.sb("Onb", [16, 4, 64], BF16, es1); BOn = Buf("Onb")
        Vst = c.sb("Vst", [4, 256], F32, es1); BVst = Buf("Vst")

        for t0 in range(0, NKT, 8):
            c.dma("sp", rk[:, t0:t0 + 8, :], ropek[t0 * P:(t0 + 8) * P, :].rearrange("(t p) c -> p t c", p=P), Brk,
                  writes=[Brk], nowaw=True)
        for t0 in range(0, NQT, 6):
            t1 = min(NQT, t0 + 6)
            c.dma("sp", rq[:, t0:t1, :], ropeq[t0 * P:t1 * P, :].rearrange("(t p) c -> p t c", p=P), Brq,
                  writes=[Brq], nowaw=True)
        c.dma("sp", rq[0:TS, NQT, :], ropes[:, :], Brq, writes=[Brq], nowaw=True)
        c.dma("sp", qpos[:], qpos_d[:, :], Bqp, writes=[Bqp])
        c.dma("sp", gb0[:, 0:D], bass.AP(tensor=lng_d.tensor, offset=0, ap=[[0, P], [1, D]]), Bgb0, writes=[Bgb0], nowaw=True)
        c.dma("sp", gb0[:, D:2 * D], bass.AP(tensor=lnb_d.tensor, offset=0, ap=[[0, P], [1, D]]), Bgb0, writes=[Bgb0], nowaw=True)
        c.op("pool", lambda e: e.memset(Vaug[:, :, :, 64:65], 1.0), writes=[BV])

        def front(src, Bsrc, R):
            c.op("act", lambda e: e.copy(out=xb[:R, :], in_=src), reads=[Bsrc], writes=[Bxb])
            pt, Bpt = ptbank()
            for k in range(8):
                c.op("pe", lambda e: e.transpose(pt[:, k * P:k * P + R], xb[:R, k * P:(k + 1) * P], identb[:R, :R]),
                     reads=[Bxb, Bid], writes=[Bpt])
            c.op("dve", lambda e: e.tensor_copy(out=xT[:, :, :R], in_=pt[:, :].rearrange("p (k t) -> p k t", k=8)[:, :, :R]),
                 reads=[Bpt], writes=[BxT])

        def rope(Y, BY, R, col0, H, tb):
            Yv = Y[:R, col0:col0 + 64 * H].rearrange("p (h d) -> p h d", d=64)
            tA = tmpr[:R, 0:H, 0:8]
            tB = tmpr[:R, 0:H, 8:16]
            sn = bc_mid(tb[:, 16:24], H)
            cs = bc_mid(tb[:, 0:16], H)
            c.op("dve", lambda e: e.tensor_tensor(out=tA, in0=Yv[:, :, 8:16], in1=sn, op=ALU.mult), reads=[BY], writes=[Btr])
            c.op("dve", lambda e: e.tensor_tensor(out=tB, in0=Yv[:, :, 0:8], in1=sn, op=ALU.mult), reads=[BY], writes=[Btr])
            c.op("dve", lambda e: e.tensor_tensor(out=Yv[:, :, 0:16], in0=Yv[:, :, 0:16], in1=cs, op=ALU.mult), reads=[BY], writes=[BY])
            c.op("dve", lambda e: e.tensor_tensor(out=Yv[:, :, 0:8], in0=Yv[:, :, 0:8], in1=tA, op=ALU.subtract), reads=[BY, Btr], writes=[BY])
            c.op("dve", lambda e: e.tensor_tensor(out=Yv[:, :, 8:16], in0=Yv[:, :, 8:16], in1=tB, op=ALU.add), reads=[BY, Btr], writes=[BY])

        def proj(R, wsrc, Bwsrc, ncols, ycol0):
            n0 = 0
            while n0 < ncols:
                nn = min(2048, ncols - n0)
                kper = max(1, min(8, WBN // nn))
                tiles = []
                for k0 in range(0, 8, kper):
                    kk = min(kper, 8 - k0)
                    src = wsrc[k0 * P:(k0 + kk) * P, n0:n0 + nn].rearrange("(k p) n -> p k n", p=P)
                    wb, Bw = wload([(0, kk, nn, src)], [Bwsrc])
                    tiles.append((k0, kk, wb, Bw))
                for m0 in range(0, nn, 512):
                    mm = min(512, nn - m0)
                    pm, Bpm = c.bank()
                    for (k0, kk, wb, Bw) in tiles:
                        wv = wb[:, 0:kk * nn].rearrange("p (k n) -> p k n", k=kk)
                        for k in range(kk):
                            c.op("pe", lambda e: e.matmul(pm[:R, 0:mm], lhsT=xT[:, k0 + k, :R], rhs=wv[:, k, m0:m0 + mm],
                                                          start=(k0 + k == 0), stop=(k0 + k == 7)),
                                 reads=[BxT, Bw], writes=[Bpm])
                    c.op("act", lambda e: e.copy(out=Yf[:R, ycol0 + n0 + m0:ycol0 + n0 + m0 + mm], in_=pm[:R, 0:mm]),
                         reads=[Bpm], writes=[BYf])
                n0 += nn

        def layer_norm(src, Bsrc, R, gb, Bgb, dst, Bdst):
            st = small[:R, 0:12]
            mv = small[:R, 12:14]
            sd = small[:R, 14:15]
            rs = small[:R, 15:16]
            c.op("dve", lambda e: e.bn_stats(out=st[:, 0:6], in_=src[:, 0:512]), reads=[Bsrc], writes=[Bsm])
            c.op("dve", lambda e: e.bn_stats(out=st[:, 6:12], in_=src[:, 512:1024]), reads=[Bsrc], writes=[Bsm])
            c.op("dve", lambda e: e.bn_aggr(out=mv, in_=st), reads=[Bsm], writes=[Bsm])
            c.op("act", lambda e: e.activation(out=sd, in_=mv[:, 1:2], func=AF.Sqrt, bias=EPS, scale=1.0), reads=[Bsm], writes=[Bsm])
            c.op("dve", lambda e: e.reciprocal(out=rs, in_=sd), reads=[Bsm], writes=[Bsm])
            c.op("dve", lambda e: e.tensor_scalar(out=dst, in0=src, scalar1=mv[:, 0:1], scalar2=rs, op0=ALU.subtract, op1=ALU.mult),
                 reads=[Bsrc, Bsm], writes=[Bdst])
            c.op("pool", lambda e: e.tensor_tensor(out=dst, in0=dst, in1=gb[:R, 0:D], op=ALU.mult), reads=[Bdst, Bgb], writes=[Bdst])
            c.op("pool", lambda e: e.tensor_tensor(out=dst, in0=dst, in1=gb[:R, D:2 * D], op=ALU.add), reads=[Bdst, Bgb], writes=[Bdst])

        def bisect(R, N):
            lo = small[:R, 16:17]
            w0 = small[:R, 17:18]
            mid = small[:R, 18:19]
            cnt = small[:R, 19:20]
            tt = small[:R, 20:21]
            hw = small[:R, 32:32 + NIT]
            c.op("dve", lambda e: e.tensor_scalar(out=hw, in0=pow2[:R, :], scalar1=w0, scalar2=None, op0=ALU.mult), reads=[Bsm, Bcst], writes=[Bsm])
            for k in range(NIT):
                c.op("dve", lambda e: e.tensor_tensor(out=mid, in0=lo, in1=hw[:, k:k + 1], op=ALU.add), reads=[Bsm], writes=[Bsm])
                c.op("dve", lambda e: e.tensor_scalar(out=msk[:R, :N], in0=Isb[:R, :N], scalar1=mid, scalar2=None, op0=ALU.is_ge,
                                                      op1=ALU.add, accum_out=cnt), reads=[BI, Bsm], writes=[Bmsk, Bsm])
                c.op("dve", lambda e: e.tensor_scalar(out=tt, in0=cnt, scalar1=float(TOPK), scalar2=hw[:, k:k + 1], op0=ALU.is_ge, op1=ALU.mult),
                     reads=[Bsm], writes=[Bsm])
                c.op("dve", lambda e: e.tensor_tensor(out=lo, in0=lo, in1=tt, op=ALU.add), reads=[Bsm], writes=[Bsm])
            c.op("dve", lambda e: e.tensor_scalar(out=msk[:R, :N], in0=Isb[:R, :N], scalar1=lo, scalar2=None, op0=ALU.is_ge),
                 reads=[BI, Bsm], writes=[Bmsk])

        def evac_scores(pI, BpI, R, c0, nn, qp):
            ci = c0 // 512
            qrel = small[:R, 21:22]
            c.op("dve", lambda e: e.tensor_scalar(out=qrel, in0=qp, scalar1=float(-c0), scalar2=None, op0=ALU.add), reads=[Bqp, Bsm], writes=[Bsm])
            c.op("dve", lambda e: e.tensor_reduce(out=small[:R, 100 + ci:101 + ci], in_=pI[:R, 0:nn], axis=AX.X, op=ALU.max), reads=[BpI], writes=[Bsm])
            c.op("dve", lambda e: e.tensor_reduce(out=small[:R, 110 + ci:111 + ci], in_=pI[:R, 0:nn], axis=AX.X, op=ALU.min), reads=[BpI], writes=[Bsm])
            bias = biasb[:R, 0:nn]
            c.op("dve", lambda e: e.tensor_scalar(out=bias, in0=iota[:R, 0:nn], scalar1=qrel, scalar2=NEG, op0=ALU.is_gt, op1=ALU.mult),
                 reads=[Bcst, Bsm], writes=[Bbias])
            c.op("dve", lambda e: e.tensor_tensor(out=Isb[:R, c0:c0 + nn], in0=pI[:R, 0:nn], in1=bias, op=ALU.add), reads=[BpI, Bbias], writes=[BI])

        def bounds(R, nch):
            c.op("dve", lambda e: e.tensor_reduce(out=small[:R, 22:23], in_=small[:R, 100:100 + nch], axis=AX.X, op=ALU.max), reads=[Bsm], writes=[Bsm])
            c.op("dve", lambda e: e.tensor_reduce(out=small[:R, 23:24], in_=small[:R, 110:110 + nch], axis=AX.X, op=ALU.min), reads=[Bsm], writes=[Bsm])
            c.op("dve", lambda e: e.tensor_scalar(out=small[:R, 16:17], in0=small[:R, 23:24], scalar1=-1.0, scalar2=None, op0=ALU.add), reads=[Bsm], writes=[Bsm])
            c.op("dve", lambda e: e.tensor_scalar(out=small[:R, 17:18], in0=small[:R, 22:23], scalar1=small[:R, 23:24], scalar2=2.0,
                                                  op0=ALU.subtract, op1=ALU.add), reads=[Bsm], writes=[Bsm])

        def mask_transposes(R, nkt):
            for t0 in range(0, nkt, 8):
                t1 = min(nkt, t0 + 8)
                pt, Bpt = ptbank()
                for t in range(t0, t1):
                    c.op("pe", lambda e: e.transpose(pt[:, (t - t0) * P:(t - t0) * P + R], msk[:R, t * P:(t + 1) * P], identb[:R, :R]),
                         reads=[Bmsk, Bid], writes=[Bpt])
                c.op("act", lambda e: e.copy(out=mskT[:, t0:t1, :R], in_=pt[:, 0:(t1 - t0) * P].rearrange("p (a b) -> p a b", b=P)[:, :, :R]),
                     reads=[Bpt], writes=[BmT])

        def tail(R, xsrc, Bxsrc, row0):
            pt, Bpt = ptbank()
            for k in range(8):
                c.op("pe", lambda e: e.transpose(pt[:, k * P:k * P + R], Obf[:R, k * P:(k + 1) * P], identb[:R, :R]),
                     reads=[BO, Bid], writes=[Bpt])
            c.op("dve", lambda e: e.tensor_copy(out=xT[:, :, :R], in_=pt[:, :].rearrange("p (k t) -> p k t", k=8)[:, :, :R]),
                 reads=[Bpt], writes=[BxT])
            tiles = []
            for k0 in (0, 4):
                src = wo_s[k0 * P:(k0 + 4) * P, :].rearrange("(k p) n -> p k n", p=P)
                wb, Bw = wload([(0, 4, D, src)], [Bwo])
                tiles.append((k0, wb, Bw))
            for m0 in (0, 512):
                pm, Bpm = c.bank()
                for (k0, wb, Bw) in tiles:
                    wv = wb[:, 0:4 * D].rearrange("p (k n) -> p k n", k=4)
                    for k in range(4):
                        c.op("pe", lambda e: e.matmul(pm[:R, :], lhsT=xT[:, k0 + k, :R], rhs=wv[:, k, m0:m0 + 512],
                                                      start=(k0 + k == 0), stop=(k0 + k == 7)), reads=[BxT, Bw], writes=[Bpm])
                c.op("dve", lambda e: e.scalar_tensor_tensor(out=rr[:R, m0:m0 + 512], in0=xsrc[:, m0:m0 + 512], scalar=ALPHA, in1=pm[:R, :],
                                                             op0=ALU.mult, op1=ALU.add), reads=[Bxsrc, Bpm], writes=[Brr])
            layer_norm(rr[:R, :], Brr, R, gb0, Bgb0, yy[:R, :], Byy)
            c.dma("sp", y1_s[row0:row0 + R, :], yy[:R, :], Byy, reads=[Byy])
            if stage < 3:
                if row0 < TQ:
                    c.dma("sp", o_y[row0:row0 + R, :], yy[:R, :], Byy, reads=[Byy])
                else:
                    c.dma("sp", o_ys[:, :], yy[:R, :], Byy, reads=[Byy])


        def rest_phase():
            es2 = contextlib.ExitStack()
            es2.__enter__()
            ya = c.sb("ya", [P, 4, D], F32, es2); Bya = [Buf("ya%d" % t) for t in range(4)]
            yb_ = c.sb("ybb", [P, 4, D], F32, es2); Byb = [Buf("yb%d" % t) for t in range(4)]
            yT = c.sb("yT", [P, 8, 512], BF16, es2); ByT = Buf("yT")
            hT = c.sb("hT", [P, 22, 512], BF16, es2); BhT = Buf("hT")
            aex = [(c.sb("aex%d" % i, [P, 520], F32, es2), Buf("aex%d" % i)) for i in range(2)]
            uub = [(c.sb("uu%d" % i, [P, 512], F32, es2), Buf("uu%d" % i)) for i in range(2)]
            silb = [(c.sb("sil%d" % i, [P, 512], F32, es2), Buf("sil%d" % i)) for i in range(2)]
            xb2 = c.sb("xb2", [P, D], BF16, es2); Bxb2 = Buf("xb2")
            gbs = [(c.sb("gb%d" % i, [P, 2 * D], F32, es2), Buf("gb%d" % i)) for i in (1, 2, 3)]
            halo_f = c.sb("halo_f", [P, 2, 22, 2], F32, es2); Bhf = Buf("halo_f")
            halo_c = c.sb("halo_c", [P, 8, 2], F32, es2); Bhc = Buf("halo_c")
            prm = c.sb("prm", [P, 22, 11], F32, es2)
            Bprm = Buf("prm")
            sext = c.sb("sext", [P, 22, 32], F32, es2); Bsext = Buf("sext")
            sout = c.sb("sout", [P, 22, 32], F32, es2); Bsout = Buf("sout")
            stg = c.sb("stg", [32, DFF], F32, es2); Bstg = Buf("stg")
            small2 = c.sb("small2", [P, 32], F32, es2); Bsm2 = Buf("small2")

            for li in (1, 2, 3):
                gbt, Bg = gbs[li - 1]
                c.dma("sp", gbt[:, 0:D], bass.AP(tensor=lng_d.tensor, offset=li * D, ap=[[0, P], [1, D]]), Bg, writes=[Bg], nowaw=True)
                c.dma("sp", gbt[:, D:2 * D], bass.AP(tensor=lnb_d.tensor, offset=li * D, ap=[[0, P], [1, D]]), Bg, writes=[Bg], nowaw=True)
            c.op("dve", lambda e: e.memset(stg[:, :], 0.0), writes=[Bstg])
            c.dma("sp", stg[0:6, :], fcw_d.rearrange("i j n -> (i j) n"), Bstg, reads=[Bstg], writes=[Bstg])
            c.dma("sp", stg[6:8, :], fcb_d[:, :], Bstg, reads=[Bstg], writes=[Bstg], nowaw=True)
            c.dma("sp", stg[8:11, 0:D], cw_d[:, :], Bstg, reads=[Bstg], writes=[Bstg], nowaw=True)
            pmp, Bpmp = c.bank()
            for ch in range(22):
                c.op("pe", lambda e: e.transpose(pmp[:, ch * 11:(ch + 1) * 11], stg[0:11, ch * P:(ch + 1) * P], identf[0:11, 0:11]),
                     reads=[Bstg, Bcst], writes=[Bpmp])
            c.op("act", lambda e: e.copy(out=prm[:, :, :], in_=pmp[:, 0:242].rearrange("p (a b) -> p a b", b=11)), reads=[Bpmp], writes=[Bprm])
            c.op("dve", lambda e: e.memset(halo_f[:, :, :, :], 0.0), writes=[Bhf])
            c.op("dve", lambda e: e.memset(halo_c[:, :, :], 0.0), writes=[Bhc])

            def ln2(src, Bsrc, R, gb, Bgb):
                st = small2[:R, 0:12]; mv = small2[:R, 12:14]; sd = small2[:R, 14:15]; rs = small2[:R, 15:16]
                c.op("dve", lambda e: e.bn_stats(out=st[:, 0:6], in_=src[:, 0:512]), reads=[Bsrc], writes=[Bsm2])
                c.op("dve", lambda e: e.bn_stats(out=st[:, 6:12], in_=src[:, 512:1024]), reads=[Bsrc], writes=[Bsm2])
                c.op("dve", lambda e: e.bn_aggr(out=mv, in_=st), reads=[Bsm2], writes=[Bsm2])
                c.op("act", lambda e: e.activation(out=sd, in_=mv[:, 1:2], func=AF.Sqrt, bias=EPS, scale=1.0), reads=[Bsm2], writes=[Bsm2])
                c.op("dve", lambda e: e.reciprocal(out=rs, in_=sd), reads=[Bsm2], writes=[Bsm2])
                c.op("dve", lambda e: e.tensor_scalar(out=src, in0=src, scalar1=mv[:, 0:1], scalar2=rs, op0=ALU.subtract, op1=ALU.mult),
                     reads=[Bsrc, Bsm2], writes=[Bsrc])
                c.op("pool", lambda e: e.tensor_tensor(out=src, in0=src, in1=gb[:R, 0:D], op=ALU.mult), reads=[Bsrc, Bgb], writes=[Bsrc])
                c.op("pool", lambda e: e.tensor_tensor(out=src, in0=src, in1=gb[:R, D:2 * D], op=ALU.add), reads=[Bsrc, Bgb], writes=[Bsrc])

            def to_featT(Y, BY, nt, R):
                for t in range(nt):
                    c.op("act", lambda e: e.copy(out=xb2[:R, :], in_=Y[:R, t, :]), reads=[BY[t]], writes=[Bxb2])
                    pt, Bpt = ptbank()
                    for k in range(8):
                        c.op("pe", lambda e: e.transpose(pt[:, k * P:k * P + R], xb2[:R, k * P:(k + 1) * P], identb[:R, :R]),
                             reads=[Bxb2, Bid], writes=[Bpt])
                    c.op("dve", lambda e: e.tensor_copy(out=yT[:, :, t * P:t * P + R], in_=pt[:, :].rearrange("p (k t) -> p k t", k=8)[:, :, :R]),
                         reads=[Bpt], writes=[ByT])

            def conv3(ae, Bae, N, samp, w0, w1, w2, uu, Buu):
                if samp:
                    av = ae[:, 0:96].rearrange("p (b t) -> p b t", t=6)
                    uv = uu[:, 0:64].rearrange("p (b t) -> p b t", t=4)
                    s0, s1, s2 = av[:, :, 0:4], av[:, :, 1:5], av[:, :, 2:6]
                else:
                    uv = uu[:, 0:N]
                    s0, s1, s2 = ae[:, 0:N], ae[:, 1:N + 1], ae[:, 2:N + 2]
                c.op("dve", lambda e: e.tensor_scalar(out=uv, in0=s0, scalar1=w0, scalar2=None, op0=ALU.mult), reads=[Bae, Bprm], writes=[Buu])
                c.op("dve", lambda e: e.scalar_tensor_tensor(out=uv, in0=s1, scalar=w1, in1=uv, op0=ALU.mult, op1=ALU.add), reads=[Bae, Bprm, Buu], writes=[Buu])
                c.op("dve", lambda e: e.scalar_tensor_tensor(out=uv, in0=s2, scalar=w2, in1=uv, op0=ALU.mult, op1=ALU.add), reads=[Bae, Bprm, Buu], writes=[Buu])

            def load_state_T(src_dram, nch):
                c.dma("sp", stg[:, 0:nch * P], src_dram, Bstg, writes=[Bstg])
                for c0 in range(0, nch, 16):
                    c1 = min(nch, c0 + 16)
                    pm, Bpm = c.bank()
                    for ch in range(c0, c1):
                        c.op("pe", lambda e: e.transpose(pm[:, (ch - c0) * 32:(ch - c0 + 1) * 32], stg[:, ch * P:(ch + 1) * P], identf[0:32, 0:32]),
                             reads=[Bstg, Bcst], writes=[Bpm])
                    c.op("act", lambda e: e.copy(out=sext[:, c0:c1, :], in_=pm[:, 0:(c1 - c0) * 32].rearrange("p (a b) -> p a b", b=32)),
                         reads=[Bpm], writes=[Bsext])

            def store_state_T(src, Bsrc, nch, ncol, dst_dram):
                for c0 in range(0, nch, 4):
                    c1 = min(nch, c0 + 4)
                    pm, Bpm = c.bank()
                    for ch in range(c0, c1):
                        c.op("pe", lambda e: e.transpose(pm[0:ncol, (ch - c0) * P:(ch - c0 + 1) * P], src[:, ch, :], identf[:, :]),
                             reads=[Bsrc, Bcst], writes=[Bpm])
                    c.op("act", lambda e: e.copy(out=stg[0:ncol, c0 * P:c1 * P], in_=pm[0:ncol, 0:(c1 - c0) * P]), reads=[Bpm], writes=[Bstg])
                c.dma("sp", dst_dram, stg[0:ncol, 0:nch * P], Bstg, reads=[Bstg])

            def ffn(i, Yin, BYin, Yout, BYout, nt, R, samp, last, gb, Bgb):
                N = R if samp else nt * P
                if samp:
                    load_state_T(stf_d[i, :, :], 22)
                ri = 0
                for cb in range(NCB):
                    wb, Bw = wload([(0, 8, 512, wup_s[i, cb].rearrange("p k s n -> p k (s n)"))], [Bwup])
                    wv = wb[:, 0:4096].rearrange("p (k s n) -> p k s n", k=8, s=2)
                    bks = [[c.bank() for _ in range(2)] for _ in range(2)]
                    for s_ in range(2):
                        for hf in range(2):
                            pm, Bpm = bks[s_][hf]
                            for k in range(8):
                                c.op("pe", lambda e: e.matmul(pm[:, 0:N], lhsT=wv[:, k, s_, hf * P:(hf + 1) * P], rhs=yT[:, k, 0:N],
                                                              start=(k == 0), stop=(k == 7)), reads=[Bw, ByT], writes=[Bpm])
                    for hf in range(2):
                        ch = 2 * cb + hf
                        pa, Bpa = bks[0][hf]
                        pg, Bpg = bks[1][hf]
                        ae, Bae = aex[ri % 2]; uu, Buu = uub[ri % 2]; sl, Bsl = silb[ri % 2]
                        ri += 1
                        if samp:
                            av = ae[:, 0:96].rearrange("p (b t) -> p b t", t=6)
                            c.op("dve", lambda e: e.tensor_copy(out=av[:, :, 0:2], in_=sext[:, ch, :].rearrange("p (b r) -> p b r", r=2)),
                                 reads=[Bsext], writes=[Bae])
                            c.op("act", lambda e: e.copy(out=av[:, :, 2:6], in_=pa[:, 0:64].rearrange("p (b t) -> p b t", t=4)), reads=[Bpa], writes=[Bae])
                            c.op("dve", lambda e: e.tensor_copy(out=sout[:, ch, :].rearrange("p (b r) -> p b r", r=2), in_=av[:, :, 4:6]),
                                 reads=[Bae], writes=[Bsout])
                        else:
                            c.op("dve", lambda e: e.tensor_copy(out=ae[:, 0:2], in_=halo_f[:, i, ch, :]), reads=[Bhf], writes=[Bae])
                            c.op("act", lambda e: e.copy(out=ae[:, 2:2 + N], in_=pa[:, 0:N]), reads=[Bpa], writes=[Bae])
                            c.op("dve", lambda e: e.tensor_copy(out=halo_f[:, i, ch, :], in_=ae[:, N:N + 2]), reads=[Bae], writes=[Bhf])
                        conv3(ae, Bae, N, samp, prm[:, ch, 3 * i:3 * i + 1], prm[:, ch, 3 * i + 1:3 * i + 2], prm[:, ch, 3 * i + 2:3 * i + 3], uu, Buu)
                        c.op("act", lambda e: e.activation(out=sl[:, 0:N], in_=uu[:, 0:N], func=AF.Silu, bias=prm[:, ch, 6 + i:7 + i], scale=1.0),
                             reads=[Buu, Bprm], writes=[Bsl])
                        c.op("dve", lambda e: e.tensor_tensor(out=hT[:, ch, 0:N], in0=sl[:, 0:N], in1=pg[:, 0:N], op=ALU.mult),
                             reads=[Bsl, Bpg], writes=[BhT])
                for m0 in (0, 512):
                    bks = [c.bank() for _ in range(nt)]
                    for c0 in range(0, 22, 4):
                        cc = min(4, 22 - c0)
                        src = wdn_s[i, c0 * P:(c0 + cc) * P, m0:m0 + 512].rearrange("(c p) n -> p c n", p=P)
                        wb, Bw = wload([(0, cc, 512, src)], [Bwdn])
                        wv = wb[:, 0:cc * 512].rearrange("p (c n) -> p c n", c=cc)
                        for t in range(nt):
                            pm, Bpm = bks[t]
                            for cj in range(cc):
                                c.op("pe", lambda e: e.matmul(pm[:R, :], lhsT=hT[:, c0 + cj, t * P:t * P + R], rhs=wv[:, cj, :],
                                                              start=(c0 + cj == 0), stop=(c0 + cj == 21)), reads=[BhT, Bw], writes=[Bpm])
                    for t in range(nt):
                        pm, Bpm = bks[t]
                        c.op("dve", lambda e: e.scalar_tensor_tensor(out=Yout[:R, t, m0:m0 + 512], in0=Yin[:R, t, m0:m0 + 512], scalar=ALPHA,
                                                                     in1=pm[:R, :], op0=ALU.mult, op1=ALU.add), reads=[BYin[t], Bpm], writes=[BYout[t]])
                for t in range(nt):
                    ln2(Yout[:R, t, :], BYout[t], R, gb, Bgb)
                if samp:
                    store_state_T(sout, Bsout, 22, 32, o_fs[i, :, :])
                elif last:
                    store_state_T(halo_f[:, i, :, :], Bhf, 22, 2, o_fp[i, :, :])

            def mixer(Yin, BYin, Yout, BYout, nt, R, samp, last, gb, Bgb):
                N = R if samp else nt * P
                if samp:
                    load_state_T(stc_d[:, :], 8)
                ri = 0
                for ch in range(8):
                    wb, Bw = wload([(0, 8, 384, wci_s[ch].rearrange("p k s n -> p k (s n)"))], [Bwci])
                    wv = wb[:, 0:3072].rearrange("p (k s n) -> p k s n", k=8, s=3)
                    bks = [c.bank() for _ in range(3)]
                    for s_ in range(3):
                        pm, Bpm = bks[s_]
                        for k in range(8):
                            c.op("pe", lambda e: e.matmul(pm[:, 0:N], lhsT=wv[:, k, s_, :], rhs=yT[:, k, 0:N], start=(k == 0), stop=(k == 7)),
                                 reads=[Bw, ByT], writes=[Bpm])
                    (pb_, Bpb_), (pc_, Bpc_), (pu_, Bpu_) = bks
                    ae, Bae = aex[ri % 2]; uu, Buu = uub[ri % 2]
                    ri += 1
                    if samp:
                        av = ae[:, 0:96].rearrange("p (b t) -> p b t", t=6)
                        c.op("dve", lambda e: e.tensor_copy(out=av[:, :, 0:2], in_=sext[:, ch, :].rearrange("p (b r) -> p b r", r=2)), reads=[Bsext], writes=[Bae])
                        c.op("act", lambda e: e.copy(out=av[:, :, 2:6], in_=pc_[:, 0:64].rearrange("p (b t) -> p b t", t=4)), reads=[Bpc_], writes=[Bae])
                        c.op("dve", lambda e: e.tensor_tensor(out=av[:, :, 2:6], in0=av[:, :, 2:6], in1=pu_[:, 0:64].rearrange("p (b t) -> p b t", t=4), op=ALU.mult),
                             reads=[Bae, Bpu_], writes=[Bae])
                        c.op("dve", lambda e: e.tensor_copy(out=sout[:, ch, :].rearrange("p (b r) -> p b r", r=2), in_=av[:, :, 4:6]), reads=[Bae], writes=[Bsout])
                    else:
                        c.op("dve", lambda e: e.tensor_copy(out=ae[:, 0:2], in_=halo_c[:, ch, :]), reads=[Bhc], writes=[Bae])
                        c.op("act", lambda e: e.copy(out=ae[:, 2:2 + N], in_=pc_[:, 0:N]), reads=[Bpc_], writes=[Bae])
                        c.op("dve", lambda e: e.tensor_tensor(out=ae[:, 2:2 + N], in0=ae[:, 2:2 + N], in1=pu_[:, 0:N], op=ALU.mult), reads=[Bae, Bpu_], writes=[Bae])
                        c.op("dve", lambda e: e.tensor_copy(out=halo_c[:, ch, :], in_=ae[:, N:N + 2]), reads=[Bae], writes=[Bhc])
                    conv3(ae, Bae, N, samp, prm[:, ch, 8:9], prm[:, ch, 9:10], prm[:, ch, 10:11], uu, Buu)
                    c.op("dve", lambda e: e.tensor_tensor(out=hT[:, ch, 0:N], in0=uu[:, 0:N], in1=pb_[:, 0:N], op=ALU.mult), reads=[Buu, Bpb_], writes=[BhT])
                tiles = []
                for k0 in (0, 4):
                    src = wco_s[k0 * P:(k0 + 4) * P, :].rearrange("(k p) n -> p k n", p=P)
                    wb, Bw = wload([(0, 4, D, src)], [Bwco])
                    tiles.append((k0, wb, Bw))
                for t in range(nt):
                    for m0 in (0, 512):
                        pm, Bpm = c.bank()
                        for (k0, wb, Bw) in tiles:
                            wv = wb[:, 0:4 * D].rearrange("p (k n) -> p k n", k=4)
                            for k in range(4):
                                c.op("pe", lambda e: e.matmul(pm[:R, :], lhsT=hT[:, k0 + k, t * P:t * P + R], rhs=wv[:, k, m0:m0 + 512],
                                                              start=(k0 + k == 0), stop=(k0 + k == 7)), reads=[BhT, Bw], writes=[Bpm])
                        c.op("dve", lambda e: e.scalar_tensor_tensor(out=Yout[:R, t, m0:m0 + 512], in0=Yin[:R, t, m0:m0 + 512], scalar=ALPHA,
                                                                     in1=pm[:R, :], op0=ALU.mult, op1=ALU.add), reads=[BYin[t], Bpm], writes=[BYout[t]])
                    ln2(Yout[:R, t, :], BYout[t], R, gb, Bgb)
                if samp:
                    store_state_T(sout, Bsout, 8, 32, o_cs[:, :])
                elif last:
                    store_state_T(halo_c[:, :, :], Bhc, 8, 2, o_cp[:, :])

            glist = [(t0 * P, nt, P, False, gi == len(GROUPS) - 1) for gi, (t0, nt) in enumerate(GROUPS)] + [(TQ, 1, TS, True, False)]
            for (row0, nt, R, samp, last) in glist:
                for t in range(nt):
                    c.dma("sp", ya[:R, t, :], y1_s[row0 + t * P:row0 + t * P + R, :], Bya[t], writes=[Bya[t]])
                to_featT(ya, Bya, nt, R)
                ffn(0, ya, Bya, yb_, Byb, nt, R, samp, last, gbs[0][0], gbs[0][1])
                to_featT(yb_, Byb, nt, R)
                mixer(yb_, Byb, ya, Bya, nt, R, samp, last, gbs[1][0], gbs[1][1])
                to_featT(ya, Bya, nt, R)
                ffn(1, ya, Bya, yb_, Byb, nt, R, samp, last, gbs[2][0], gbs[2][1])
                for t in range(nt):
                    dst = o_ys[:, :] if samp else o_y[row0 + t * P:row0 + (t + 1) * P, :]
                    c.dma("sp", dst, yb_[:R, t, :], Byb[t], reads=[Byb[t]])
            c.barrier()
            es2.__exit__(None, None, None)

        for kt in range(NKT):
            c.dma("sp", xt[:], xk[kt * P:(kt + 1) * P, :], Bx, writes=[Bx])
            drain(4)
            front(xt[:, :], Bx, P)
            proj(P, wkv_s, Bwkv, 576, 0)
            rope(Yf, BYf, P, 0, 4, rk[:, kt, :])
            rope(Yf, BYf, P, 512, 1, rk[:, kt, :])
            rows = slice(kt * P, (kt + 1) * P)
            c.dma("sp", o_k[rows, :], Yf[:, 0:256], BYf, reads=[BYf])
            c.dma("sp", o_v[rows, :], Yf[:, 256:512], BYf, reads=[BYf])
            c.dma("sp", o_ik[rows, :], Yf[:, 512:576], BYf, reads=[BYf])
            c.op("pool", lambda e: e.tensor_copy(out=Yb[:, 0:576], in_=Yf[:, 0:576]), reads=[BYf], writes=[BYb])
            c.op("pool", lambda e: e.tensor_copy(out=Vaug[:, kt, :, 0:64], in_=Yf[:, 256:512].rearrange("p (g d) -> p g d", d=64)),
                 reads=[BYf], writes=[BV])
            pt, Bpt = ptbank()
            for gp in range(2):
                c.op("pe", lambda e: e.transpose(pt[:, gp * P:(gp + 1) * P], Yb[:, gp * P:(gp + 1) * P], identb[:, :]), reads=[BYb, Bid], writes=[Bpt])
            c.op("pe", lambda e: e.transpose(pt[0:64, 2 * P:3 * P], Yb[:, 512:576], identb[:, :]), reads=[BYb, Bid], writes=[Bpt])
            c.op("act", lambda e: e.copy(out=KT2[:, :, kt * P:(kt + 1) * P], in_=pt[:, 0:2 * P].rearrange("p (a b) -> p a b", b=P)),
                 reads=[Bpt], writes=[BKT])
            c.op("act", lambda e: e.copy(out=kiT[:, kt * P:(kt + 1) * P], in_=pt[0:64, 2 * P:3 * P]), reads=[Bpt], writes=[BkiT])

        def q_front(R, tb):
            rope(Yf, BYf, R, 0, 32, tb)
            for gp in range(2):
                c.op("pool", lambda e: e.tensor_copy(
                    out=Yb[:R, gp * 512:(gp + 1) * 512].rearrange("p (h r d) -> p h r d", h=4, r=2),
                    in_=Yf[:R, gp * 512:(gp + 1) * 512].rearrange("p (r h d) -> p h r d", h=4, r=2)), reads=[BYf], writes=[BYb])
            c.op("pool", lambda e: e.tensor_copy(out=Yb[:R, 1024:2048], in_=Yf[:R, 1024:2048]), reads=[BYf], writes=[BYb])
            c.op("dve", lambda e: e.tensor_scalar(out=wsc[:R, :], in0=Yf[:R, 2048:2064], scalar1=IDX_SCALE, scalar2=None, op0=ALU.mult),
                 reads=[BYf], writes=[Bwsc])

        def q_transposes(R):
            pt, Bpt = ptbank()
            for gp in range(2):
                for hh in range(4):
                    idx = gp * 4 + hh
                    src = Yb[:R, idx * P:(idx + 1) * P]
                    c.op("pe", lambda e: e.transpose(pt[:, idx * P:idx * P + R], src, identb[:R, :R]), reads=[BYb, Bid], writes=[Bpt])
            c.op("act", lambda e: e.copy(out=QT2[:, :, :R], in_=pt[:, :].rearrange("p (a b) -> p a b", b=P)[:, :, :R]), reads=[Bpt], writes=[BQT])
            for h0 in (0, 8):
                pt, Bpt = ptbank()
                for h in range(8):
                    col = 1024 + (h0 + h) * 64
                    c.op("pe", lambda e: e.transpose(pt[0:64, h * P:h * P + R], Yb[:R, col:col + 64], identb[:R, :R]), reads=[BYb, Bid], writes=[Bpt])
                c.op("act", lambda e: e.copy(out=qiT[:, h0:h0 + 8, :R], in_=pt[0:64, :].rearrange("p (a b) -> p a b", b=P)[:, :, :R]),
                     reads=[Bpt], writes=[BqiT])

        import os as _os
        SUB = int(_os.environ.get("DBG_SUB", "99"))
        NQR = int(_os.environ.get("DBG_NQ", str(NQT)))
        def attention(NK, maskfn, Bmask, Od, BOd):
            obanks = [c.banks[0], c.banks[1], c.banks[2]]
            for (ob, Bob) in obanks:
                c.op("pe", lambda e: e.matmul(ob[:, :], lhsT=zerob[:, 0:P], rhs=zerob[:, :], start=True, stop=False),
                     reads=[Bzero], writes=[Bob])
            ei = 0
            for kt in range(NK):
                for g in range(4):
                    pS, BpS = c.bank((3, 4, 5))
                    pb = (g % 2) * 64
                    c.op("pe", lambda e: e.matmul(pS[:, :], lhsT=KT2[pb:pb + 64, g // 2, kt * P:(kt + 1) * P],
                                                  rhs=QT2[pb:pb + 64, (g // 2) * 4:(g // 2) * 4 + 4, :], start=True, stop=True),
                         reads=[BKT, BQT], writes=[BpS])
                    E_, BE_ = Eb[ei % 2]
                    Pm_, BPm_ = Pb[ei % 2]
                    ei += 1
                    c.op("act", lambda e: e.activation(out=E_[:, :], in_=pS[:, :], func=AF.Exp, scale=0.125), reads=[BpS], writes=[BE_])
                    c.op("pool", lambda e: e.tensor_tensor(out=Pm_[:, :].rearrange("p (a b) -> p a b", a=4),
                                                           in0=E_[:, :].rearrange("p (a b) -> p a b", a=4),
                                                           in1=bc_mid(maskfn(kt), 4), op=ALU.mult), reads=[BE_, Bmask], writes=[BPm_])
                    for hh in range(4):
                        h = 4 * g + hh
                        ob, Bob = obanks[h // 7]
                        oc = (h % 7) * 65
                        c.op("pe", lambda e: e.matmul(ob[:, oc:oc + 65], lhsT=Pm_[:, hh * P:(hh + 1) * P], rhs=Vaug[:, kt, g, :],
                                                      start=False, stop=(kt == NK - 1 and (h % 7 == 6 or h == 15))), reads=[BPm_, BV], writes=[Bob])
            for bi, (ob, Bob) in enumerate(obanks):
                nh = 7 if bi < 2 else 2
                ov = ob[:, 0:nh * 65].rearrange("p (h d) -> p h d", d=65)
                rec = small[:, 200 + 7 * bi:200 + 7 * bi + nh]
                c.op("dve", lambda e: e.reciprocal(out=rec, in_=ov[:, :, 64]), reads=[Bob], writes=[Bsm])
                c.op("dve", lambda e: e.tensor_tensor(out=Od[:, bi * 7 * 64:(bi * 7 + nh) * 64].rearrange("p (h d) -> p h d", d=64),
                                                      in0=ov[:, :, 0:64], in1=bc_last(rec, 64), op=ALU.mult), reads=[Bob, Bsm], writes=[BOd])

        if stage >= 1:
            for j in range(NQR):
                NK = 16 + j
                N = NK * P
                c.dma("sp", xt[:], xq[j * P:(j + 1) * P, :], Bx, writes=[Bx])
                drain(8)
                front(xt[:, :], Bx, P)
                proj(P, wq_s, Bwq, 2064, 0)
                q_front(P, rq[:, j, :])
                if SUB < 1:
                    continue
                q_transposes(P)
                c.op("dve", lambda e: e.tensor_tensor(out=diag[:, :, :], in0=bc_mid(identb[:, :], 16), in1=bc_last(wsc[:, :], P), op=ALU.mult),
                     reads=[Bid, Bwsc], writes=[Bdiag])
                if SUB < 2:
                    continue
                ri = 0
                nch = (N + 511) // 512
                for ci in range(nch):
                    c0 = ci * 512
                    nn = min(512, N - c0)
                    pI, BpI = c.bank((4, 5))
                    for h in range(16):
                        psc, Bpsc = c.bank((0, 1, 2, 3))
                        c.op("pe", lambda e: e.matmul(psc[:, 0:nn], lhsT=qiT[:, h, :], rhs=kiT[:, c0:c0 + nn], start=True, stop=True),
                             reads=[BqiT, BkiT], writes=[Bpsc])
                        R_, BR_ = Rb[ri % 4]
                        ri += 1
                        c.op("act", lambda e: e.activation(out=R_[:, 0:nn], in_=psc[:, 0:nn], func=AF.Relu), reads=[Bpsc], writes=[BR_])
                        c.op("pe", lambda e: e.matmul(pI[:, 0:nn], lhsT=diag[:, h, :], rhs=R_[:, 0:nn], start=(h == 0), stop=(h == 15)),
                             reads=[Bdiag, BR_], writes=[BpI])
                    evac_scores(pI, BpI, P, c0, nn, qpos[:, j:j + 1])
                if SUB < 3:
                    continue
                bounds(P, nch)
                bisect(P, N)
                if SUB < 4:
                    continue
                mask_transposes(P, NK)
                if SUB < 5:
                    continue
                attention(NK, lambda kt: mskT[:, kt, :], BmT, Obf, BO)
                if SUB < 6:
                    continue
                tail(P, xt[:, :], Bx, j * P)
                if _os.environ.get("DBG_DUMP") and j == 0:
                    c.dma("sp", o_y[128:256, :], Isb[:, 0:1024], BI, reads=[BI])
                    c.dma("sp", o_y[256:384, 0:256], small[:, :], Bsm, reads=[Bsm])
                    c.dma("sp", o_y[384:512, :], rr[:, :], Brr, reads=[Brr])


        def sample_phase():
            R = TS
            IOA = bass.IndirectOffsetOnAxis
            c.dma("sp", xt[:R, :], xs[:, :], Bx, writes=[Bx])
            c.dma("pool", mselb[:, :], msel_d[:, :], Bmsel, writes=[Bmsel])
            c.dma("sp", ptall[:, :], bass.AP(tensor=pt_d.tensor, offset=0, ap=[[0, P], [1, NSQ * 16]]), Bptall, writes=[Bptall])
            c.op("dve", lambda e: e.tensor_scalar(out=idxall[:, :], in0=ptall[:, :], scalar1=128.0, scalar2=pidx, op0=ALU.mult, op1=ALU.add),
                 reads=[Bptall, Bcst], writes=[Bidx])
            front(xt[:R, :], Bx, R)
            proj(R, wq_s, Bwq, 2064, 0)
            proj(R, wkv_s, Bwkv, 576, 2064)
            tb = rq[0:R, NQT, :]
            q_front(R, tb)
            rope(Yf, BYf, R, 2064, 4, tb)
            rope(Yf, BYf, R, 2064 + 512, 1, tb)
            c.dma("sp", o_ks[:, :], Yf[:R, 2064:2320], BYf, reads=[BYf])
            c.dma("sp", o_vs[:, :], Yf[:R, 2320:2576], BYf, reads=[BYf])
            c.dma("sp", o_iks[:, :], Yf[:R, 2576:2640], BYf, reads=[BYf])
            c.op("pool", lambda e: e.tensor_copy(out=Yb[:R, 2064:2640], in_=Yf[:R, 2064:2640]), reads=[BYf], writes=[BYb])
            q_transposes(R)
            pt, Bpt = ptbank()
            for gp in range(2):
                c.op("pe", lambda e: e.transpose(pt[:, gp * P:gp * P + R], Yb[:R, 2064 + gp * P:2064 + (gp + 1) * P], identb[:R, :R]),
                     reads=[BYb, Bid], writes=[Bpt])
            c.op("pe", lambda e: e.transpose(pt[0:64, 2 * P:2 * P + R], Yb[:R, 2576:2640], identb[:R, :R]), reads=[BYb, Bid], writes=[Bpt])
            c.op("act", lambda e: e.copy(out=KnT2[:, :, :], in_=pt[:, 0:2 * P].rearrange("p (a b) -> p a b", b=P)[:, :, 0:R]), reads=[Bpt], writes=[BKn])
            c.op("act", lambda e: e.copy(out=kinT[:, :], in_=pt[0:64, 2 * P:2 * P + R]), reads=[Bpt], writes=[Bkin])
            Bwsd = Buf("wsd")
            c.dma("sp", wsd_s[:, :], wsc[:R, :], Bwsd, reads=[Bwsc], writes=[Bwsd])
            for h in range(16):
                src = bass.AP(tensor=wsd_s.tensor, offset=h, ap=[[16, 4], [64, NSQ]])
                c.dma("sp", Wht[4 * h:4 * h + 4, :], src, BWht, reads=[Bwsd], writes=[BWht], nowaw=(h > 0), slow=True)
            c.op("dve", lambda e: e.tensor_tensor(out=Wsel[:, :, :], in0=mselb[:, :].rearrange("p (a b) -> p a b", b=64), in1=bc_last(Wht[:, :], 64), op=ALU.mult),
                 reads=[Bmsel, BWht], writes=[BWsel])
            c.op("dve", lambda e: e.memset(kiT[:, PAST:PAST + P], 0.0), writes=[BkiT])
            c.op("dve", lambda e: e.memset(KT2[:, :, PAST:PAST + P], 0.0), writes=[BKT])
            c.op("pool", lambda e: e.memset(Vaug[:, 16, :, 0:64], 0.0), writes=[BV])
            nch = 5
            SS = float(_os.environ.get("DBG_SS", "99"))
            if SS < 1:
                return
            for b in range(NSQ):
                for j in range(16):
                    col = b * 16 + j
                    c.dma("pool", ikp[:, j, :], cik_d[:, :], Bikp, reads=[Bidx], writes=[Bikp], nowaw=(j > 0),
                          indirect=IOA(ap=idxall[:, col:col + 1], axis=0))
                for j0 in (0, 8):
                    pt, Bpt = ptbank()
                    for j in range(8):
                        c.op("pe", lambda e: e.transpose(pt[0:64, j * P:(j + 1) * P], ikp[:, j0 + j, :], identb[:, :]), reads=[Bikp, Bid], writes=[Bpt])
                    c.op("act", lambda e: e.copy(out=kiT[:, j0 * P:(j0 + 8) * P], in_=pt[0:64, :]), reads=[Bpt], writes=[BkiT])
                c.op("dve", lambda e: e.tensor_copy(out=kiT[:, PAST:PAST + 4], in_=kinT[:, 4 * b:4 * b + 4]), reads=[Bkin], writes=[BkiT])
                c.op("dve", lambda e: e.tensor_copy(out=qisb[:, :].rearrange("p (h t) -> p h t", t=4), in_=qiT[:, :, 4 * b:4 * b + 4]), reads=[BqiT], writes=[Bqisb])
                for ci in range(nch):
                    c0 = ci * 512
                    nn = min(512, NKS * P - c0)
                    psc, Bpsc = c.banks[5]
                    pI, BpI = c.banks[ci]
                    c.op("pe", lambda e: e.matmul(psc[0:64, 0:nn], lhsT=qisb[:, :], rhs=kiT[:, c0:c0 + nn], start=True, stop=True),
                         reads=[Bqisb, BkiT], writes=[Bpsc])
                    R_, BR_ = Rb[(b * nch + ci) % 4]
                    c.op("act", lambda e: e.activation(out=R_[0:64, 0:nn], in_=psc[0:64, 0:nn], func=AF.Relu), reads=[Bpsc], writes=[BR_])
                    c.op("pe", lambda e: e.matmul(pI[0:64, 0:nn], lhsT=Wsel[:, b, :], rhs=R_[0:64, 0:nn], start=(b == 0), stop=(b == NSQ - 1)),
                         reads=[BWsel, BR_], writes=[BpI])
            if SS < 2:
                return
            for ci in range(nch):
                c0 = ci * 512
                nn = min(512, NKS * P - c0)
                pI, BpI = c.banks[ci]
                evac_scores(pI, BpI, R, c0, nn, qpos[0:R, NQT:NQT + 1])
            bounds(R, nch)
            bisect(R, NKS * P)
            mask_transposes(R, NKS)
            if SS < 2.5:
                return
            mskS = msk[:, 0:NKS * P].rearrange("p (k t) -> p k t", t=P)
            c.op("pool", lambda e: e.memset(msk[:, 0:NKS * P], 0.0), writes=[Bmsk])
            for b in range(NSQ):
                for j in range(16):
                    col = b * 16 + j
                    c.dma("pool", Kp[:, j, :], ck_d[:, :], BKp, reads=[Bidx], writes=[BKp], nowaw=(j > 0),
                          indirect=IOA(ap=idxall[:, col:col + 1], axis=0))
                for j in range(16):
                    col = b * 16 + j
                    c.dma("pool", Vp[:, j, :], cv_d[:, :], BVp, reads=[Bidx], writes=[BVp], nowaw=(j > 0),
                          indirect=IOA(ap=idxall[:, col:col + 1], axis=0))
                c.op("pool", lambda e: e.tensor_copy(out=Vaug[:, 0:16, :, 0:64], in_=Vp[:, :, :].rearrange("p j (g d) -> p j g d", d=64)),
                     reads=[BVp], writes=[BV])
                c.dma("sp", Vst[:, :], Yf[4 * b:4 * b + 4, 2320:2576], BVst, reads=[BYf], writes=[BVst])
                c.op("pool", lambda e: e.tensor_copy(out=Vaug[0:4, 16, :, 0:64], in_=Vst[:, :].rearrange("p (g d) -> p g d", d=64)), reads=[BVst], writes=[BV])
                for j0 in range(0, 16, 4):
                    pt, Bpt = ptbank()
                    for gp in range(2):
                        for j in range(4):
                            c.op("pe", lambda e: e.transpose(pt[:, (gp * 4 + j) * P:(gp * 4 + j + 1) * P], Kp[:, j0 + j, gp * P:(gp + 1) * P], identb[:, :]),
                                 reads=[BKp, Bid], writes=[Bpt])
                    c.op("act", lambda e: e.copy(out=KT2[:, :, j0 * P:(j0 + 4) * P], in_=pt[:, :].rearrange("p (a b) -> p a b", a=2)), reads=[Bpt], writes=[BKT])
                c.op("dve", lambda e: e.tensor_copy(out=KT2[:, :, PAST:PAST + 4], in_=KnT2[:, :, 4 * b:4 * b + 4]), reads=[BKn], writes=[BKT])
                c.op("dve", lambda e: e.tensor_copy(out=Qsb[:, :, :], in_=QT2[:, :, 4 * b:4 * b + 4]), reads=[BQT], writes=[BQsb])
                if SS < 2.7:
                    continue
                if b > 0:
                    c.op("pool", lambda e: e.memset(mskS[:, :, 4 * (b - 1):4 * b], 0.0), writes=[Bmsk])
                c.op("pool", lambda e: e.tensor_copy(out=mskS[:, :, 4 * b:4 * b + 4], in_=mskT[:, 0:NKS, 4 * b:4 * b + 4]), reads=[BmT], writes=[Bmsk])
                attention(NKS, lambda kt: mskS[:, kt, :], Bmsk, xb, Bxb)
                c.dma("sp", Obf[4 * b:4 * b + 4, :], xb[4 * b:4 * b + 4, :], BO, reads=[Bxb], writes=[BO], nowaw=True)
            if SS < 4:
                return
            tail(R, xt[:R, :], Bx, TQ)

        if stage >= 2:
            sample_phase()
        drain(10000)
        c.barrier()
        es1.__exit__(None, None, None)
        if stage >= 3:
            rest_phase()
        c.finish()
    return nc


def _rope_table(pos):
    half = 8
    inv = (500000.0 ** (-np.arange(half, dtype=np.float32) * np.float32(2.0 / 16))).astype(np.float32)
    ang = pos.astype(np.float32)[:, None] * inv[None, :]
    cs = np.cos(ang).astype(np.float32)
    sn = np.sin(ang).astype(np.float32)
    return np.concatenate([cs, cs, sn], axis=1).astype(np.float32)


_NC_CACHE = {}


def _run(inputs, nphys=None, stage=99, compact=False):
    f = lambda a: np.ascontiguousarray(np.asarray(a))
    x_prompt = f(inputs["x_prompt"]); x_sample = f(inputs["x_sample"])
    cache_k = f(inputs["cache_k"])[0]; cache_v = f(inputs["cache_v"])[0]; cache_ik = f(inputs["cache_idx_k"])[0]
    page_table = f(inputs["page_table"]).astype(np.int32)
    full_nphys = cache_k.shape[0]
    if nphys is None:
        nphys = full_nphys
    key = (nphys, stage)
    if key not in _NC_CACHE:
        _NC_CACHE[key] = build(nphys, stage)
    nc = _NC_CACHE[key]
    consts = np.zeros((P, 512 + 128 + NIT + 1), np.float32)
    consts[:, 0:512] = np.arange(512, dtype=np.float32)[None, :]
    consts[:, 512:640] = np.eye(P, dtype=np.float32)
    consts[:, 640:640 + NIT] = (0.5 ** np.arange(1, NIT + 1, dtype=np.float64)).astype(np.float32)[None, :]
    consts[:, 640 + NIT] = np.arange(P, dtype=np.float32)
    msel = np.zeros((64, NSQ, 64), np.float32)
    for h in range(16):
        for t in range(4):
            for b in range(NSQ):
                msel[h * 4 + t, b, b * 4 + t] = 1.0
    ropek = _rope_table(np.arange(SEQ))
    ropes = _rope_table(PAST + (np.arange(TS) % 4))
    in_maps = []
    for core in range(8):
        b, h = core // 2, core % 2
        pos0 = 0 if h == 0 else POS0
        qp = np.zeros((P, NQT + 1), np.float32)
        qp[:, :NQT] = pos0 + np.arange(P)[:, None] + P * np.arange(NQT)[None, :]
        qp[:TS, NQT] = PAST + (np.arange(TS) % 4)
        sl = slice(core * NSQ, (core + 1) * NSQ)
        pt = page_table[sl]
        if compact:
            pages = np.unique(pt)
            remap = {int(p): i for i, p in enumerate(pages)}
            ck = np.zeros((nphys, P, 256), np.float32); cv = np.zeros((nphys, P, 256), np.float32); ci = np.zeros((nphys, P, 64), np.float32)
            ck[:len(pages)] = cache_k[pages].reshape(-1, P, 256); cv[:len(pages)] = cache_v[pages].reshape(-1, P, 256)
            ci[:len(pages)] = cache_ik[pages]
            pt = np.vectorize(remap.get)(pt).astype(np.int32)
        else:
            ck = cache_k.reshape(-1, P, 256); cv = cache_v.reshape(-1, P, 256); ci = cache_ik
        m = {
            "xk": x_prompt[b], "xq": x_prompt[b, pos0:pos0 + TQ], "xs": x_sample[sl].reshape(TS, D),
            "ropek": ropek, "ropeq": ropek[pos0:pos0 + TQ], "ropes": ropes, "qpos": qp, "consts": consts,
            "msel": msel.reshape(64, NSQ * 64), "pt": pt,
            "cache_k": ck.reshape(nphys * P, 256), "cache_v": cv.reshape(nphys * P, 256), "cache_ik": ci.reshape(nphys * P, 64),
            "st_conv": f(inputs["state_conv"])[0, sl].reshape(NSQ * 2, D),
            "st_ffn": f(inputs["state_ffn"])[:, sl].reshape(2, NSQ * 2, DFF),
            "w_attn_in": f(inputs["w_attn_in"])[0], "w_attn_out": f(inputs["w_attn_out"])[0],
            "w_conv_in": f(inputs["w_conv_in"])[0], "conv_w": f(inputs["conv_w"])[0], "w_conv_out": f(inputs["w_conv_out"])[0],
            "w_ffn_up": f(inputs["w_ffn_up"]), "ffn_conv_w": f(inputs["ffn_conv_w"]), "ffn_conv_b": f(inputs["ffn_conv_b"]),
            "w_ffn_down": f(inputs["w_ffn_down"]), "ln_g": f(inputs["ln_g"]).reshape(4, D), "ln_b": f(inputs["ln_b"]).reshape(4, D),
        }
        in_maps.append({k: np.ascontiguousarray(v) for k, v in m.items()})
    res = run_bass_kernel_spmd(nc, in_maps, core_ids=list(range(8)))
    R = res.results
    B = 4
    y_prompt = np.zeros((B, SEQ, D), np.float32)
    nk = np.zeros((1, B, SEQ, 4, 64), np.float32); nv = np.zeros_like(nk); nik = np.zeros((1, B, SEQ, 64), np.float32)
    cp = np.zeros((1, B, 2, D), np.float32); fp = np.zeros((2, B, 2, DFF), np.float32)
    y_sample = np.zeros((128, 4, D), np.float32)
    nks = np.zeros((1, 128, 4, 4, 64), np.float32); nvs = np.zeros_like(nks); niks = np.zeros((1, 128, 4, 64), np.float32)
    cs = np.zeros((1, 128, 2, D), np.float32); fs = np.zeros((2, 128, 2, DFF), np.float32)
    for core in range(8):
        b, h = core // 2, core % 2
        r = R[core]
        sl = slice(core * NSQ, (core + 1) * NSQ)
        if h == 0:
            y_prompt[b, 0:2048] = r["o_y"][0:2048]
            nk[0, b] = r["o_k"].reshape(SEQ, 4, 64); nv[0, b] = r["o_v"].reshape(SEQ, 4, 64); nik[0, b] = r["o_ik"]
        else:
            y_prompt[b, 2048:4096] = r["o_y"][128:TQ]
            cp[0, b] = r["o_cp"]; fp[:, b] = r["o_fp"]
        y_sample[sl] = r["o_ys"].reshape(NSQ, 4, D)
        nks[0, sl] = r["o_ks"].reshape(NSQ, 4, 4, 64); nvs[0, sl] = r["o_vs"].reshape(NSQ, 4, 4, 64); niks[0, sl] = r["o_iks"].reshape(NSQ, 4, 64)
        cs[0, sl] = r["o_cs"].reshape(NSQ, 2, D); fs[:, sl] = r["o_fs"].reshape(2, NSQ, 2, DFF)
    return (y_prompt, y_sample, nk, nv, nik, nks, nvs, niks, cp, cs, fp, fs), R


def kernel(**inputs):
    outs, _ = _run(inputs)
    return outs
```

```python
import contextlib
import numpy as np
import concourse.bass as bass
import concourse.mybir as mybir
from concourse.bass_utils import run_bass_kernel_spmd

F32 = mybir.dt.float32
BF16 = mybir.dt.bfloat16
I32 = mybir.dt.int32
AF = mybir.ActivationFunctionType
ALU = mybir.AluOpType
AX = mybir.AxisListType

P = 128
D = 1024
DFF = 2816
NCB = 11
SEQ = 4096
NKT = 32
NQT = 17
TQ = NQT * P
POS0 = 1920
NSQ = 16
TS = 64
PAST = 2048
NKS = 17
LS = PAST + 4
TOPK = 256
NIT = 14
ALPHA = 4.0 ** 0.25
IDX_SCALE = 1.0 / 32.0
EPS = 1e-5
NEG = -1.0e30
GROUPS = [(0, 1), (1, 4), (5, 4), (9, 4), (13, 4)]


class Buf:
    __slots__ = ("name", "w", "r", "dsem", "dtot")

    def __init__(self, name):
        self.name = name
        self.w = None
        self.r = {}
        self.dsem = None
        self.dtot = 0


def bc_mid(ap, n):
    a = [list(x) for x in ap.ap]
    return bass.AP(tensor=ap.tensor, offset=ap.offset, ap=[a[0], [0, n]] + a[1:])


def bc_last(ap, n):
    a = [list(x) for x in ap.ap]
    return bass.AP(tensor=ap.tensor, offset=ap.offset, ap=a + [[0, n]])


class Ctx:
    CH = 16000

    def __init__(self, nc, es):
        self.nc = nc
        self.es = es
        self.engs = {"pe": nc.tensor, "dve": nc.vector, "act": nc.scalar, "pool": nc.gpsimd, "sp": nc.sync}
        self.cnt = {e: 0 for e in self.engs}
        self.sems = {e: [] for e in self.engs}
        self.waited = {e: {} for e in self.engs}
        self.dma_bufs = []
        self.nsem = 0
        self.banks = []
        self.bank_i = 0
        self.wbufs = []
        self.wb_i = 0

    def new_sem(self, name):
        self.nsem += 1
        return self.es.enter_context(self.nc.semaphore(name))

    def sb(self, name, shape, dt, es=None):
        return (es or self.es).enter_context(self.nc.sbuf_tensor("sb_" + name, list(shape), dt))

    def ps(self, name, shape, dt):
        return self.es.enter_context(self.nc.psum_tensor("ps_" + name, list(shape), dt))

    def _esem(self, e, tick):
        ch = (tick - 1) // self.CH
        while len(self.sems[e]) <= ch:
            self.sems[e].append(self.new_sem("s_%s_%d" % (e, len(self.sems[e]))))
        return self.sems[e][ch], (tick - 1) % self.CH + 1

    def _wait(self, e, tok):
        if tok[0] == "eng":
            _, f, tick = tok
            key = f
            if self.waited[e].get(key, 0) >= tick:
                return
            sem, val = self._esem(f, tick)
        else:
            _, buf, val = tok
            key = ("d", id(buf))
            tick = val
            if self.waited[e].get(key, 0) >= tick:
                return
            sem = buf.dsem
        self.engs[e].wait_ge(sem, val)
        self.waited[e][key] = tick

    def _deps(self, e, reads, writes, nowaw=False):
        for b in reads:
            t = b.w
            if t is not None:
                if t[0] == "eng" and t[1] == e and e in ("pe", "sp"):
                    continue
                self._wait(e, t)
        if nowaw:
            return
        for b in writes:
            t = b.w
            if t is not None:
                if not (t[0] == "eng" and t[1] == e and e in ("pe", "sp", "dve", "act")):
                    self._wait(e, t)
            for t in b.r.values():
                if t[0] == "eng" and t[1] == e and e != "pool":
                    continue
                self._wait(e, t)

    def _commit(self, tok, reads, writes):
        key = tok[1] if tok[0] == "eng" else ("d", id(tok[1]))
        for b in reads:
            b.r[key] = tok
        for b in writes:
            b.w = tok
            b.r = {}

    def op(self, e, fn, reads=(), writes=()):
        self._deps(e, reads, writes)
        ins = fn(self.engs[e])
        self.cnt[e] += 1
        tick = self.cnt[e]
        sem, _ = self._esem(e, tick)
        ins.then_inc(sem, 1)
        self._commit(("eng", e, tick), reads, writes)
        return ins

    def dma(self, q, out, in_, sbuf, reads=(), writes=(), nowaw=False, indirect=None, slow=False):
        self._deps(q, reads, writes, nowaw)
        if sbuf.dsem is None:
            sbuf.dsem = self.new_sem("d_" + sbuf.name)
            self.dma_bufs.append(sbuf)
        sbuf.dtot += 16
        if indirect is not None:
            ins = self.nc.gpsimd.indirect_dma_start(out=out, out_offset=None, in_=in_, in_offset=indirect)
        else:
            ins = self.engs[q].dma_start(out=out, in_=in_, allow_slow_non_contiguous=True) if slow else self.engs[q].dma_start(out=out, in_=in_)
        ins.then_inc(sbuf.dsem, 16)
        self._commit(("dma", sbuf, sbuf.dtot), reads, writes)

    def barrier(self):
        for e in self.engs:
            for f in self.engs:
                if f != e and self.cnt[f] > 0:
                    self._wait(e, ("eng", f, self.cnt[f]))
            for b in self.dma_bufs:
                if b.dtot > 0:
                    self._wait(e, ("dma", b, b.dtot))

    def finish(self):
        for b in self.dma_bufs:
            self.engs["sp"].wait_ge(b.dsem, b.dtot)

    def bank(self, subset=None):
        if subset is None:
            subset = range(len(self.banks))
        self.bank_i += 1
        return self.banks[subset[self.bank_i % len(subset)]]

    def wbuf(self):
        i = self.wb_i % len(self.wbufs)
        self.wb_i += 1
        return self.wbufs[i]


def build(nphys, stage=99):
    nc = bass.Bass("TRN2", target_bir_lowering=False)

    def din(name, shape, dt=F32):
        return nc.dram_tensor(name, list(shape), dt, kind="ExternalInput").ap()

    def dout(name, shape, dt=F32):
        return nc.dram_tensor(name, list(shape), dt, kind="ExternalOutput").ap()

    def dscr(name, shape, dt):
        return nc.dram_tensor(name, list(shape), dt, kind="Internal").ap()

    NROW = nphys * P
    xk = din("xk", [SEQ, D])
    xq = din("xq", [TQ, D])
    xs = din("xs", [TS, D])
    ropek = din("ropek", [SEQ, 24])
    ropeq = din("ropeq", [TQ, 24])
    ropes = din("ropes", [TS, 24])
    qpos_d = din("qpos", [P, NQT + 1])
    consts = din("consts", [P, 512 + 128 + NIT + 1])
    msel_d = din("msel", [64, NSQ * 64])
    pt_d = din("pt", [NSQ, 16], I32)
    ck_d = din("cache_k", [NROW, 256])
    cv_d = din("cache_v", [NROW, 256])
    cik_d = din("cache_ik", [NROW, 64])
    stc_d = din("st_conv", [NSQ * 2, D])
    stf_d = din("st_ffn", [2, NSQ * 2, DFF])
    w_ai = din("w_attn_in", [D, 2640])
    w_ao = din("w_attn_out", [D, D])
    w_ci = din("w_conv_in", [D, 3 * D])
    cw_d = din("conv_w", [3, D])
    w_co = din("w_conv_out", [D, D])
    w_up = din("w_ffn_up", [2, D, 2 * DFF])
    fcw_d = din("ffn_conv_w", [2, 3, DFF])
    fcb_d = din("ffn_conv_b", [2, DFF])
    w_dn = din("w_ffn_down", [2, DFF, D])
    lng_d = din("ln_g", [4, D])
    lnb_d = din("ln_b", [4, D])

    o_y = dout("o_y", [TQ, D])
    o_ys = dout("o_ys", [TS, D])
    o_k = dout("o_k", [SEQ, 256])
    o_v = dout("o_v", [SEQ, 256])
    o_ik = dout("o_ik", [SEQ, 64])
    o_ks = dout("o_ks", [TS, 256])
    o_vs = dout("o_vs", [TS, 256])
    o_iks = dout("o_iks", [TS, 64])
    o_cp = dout("o_cp", [2, D])
    o_cs = dout("o_cs", [NSQ * 2, D])
    o_fp = dout("o_fp", [2, 2, DFF])
    o_fs = dout("o_fs", [2, NSQ * 2, DFF])

    wq_s = dscr("wq_s", [D, 2064], BF16)
    wkv_s = dscr("wkv_s", [D, 576], BF16)
    wo_s = dscr("wo_s", [D, D], BF16)
    wup_s = dscr("wup_s", [2, NCB, P, 8, 2, 256], BF16)
    wdn_s = dscr("wdn_s", [2, DFF, D], BF16)
    wci_s = dscr("wci_s", [8, P, 8, 3, 128], BF16)
    wco_s = dscr("wco_s", [D, D], BF16)
    y1_s = dscr("y1_s", [TQ + TS, D], F32)
    wsd_s = dscr("wsd_s", [TS, 16], F32)

    es = contextlib.ExitStack()
    with es:
        c = Ctx(nc, es)
        for i in range(6):
            c.banks.append((c.ps("pm%d" % i, [P, 512], F32), Buf("pm%d" % i)))
        ptb = [(c.ps("pt%d" % i, [P, 1024], BF16), Buf("pt%d" % i)) for i in range(2)]
        pt_i = [0]

        def ptbank():
            i = pt_i[0] % 2
            pt_i[0] += 1
            return ptb[i]

        WBN = 4608
        for i in range(4):
            c.wbufs.append((c.sb("wb%d" % i, [P, WBN], BF16), Buf("wb%d" % i)))
        cst = c.sb("cst", [P, 512 + 128 + NIT + 1], F32)
        Bcst = Buf("cst")
        identb = c.sb("identb", [P, P], BF16)
        Bid = Buf("identb")
        zerob = c.sb("zerob", [P, 512], BF16); Bzero = Buf("zerob")
        c.op("pool", lambda e: e.memset(zerob[:, :], 0.0), writes=[Bzero])
        c.dma("sp", cst[:], consts[:, :], Bcst, writes=[Bcst])
        c.dma("pool", identb[:], consts[:, 512:640], Bid, writes=[Bid])
        iota = cst[:, 0:512]
        identf = cst[:, 512:640]
        pow2 = cst[:, 640:640 + NIT]
        pidx = cst[:, 640 + NIT:641 + NIT]

        Bwq, Bwkv, Bwo, Bwup, Bwdn, Bwci, Bwco = (Buf(n) for n in ("wq", "wkv", "wo", "wup", "wdn", "wci", "wco"))

        pending = []

        def conv_now(dst, src, B):
            c.dma("pool", dst, src, B, writes=[B], nowaw=True)

        def conv(dst, src, B):
            if B in (Bwkv, Bwq, Bwo):
                conv_now(dst, src, B)
            else:
                pending.append((dst, src, B))

        def drain(n):
            for _ in range(n):
                if pending:
                    conv_now(*pending.pop(0))

        for r0 in range(0, D, 256):
            rs = slice(r0, r0 + 256)
            conv(wkv_s[rs, 0:512], w_ai[rs, 1024:1536], Bwkv)
            conv(wkv_s[rs, 512:576], w_ai[rs, 2560:2624], Bwkv)
        for r0 in range(0, D, 256):
            rs = slice(r0, r0 + 256)
            conv(wq_s[rs, 0:1024], w_ai[rs, 0:1024], Bwq)
            conv(wq_s[rs, 1024:2048], w_ai[rs, 1536:2560], Bwq)
            conv(wq_s[rs, 2048:2064], w_ai[rs, 2624:2640], Bwq)
            conv(wo_s[rs, :], w_ao[rs, :], Bwo)
        for i in range(2):
            wv = w_up[i].rearrange("(k p) (s c n) -> c k p s n", p=P, s=2, c=NCB, n=256)
            for cb in range(NCB):
                for k in range(8):
                    conv(wup_s[i, cb, :, k, :, :], wv[cb, k], Bwup)
            for r0 in range(0, DFF, 704):
                conv(wdn_s[i, r0:r0 + 704, :], w_dn[i, r0:r0 + 704, :], Bwdn)
            if i == 0:
                wv = w_ci.rearrange("(k p) (s c n) -> c k p s n", p=P, s=3, c=8, n=128)
                for ch in range(8):
                    for k in range(8):
                        conv(wci_s[ch, :, k, :, :], wv[ch, k], Bwci)
                for r0 in range(0, D, 256):
                    conv(wco_s[r0:r0 + 256, :], w_co[r0:r0 + 256, :], Bwco)

        def wload(parts, reads):
            wb, Bw = c.wbuf()
            for (off, a, b, src) in parts:
                dst = wb[:, off:off + a * b].rearrange("p (a b) -> p a b", a=a)
                c.dma("sp", dst, src, Bw, reads=reads, writes=[Bw])
            return wb, Bw

        es1 = contextlib.ExitStack()
        es1.__enter__()
        KT2 = c.sb("KT2", [P, 2, SEQ], BF16, es1); BKT = Buf("KT2")
        kiT = c.sb("kiT", [64, SEQ], BF16, es1); BkiT = Buf("kiT")
        Vaug = c.sb("Vaug", [P, NKT, 4, 65], BF16, es1); BV = Buf("Vaug")
        Isb = c.sb("Isb", [P, SEQ], F32, es1); BI = Buf("Isb")
        msk = c.sb("msk", [P, SEQ], BF16, es1); Bmsk = Buf("msk")
        mskT = c.sb("mskT", [P, NKT, P], BF16, es1); BmT = Buf("mskT")
        QT2 = c.sb("QT2", [P, 8, P], BF16, es1); BQT = Buf("QT2")
        qiT = c.sb("qiT", [64, 16, P], BF16, es1); BqiT = Buf("qiT")
        diag = c.sb("diag", [P, 16, P], BF16, es1); Bdiag = Buf("diag")
        Rb = [(c.sb("R%d" % i, [P, 512], BF16, es1), Buf("R%d" % i)) for i in range(4)]
        Eb = [(c.sb("E%d" % i, [P, 512], BF16, es1), Buf("E%d" % i)) for i in range(2)]
        Pb = [(c.sb("Pm%d" % i, [P, 512], BF16, es1), Buf("Pm%d" % i)) for i in range(2)]
        Yf = c.sb("Yf", [P, 2640], F32, es1); BYf = Buf("Yf")
        Yb = c.sb("Yb", [P, 2640], BF16, es1); BYb = Buf("Yb")
        xt = c.sb("xt", [P, D], F32, es1); Bx = Buf("xt")
        xb = c.sb("xb", [P, D], BF16, es1); Bxb = Buf("xb")
        xT = c.sb("xT", [P, 8, P], BF16, es1); BxT = Buf("xT")
        Obf = c.sb("Obf", [P, D], BF16, es1); BO = Buf("Obf")
        rr = c.sb("rr", [P, D], F32, es1); Brr = Buf("rr")
        yy = rr; Byy = Brr
        gb0 = c.sb("gb0", [P, 2 * D], F32, es1); Bgb0 = Buf("gb0")
        rk = c.sb("rk", [P, NKT, 24], F32, es1); Brk = Buf("rk")
        rq = c.sb("rq", [P, NQT + 1, 24], F32, es1); Brq = Buf("rq")
        qpos = c.sb("qpos", [P, NQT + 1], F32, es1); Bqp = Buf("qpos")
        small = c.sb("small", [P, 256], F32, es1); Bsm = Buf("small")
        tmpr = c.sb("tmpr", [P, 33, 16], F32, es1); Btr = Buf("tmpr")
        wsc = c.sb("wsc", [P, 16], F32, es1); Bwsc = Buf("wsc")
        biasb = c.sb("biasb", [P, 512], F32, es1); Bbias = Buf("biasb")
        ikp = c.sb("ikp", [P, 16, 64], BF16, es1); Bikp = Buf("ikp")
        Kp = c.sb("Kp", [P, 16, 256], BF16, es1); BKp = Buf("Kp")
        Vp = c.sb("Vp", [P, 16, 256], BF16, es1); BVp = Buf("Vp")
        ptall = c.sb("ptall", [P, NSQ * 16], I32, es1); Bptall = Buf("ptall")
        idxall = c.sb("idxall", [P, NSQ * 16], I32, es1); Bidx = Buf("idxall")
        qisb = c.sb("qisb", [64, 64], BF16, es1); Bqisb = Buf("qisb")
        Qsb = c.sb("Qsb", [P, 8, 4], BF16, es1); BQsb = Buf("Qsb")
        KnT2 = c.sb("KnT2", [P, 2, 64], BF16, es1); BKn = Buf("KnT2")
        kinT = c.sb("kinT", [64, 64], BF16, es1); Bkin = Buf("kinT")
        Wht = c.sb("Wht", [64, 16], F32, es1); BWht = Buf("Wht")
        Wsel = c.sb("Wsel", [64, NSQ, 64], BF16, es1); BWsel = Buf("Wsel")
        mselb = c.sb("mselb", [64, NSQ * 64], BF16, es1); Bmsel = Buf("mselb")
        Onb = c.sb("Onb", [16, 4, 64], BF16, es1); BOn = Buf("Onb")
        Vst = c.sb("Vst", [4, 256], F32, es1); BVst = Buf("Vst")

        for t0 in range(0, NKT, 8):
            c.dma("sp", rk[:, t0:t0 + 8, :], ropek[t0 * P:(t0 + 8) * P, :].rearrange("(t p) c -> p t c", p=P), Brk,
                  writes=[Brk], nowaw=True)
        for t0 in range(0, NQT, 6):
            t1 = min(NQT, t0 + 6)
            c.dma("sp", rq[:, t0:t1, :], ropeq[t0 * P:t1 * P, :].rearrange("(t p) c -> p t c", p=P), Brq,
                  writes=[Brq], nowaw=True)
        c.dma("sp", rq[0:TS, NQT, :], ropes[:, :], Brq, writes=[Brq], nowaw=True)
        c.dma("sp", qpos[:], qpos_d[:, :], Bqp, writes=[Bqp])
        c.dma("sp", gb0[:, 0:D], bass.AP(tensor=lng_d.tensor, offset=0, ap=[[0, P], [1, D]]), Bgb0, writes=[Bgb0], nowaw=True)
        c.dma("sp", gb0[:, D:2 * D], bass.AP(tensor=lnb_d.tensor, offset=0, ap=[[0, P], [1, D]]), Bgb0, writes=[Bgb0], nowaw=True)
        c.op("pool", lambda e: e.memset(Vaug[:, :, :, 64:65], 1.0), writes=[BV])

        def front(src, Bsrc, R):
            c.op("act", lambda e: e.copy(out=xb[:R, :], in_=src), reads=[Bsrc], writes=[Bxb])
            pt, Bpt = ptbank()
            for k in range(8):
                c.op("pe", lambda e: e.transpose(pt[:, k * P:k * P + R], xb[:R, k * P:(k + 1) * P], identb[:R, :R]),
                     reads=[Bxb, Bid], writes=[Bpt])
            c.op("dve", lambda e: e.tensor_copy(out=xT[:, :, :R], in_=pt[:, :].rearrange("p (k t) -> p k t", k=8)[:, :, :R]),
                 reads=[Bpt], writes=[BxT])

        def rope(Y, BY, R, col0, H, tb):
            Yv = Y[:R, col0:col0 + 64 * H].rearrange("p (h d) -> p h d", d=64)
            tA = tmpr[:R, 0:H, 0:8]
            tB = tmpr[:R, 0:H, 8:16]
            sn = bc_mid(tb[:, 16:24], H)
            cs = bc_mid(tb[:, 0:16], H)
            c.op("dve", lambda e: e.tensor_tensor(out=tA, in0=Yv[:, :, 8:16], in1=sn, op=ALU.mult), reads=[BY], writes=[Btr])
            c.op("dve", lambda e: e.tensor_tensor(out=tB, in0=Yv[:, :, 0:8], in1=sn, op=ALU.mult), reads=[BY], writes=[Btr])
            c.op("dve", lambda e: e.tensor_tensor(out=Yv[:, :, 0:16], in0=Yv[:, :, 0:16], in1=cs, op=ALU.mult), reads=[BY], writes=[BY])
            c.op("dve", lambda e: e.tensor_tensor(out=Yv[:, :, 0:8], in0=Yv[:, :, 0:8], in1=tA, op=ALU.subtract), reads=[BY, Btr], writes=[BY])
            c.op("dve", lambda e: e.tensor_tensor(out=Yv[:, :, 8:16], in0=Yv[:, :, 8:16], in1=tB, op=ALU.add), reads=[BY, Btr], writes=[BY])

        def proj(R, wsrc, Bwsrc, ncols, ycol0):
            n0 = 0
            while n0 < ncols:
                nn = min(2048, ncols - n0)
                kper = max(1, min(8, WBN // nn))
                mts = [(m0, min(512, nn - m0)) for m0 in range(0, nn, 512)]
                bks = [c.bank() for _ in mts]
                for k0 in range(0, 8, kper):
                    kk = min(kper, 8 - k0)
                    src = wsrc[k0 * P:(k0 + kk) * P, n0:n0 + nn].rearrange("(k p) n -> p k n", p=P)
                    wb, Bw = wload([(0, kk, nn, src)], [Bwsrc])
                    wv = wb[:, 0:kk * nn].rearrange("p (k n) -> p k n", k=kk)
                    for (m0, mm), (pm, Bpm) in zip(mts, bks):
                        for k in range(kk):
                            c.op("pe", lambda e: e.matmul(pm[:R, 0:mm], lhsT=xT[:, k0 + k, :R], rhs=wv[:, k, m0:m0 + mm],
                                                          start=(k0 + k == 0), stop=(k0 + k == 7)),
                                 reads=[BxT, Bw], writes=[Bpm])
                for (m0, mm), (pm, Bpm) in zip(mts, bks):
                    c.op("act", lambda e: e.copy(out=Yf[:R, ycol0 + n0 + m0:ycol0 + n0 + m0 + mm], in_=pm[:R, 0:mm]),
                         reads=[Bpm], writes=[BYf])
                n0 += nn

        def layer_norm(src, Bsrc, R, gb, Bgb, dst, Bdst):
            st = small[:R, 0:12]
            mv = small[:R, 12:14]
            sd = small[:R, 14:15]
            rs = small[:R, 15:16]
            c.op("dve", lambda e: e.bn_stats(out=st[:, 0:6], in_=src[:, 0:512]), reads=[Bsrc], writes=[Bsm])
            c.op("dve", lambda e: e.bn_stats(out=st[:, 6:12], in_=src[:, 512:1024]), reads=[Bsrc], writes=[Bsm])
            c.op("dve", lambda e: e.bn_aggr(out=mv, in_=st), reads=[Bsm], writes=[Bsm])
            c.op("act", lambda e: e.activation(out=sd, in_=mv[:, 1:2], func=AF.Sqrt, bias=EPS, scale=1.0), reads=[Bsm], writes=[Bsm])
            c.op("dve", lambda e: e.reciprocal(out=rs, in_=sd), reads=[Bsm], writes=[Bsm])
            c.op("dve", lambda e: e.tensor_scalar(out=dst, in0=src, scalar1=mv[:, 0:1], scalar2=rs, op0=ALU.subtract, op1=ALU.mult),
                 reads=[Bsrc, Bsm], writes=[Bdst])
            c.op("pool", lambda e: e.tensor_tensor(out=dst, in0=dst, in1=gb[:R, 0:D], op=ALU.mult), reads=[Bdst, Bgb], writes=[Bdst])
            c.op("pool", lambda e: e.tensor_tensor(out=dst, in0=dst, in1=gb[:R, D:2 * D], op=ALU.add), reads=[Bdst, Bgb], writes=[Bdst])

        def bisect(R, N):
            lo = small[:R, 16:17]
            w0 = small[:R, 17:18]
            mid = small[:R, 18:19]
            cnt = small[:R, 19:20]
            tt = small[:R, 20:21]
            hw = small[:R, 32:32 + NIT]
            c.op("dve", lambda e: e.tensor_scalar(out=hw, in0=pow2[:R, :], scalar1=w0, scalar2=None, op0=ALU.mult), reads=[Bsm, Bcst], writes=[Bsm])
            for k in range(NIT):
                c.op("dve", lambda e: e.tensor_tensor(out=mid, in0=lo, in1=hw[:, k:k + 1], op=ALU.add), reads=[Bsm], writes=[Bsm])
                c.op("dve", lambda e: e.tensor_scalar(out=msk[:R, :N], in0=Isb[:R, :N], scalar1=mid, scalar2=None, op0=ALU.is_ge,
                                                      op1=ALU.add, accum_out=cnt), reads=[BI, Bsm], writes=[Bmsk, Bsm])
                c.op("dve", lambda e: e.tensor_scalar(out=tt, in0=cnt, scalar1=float(TOPK), scalar2=hw[:, k:k + 1], op0=ALU.is_ge, op1=ALU.mult),
                     reads=[Bsm], writes=[Bsm])
                c.op("dve", lambda e: e.tensor_tensor(out=lo, in0=lo, in1=tt, op=ALU.add), reads=[Bsm], writes=[Bsm])
            c.op("dve", lambda e: e.tensor_scalar(out=msk[:R, :N], in0=Isb[:R, :N], scalar1=lo, scalar2=None, op0=ALU.is_ge),
                 reads=[BI, Bsm], writes=[Bmsk])

        def evac_scores(pI, BpI, R, c0, nn, qp):
            ci = c0 // 512
            qrel = small[:R, 21:22]
            c.op("dve", lambda e: e.tensor_scalar(out=qrel, in0=qp, scalar1=float(-c0), scalar2=None, op0=ALU.add), reads=[Bqp, Bsm], writes=[Bsm])
            c.op("dve", lambda e: e.tensor_reduce(out=small[:R, 100 + ci:101 + ci], in_=pI[:R, 0:nn], axis=AX.X, op=ALU.max), reads=[BpI], writes=[Bsm])
            c.op("dve", lambda e: e.tensor_reduce(out=small[:R, 110 + ci:111 + ci], in_=pI[:R, 0:nn], axis=AX.X, op=ALU.min), reads=[BpI], writes=[Bsm])
            bias = biasb[:R, 0:nn]
            c.op("dve", lambda e: e.tensor_scalar(out=bias, in0=iota[:R, 0:nn], scalar1=qrel, scalar2=NEG, op0=ALU.is_gt, op1=ALU.mult),
                 reads=[Bcst, Bsm], writes=[Bbias])
            c.op("dve", lambda e: e.tensor_tensor(out=Isb[:R, c0:c0 + nn], in0=pI[:R, 0:nn], in1=bias, op=ALU.add), reads=[BpI, Bbias], writes=[BI])

        def bounds(R, nch):
            c.op("dve", lambda e: e.tensor_reduce(out=small[:R, 22:23], in_=small[:R, 100:100 + nch], axis=AX.X, op=ALU.max), reads=[Bsm], writes=[Bsm])
            c.op("dve", lambda e: e.tensor_reduce(out=small[:R, 23:24], in_=small[:R, 110:110 + nch], axis=AX.X, op=ALU.min), reads=[Bsm], writes=[Bsm])
            c.op("dve", lambda e: e.tensor_scalar(out=small[:R, 16:17], in0=small[:R, 23:24], scalar1=-1.0, scalar2=None, op0=ALU.add), reads=[Bsm], writes=[Bsm])
            c.op("dve", lambda e: e.tensor_scalar(out=small[:R, 17:18], in0=small[:R, 22:23], scalar1=small[:R, 23:24], scalar2=2.0,
                                                  op0=ALU.subtract, op1=ALU.add), reads=[Bsm], writes=[Bsm])

        def mask_transposes(R, nkt):
            for t0 in range(0, nkt, 8):
                t1 = min(nkt, t0 + 8)
                pt, Bpt = ptbank()
                for t in range(t0, t1):
                    c.op("pe", lambda e: e.transpose(pt[:, (t - t0) * P:(t - t0) * P + R], msk[:R, t * P:(t + 1) * P], identb[:R, :R]),
                         reads=[Bmsk, Bid], writes=[Bpt])
                c.op("act", lambda e: e.copy(out=mskT[:, t0:t1, :R], in_=pt[:, 0:(t1 - t0) * P].rearrange("p (a b) -> p a b", b=P)[:, :, :R]),
                     reads=[Bpt], writes=[BmT])

        def tail(R, xdram, row0):
            c.dma("sp", rr[:R, :], xdram, Brr, writes=[Brr])
            pt, Bpt = ptbank()
            for k in range(8):
                c.op("pe", lambda e: e.transpose(pt[:, k * P:k * P + R], Obf[:R, k * P:(k + 1) * P], identb[:R, :R]),
                     reads=[BO, Bid], writes=[Bpt])
            c.op("dve", lambda e: e.tensor_copy(out=xT[:, :, :R], in_=pt[:, :].rearrange("p (k t) -> p k t", k=8)[:, :, :R]),
                 reads=[Bpt], writes=[BxT])
            tiles = []
            for k0 in (0, 4):
                src = wo_s[k0 * P:(k0 + 4) * P, :].rearrange("(k p) n -> p k n", p=P)
                wb, Bw = wload([(0, 4, D, src)], [Bwo])
                tiles.append((k0, wb, Bw))
            for m0 in (0, 512):
                pm, Bpm = c.bank()
                for (k0, wb, Bw) in tiles:
                    wv = wb[:, 0:4 * D].rearrange("p (k n) -> p k n", k=4)
                    for k in range(4):
                        c.op("pe", lambda e: e.matmul(pm[:R, :], lhsT=xT[:, k0 + k, :R], rhs=wv[:, k, m0:m0 + 512],
                                                      start=(k0 + k == 0), stop=(k0 + k == 7)), reads=[BxT, Bw], writes=[Bpm])
                c.op("dve", lambda e: e.scalar_tensor_tensor(out=rr[:R, m0:m0 + 512], in0=rr[:R, m0:m0 + 512], scalar=ALPHA, in1=pm[:R, :],
                                                             op0=ALU.mult, op1=ALU.add), reads=[Brr, Bpm], writes=[Brr])
            layer_norm(rr[:R, :], Brr, R, gb0, Bgb0, yy[:R, :], Byy)
            c.dma("sp", y1_s[row0:row0 + R, :], yy[:R, :], Byy, reads=[Byy])
            if stage < 3:
                if row0 < TQ:
                    c.dma("sp", o_y[row0:row0 + R, :], yy[:R, :], Byy, reads=[Byy])
                else:
                    c.dma("sp", o_ys[:, :], yy[:R, :], Byy, reads=[Byy])


        def rest_phase():
            es2 = contextlib.ExitStack()
            es2.__enter__()
            ya = c.sb("ya", [P, 4, D], F32, es2); Bya = [Buf("ya%d" % t) for t in range(4)]
            yb_ = c.sb("ybb", [P, 4, D], F32, es2); Byb = [Buf("yb%d" % t) for t in range(4)]
            yT = c.sb("yT", [P, 8, 512], BF16, es2); ByT = Buf("yT")
            hT = c.sb("hT", [P, 22, 512], BF16, es2); BhT = Buf("hT")
            aex = [(c.sb("aex%d" % i, [P, 520], F32, es2), Buf("aex%d" % i)) for i in range(2)]
            uub = [(c.sb("uu%d" % i, [P, 512], F32, es2), Buf("uu%d" % i)) for i in range(2)]
            silb = [(c.sb("sil%d" % i, [P, 512], F32, es2), Buf("sil%d" % i)) for i in range(2)]
            xb2 = c.sb("xb2", [P, D], BF16, es2); Bxb2 = Buf("xb2")
            gbs = [(c.sb("gb%d" % i, [P, 2 * D], F32, es2), Buf("gb%d" % i)) for i in (1, 2, 3)]
            halo_f = c.sb("halo_f", [P, 2, 22, 2], F32, es2); Bhf = Buf("halo_f")
            halo_c = c.sb("halo_c", [P, 8, 2], F32, es2); Bhc = Buf("halo_c")
            prm = c.sb("prm", [P, 22, 11], F32, es2)
            Bprm = Buf("prm")
            sext = c.sb("sext", [P, 22, 32], F32, es2); Bsext = Buf("sext")
            sout = c.sb("sout", [P, 22, 32], F32, es2); Bsout = Buf("sout")
            stg = c.sb("stg", [32, DFF], F32, es2); Bstg = Buf("stg")
            small2 = c.sb("small2", [P, 32], F32, es2); Bsm2 = Buf("small2")

            for li in (1, 2, 3):
                gbt, Bg = gbs[li - 1]
                c.dma("sp", gbt[:, 0:D], bass.AP(tensor=lng_d.tensor, offset=li * D, ap=[[0, P], [1, D]]), Bg, writes=[Bg], nowaw=True)
                c.dma("sp", gbt[:, D:2 * D], bass.AP(tensor=lnb_d.tensor, offset=li * D, ap=[[0, P], [1, D]]), Bg, writes=[Bg], nowaw=True)
            c.op("dve", lambda e: e.memset(stg[:, :], 0.0), writes=[Bstg])
            c.dma("sp", stg[0:6, :], fcw_d.rearrange("i j n -> (i j) n"), Bstg, reads=[Bstg], writes=[Bstg])
            c.dma("sp", stg[6:8, :], fcb_d[:, :], Bstg, reads=[Bstg], writes=[Bstg], nowaw=True)
            c.dma("sp", stg[8:11, 0:D], cw_d[:, :], Bstg, reads=[Bstg], writes=[Bstg], nowaw=True)
            pmp, Bpmp = c.bank()
            for ch in range(22):
                c.op("pe", lambda e: e.transpose(pmp[:, ch * 11:(ch + 1) * 11], stg[0:11, ch * P:(ch + 1) * P], identf[0:11, 0:11]),
                     reads=[Bstg, Bcst], writes=[Bpmp])
            c.op("act", lambda e: e.copy(out=prm[:, :, :], in_=pmp[:, 0:242].rearrange("p (a b) -> p a b", b=11)), reads=[Bpmp], writes=[Bprm])
            c.op("dve", lambda e: e.memset(halo_f[:, :, :, :], 0.0), writes=[Bhf])
            c.op("dve", lambda e: e.memset(halo_c[:, :, :], 0.0), writes=[Bhc])

            def ln2(src, Bsrc, R, gb, Bgb):
                st = small2[:R, 0:12]; mv = small2[:R, 12:14]; sd = small2[:R, 14:15]; rs = small2[:R, 15:16]
                c.op("dve", lambda e: e.bn_stats(out=st[:, 0:6], in_=src[:, 0:512]), reads=[Bsrc], writes=[Bsm2])
                c.op("dve", lambda e: e.bn_stats(out=st[:, 6:12], in_=src[:, 512:1024]), reads=[Bsrc], writes=[Bsm2])
                c.op("dve", lambda e: e.bn_aggr(out=mv, in_=st), reads=[Bsm2], writes=[Bsm2])
                c.op("act", lambda e: e.activation(out=sd, in_=mv[:, 1:2], func=AF.Sqrt, bias=EPS, scale=1.0), reads=[Bsm2], writes=[Bsm2])
                c.op("dve", lambda e: e.reciprocal(out=rs, in_=sd), reads=[Bsm2], writes=[Bsm2])
                c.op("dve", lambda e: e.tensor_scalar(out=src, in0=src, scalar1=mv[:, 0:1], scalar2=rs, op0=ALU.subtract, op1=ALU.mult),
                     reads=[Bsrc, Bsm2], writes=[Bsrc])
                c.op("pool", lambda e: e.tensor_tensor(out=src, in0=src, in1=gb[:R, 0:D], op=ALU.mult), reads=[Bsrc, Bgb], writes=[Bsrc])
                c.op("pool", lambda e: e.tensor_tensor(out=src, in0=src, in1=gb[:R, D:2 * D], op=ALU.add), reads=[Bsrc, Bgb], writes=[Bsrc])

            def to_featT(Y, BY, nt, R):
                for t in range(nt):
                    c.op("act", lambda e: e.copy(out=xb2[:R, :], in_=Y[:R, t, :]), reads=[BY[t]], writes=[Bxb2])
                    pt, Bpt = ptbank()
                    for k in range(8):
                        c.op("pe", lambda e: e.transpose(pt[:, k * P:k * P + R], xb2[:R, k * P:(k + 1) * P], identb[:R, :R]),
                             reads=[Bxb2, Bid], writes=[Bpt])
                    c.op("dve", lambda e: e.tensor_copy(out=yT[:, :, t * P:t * P + R], in_=pt[:, :].rearrange("p (k t) -> p k t", k=8)[:, :, :R]),
                         reads=[Bpt], writes=[ByT])

            def conv3(ae, Bae, N, samp, w0, w1, w2, uu, Buu):
                if samp:
                    av = ae[:, 0:96].rearrange("p (b t) -> p b t", t=6)
                    uv = uu[:, 0:64].rearrange("p (b t) -> p b t", t=4)
                    s0, s1, s2 = av[:, :, 0:4], av[:, :, 1:5], av[:, :, 2:6]
                else:
                    uv = uu[:, 0:N]
                    s0, s1, s2 = ae[:, 0:N], ae[:, 1:N + 1], ae[:, 2:N + 2]
                c.op("dve", lambda e: e.tensor_scalar(out=uv, in0=s0, scalar1=w0, scalar2=None, op0=ALU.mult), reads=[Bae, Bprm], writes=[Buu])
                c.op("dve", lambda e: e.scalar_tensor_tensor(out=uv, in0=s1, scalar=w1, in1=uv, op0=ALU.mult, op1=ALU.add), reads=[Bae, Bprm, Buu], writes=[Buu])
                c.op("dve", lambda e: e.scalar_tensor_tensor(out=uv, in0=s2, scalar=w2, in1=uv, op0=ALU.mult, op1=ALU.add), reads=[Bae, Bprm, Buu], writes=[Buu])

            def load_state_T(src_dram, nch):
                c.dma("sp", stg[:, 0:nch * P], src_dram, Bstg, writes=[Bstg])
                for c0 in range(0, nch, 16):
                    c1 = min(nch, c0 + 16)
                    pm, Bpm = c.bank()
                    for ch in range(c0, c1):
                        c.op("pe", lambda e: e.transpose(pm[:, (ch - c0) * 32:(ch - c0 + 1) * 32], stg[:, ch * P:(ch + 1) * P], identf[0:32, 0:32]),
                             reads=[Bstg, Bcst], writes=[Bpm])
                    c.op("act", lambda e: e.copy(out=sext[:, c0:c1, :], in_=pm[:, 0:(c1 - c0) * 32].rearrange("p (a b) -> p a b", b=32)),
                         reads=[Bpm], writes=[Bsext])

            def store_state_T(src, Bsrc, nch, ncol, dst_dram):
                for c0 in range(0, nch, 4):
                    c1 = min(nch, c0 + 4)
                    pm, Bpm = c.bank()
                    for ch in range(c0, c1):
                        c.op("pe", lambda e: e.transpose(pm[0:ncol, (ch - c0) * P:(ch - c0 + 1) * P], src[:, ch, :], identf[:, :]),
                             reads=[Bsrc, Bcst], writes=[Bpm])
                    c.op("act", lambda e: e.copy(out=stg[0:ncol, c0 * P:c1 * P], in_=pm[0:ncol, 0:(c1 - c0) * P]), reads=[Bpm], writes=[Bstg])
                c.dma("sp", dst_dram, stg[0:ncol, 0:nch * P], Bstg, reads=[Bstg])

            def ffn(i, Yin, BYin, Yout, BYout, nt, R, samp, last, gb, Bgb):
                N = R if samp else nt * P
                if samp:
                    load_state_T(stf_d[i, :, :], 22)
                ri = 0
                for cb in range(NCB):
                    wb, Bw = wload([(0, 8, 512, wup_s[i, cb].rearrange("p k s n -> p k (s n)"))], [Bwup])
                    wv = wb[:, 0:4096].rearrange("p (k s n) -> p k s n", k=8, s=2)
                    bks = [[c.bank() for _ in range(2)] for _ in range(2)]
                    for s_ in range(2):
                        for hf in range(2):
                            pm, Bpm = bks[s_][hf]
                            for k in range(8):
                                c.op("pe", lambda e: e.matmul(pm[:, 0:N], lhsT=wv[:, k, s_, hf * P:(hf + 1) * P], rhs=yT[:, k, 0:N],
                                                              start=(k == 0), stop=(k == 7)), reads=[Bw, ByT], writes=[Bpm])
                    for hf in range(2):
                        ch = 2 * cb + hf
                        pa, Bpa = bks[0][hf]
                        pg, Bpg = bks[1][hf]
                        ae, Bae = aex[ri % 2]; uu, Buu = uub[ri % 2]; sl, Bsl = silb[ri % 2]
                        ri += 1
                        if samp:
                            av = ae[:, 0:96].rearrange("p (b t) -> p b t", t=6)
                            c.op("dve", lambda e: e.tensor_copy(out=av[:, :, 0:2], in_=sext[:, ch, :].rearrange("p (b r) -> p b r", r=2)),
                                 reads=[Bsext], writes=[Bae])
                            c.op("act", lambda e: e.copy(out=av[:, :, 2:6], in_=pa[:, 0:64].rearrange("p (b t) -> p b t", t=4)), reads=[Bpa], writes=[Bae])
                            c.op("dve", lambda e: e.tensor_copy(out=sout[:, ch, :].rearrange("p (b r) -> p b r", r=2), in_=av[:, :, 4:6]),
                                 reads=[Bae], writes=[Bsout])
                        else:
                            c.op("dve", lambda e: e.tensor_copy(out=ae[:, 0:2], in_=halo_f[:, i, ch, :]), reads=[Bhf], writes=[Bae])
                            c.op("act", lambda e: e.copy(out=ae[:, 2:2 + N], in_=pa[:, 0:N]), reads=[Bpa], writes=[Bae])
                            c.op("dve", lambda e: e.tensor_copy(out=halo_f[:, i, ch, :], in_=ae[:, N:N + 2]), reads=[Bae], writes=[Bhf])
                        conv3(ae, Bae, N, samp, prm[:, ch, 3 * i:3 * i + 1], prm[:, ch, 3 * i + 1:3 * i + 2], prm[:, ch, 3 * i + 2:3 * i + 3], uu, Buu)
                        c.op("act", lambda e: e.activation(out=sl[:, 0:N], in_=uu[:, 0:N], func=AF.Silu, bias=prm[:, ch, 6 + i:7 + i], scale=1.0),
                             reads=[Buu, Bprm], writes=[Bsl])
                        c.op("dve", lambda e: e.tensor_tensor(out=hT[:, ch, 0:N], in0=sl[:, 0:N], in1=pg[:, 0:N], op=ALU.mult),
                             reads=[Bsl, Bpg], writes=[BhT])
                for m0 in (0, 512):
                    bks = [c.bank() for _ in range(nt)]
                    for c0 in range(0, 22, 4):
                        cc = min(4, 22 - c0)
                        src = wdn_s[i, c0 * P:(c0 + cc) * P, m0:m0 + 512].rearrange("(c p) n -> p c n", p=P)
                        wb, Bw = wload([(0, cc, 512, src)], [Bwdn])
                        wv = wb[:, 0:cc * 512].rearrange("p (c n) -> p c n", c=cc)
                        for t in range(nt):
                            pm, Bpm = bks[t]
                            for cj in range(cc):
                                c.op("pe", lambda e: e.matmul(pm[:R, :], lhsT=hT[:, c0 + cj, t * P:t * P + R], rhs=wv[:, cj, :],
                                                              start=(c0 + cj == 0), stop=(c0 + cj == 21)), reads=[BhT, Bw], writes=[Bpm])
                    for t in range(nt):
                        pm, Bpm = bks[t]
                        c.op("dve", lambda e: e.scalar_tensor_tensor(out=Yout[:R, t, m0:m0 + 512], in0=Yin[:R, t, m0:m0 + 512], scalar=ALPHA,
                                                                     in1=pm[:R, :], op0=ALU.mult, op1=ALU.add), reads=[BYin[t], Bpm], writes=[BYout[t]])
                for t in range(nt):
                    ln2(Yout[:R, t, :], BYout[t], R, gb, Bgb)
                if samp:
                    store_state_T(sout, Bsout, 22, 32, o_fs[i, :, :])
                elif last:
                    store_state_T(halo_f[:, i, :, :], Bhf, 22, 2, o_fp[i, :, :])

            def mixer(Yin, BYin, Yout, BYout, nt, R, samp, last, gb, Bgb):
                N = R if samp else nt * P
                if samp:
                    load_state_T(stc_d[:, :], 8)
                ri = 0
                for ch in range(8):
                    wb, Bw = wload([(0, 8, 384, wci_s[ch].rearrange("p k s n -> p k (s n)"))], [Bwci])
                    wv = wb[:, 0:3072].rearrange("p (k s n) -> p k s n", k=8, s=3)
                    bks = [c.bank() for _ in range(3)]
                    for s_ in range(3):
                        pm, Bpm = bks[s_]
                        for k in range(8):
                            c.op("pe", lambda e: e.matmul(pm[:, 0:N], lhsT=wv[:, k, s_, :], rhs=yT[:, k, 0:N], start=(k == 0), stop=(k == 7)),
                                 reads=[Bw, ByT], writes=[Bpm])
                    (pb_, Bpb_), (pc_, Bpc_), (pu_, Bpu_) = bks
                    ae, Bae = aex[ri % 2]; uu, Buu = uub[ri % 2]
                    ri += 1
                    if samp:
                        av = ae[:, 0:96].rearrange("p (b t) -> p b t", t=6)
                        c.op("dve", lambda e: e.tensor_copy(out=av[:, :, 0:2], in_=sext[:, ch, :].rearrange("p (b r) -> p b r", r=2)), reads=[Bsext], writes=[Bae])
                        c.op("act", lambda e: e.copy(out=av[:, :, 2:6], in_=pc_[:, 0:64].rearrange("p (b t) -> p b t", t=4)), reads=[Bpc_], writes=[Bae])
                        c.op("dve", lambda e: e.tensor_tensor(out=av[:, :, 2:6], in0=av[:, :, 2:6], in1=pu_[:, 0:64].rearrange("p (b t) -> p b t", t=4), op=ALU.mult),
                             reads=[Bae, Bpu_], writes=[Bae])
                        c.op("dve", lambda e: e.tensor_copy(out=sout[:, ch, :].rearrange("p (b r) -> p b r", r=2), in_=av[:, :, 4:6]), reads=[Bae], writes=[Bsout])
                    else:
                        c.op("dve", lambda e: e.tensor_copy(out=ae[:, 0:2], in_=halo_c[:, ch, :]), reads=[Bhc], writes=[Bae])
                        c.op("act", lambda e: e.copy(out=ae[:, 2:2 + N], in_=pc_[:, 0:N]), reads=[Bpc_], writes=[Bae])
                        c.op("dve", lambda e: e.tensor_tensor(out=ae[:, 2:2 + N], in0=ae[:, 2:2 + N], in1=pu_[:, 0:N], op=ALU.mult), reads=[Bae, Bpu_], writes=[Bae])
                        c.op("dve", lambda e: e.tensor_copy(out=halo_c[:, ch, :], in_=ae[:, N:N + 2]), reads=[Bae], writes=[Bhc])
                    conv3(ae, Bae, N, samp, prm[:, ch, 8:9], prm[:, ch, 9:10], prm[:, ch, 10:11], uu, Buu)
                    c.op("dve", lambda e: e.tensor_tensor(out=hT[:, ch, 0:N], in0=uu[:, 0:N], in1=pb_[:, 0:N], op=ALU.mult), reads=[Buu, Bpb_], writes=[BhT])
                tiles = []
                for k0 in (0, 4):
                    src = wco_s[k0 * P:(k0 + 4) * P, :].rearrange("(k p) n -> p k n", p=P)
                    wb, Bw = wload([(0, 4, D, src)], [Bwco])
                    tiles.append((k0, wb, Bw))
                for t in range(nt):
                    for m0 in (0, 512):
                        pm, Bpm = c.bank()
                        for (k0, wb, Bw) in tiles:
                            wv = wb[:, 0:4 * D].rearrange("p (k n) -> p k n", k=4)
                            for k in range(4):
                                c.op("pe", lambda e: e.matmul(pm[:R, :], lhsT=hT[:, k0 + k, t * P:t * P + R], rhs=wv[:, k, m0:m0 + 512],
                                                              start=(k0 + k == 0), stop=(k0 + k == 7)), reads=[BhT, Bw], writes=[Bpm])
                        c.op("dve", lambda e: e.scalar_tensor_tensor(out=Yout[:R, t, m0:m0 + 512], in0=Yin[:R, t, m0:m0 + 512], scalar=ALPHA,
                                                                     in1=pm[:R, :], op0=ALU.mult, op1=ALU.add), reads=[BYin[t], Bpm], writes=[BYout[t]])
                    ln2(Yout[:R, t, :], BYout[t], R, gb, Bgb)
                if samp:
                    store_state_T(sout, Bsout, 8, 32, o_cs[:, :])
                elif last:
                    store_state_T(halo_c[:, :, :], Bhc, 8, 2, o_cp[:, :])

            glist = [(t0 * P, nt, P, False, gi == len(GROUPS) - 1) for gi, (t0, nt) in enumerate(GROUPS)] + [(TQ, 1, TS, True, False)]
            for (row0, nt, R, samp, last) in glist:
                for t in range(nt):
                    c.dma("sp", ya[:R, t, :], y1_s[row0 + t * P:row0 + t * P + R, :], Bya[t], writes=[Bya[t]])
                to_featT(ya, Bya, nt, R)
                ffn(0, ya, Bya, yb_, Byb, nt, R, samp, last, gbs[0][0], gbs[0][1])
                to_featT(yb_, Byb, nt, R)
                mixer(yb_, Byb, ya, Bya, nt, R, samp, last, gbs[1][0], gbs[1][1])
                to_featT(ya, Bya, nt, R)
                ffn(1, ya, Bya, yb_, Byb, nt, R, samp, last, gbs[2][0], gbs[2][1])
                for t in range(nt):
                    dst = o_ys[:, :] if samp else o_y[row0 + t * P:row0 + (t + 1) * P, :]
                    c.dma("sp", dst, yb_[:R, t, :], Byb[t], reads=[Byb[t]])
            c.barrier()
            es2.__exit__(None, None, None)

        for kt in range(NKT):
            c.dma("sp", xt[:], xk[kt * P:(kt + 1) * P, :], Bx, writes=[Bx])
            drain(4)
            front(xt[:, :], Bx, P)
            proj(P, wkv_s, Bwkv, 576, 0)
            rope(Yf, BYf, P, 0, 4, rk[:, kt, :])
            rope(Yf, BYf, P, 512, 1, rk[:, kt, :])
            rows = slice(kt * P, (kt + 1) * P)
            c.dma("sp", o_k[rows, :], Yf[:, 0:256], BYf, reads=[BYf])
            c.dma("sp", o_v[rows, :], Yf[:, 256:512], BYf, reads=[BYf])
            c.dma("sp", o_ik[rows, :], Yf[:, 512:576], BYf, reads=[BYf])
            c.op("pool", lambda e: e.tensor_copy(out=Yb[:, 0:576], in_=Yf[:, 0:576]), reads=[BYf], writes=[BYb])
            c.op("pool", lambda e: e.tensor_copy(out=Vaug[:, kt, :, 0:64], in_=Yf[:, 256:512].rearrange("p (g d) -> p g d", d=64)),
                 reads=[BYf], writes=[BV])
            pt, Bpt = ptbank()
            for gp in range(2):
                c.op("pe", lambda e: e.transpose(pt[:, gp * P:(gp + 1) * P], Yb[:, gp * P:(gp + 1) * P], identb[:, :]), reads=[BYb, Bid], writes=[Bpt])
            c.op("pe", lambda e: e.transpose(pt[0:64, 2 * P:3 * P], Yb[:, 512:576], identb[:, :]), reads=[BYb, Bid], writes=[Bpt])
            c.op("act", lambda e: e.copy(out=KT2[:, :, kt * P:(kt + 1) * P], in_=pt[:, 0:2 * P].rearrange("p (a b) -> p a b", b=P)),
                 reads=[Bpt], writes=[BKT])
            c.op("act", lambda e: e.copy(out=kiT[:, kt * P:(kt + 1) * P], in_=pt[0:64, 2 * P:3 * P]), reads=[Bpt], writes=[BkiT])

        def q_front(R, tb):
            rope(Yf, BYf, R, 0, 32, tb)
            for gp in range(2):
                c.op("pool", lambda e: e.tensor_copy(
                    out=Yb[:R, gp * 512:(gp + 1) * 512].rearrange("p (h r d) -> p h r d", h=4, r=2),
                    in_=Yf[:R, gp * 512:(gp + 1) * 512].rearrange("p (r h d) -> p h r d", h=4, r=2)), reads=[BYf], writes=[BYb])
            c.op("pool", lambda e: e.tensor_copy(out=Yb[:R, 1024:2048], in_=Yf[:R, 1024:2048]), reads=[BYf], writes=[BYb])
            c.op("dve", lambda e: e.tensor_scalar(out=wsc[:R, :], in0=Yf[:R, 2048:2064], scalar1=IDX_SCALE, scalar2=None, op0=ALU.mult),
                 reads=[BYf], writes=[Bwsc])

        def q_transposes(R):
            pt, Bpt = ptbank()
            for gp in range(2):
                for hh in range(4):
                    idx = gp * 4 + hh
                    src = Yb[:R, idx * P:(idx + 1) * P]
                    c.op("pe", lambda e: e.transpose(pt[:, idx * P:idx * P + R], src, identb[:R, :R]), reads=[BYb, Bid], writes=[Bpt])
            c.op("act", lambda e: e.copy(out=QT2[:, :, :R], in_=pt[:, :].rearrange("p (a b) -> p a b", b=P)[:, :, :R]), reads=[Bpt], writes=[BQT])
            for h0 in (0, 8):
                pt, Bpt = ptbank()
                for h in range(8):
                    col = 1024 + (h0 + h) * 64
                    c.op("pe", lambda e: e.transpose(pt[0:64, h * P:h * P + R], Yb[:R, col:col + 64], identb[:R, :R]), reads=[BYb, Bid], writes=[Bpt])
                c.op("act", lambda e: e.copy(out=qiT[:, h0:h0 + 8, :R], in_=pt[0:64, :].rearrange("p (a b) -> p a b", b=P)[:, :, :R]),
                     reads=[Bpt], writes=[BqiT])

        import os as _os
        SUB = int(_os.environ.get("DBG_SUB", "99"))
        NQR = int(_os.environ.get("DBG_NQ", str(NQT)))
        def attention(NK, maskfn, Bmask, Od, BOd):
            obanks = [c.banks[0], c.banks[1], c.banks[2]]
            for (ob, Bob) in obanks:
                c.op("pe", lambda e: e.matmul(ob[:, :], lhsT=zerob[:, 0:P], rhs=zerob[:, :], start=True, stop=False),
                     reads=[Bzero], writes=[Bob])
            ei = 0
            for kt in range(NK):
                for g in range(4):
                    pS, BpS = c.bank((3, 4, 5))
                    pb = (g % 2) * 64
                    c.op("pe", lambda e: e.matmul(pS[:, :], lhsT=KT2[pb:pb + 64, g // 2, kt * P:(kt + 1) * P],
                                                  rhs=QT2[pb:pb + 64, (g // 2) * 4:(g // 2) * 4 + 4, :], start=True, stop=True),
                         reads=[BKT, BQT], writes=[BpS])
                    E_, BE_ = Eb[ei % 2]
                    Pm_, BPm_ = Pb[ei % 2]
                    ei += 1
                    c.op("act", lambda e: e.activation(out=E_[:, :], in_=pS[:, :], func=AF.Exp, scale=0.125), reads=[BpS], writes=[BE_])
                    c.op("pool" if ei % 2 == 0 else "dve", lambda e: e.tensor_tensor(out=Pm_[:, :].rearrange("p (a b) -> p a b", a=4),
                                                           in0=E_[:, :].rearrange("p (a b) -> p a b", a=4),
                                                           in1=bc_mid(maskfn(kt), 4), op=ALU.mult), reads=[BE_, Bmask], writes=[BPm_])
                    for hh in range(4):
                        h = 4 * g + hh
                        ob, Bob = obanks[h // 7]
                        oc = (h % 7) * 65
                        c.op("pe", lambda e: e.matmul(ob[:, oc:oc + 65], lhsT=Pm_[:, hh * P:(hh + 1) * P], rhs=Vaug[:, kt, g, :],
                                                      start=False, stop=(kt == NK - 1 and (h % 7 == 6 or h == 15))), reads=[BPm_, BV], writes=[Bob])
            for bi, (ob, Bob) in enumerate(obanks):
                nh = 7 if bi < 2 else 2
                ov = ob[:, 0:nh * 65].rearrange("p (h d) -> p h d", d=65)
                rec = small[:, 200 + 7 * bi:200 + 7 * bi + nh]
                c.op("dve", lambda e: e.reciprocal(out=rec, in_=ov[:, :, 64]), reads=[Bob], writes=[Bsm])
                c.op("dve", lambda e: e.tensor_tensor(out=Od[:, bi * 7 * 64:(bi * 7 + nh) * 64].rearrange("p (h d) -> p h d", d=64),
                                                      in0=ov[:, :, 0:64], in1=bc_last(rec, 64), op=ALU.mult), reads=[Bob, Bsm], writes=[BOd])

        if stage >= 1:
            for j in range(NQR):
                NK = 16 + j
                N = NK * P
                c.dma("sp", xt[:], xq[j * P:(j + 1) * P, :], Bx, writes=[Bx])
                drain(8)
                front(xt[:, :], Bx, P)
                proj(P, wq_s, Bwq, 2064, 0)
                q_front(P, rq[:, j, :])
                if SUB < 1:
                    continue
                q_transposes(P)
                c.op("dve", lambda e: e.tensor_tensor(out=diag[:, :, :], in0=bc_mid(identb[:, :], 16), in1=bc_last(wsc[:, :], P), op=ALU.mult),
                     reads=[Bid, Bwsc], writes=[Bdiag])
                if SUB < 2:
                    continue
                ri = 0
                nch = (N + 511) // 512
                for ci in range(nch):
                    c0 = ci * 512
                    nn = min(512, N - c0)
                    pI, BpI = c.bank((4, 5))
                    for h in range(16):
                        psc, Bpsc = c.bank((0, 1, 2, 3))
                        c.op("pe", lambda e: e.matmul(psc[:, 0:nn], lhsT=qiT[:, h, :], rhs=kiT[:, c0:c0 + nn], start=True, stop=True),
                             reads=[BqiT, BkiT], writes=[Bpsc])
                        R_, BR_ = Rb[ri % 4]
                        ri += 1
                        if h % 2 == 0:
                            c.op("act", lambda e: e.activation(out=R_[:, 0:nn], in_=psc[:, 0:nn], func=AF.Relu), reads=[Bpsc], writes=[BR_])
                        else:
                            c.op("dve", lambda e: e.tensor_scalar(out=R_[:, 0:nn], in0=psc[:, 0:nn], scalar1=0.0, scalar2=None, op0=ALU.max),
                                 reads=[Bpsc], writes=[BR_])
                        c.op("pe", lambda e: e.matmul(pI[:, 0:nn], lhsT=diag[:, h, :], rhs=R_[:, 0:nn], start=(h == 0), stop=(h == 15)),
                             reads=[Bdiag, BR_], writes=[BpI])
                    evac_scores(pI, BpI, P, c0, nn, qpos[:, j:j + 1])
                if SUB < 3:
                    continue
                bounds(P, nch)
                bisect(P, N)
                if SUB < 4:
                    continue
                mask_transposes(P, NK)
                if SUB < 5:
                    continue
                attention(NK, lambda kt: mskT[:, kt, :], BmT, Obf, BO)
                if SUB < 6:
                    continue
                tail(P, xq[j * P:(j + 1) * P, :], j * P)
                if _os.environ.get("DBG_DUMP") and j == 0:
                    c.dma("sp", o_y[128:256, :], Isb[:, 0:1024], BI, reads=[BI])
                    c.dma("sp", o_y[256:384, 0:256], small[:, :], Bsm, reads=[Bsm])
                    c.dma("sp", o_y[384:512, :], rr[:, :], Brr, reads=[Brr])


        def sample_phase():
            R = TS
            IOA = bass.IndirectOffsetOnAxis
            c.dma("sp", xt[:R, :], xs[:, :], Bx, writes=[Bx])
            c.dma("pool", mselb[:, :], msel_d[:, :], Bmsel, writes=[Bmsel])
            c.dma("sp", ptall[:, :], bass.AP(tensor=pt_d.tensor, offset=0, ap=[[0, P], [1, NSQ * 16]]), Bptall, writes=[Bptall])
            c.op("dve", lambda e: e.tensor_scalar(out=idxall[:, :], in0=ptall[:, :], scalar1=128.0, scalar2=pidx, op0=ALU.mult, op1=ALU.add),
                 reads=[Bptall, Bcst], writes=[Bidx])
            front(xt[:R, :], Bx, R)
            proj(R, wq_s, Bwq, 2064, 0)
            proj(R, wkv_s, Bwkv, 576, 2064)
            tb = rq[0:R, NQT, :]
            q_front(R, tb)
            rope(Yf, BYf, R, 2064, 4, tb)
            rope(Yf, BYf, R, 2064 + 512, 1, tb)
            c.dma("sp", o_ks[:, :], Yf[:R, 2064:2320], BYf, reads=[BYf])
            c.dma("sp", o_vs[:, :], Yf[:R, 2320:2576], BYf, reads=[BYf])
            c.dma("sp", o_iks[:, :], Yf[:R, 2576:2640], BYf, reads=[BYf])
            c.op("pool", lambda e: e.tensor_copy(out=Yb[:R, 2064:2640], in_=Yf[:R, 2064:2640]), reads=[BYf], writes=[BYb])
            q_transposes(R)
            pt, Bpt = ptbank()
            for gp in range(2):
                c.op("pe", lambda e: e.transpose(pt[:, gp * P:gp * P + R], Yb[:R, 2064 + gp * P:2064 + (gp + 1) * P], identb[:R, :R]),
                     reads=[BYb, Bid], writes=[Bpt])
            c.op("pe", lambda e: e.transpose(pt[0:64, 2 * P:2 * P + R], Yb[:R, 2576:2640], identb[:R, :R]), reads=[BYb, Bid], writes=[Bpt])
            c.op("act", lambda e: e.copy(out=KnT2[:, :, :], in_=pt[:, 0:2 * P].rearrange("p (a b) -> p a b", b=P)[:, :, 0:R]), reads=[Bpt], writes=[BKn])
            c.op("act", lambda e: e.copy(out=kinT[:, :], in_=pt[0:64, 2 * P:2 * P + R]), reads=[Bpt], writes=[Bkin])
            Bwsd = Buf("wsd")
            c.dma("sp", wsd_s[:, :], wsc[:R, :], Bwsd, reads=[Bwsc], writes=[Bwsd])
            for h in range(16):
                src = bass.AP(tensor=wsd_s.tensor, offset=h, ap=[[16, 4], [64, NSQ]])
                c.dma("sp", Wht[4 * h:4 * h + 4, :], src, BWht, reads=[Bwsd], writes=[BWht], nowaw=(h > 0), slow=True)
            c.op("dve", lambda e: e.tensor_tensor(out=Wsel[:, :, :], in0=mselb[:, :].rearrange("p (a b) -> p a b", b=64), in1=bc_last(Wht[:, :], 64), op=ALU.mult),
                 reads=[Bmsel, BWht], writes=[BWsel])
            c.op("dve", lambda e: e.memset(kiT[:, PAST:PAST + P], 0.0), writes=[BkiT])
            c.op("dve", lambda e: e.memset(KT2[:, :, PAST:PAST + P], 0.0), writes=[BKT])
            c.op("pool", lambda e: e.memset(Vaug[:, 16, :, 0:64], 0.0), writes=[BV])
            nch = 5
            SS = float(_os.environ.get("DBG_SS", "99"))
            if SS < 1:
                return
            for b in range(NSQ):
                for j in range(16):
                    col = b * 16 + j
                    c.dma("pool", ikp[:, j, :], cik_d[:, :], Bikp, reads=[Bidx], writes=[Bikp], nowaw=(j > 0),
                          indirect=IOA(ap=idxall[:, col:col + 1], axis=0))
                for j0 in (0, 8):
                    pt, Bpt = ptbank()
                    for j in range(8):
                        c.op("pe", lambda e: e.transpose(pt[0:64, j * P:(j + 1) * P], ikp[:, j0 + j, :], identb[:, :]), reads=[Bikp, Bid], writes=[Bpt])
                    c.op("act", lambda e: e.copy(out=kiT[:, j0 * P:(j0 + 8) * P], in_=pt[0:64, :]), reads=[Bpt], writes=[BkiT])
                c.op("dve", lambda e: e.tensor_copy(out=kiT[:, PAST:PAST + 4], in_=kinT[:, 4 * b:4 * b + 4]), reads=[Bkin], writes=[BkiT])
                c.op("dve", lambda e: e.tensor_copy(out=qisb[:, :].rearrange("p (h t) -> p h t", t=4), in_=qiT[:, :, 4 * b:4 * b + 4]), reads=[BqiT], writes=[Bqisb])
                for ci in range(nch):
                    c0 = ci * 512
                    nn = min(512, NKS * P - c0)
                    psc, Bpsc = c.banks[5]
                    pI, BpI = c.banks[ci]
                    c.op("pe", lambda e: e.matmul(psc[0:64, 0:nn], lhsT=qisb[:, :], rhs=kiT[:, c0:c0 + nn], start=True, stop=True),
                         reads=[Bqisb, BkiT], writes=[Bpsc])
                    R_, BR_ = Rb[(b * nch + ci) % 4]
                    c.op("act", lambda e: e.activation(out=R_[0:64, 0:nn], in_=psc[0:64, 0:nn], func=AF.Relu), reads=[Bpsc], writes=[BR_])
                    c.op("pe", lambda e: e.matmul(pI[0:64, 0:nn], lhsT=Wsel[:, b, :], rhs=R_[0:64, 0:nn], start=(b == 0), stop=(b == NSQ - 1)),
                         reads=[BWsel, BR_], writes=[BpI])
            if SS < 2:
                return
            for ci in range(nch):
                c0 = ci * 512
                nn = min(512, NKS * P - c0)
                pI, BpI = c.banks[ci]
                evac_scores(pI, BpI, R, c0, nn, qpos[0:R, NQT:NQT + 1])
            bounds(R, nch)
            bisect(R, NKS * P)
            mask_transposes(R, NKS)
            if SS < 2.5:
                return
            mskS = msk[:, 0:NKS * P].rearrange("p (k t) -> p k t", t=P)
            c.op("pool", lambda e: e.memset(msk[:, 0:NKS * P], 0.0), writes=[Bmsk])
            for b in range(NSQ):
                for j in range(16):
                    col = b * 16 + j
                    c.dma("pool", Kp[:, j, :], ck_d[:, :], BKp, reads=[Bidx], writes=[BKp], nowaw=(j > 0),
                          indirect=IOA(ap=idxall[:, col:col + 1], axis=0))
                for j in range(16):
                    col = b * 16 + j
                    c.dma("pool", Vp[:, j, :], cv_d[:, :], BVp, reads=[Bidx], writes=[BVp], nowaw=(j > 0),
                          indirect=IOA(ap=idxall[:, col:col + 1], axis=0))
                c.op("pool", lambda e: e.tensor_copy(out=Vaug[:, 0:16, :, 0:64], in_=Vp[:, :, :].rearrange("p j (g d) -> p j g d", d=64)),
                     reads=[BVp], writes=[BV])
                c.dma("sp", Vst[:, :], Yf[4 * b:4 * b + 4, 2320:2576], BVst, reads=[BYf], writes=[BVst])
                c.op("pool", lambda e: e.tensor_copy(out=Vaug[0:4, 16, :, 0:64], in_=Vst[:, :].rearrange("p (g d) -> p g d", d=64)), reads=[BVst], writes=[BV])
                for j0 in range(0, 16, 4):
                    pt, Bpt = ptbank()
                    for gp in range(2):
                        for j in range(4):
                            c.op("pe", lambda e: e.transpose(pt[:, (gp * 4 + j) * P:(gp * 4 + j + 1) * P], Kp[:, j0 + j, gp * P:(gp + 1) * P], identb[:, :]),
                                 reads=[BKp, Bid], writes=[Bpt])
                    c.op("act", lambda e: e.copy(out=KT2[:, :, j0 * P:(j0 + 4) * P], in_=pt[:, :].rearrange("p (a b) -> p a b", a=2)), reads=[Bpt], writes=[BKT])
                c.op("dve", lambda e: e.tensor_copy(out=KT2[:, :, PAST:PAST + 4], in_=KnT2[:, :, 4 * b:4 * b + 4]), reads=[BKn], writes=[BKT])
                c.op("dve", lambda e: e.tensor_copy(out=Qsb[:, :, :], in_=QT2[:, :, 4 * b:4 * b + 4]), reads=[BQT], writes=[BQsb])
                if SS < 2.7:
                    continue
                if b > 0:
                    c.op("pool", lambda e: e.memset(mskS[:, :, 4 * (b - 1):4 * b], 0.0), writes=[Bmsk])
                c.op("pool", lambda e: e.tensor_copy(out=mskS[:, :, 4 * b:4 * b + 4], in_=mskT[:, 0:NKS, 4 * b:4 * b + 4]), reads=[BmT], writes=[Bmsk])
                attention(NKS, lambda kt: mskS[:, kt, :], Bmsk, xb, Bxb)
                c.dma("sp", Obf[4 * b:4 * b + 4, :], xb[4 * b:4 * b + 4, :], BO, reads=[Bxb], writes=[BO], nowaw=True)
            if SS < 4:
                return
            tail(R, xs[:, :], TQ)

        if stage >= 2:
            sample_phase()
        drain(10000)
        c.barrier()
        es1.__exit__(None, None, None)
        if stage >= 3:
            rest_phase()
        c.finish()
    return nc


def _rope_table(pos):
    half = 8
    inv = (500000.0 ** (-np.arange(half, dtype=np.float32) * np.float32(2.0 / 16))).astype(np.float32)
    ang = pos.astype(np.float32)[:, None] * inv[None, :]
    cs = np.cos(ang).astype(np.float32)
    sn = np.sin(ang).astype(np.float32)
    return np.concatenate([cs, cs, sn], axis=1).astype(np.float32)


_NC_CACHE = {}


def _run(inputs, nphys=None, stage=99, compact=False):
    f = lambda a: np.ascontiguousarray(np.asarray(a))
    x_prompt = f(inputs["x_prompt"]); x_sample = f(inputs["x_sample"])
    cache_k = f(inputs["cache_k"])[0]; cache_v = f(inputs["cache_v"])[0]; cache_ik = f(inputs["cache_idx_k"])[0]
    page_table = f(inputs["page_table"]).astype(np.int32)
    full_nphys = cache_k.shape[0]
    if nphys is None:
        nphys = full_nphys
    key = (nphys, stage)
    if key not in _NC_CACHE:
        _NC_CACHE[key] = build(nphys, stage)
    nc = _NC_CACHE[key]
    consts = np.zeros((P, 512 + 128 + NIT + 1), np.float32)
    consts[:, 0:512] = np.arange(512, dtype=np.float32)[None, :]
    consts[:, 512:640] = np.eye(P, dtype=np.float32)
    consts[:, 640:640 + NIT] = (0.5 ** np.arange(1, NIT + 1, dtype=np.float64)).astype(np.float32)[None, :]
    consts[:, 640 + NIT] = np.arange(P, dtype=np.float32)
    msel = np.zeros((64, NSQ, 64), np.float32)
    for h in range(16):
        for t in range(4):
            for b in range(NSQ):
                msel[h * 4 + t, b, b * 4 + t] = 1.0
    ropek = _rope_table(np.arange(SEQ))
    ropes = _rope_table(PAST + (np.arange(TS) % 4))
    in_maps = []
    for core in range(8):
        b, h = core // 2, core % 2
        pos0 = 0 if h == 0 else POS0
        qp = np.zeros((P, NQT + 1), np.float32)
        qp[:, :NQT] = pos0 + np.arange(P)[:, None] + P * np.arange(NQT)[None, :]
        qp[:TS, NQT] = PAST + (np.arange(TS) % 4)
        sl = slice(core * NSQ, (core + 1) * NSQ)
        pt = page_table[sl]
        if compact:
            pages = np.unique(pt)
            remap = {int(p): i for i, p in enumerate(pages)}
            ck = np.zeros((nphys, P, 256), np.float32); cv = np.zeros((nphys, P, 256), np.float32); ci = np.zeros((nphys, P, 64), np.float32)
            ck[:len(pages)] = cache_k[pages].reshape(-1, P, 256); cv[:len(pages)] = cache_v[pages].reshape(-1, P, 256)
            ci[:len(pages)] = cache_ik[pages]
            pt = np.vectorize(remap.get)(pt).astype(np.int32)
        else:
            ck = cache_k.reshape(-1, P, 256); cv = cache_v.reshape(-1, P, 256); ci = cache_ik
        m = {
            "xk": x_prompt[b], "xq": x_prompt[b, pos0:pos0 + TQ], "xs": x_sample[sl].reshape(TS, D),
            "ropek": ropek, "ropeq": ropek[pos0:pos0 + TQ], "ropes": ropes, "qpos": qp, "consts": consts,
            "msel": msel.reshape(64, NSQ * 64), "pt": pt,
            "cache_k": ck.reshape(nphys * P, 256), "cache_v": cv.reshape(nphys * P, 256), "cache_ik": ci.reshape(nphys * P, 64),
            "st_conv": f(inputs["state_conv"])[0, sl].reshape(NSQ * 2, D),
            "st_ffn": f(inputs["state_ffn"])[:, sl].reshape(2, NSQ * 2, DFF),
            "w_attn_in": f(inputs["w_attn_in"])[0], "w_attn_out": f(inputs["w_attn_out"])[0],
            "w_conv_in": f(inputs["w_conv_in"])[0], "conv_w": f(inputs["conv_w"])[0], "w_conv_out": f(inputs["w_conv_out"])[0],
            "w_ffn_up": f(inputs["w_ffn_up"]), "ffn_conv_w": f(inputs["ffn_conv_w"]), "ffn_conv_b": f(inputs["ffn_conv_b"]),
            "w_ffn_down": f(inputs["w_ffn_down"]), "ln_g": f(inputs["ln_g"]).reshape(4, D), "ln_b": f(inputs["ln_b"]).reshape(4, D),
        }
        in_maps.append({k: np.ascontiguousarray(v) for k, v in m.items()})
    res = run_bass_kernel_spmd(nc, in_maps, core_ids=list(range(8)))
    R = res.results
    B = 4
    y_prompt = np.zeros((B, SEQ, D), np.float32)
    nk = np.zeros((1, B, SEQ, 4, 64), np.float32); nv = np.zeros_like(nk); nik = np.zeros((1, B, SEQ, 64), np.float32)
    cp = np.zeros((1, B, 2, D), np.float32); fp = np.zeros((2, B, 2, DFF), np.float32)
    y_sample = np.zeros((128, 4, D), np.float32)
    nks = np.zeros((1, 128, 4, 4, 64), np.float32); nvs = np.zeros_like(nks); niks = np.zeros((1, 128, 4, 64), np.float32)
    cs = np.zeros((1, 128, 2, D), np.float32); fs = np.zeros((2, 128, 2, DFF), np.float32)
    for core in range(8):
        b, h = core // 2, core % 2
        r = R[core]
        sl = slice(core * NSQ, (core + 1) * NSQ)
        if h == 0:
            y_prompt[b, 0:2048] = r["o_y"][0:2048]
            nk[0, b] = r["o_k"].reshape(SEQ, 4, 64); nv[0, b] = r["o_v"].reshape(SEQ, 4, 64); nik[0, b] = r["o_ik"]
        else:
            y_prompt[b, 2048:4096] = r["o_y"][128:TQ]
            cp[0, b] = r["o_cp"]; fp[:, b] = r["o_fp"]
        y_sample[sl] = r["o_ys"].reshape(NSQ, 4, D)
        nks[0, sl] = r["o_ks"].reshape(NSQ, 4, 4, 64); nvs[0, sl] = r["o_vs"].reshape(NSQ, 4, 4, 64); niks[0, sl] = r["o_iks"].reshape(NSQ, 4, 64)
        cs[0, sl] = r["o_cs"].reshape(NSQ, 2, D); fs[:, sl] = r["o_fs"].reshape(2, NSQ, 2, DFF)
    return (y_prompt, y_sample, nk, nv, nik, nks, nvs, niks, cp, cs, fp, fs), R


def kernel(**inputs):
    outs, _ = _run(inputs)
    return outs
```

```python
import contextlib
import numpy as np
import concourse.bass as bass
import concourse.mybir as mybir
from concourse.bass_utils import run_bass_kernel_spmd

F32 = mybir.dt.float32
BF16 = mybir.dt.bfloat16
I32 = mybir.dt.int32
AF = mybir.ActivationFunctionType
ALU = mybir.AluOpType
AX = mybir.AxisListType

P = 128
D = 1024
DFF = 2816
NCB = 11
SEQ = 4096
NKT = 32
NQT = 17
TQ = NQT * P
POS0 = 1920
NSQ = 16
TS = 64
PAST = 2048
NKS = 17
LS = PAST + 4
TOPK = 256
NIT = 14
ALPHA = 4.0 ** 0.25
IDX_SCALE = 1.0 / 32.0
EPS = 1e-5
NEG = -1.0e30
GROUPS = [(0, 1), (1, 4), (5, 4), (9, 4), (13, 4)]


class Buf:
    __slots__ = ("name", "w", "r", "dsem", "dtot")

    def __init__(self, name):
        self.name = name
        self.w = None
        self.r = {}
        self.dsem = None
        self.dtot = 0


def bc_mid(ap, n):
    a = [list(x) for x in ap.ap]
    return bass.AP(tensor=ap.tensor, offset=ap.offset, ap=[a[0], [0, n]] + a[1:])


def bc_last(ap, n):
    a = [list(x) for x in ap.ap]
    return bass.AP(tensor=ap.tensor, offset=ap.offset, ap=a + [[0, n]])


class Ctx:
    CH = 16000

    def __init__(self, nc, es):
        self.nc = nc
        self.es = es
        self.engs = {"pe": nc.tensor, "dve": nc.vector, "act": nc.scalar, "pool": nc.gpsimd, "sp": nc.sync}
        self.cnt = {e: 0 for e in self.engs}
        self.sems = {e: [] for e in self.engs}
        self.waited = {e: {} for e in self.engs}
        self.dma_bufs = []
        self.nsem = 0
        self.banks = []
        self.bank_i = 0
        self.wbufs = []
        self.wb_i = 0

    def new_sem(self, name):
        self.nsem += 1
        return self.es.enter_context(self.nc.semaphore(name))

    def sb(self, name, shape, dt, es=None):
        return (es or self.es).enter_context(self.nc.sbuf_tensor("sb_" + name, list(shape), dt))

    def ps(self, name, shape, dt):
        return self.es.enter_context(self.nc.psum_tensor("ps_" + name, list(shape), dt))

    def _esem(self, e, tick):
        ch = (tick - 1) // self.CH
        while len(self.sems[e]) <= ch:
            self.sems[e].append(self.new_sem("s_%s_%d" % (e, len(self.sems[e]))))
        return self.sems[e][ch], (tick - 1) % self.CH + 1

    def _wait(self, e, tok):
        if tok[0] == "eng":
            _, f, tick = tok
            key = f
            if self.waited[e].get(key, 0) >= tick:
                return
            sem, val = self._esem(f, tick)
        else:
            _, buf, val = tok
            key = ("d", id(buf))
            tick = val
            if self.waited[e].get(key, 0) >= tick:
                return
            sem = buf.dsem
        self.engs[e].wait_ge(sem, val)
        self.waited[e][key] = tick

    def _deps(self, e, reads, writes, nowaw=False):
        for b in reads:
            t = b.w
            if t is not None:
                if t[0] == "eng" and t[1] == e and e in ("pe", "sp"):
                    continue
                self._wait(e, t)
        if nowaw:
            return
        for b in writes:
            t = b.w
            if t is not None:
                if not (t[0] == "eng" and t[1] == e and e in ("pe", "sp", "dve", "act")):
                    self._wait(e, t)
            for t in b.r.values():
                if t[0] == "eng" and t[1] == e and e != "pool":
                    continue
                self._wait(e, t)

    def _commit(self, tok, reads, writes):
        key = tok[1] if tok[0] == "eng" else ("d", id(tok[1]))
        for b in reads:
            b.r[key] = tok
        for b in writes:
            b.w = tok
            b.r = {}

    def op(self, e, fn, reads=(), writes=()):
        self._deps(e, reads, writes)
        ins = fn(self.engs[e])
        self.cnt[e] += 1
        tick = self.cnt[e]
        sem, _ = self._esem(e, tick)
        ins.then_inc(sem, 1)
        self._commit(("eng", e, tick), reads, writes)
        return ins

    def dma(self, q, out, in_, sbuf, reads=(), writes=(), nowaw=False, indirect=None, slow=False):
        self._deps(q, reads, writes, nowaw)
        if sbuf.dsem is None:
            sbuf.dsem = self.new_sem("d_" + sbuf.name)
            self.dma_bufs.append(sbuf)
        sbuf.dtot += 16
        if indirect is not None:
            ins = self.nc.gpsimd.indirect_dma_start(out=out, out_offset=None, in_=in_, in_offset=indirect)
        else:
            ins = self.engs[q].dma_start(out=out, in_=in_, allow_slow_non_contiguous=True) if slow else self.engs[q].dma_start(out=out, in_=in_)
        ins.then_inc(sbuf.dsem, 16)
        self._commit(("dma", sbuf, sbuf.dtot), reads, writes)

    def barrier(self):
        for e in self.engs:
            for f in self.engs:
                if f != e and self.cnt[f] > 0:
                    self._wait(e, ("eng", f, self.cnt[f]))
            for b in self.dma_bufs:
                if b.dtot > 0:
                    self._wait(e, ("dma", b, b.dtot))

    def finish(self):
        for b in self.dma_bufs:
            self.engs["sp"].wait_ge(b.dsem, b.dtot)

    def bank(self, subset=None):
        if subset is None:
            subset = range(len(self.banks))
        self.bank_i += 1
        return self.banks[subset[self.bank_i % len(subset)]]

    def wbuf(self):
        i = self.wb_i % len(self.wbufs)
        self.wb_i += 1
        return self.wbufs[i]


def build(nphys, stage=99):
    nc = bass.Bass("TRN2", target_bir_lowering=False)

    def din(name, shape, dt=F32):
        return nc.dram_tensor(name, list(shape), dt, kind="ExternalInput").ap()

    def dout(name, shape, dt=F32):
        return nc.dram_tensor(name, list(shape), dt, kind="ExternalOutput").ap()

    def dscr(name, shape, dt):
        return nc.dram_tensor(name, list(shape), dt, kind="Internal").ap()

    NROW = nphys * P
    xk = din("xk", [SEQ, D])
    xq = din("xq", [TQ, D])
    xs = din("xs", [TS, D])
    ropek = din("ropek", [SEQ, 24])
    ropeq = din("ropeq", [TQ, 24])
    ropes = din("ropes", [TS, 24])
    qpos_d = din("qpos", [P, NQT + 1])
    consts = din("consts", [P, 512 + 128 + NIT + 1])
    msel_d = din("msel", [64, NSQ * 64])
    pt_d = din("pt", [NSQ, 16], I32)
    ck_d = din("cache_k", [NROW, 256])
    cv_d = din("cache_v", [NROW, 256])
    cik_d = din("cache_ik", [NROW, 64])
    stc_d = din("st_conv", [NSQ * 2, D])
    stf_d = din("st_ffn", [2, NSQ * 2, DFF])
    w_ai = din("w_attn_in", [D, 2640])
    w_ao = din("w_attn_out", [D, D])
    w_ci = din("w_conv_in", [D, 3 * D])
    cw_d = din("conv_w", [3, D])
    w_co = din("w_conv_out", [D, D])
    w_up = din("w_ffn_up", [2, D, 2 * DFF])
    fcw_d = din("ffn_conv_w", [2, 3, DFF])
    fcb_d = din("ffn_conv_b", [2, DFF])
    w_dn = din("w_ffn_down", [2, DFF, D])
    lng_d = din("ln_g", [4, D])
    lnb_d = din("ln_b", [4, D])

    o_y = dout("o_y", [TQ, D])
    o_ys = dout("o_ys", [TS, D])
    o_k = dout("o_k", [SEQ, 256])
    o_v = dout("o_v", [SEQ, 256])
    o_ik = dout("o_ik", [SEQ, 64])
    o_ks = dout("o_ks", [TS, 256])
    o_vs = dout("o_vs", [TS, 256])
    o_iks = dout("o_iks", [TS, 64])
    o_cp = dout("o_cp", [2, D])
    o_cs = dout("o_cs", [NSQ * 2, D])
    o_fp = dout("o_fp", [2, 2, DFF])
    o_fs = dout("o_fs", [2, NSQ * 2, DFF])

    wq_s = dscr("wq_s", [D, 2064], BF16)
    wkv_s = dscr("wkv_s", [D, 576], BF16)
    wo_s = dscr("wo_s", [D, D], BF16)
    wup_s = dscr("wup_s", [2, NCB, P, 8, 2, 256], BF16)
    wdn_s = dscr("wdn_s", [2, DFF, D], BF16)
    wci_s = dscr("wci_s", [8, P, 8, 3, 128], BF16)
    wco_s = dscr("wco_s", [D, D], BF16)
    y1_s = dscr("y1_s", [TQ + TS, D], F32)
    wsd_s = dscr("wsd_s", [TS, 16], F32)

    es = contextlib.ExitStack()
    with es:
        c = Ctx(nc, es)
        for i in range(6):
            c.banks.append((c.ps("pm%d" % i, [P, 512], F32), Buf("pm%d" % i)))
        ptb = [(c.ps("pt%d" % i, [P, 1024], BF16), Buf("pt%d" % i)) for i in range(2)]
        pt_i = [0]

        def ptbank():
            i = pt_i[0] % 2
            pt_i[0] += 1
            return ptb[i]

        WBN = 4608
        for i in range(4):
            c.wbufs.append((c.sb("wb%d" % i, [P, WBN], BF16), Buf("wb%d" % i)))
        cst = c.sb("cst", [P, 512 + 128 + NIT + 1], F32)
        Bcst = Buf("cst")
        identb = c.sb("identb", [P, P], BF16)
        Bid = Buf("identb")
        zerob = c.sb("zerob", [P, 512], BF16); Bzero = Buf("zerob")
        c.op("pool", lambda e: e.memset(zerob[:, :], 0.0), writes=[Bzero])
        c.dma("sp", cst[:], consts[:, :], Bcst, writes=[Bcst])
        c.dma("pool", identb[:], consts[:, 512:640], Bid, writes=[Bid])
        iota = cst[:, 0:512]
        identf = cst[:, 512:640]
        pow2 = cst[:, 640:640 + NIT]
        pidx = cst[:, 640 + NIT:641 + NIT]

        Bwq, Bwkv, Bwo, Bwup, Bwdn, Bwci, Bwco = (Buf(n) for n in ("wq", "wkv", "wo", "wup", "wdn", "wci", "wco"))

        pending = []

        def conv_now(dst, src, B):
            c.dma("pool", dst, src, B, writes=[B], nowaw=True)

        def conv(dst, src, B):
            if B in (Bwkv, Bwq, Bwo):
                conv_now(dst, src, B)
            else:
                pending.append((dst, src, B))

        def drain(n):
            for _ in range(n):
                if pending:
                    conv_now(*pending.pop(0))

        for r0 in range(0, D, 256):
            rs = slice(r0, r0 + 256)
            conv(wkv_s[rs, 0:512], w_ai[rs, 1024:1536], Bwkv)
            conv(wkv_s[rs, 512:576], w_ai[rs, 2560:2624], Bwkv)
        for r0 in range(0, D, 256):
            rs = slice(r0, r0 + 256)
            conv(wq_s[rs, 0:1024], w_ai[rs, 0:1024], Bwq)
            conv(wq_s[rs, 1024:2048], w_ai[rs, 1536:2560], Bwq)
            conv(wq_s[rs, 2048:2064], w_ai[rs, 2624:2640], Bwq)
            conv(wo_s[rs, :], w_ao[rs, :], Bwo)
        for i in range(2):
            wv = w_up[i].rearrange("(k p) (s c n) -> c k p s n", p=P, s=2, c=NCB, n=256)
            for cb in range(NCB):
                for k in range(8):
                    conv(wup_s[i, cb, :, k, :, :], wv[cb, k], Bwup)
            for r0 in range(0, DFF, 704):
                conv(wdn_s[i, r0:r0 + 704, :], w_dn[i, r0:r0 + 704, :], Bwdn)
            if i == 0:
                wv = w_ci.rearrange("(k p) (s c n) -> c k p s n", p=P, s=3, c=8, n=128)
                for ch in range(8):
                    for k in range(8):
                        conv(wci_s[ch, :, k, :, :], wv[ch, k], Bwci)
                for r0 in range(0, D, 256):
                    conv(wco_s[r0:r0 + 256, :], w_co[r0:r0 + 256, :], Bwco)

        def wload(parts, reads):
            wb, Bw = c.wbuf()
            for (off, a, b, src) in parts:
                dst = wb[:, off:off + a * b].rearrange("p (a b) -> p a b", a=a)
                c.dma("sp", dst, src, Bw, reads=reads, writes=[Bw])
            return wb, Bw

        es1 = contextlib.ExitStack()
        es1.__enter__()
        KT2 = c.sb("KT2", [P, 2, SEQ], BF16, es1); BKT = Buf("KT2")
        kiT = c.sb("kiT", [64, SEQ], BF16, es1); BkiT = Buf("kiT")
        Vaug = c.sb("Vaug", [P, NKT, 4, 65], BF16, es1); BV = Buf("Vaug")
        Isb = c.sb("Isb", [P, SEQ], F32, es1); BI = Buf("Isb")
        msk = c.sb("msk", [P, SEQ], BF16, es1); Bmsk = Buf("msk")
        mskT = c.sb("mskT", [P, NKT, P], BF16, es1); BmT = Buf("mskT")
        QT2 = c.sb("QT2", [P, 8, P], BF16, es1); BQT = Buf("QT2")
        qiT = c.sb("qiT", [64, 16, P], BF16, es1); BqiT = Buf("qiT")
        diag = c.sb("diag", [P, 16, P], BF16, es1); Bdiag = Buf("diag")
        Rb = [(c.sb("R%d" % i, [P, 512], BF16, es1), Buf("R%d" % i)) for i in range(4)]
        Eb = [(c.sb("E%d" % i, [P, 512], BF16, es1), Buf("E%d" % i)) for i in range(2)]
        Pb = [(c.sb("Pm%d" % i, [P, 512], BF16, es1), Buf("Pm%d" % i)) for i in range(3)]
        Yf = c.sb("Yf", [P, 2640], F32, es1); BYf = Buf("Yf")
        Yb = c.sb("Yb", [P, 2640], BF16, es1); BYb = Buf("Yb")
        xt = c.sb("xt", [P, D], F32, es1); Bx = Buf("xt")
        xb = c.sb("xb", [P, D], BF16, es1); Bxb = Buf("xb")
        xT = c.sb("xT", [P, 8, P], BF16, es1); BxT = Buf("xT")
        Obf = c.sb("Obf", [P, D], BF16, es1); BO = Buf("Obf")
        rr = c.sb("rr", [P, D], F32, es1); Brr = Buf("rr")
        yy = rr; Byy = Brr
        gb0 = c.sb("gb0", [P, 2 * D], F32, es1); Bgb0 = Buf("gb0")
        rk = c.sb("rk", [P, NKT, 24], F32, es1); Brk = Buf("rk")
        rq = c.sb("rq", [P, NQT + 1, 24], F32, es1); Brq = Buf("rq")
        qpos = c.sb("qpos", [P, NQT + 1], F32, es1); Bqp = Buf("qpos")
        small = c.sb("small", [P, 256], F32, es1); Bsm = Buf("small")
        tmpr = c.sb("tmpr", [P, 33, 16], F32, es1); Btr = Buf("tmpr")
        wsc = c.sb("wsc", [P, 16], F32, es1); Bwsc = Buf("wsc")
        biasb = c.sb("biasb", [P, 512], F32, es1); Bbias = Buf("biasb")
        ikp = c.sb("ikp", [P, 16, 64], BF16, es1); Bikp = Buf("ikp")
        Kp = c.sb("Kp", [P, 16, 256], BF16, es1); BKp = Buf("Kp")
        Vp = c.sb("Vp", [P, 16, 256], BF16, es1); BVp = Buf("Vp")
        idxall = c.sb("idxall", [P, NSQ * 16], I32, es1); Bidx = Buf("idxall")
        ptall = idxall; Bptall = Bidx
        qisb = c.sb("qisb", [64, 64], BF16, es1); Bqisb = Buf("qisb")
        KnT2 = c.sb("KnT2", [P, 2, 64], BF16, es1); BKn = Buf("KnT2")
        kinT = c.sb("kinT", [64, 64], BF16, es1); Bkin = Buf("kinT")
        Wht = c.sb("Wht", [64, 16], F32, es1); BWht = Buf("Wht")
        Wsel = c.sb("Wsel", [64, NSQ, 64], BF16, es1); BWsel = Buf("Wsel")
        mselb = c.sb("mselb", [64, NSQ * 64], BF16, es1); Bmsel = Buf("mselb")
        Vst = c.sb("Vst", [4, 256], F32, es1); BVst = Buf("Vst")

        for t0 in range(0, NKT, 8):
            c.dma("sp", rk[:, t0:t0 + 8, :], ropek[t0 * P:(t0 + 8) * P, :].rearrange("(t p) c -> p t c", p=P), Brk,
                  writes=[Brk], nowaw=True)
        for t0 in range(0, NQT, 6):
            t1 = min(NQT, t0 + 6)
            c.dma("sp", rq[:, t0:t1, :], ropeq[t0 * P:t1 * P, :].rearrange("(t p) c -> p t c", p=P), Brq,
                  writes=[Brq], nowaw=True)
        c.dma("sp", rq[0:TS, NQT, :], ropes[:, :], Brq, writes=[Brq], nowaw=True)
        c.dma("sp", qpos[:], qpos_d[:, :], Bqp, writes=[Bqp])
        c.dma("sp", gb0[:, 0:D], bass.AP(tensor=lng_d.tensor, offset=0, ap=[[0, P], [1, D]]), Bgb0, writes=[Bgb0], nowaw=True)
        c.dma("sp", gb0[:, D:2 * D], bass.AP(tensor=lnb_d.tensor, offset=0, ap=[[0, P], [1, D]]), Bgb0, writes=[Bgb0], nowaw=True)
        c.op("pool", lambda e: e.memset(Vaug[:, :, :, 64:65], 1.0), writes=[BV])

        def front(src, Bsrc, R):
            c.op("act", lambda e: e.copy(out=xb[:R, :], in_=src), reads=[Bsrc], writes=[Bxb])
            pt, Bpt = ptbank()
            for k in range(8):
                c.op("pe", lambda e: e.transpose(pt[:, k * P:k * P + R], xb[:R, k * P:(k + 1) * P], identb[:R, :R]),
                     reads=[Bxb, Bid], writes=[Bpt])
            c.op("dve", lambda e: e.tensor_copy(out=xT[:, :, :R], in_=pt[:, :].rearrange("p (k t) -> p k t", k=8)[:, :, :R]),
                 reads=[Bpt], writes=[BxT])

        def rope(Y, BY, R, col0, H, tb):
            Yv = Y[:R, col0:col0 + 64 * H].rearrange("p (h d) -> p h d", d=64)
            tA = tmpr[:R, 0:H, 0:8]
            tB = tmpr[:R, 0:H, 8:16]
            sn = bc_mid(tb[:, 16:24], H)
            cs = bc_mid(tb[:, 0:16], H)
            c.op("dve", lambda e: e.tensor_tensor(out=tA, in0=Yv[:, :, 8:16], in1=sn, op=ALU.mult), reads=[BY], writes=[Btr])
            c.op("dve", lambda e: e.tensor_tensor(out=tB, in0=Yv[:, :, 0:8], in1=sn, op=ALU.mult), reads=[BY], writes=[Btr])
            c.op("dve", lambda e: e.tensor_tensor(out=Yv[:, :, 0:16], in0=Yv[:, :, 0:16], in1=cs, op=ALU.mult), reads=[BY], writes=[BY])
            c.op("dve", lambda e: e.tensor_tensor(out=Yv[:, :, 0:8], in0=Yv[:, :, 0:8], in1=tA, op=ALU.subtract), reads=[BY, Btr], writes=[BY])
            c.op("dve", lambda e: e.tensor_tensor(out=Yv[:, :, 8:16], in0=Yv[:, :, 8:16], in1=tB, op=ALU.add), reads=[BY, Btr], writes=[BY])

        def proj(R, wsrc, Bwsrc, ncols, ycol0):
            n0 = 0
            while n0 < ncols:
                nn = min(2048, ncols - n0)
                kper = max(1, min(8, WBN // nn))
                mts = [(m0, min(512, nn - m0)) for m0 in range(0, nn, 512)]
                bks = [c.bank() for _ in mts]
                for k0 in range(0, 8, kper):
                    kk = min(kper, 8 - k0)
                    src = wsrc[k0 * P:(k0 + kk) * P, n0:n0 + nn].rearrange("(k p) n -> p k n", p=P)
                    wb, Bw = wload([(0, kk, nn, src)], [Bwsrc])
                    wv = wb[:, 0:kk * nn].rearrange("p (k n) -> p k n", k=kk)
                    for (m0, mm), (pm, Bpm) in zip(mts, bks):
                        for k in range(kk):
                            c.op("pe", lambda e: e.matmul(pm[:R, 0:mm], lhsT=xT[:, k0 + k, :R], rhs=wv[:, k, m0:m0 + mm],
                                                          start=(k0 + k == 0), stop=(k0 + k == 7)),
                                 reads=[BxT, Bw], writes=[Bpm])
                for (m0, mm), (pm, Bpm) in zip(mts, bks):
                    c.op("act", lambda e: e.copy(out=Yf[:R, ycol0 + n0 + m0:ycol0 + n0 + m0 + mm], in_=pm[:R, 0:mm]),
                         reads=[Bpm], writes=[BYf])
                n0 += nn

        def layer_norm(src, Bsrc, R, gb, Bgb, dst, Bdst):
            st = small[:R, 0:12]
            mv = small[:R, 12:14]
            sd = small[:R, 14:15]
            rs = small[:R, 15:16]
            c.op("dve", lambda e: e.bn_stats(out=st[:, 0:6], in_=src[:, 0:512]), reads=[Bsrc], writes=[Bsm])
            c.op("dve", lambda e: e.bn_stats(out=st[:, 6:12], in_=src[:, 512:1024]), reads=[Bsrc], writes=[Bsm])
            c.op("dve", lambda e: e.bn_aggr(out=mv, in_=st), reads=[Bsm], writes=[Bsm])
            c.op("act", lambda e: e.activation(out=sd, in_=mv[:, 1:2], func=AF.Sqrt, bias=EPS, scale=1.0), reads=[Bsm], writes=[Bsm])
            c.op("dve", lambda e: e.reciprocal(out=rs, in_=sd), reads=[Bsm], writes=[Bsm])
            c.op("dve", lambda e: e.tensor_scalar(out=dst, in0=src, scalar1=mv[:, 0:1], scalar2=rs, op0=ALU.subtract, op1=ALU.mult),
                 reads=[Bsrc, Bsm], writes=[Bdst])
            c.op("pool", lambda e: e.tensor_tensor(out=dst, in0=dst, in1=gb[:R, 0:D], op=ALU.mult), reads=[Bdst, Bgb], writes=[Bdst])
            c.op("pool", lambda e: e.tensor_tensor(out=dst, in0=dst, in1=gb[:R, D:2 * D], op=ALU.add), reads=[Bdst, Bgb], writes=[Bdst])

        def bisect(R, N):
            lo = small[:R, 16:17]
            w0 = small[:R, 17:18]
            mid = small[:R, 18:19]
            cnt = small[:R, 19:20]
            tt = small[:R, 20:21]
            hw = small[:R, 32:32 + NIT]
            c.op("dve", lambda e: e.tensor_scalar(out=hw, in0=pow2[:R, :], scalar1=w0, scalar2=None, op0=ALU.mult), reads=[Bsm, Bcst], writes=[Bsm])
            for k in range(NIT):
                c.op("dve", lambda e: e.tensor_tensor(out=mid, in0=lo, in1=hw[:, k:k + 1], op=ALU.add), reads=[Bsm], writes=[Bsm])
                c.op("dve", lambda e: e.tensor_scalar(out=msk[:R, :N], in0=Isb[:R, :N], scalar1=mid, scalar2=None, op0=ALU.is_ge,
                                                      op1=ALU.add, accum_out=cnt), reads=[BI, Bsm], writes=[Bmsk, Bsm])
                c.op("dve", lambda e: e.tensor_scalar(out=tt, in0=cnt, scalar1=float(TOPK), scalar2=hw[:, k:k + 1], op0=ALU.is_ge, op1=ALU.mult),
                     reads=[Bsm], writes=[Bsm])
                c.op("dve", lambda e: e.tensor_tensor(out=lo, in0=lo, in1=tt, op=ALU.add), reads=[Bsm], writes=[Bsm])
            c.op("dve", lambda e: e.tensor_scalar(out=msk[:R, :N], in0=Isb[:R, :N], scalar1=lo, scalar2=None, op0=ALU.is_ge),
                 reads=[BI, Bsm], writes=[Bmsk])

        def evac_scores(pI, BpI, R, c0, nn, qp):
            ci = c0 // 512
            qrel = small[:R, 21:22]
            c.op("dve", lambda e: e.tensor_scalar(out=qrel, in0=qp, scalar1=float(-c0), scalar2=None, op0=ALU.add), reads=[Bqp, Bsm], writes=[Bsm])
            c.op("dve", lambda e: e.tensor_reduce(out=small[:R, 100 + ci:101 + ci], in_=pI[:R, 0:nn], axis=AX.X, op=ALU.max), reads=[BpI], writes=[Bsm])
            c.op("dve", lambda e: e.tensor_reduce(out=small[:R, 110 + ci:111 + ci], in_=pI[:R, 0:nn], axis=AX.X, op=ALU.min), reads=[BpI], writes=[Bsm])
            bias = biasb[:R, 0:nn]
            c.op("dve", lambda e: e.tensor_scalar(out=bias, in0=iota[:R, 0:nn], scalar1=qrel, scalar2=NEG, op0=ALU.is_gt, op1=ALU.mult),
                 reads=[Bcst, Bsm], writes=[Bbias])
            c.op("dve", lambda e: e.tensor_tensor(out=Isb[:R, c0:c0 + nn], in0=pI[:R, 0:nn], in1=bias, op=ALU.add), reads=[BpI, Bbias], writes=[BI])

        def bounds(R, nch):
            c.op("dve", lambda e: e.tensor_reduce(out=small[:R, 22:23], in_=small[:R, 100:100 + nch], axis=AX.X, op=ALU.max), reads=[Bsm], writes=[Bsm])
            c.op("dve", lambda e: e.tensor_reduce(out=small[:R, 23:24], in_=small[:R, 110:110 + nch], axis=AX.X, op=ALU.min), reads=[Bsm], writes=[Bsm])
            c.op("dve", lambda e: e.tensor_scalar(out=small[:R, 16:17], in0=small[:R, 23:24], scalar1=-1.0, scalar2=None, op0=ALU.add), reads=[Bsm], writes=[Bsm])
            c.op("dve", lambda e: e.tensor_scalar(out=small[:R, 17:18], in0=small[:R, 22:23], scalar1=small[:R, 23:24], scalar2=2.0,
                                                  op0=ALU.subtract, op1=ALU.add), reads=[Bsm], writes=[Bsm])

        def mask_transposes(R, nkt):
            for t0 in range(0, nkt, 8):
                t1 = min(nkt, t0 + 8)
                pt, Bpt = ptbank()
                for t in range(t0, t1):
                    c.op("pe", lambda e: e.transpose(pt[:, (t - t0) * P:(t - t0) * P + R], msk[:R, t * P:(t + 1) * P], identb[:R, :R]),
                         reads=[Bmsk, Bid], writes=[Bpt])
                c.op("act", lambda e: e.copy(out=mskT[:, t0:t1, :R], in_=pt[:, 0:(t1 - t0) * P].rearrange("p (a b) -> p a b", b=P)[:, :, :R]),
                     reads=[Bpt], writes=[BmT])

        def tail(R, xdram, row0):
            c.dma("sp", rr[:R, :], xdram, Brr, writes=[Brr])
            pt, Bpt = ptbank()
            for k in range(8):
                c.op("pe", lambda e: e.transpose(pt[:, k * P:k * P + R], Obf[:R, k * P:(k + 1) * P], identb[:R, :R]),
                     reads=[BO, Bid], writes=[Bpt])
            c.op("dve", lambda e: e.tensor_copy(out=xT[:, :, :R], in_=pt[:, :].rearrange("p (k t) -> p k t", k=8)[:, :, :R]),
                 reads=[Bpt], writes=[BxT])
            tiles = []
            for k0 in (0, 4):
                src = wo_s[k0 * P:(k0 + 4) * P, :].rearrange("(k p) n -> p k n", p=P)
                wb, Bw = wload([(0, 4, D, src)], [Bwo])
                tiles.append((k0, wb, Bw))
            for m0 in (0, 512):
                pm, Bpm = c.bank()
                for (k0, wb, Bw) in tiles:
                    wv = wb[:, 0:4 * D].rearrange("p (k n) -> p k n", k=4)
                    for k in range(4):
                        c.op("pe", lambda e: e.matmul(pm[:R, :], lhsT=xT[:, k0 + k, :R], rhs=wv[:, k, m0:m0 + 512],
                                                      start=(k0 + k == 0), stop=(k0 + k == 7)), reads=[BxT, Bw], writes=[Bpm])
                c.op("dve", lambda e: e.scalar_tensor_tensor(out=rr[:R, m0:m0 + 512], in0=rr[:R, m0:m0 + 512], scalar=ALPHA, in1=pm[:R, :],
                                                             op0=ALU.mult, op1=ALU.add), reads=[Brr, Bpm], writes=[Brr])
            layer_norm(rr[:R, :], Brr, R, gb0, Bgb0, yy[:R, :], Byy)
            c.dma("sp", y1_s[row0:row0 + R, :], yy[:R, :], Byy, reads=[Byy])
            if stage < 3:
                if row0 < TQ:
                    c.dma("sp", o_y[row0:row0 + R, :], yy[:R, :], Byy, reads=[Byy])
                else:
                    c.dma("sp", o_ys[:, :], yy[:R, :], Byy, reads=[Byy])


        def rest_phase():
            es2 = contextlib.ExitStack()
            es2.__enter__()
            ya = c.sb("ya", [P, 4, D], F32, es2); Bya = [Buf("ya%d" % t) for t in range(4)]
            yb_ = c.sb("ybb", [P, 4, D], F32, es2); Byb = [Buf("yb%d" % t) for t in range(4)]
            yT = c.sb("yT", [P, 8, 512], BF16, es2); ByT = Buf("yT")
            hT = c.sb("hT", [P, 22, 512], BF16, es2); BhT = Buf("hT")
            aex = [(c.sb("aex%d" % i, [P, 520], F32, es2), Buf("aex%d" % i)) for i in range(2)]
            uub = [(c.sb("uu%d" % i, [P, 512], F32, es2), Buf("uu%d" % i)) for i in range(2)]
            silb = [(c.sb("sil%d" % i, [P, 512], F32, es2), Buf("sil%d" % i)) for i in range(2)]
            xb2 = c.sb("xb2", [P, D], BF16, es2); Bxb2 = Buf("xb2")
            gbs = [(c.sb("gb%d" % i, [P, 2 * D], F32, es2), Buf("gb%d" % i)) for i in (1, 2, 3)]
            halo_f = c.sb("halo_f", [P, 2, 22, 2], F32, es2); Bhf = Buf("halo_f")
            halo_c = c.sb("halo_c", [P, 8, 2], F32, es2); Bhc = Buf("halo_c")
            prm = c.sb("prm", [P, 22, 11], F32, es2)
            Bprm = Buf("prm")
            sext = c.sb("sext", [P, 22, 32], F32, es2); Bsext = Buf("sext")
            sout = c.sb("sout", [P, 22, 32], F32, es2); Bsout = Buf("sout")
            stg = c.sb("stg", [32, DFF], F32, es2); Bstg = Buf("stg")
            small2 = c.sb("small2", [P, 32], F32, es2); Bsm2 = Buf("small2")

            for li in (1, 2, 3):
                gbt, Bg = gbs[li - 1]
                c.dma("sp", gbt[:, 0:D], bass.AP(tensor=lng_d.tensor, offset=li * D, ap=[[0, P], [1, D]]), Bg, writes=[Bg], nowaw=True)
                c.dma("sp", gbt[:, D:2 * D], bass.AP(tensor=lnb_d.tensor, offset=li * D, ap=[[0, P], [1, D]]), Bg, writes=[Bg], nowaw=True)
            c.op("dve", lambda e: e.memset(stg[:, :], 0.0), writes=[Bstg])
            c.dma("sp", stg[0:6, :], fcw_d.rearrange("i j n -> (i j) n"), Bstg, reads=[Bstg], writes=[Bstg])
            c.dma("sp", stg[6:8, :], fcb_d[:, :], Bstg, reads=[Bstg], writes=[Bstg], nowaw=True)
            c.dma("sp", stg[8:11, 0:D], cw_d[:, :], Bstg, reads=[Bstg], writes=[Bstg], nowaw=True)
            pmp, Bpmp = c.bank()
            for ch in range(22):
                c.op("pe", lambda e: e.transpose(pmp[:, ch * 11:(ch + 1) * 11], stg[0:11, ch * P:(ch + 1) * P], identf[0:11, 0:11]),
                     reads=[Bstg, Bcst], writes=[Bpmp])
            c.op("act", lambda e: e.copy(out=prm[:, :, :], in_=pmp[:, 0:242].rearrange("p (a b) -> p a b", b=11)), reads=[Bpmp], writes=[Bprm])
            c.op("dve", lambda e: e.memset(halo_f[:, :, :, :], 0.0), writes=[Bhf])
            c.op("dve", lambda e: e.memset(halo_c[:, :, :], 0.0), writes=[Bhc])

            def ln2(src, Bsrc, R, gb, Bgb):
                st = small2[:R, 0:12]; mv = small2[:R, 12:14]; sd = small2[:R, 14:15]; rs = small2[:R, 15:16]
                c.op("dve", lambda e: e.bn_stats(out=st[:, 0:6], in_=src[:, 0:512]), reads=[Bsrc], writes=[Bsm2])
                c.op("dve", lambda e: e.bn_stats(out=st[:, 6:12], in_=src[:, 512:1024]), reads=[Bsrc], writes=[Bsm2])
                c.op("dve", lambda e: e.bn_aggr(out=mv, in_=st), reads=[Bsm2], writes=[Bsm2])
                c.op("act", lambda e: e.activation(out=sd, in_=mv[:, 1:2], func=AF.Sqrt, bias=EPS, scale=1.0), reads=[Bsm2], writes=[Bsm2])
                c.op("dve", lambda e: e.reciprocal(out=rs, in_=sd), reads=[Bsm2], writes=[Bsm2])
                c.op("dve", lambda e: e.tensor_scalar(out=src, in0=src, scalar1=mv[:, 0:1], scalar2=rs, op0=ALU.subtract, op1=ALU.mult),
                     reads=[Bsrc, Bsm2], writes=[Bsrc])
                c.op("pool", lambda e: e.tensor_tensor(out=src, in0=src, in1=gb[:R, 0:D], op=ALU.mult), reads=[Bsrc, Bgb], writes=[Bsrc])
                c.op("pool", lambda e: e.tensor_tensor(out=src, in0=src, in1=gb[:R, D:2 * D], op=ALU.add), reads=[Bsrc, Bgb], writes=[Bsrc])

            def to_featT(Y, BY, nt, R):
                for t in range(nt):
                    c.op("act", lambda e: e.copy(out=xb2[:R, :], in_=Y[:R, t, :]), reads=[BY[t]], writes=[Bxb2])
                    pt, Bpt = ptbank()
                    for k in range(8):
                        c.op("pe", lambda e: e.transpose(pt[:, k * P:k * P + R], xb2[:R, k * P:(k + 1) * P], identb[:R, :R]),
                             reads=[Bxb2, Bid], writes=[Bpt])
                    c.op("dve", lambda e: e.tensor_copy(out=yT[:, :, t * P:t * P + R], in_=pt[:, :].rearrange("p (k t) -> p k t", k=8)[:, :, :R]),
                         reads=[Bpt], writes=[ByT])

            def conv3(ae, Bae, N, samp, w0, w1, w2, uu, Buu):
                if samp:
                    av = ae[:, 0:96].rearrange("p (b t) -> p b t", t=6)
                    uv = uu[:, 0:64].rearrange("p (b t) -> p b t", t=4)
                    s0, s1, s2 = av[:, :, 0:4], av[:, :, 1:5], av[:, :, 2:6]
                else:
                    uv = uu[:, 0:N]
                    s0, s1, s2 = ae[:, 0:N], ae[:, 1:N + 1], ae[:, 2:N + 2]
                c.op("dve", lambda e: e.tensor_scalar(out=uv, in0=s0, scalar1=w0, scalar2=None, op0=ALU.mult), reads=[Bae, Bprm], writes=[Buu])
                c.op("dve", lambda e: e.scalar_tensor_tensor(out=uv, in0=s1, scalar=w1, in1=uv, op0=ALU.mult, op1=ALU.add), reads=[Bae, Bprm, Buu], writes=[Buu])
                c.op("dve", lambda e: e.scalar_tensor_tensor(out=uv, in0=s2, scalar=w2, in1=uv, op0=ALU.mult, op1=ALU.add), reads=[Bae, Bprm, Buu], writes=[Buu])

            def load_state_T(src_dram, nch):
                c.dma("sp", stg[:, 0:nch * P], src_dram, Bstg, writes=[Bstg])
                for c0 in range(0, nch, 16):
                    c1 = min(nch, c0 + 16)
                    pm, Bpm = c.bank()
                    for ch in range(c0, c1):
                        c.op("pe", lambda e: e.transpose(pm[:, (ch - c0) * 32:(ch - c0 + 1) * 32], stg[:, ch * P:(ch + 1) * P], identf[0:32, 0:32]),
                             reads=[Bstg, Bcst], writes=[Bpm])
                    c.op("act", lambda e: e.copy(out=sext[:, c0:c1, :], in_=pm[:, 0:(c1 - c0) * 32].rearrange("p (a b) -> p a b", b=32)),
                         reads=[Bpm], writes=[Bsext])

            def store_state_T(src, Bsrc, nch, ncol, dst_dram):
                for c0 in range(0, nch, 4):
                    c1 = min(nch, c0 + 4)
                    pm, Bpm = c.bank()
                    for ch in range(c0, c1):
                        c.op("pe", lambda e: e.transpose(pm[0:ncol, (ch - c0) * P:(ch - c0 + 1) * P], src[:, ch, :], identf[:, :]),
                             reads=[Bsrc, Bcst], writes=[Bpm])
                    c.op("act", lambda e: e.copy(out=stg[0:ncol, c0 * P:c1 * P], in_=pm[0:ncol, 0:(c1 - c0) * P]), reads=[Bpm], writes=[Bstg])
                c.dma("sp", dst_dram, stg[0:ncol, 0:nch * P], Bstg, reads=[Bstg])

            def ffn(i, Yin, BYin, Yout, BYout, nt, R, samp, last, gb, Bgb):
                N = R if samp else nt * P
                if samp:
                    load_state_T(stf_d[i, :, :], 22)
                ri = 0
                for cb in range(NCB):
                    wb, Bw = wload([(0, 8, 512, wup_s[i, cb].rearrange("p k s n -> p k (s n)"))], [Bwup])
                    wv = wb[:, 0:4096].rearrange("p (k s n) -> p k s n", k=8, s=2)
                    bks = [[c.bank() for _ in range(2)] for _ in range(2)]
                    for s_ in range(2):
                        for hf in range(2):
                            pm, Bpm = bks[s_][hf]
                            for k in range(8):
                                c.op("pe", lambda e: e.matmul(pm[:, 0:N], lhsT=wv[:, k, s_, hf * P:(hf + 1) * P], rhs=yT[:, k, 0:N],
                                                              start=(k == 0), stop=(k == 7)), reads=[Bw, ByT], writes=[Bpm])
                    for hf in range(2):
                        ch = 2 * cb + hf
                        pa, Bpa = bks[0][hf]
                        pg, Bpg = bks[1][hf]
                        ae, Bae = aex[ri % 2]; uu, Buu = uub[ri % 2]; sl, Bsl = silb[ri % 2]
                        ri += 1
                        if samp:
                            av = ae[:, 0:96].rearrange("p (b t) -> p b t", t=6)
                            c.op("dve", lambda e: e.tensor_copy(out=av[:, :, 0:2], in_=sext[:, ch, :].rearrange("p (b r) -> p b r", r=2)),
                                 reads=[Bsext], writes=[Bae])
                            c.op("act", lambda e: e.copy(out=av[:, :, 2:6], in_=pa[:, 0:64].rearrange("p (b t) -> p b t", t=4)), reads=[Bpa], writes=[Bae])
                            c.op("dve", lambda e: e.tensor_copy(out=sout[:, ch, :].rearrange("p (b r) -> p b r", r=2), in_=av[:, :, 4:6]),
                                 reads=[Bae], writes=[Bsout])
                        else:
                            c.op("dve", lambda e: e.tensor_copy(out=ae[:, 0:2], in_=halo_f[:, i, ch, :]), reads=[Bhf], writes=[Bae])
                            c.op("act", lambda e: e.copy(out=ae[:, 2:2 + N], in_=pa[:, 0:N]), reads=[Bpa], writes=[Bae])
                            c.op("dve", lambda e: e.tensor_copy(out=halo_f[:, i, ch, :], in_=ae[:, N:N + 2]), reads=[Bae], writes=[Bhf])
                        conv3(ae, Bae, N, samp, prm[:, ch, 3 * i:3 * i + 1], prm[:, ch, 3 * i + 1:3 * i + 2], prm[:, ch, 3 * i + 2:3 * i + 3], uu, Buu)
                        c.op("act", lambda e: e.activation(out=sl[:, 0:N], in_=uu[:, 0:N], func=AF.Silu, bias=prm[:, ch, 6 + i:7 + i], scale=1.0),
                             reads=[Buu, Bprm], writes=[Bsl])
                        c.op("dve", lambda e: e.tensor_tensor(out=hT[:, ch, 0:N], in0=sl[:, 0:N], in1=pg[:, 0:N], op=ALU.mult),
                             reads=[Bsl, Bpg], writes=[BhT])
                for m0 in (0, 512):
                    bks = [c.bank() for _ in range(nt)]
                    for c0 in range(0, 22, 4):
                        cc = min(4, 22 - c0)
                        src = wdn_s[i, c0 * P:(c0 + cc) * P, m0:m0 + 512].rearrange("(c p) n -> p c n", p=P)
                        wb, Bw = wload([(0, cc, 512, src)], [Bwdn])
                        wv = wb[:, 0:cc * 512].rearrange("p (c n) -> p c n", c=cc)
                        for t in range(nt):
                            pm, Bpm = bks[t]
                            for cj in range(cc):
                                c.op("pe", lambda e: e.matmul(pm[:R, :], lhsT=hT[:, c0 + cj, t * P:t * P + R], rhs=wv[:, cj, :],
                                                              start=(c0 + cj == 0), stop=(c0 + cj == 21)), reads=[BhT, Bw], writes=[Bpm])
                    for t in range(nt):
                        pm, Bpm = bks[t]
                        c.op("dve", lambda e: e.scalar_tensor_tensor(out=Yout[:R, t, m0:m0 + 512], in0=Yin[:R, t, m0:m0 + 512], scalar=ALPHA,
                                                                     in1=pm[:R, :], op0=ALU.mult, op1=ALU.add), reads=[BYin[t], Bpm], writes=[BYout[t]])
                for t in range(nt):
                    ln2(Yout[:R, t, :], BYout[t], R, gb, Bgb)
                if samp:
                    store_state_T(sout, Bsout, 22, 32, o_fs[i, :, :])
                elif last:
                    store_state_T(halo_f[:, i, :, :], Bhf, 22, 2, o_fp[i, :, :])

            def mixer(Yin, BYin, Yout, BYout, nt, R, samp, last, gb, Bgb):
                N = R if samp else nt * P
                if samp:
                    load_state_T(stc_d[:, :], 8)
                ri = 0
                for ch in range(8):
                    wb, Bw = wload([(0, 8, 384, wci_s[ch].rearrange("p k s n -> p k (s n)"))], [Bwci])
                    wv = wb[:, 0:3072].rearrange("p (k s n) -> p k s n", k=8, s=3)
                    bks = [c.bank() for _ in range(3)]
                    for s_ in range(3):
                        pm, Bpm = bks[s_]
                        for k in range(8):
                            c.op("pe", lambda e: e.matmul(pm[:, 0:N], lhsT=wv[:, k, s_, :], rhs=yT[:, k, 0:N], start=(k == 0), stop=(k == 7)),
                                 reads=[Bw, ByT], writes=[Bpm])
                    (pb_, Bpb_), (pc_, Bpc_), (pu_, Bpu_) = bks
                    ae, Bae = aex[ri % 2]; uu, Buu = uub[ri % 2]
                    ri += 1
                    if samp:
                        av = ae[:, 0:96].rearrange("p (b t) -> p b t", t=6)
                        c.op("dve", lambda e: e.tensor_copy(out=av[:, :, 0:2], in_=sext[:, ch, :].rearrange("p (b r) -> p b r", r=2)), reads=[Bsext], writes=[Bae])
                        c.op("act", lambda e: e.copy(out=av[:, :, 2:6], in_=pc_[:, 0:64].rearrange("p (b t) -> p b t", t=4)), reads=[Bpc_], writes=[Bae])
                        c.op("dve", lambda e: e.tensor_tensor(out=av[:, :, 2:6], in0=av[:, :, 2:6], in1=pu_[:, 0:64].rearrange("p (b t) -> p b t", t=4), op=ALU.mult),
                             reads=[Bae, Bpu_], writes=[Bae])
                        c.op("dve", lambda e: e.tensor_copy(out=sout[:, ch, :].rearrange("p (b r) -> p b r", r=2), in_=av[:, :, 4:6]), reads=[Bae], writes=[Bsout])
                    else:
                        c.op("dve", lambda e: e.tensor_copy(out=ae[:, 0:2], in_=halo_c[:, ch, :]), reads=[Bhc], writes=[Bae])
                        c.op("act", lambda e: e.copy(out=ae[:, 2:2 + N], in_=pc_[:, 0:N]), reads=[Bpc_], writes=[Bae])
                        c.op("dve", lambda e: e.tensor_tensor(out=ae[:, 2:2 + N], in0=ae[:, 2:2 + N], in1=pu_[:, 0:N], op=ALU.mult), reads=[Bae, Bpu_], writes=[Bae])
                        c.op("dve", lambda e: e.tensor_copy(out=halo_c[:, ch, :], in_=ae[:, N:N + 2]), reads=[Bae], writes=[Bhc])
                    conv3(ae, Bae, N, samp, prm[:, ch, 8:9], prm[:, ch, 9:10], prm[:, ch, 10:11], uu, Buu)
                    c.op("dve", lambda e: e.tensor_tensor(out=hT[:, ch, 0:N], in0=uu[:, 0:N], in1=pb_[:, 0:N], op=ALU.mult), reads=[Buu, Bpb_], writes=[BhT])
                tiles = []
                for k0 in (0, 4):
                    src = wco_s[k0 * P:(k0 + 4) * P, :].rearrange("(k p) n -> p k n", p=P)
                    wb, Bw = wload([(0, 4, D, src)], [Bwco])
                    tiles.append((k0, wb, Bw))
                for t in range(nt):
                    for m0 in (0, 512):
                        pm, Bpm = c.bank()
                        for (k0, wb, Bw) in tiles:
                            wv = wb[:, 0:4 * D].rearrange("p (k n) -> p k n", k=4)
                            for k in range(4):
                                c.op("pe", lambda e: e.matmul(pm[:R, :], lhsT=hT[:, k0 + k, t * P:t * P + R], rhs=wv[:, k, m0:m0 + 512],
                                                              start=(k0 + k == 0), stop=(k0 + k == 7)), reads=[BhT, Bw], writes=[Bpm])
                        c.op("dve", lambda e: e.scalar_tensor_tensor(out=Yout[:R, t, m0:m0 + 512], in0=Yin[:R, t, m0:m0 + 512], scalar=ALPHA,
                                                                     in1=pm[:R, :], op0=ALU.mult, op1=ALU.add), reads=[BYin[t], Bpm], writes=[BYout[t]])
                    ln2(Yout[:R, t, :], BYout[t], R, gb, Bgb)
                if samp:
                    store_state_T(sout, Bsout, 8, 32, o_cs[:, :])
                elif last:
                    store_state_T(halo_c[:, :, :], Bhc, 8, 2, o_cp[:, :])

            glist = [(t0 * P, nt, P, False, gi == len(GROUPS) - 1) for gi, (t0, nt) in enumerate(GROUPS)] + [(TQ, 1, TS, True, False)]
            for (row0, nt, R, samp, last) in glist:
                for t in range(nt):
                    c.dma("sp", ya[:R, t, :], y1_s[row0 + t * P:row0 + t * P + R, :], Bya[t], writes=[Bya[t]])
                to_featT(ya, Bya, nt, R)
                ffn(0, ya, Bya, yb_, Byb, nt, R, samp, last, gbs[0][0], gbs[0][1])
                to_featT(yb_, Byb, nt, R)
                mixer(yb_, Byb, ya, Bya, nt, R, samp, last, gbs[1][0], gbs[1][1])
                to_featT(ya, Bya, nt, R)
                ffn(1, ya, Bya, yb_, Byb, nt, R, samp, last, gbs[2][0], gbs[2][1])
                for t in range(nt):
                    dst = o_ys[:, :] if samp else o_y[row0 + t * P:row0 + (t + 1) * P, :]
                    c.dma("sp", dst, yb_[:R, t, :], Byb[t], reads=[Byb[t]])
            c.barrier()
            es2.__exit__(None, None, None)

        for kt in range(NKT):
            c.dma("sp", xt[:], xk[kt * P:(kt + 1) * P, :], Bx, writes=[Bx])
            drain(4)
            front(xt[:, :], Bx, P)
            proj(P, wkv_s, Bwkv, 576, 0)
            rope(Yf, BYf, P, 0, 4, rk[:, kt, :])
            rope(Yf, BYf, P, 512, 1, rk[:, kt, :])
            rows = slice(kt * P, (kt + 1) * P)
            c.dma("sp", o_k[rows, :], Yf[:, 0:256], BYf, reads=[BYf])
            c.dma("sp", o_v[rows, :], Yf[:, 256:512], BYf, reads=[BYf])
            c.dma("sp", o_ik[rows, :], Yf[:, 512:576], BYf, reads=[BYf])
            c.op("pool", lambda e: e.tensor_copy(out=Yb[:, 0:576], in_=Yf[:, 0:576]), reads=[BYf], writes=[BYb])
            c.op("pool", lambda e: e.tensor_copy(out=Vaug[:, kt, :, 0:64], in_=Yf[:, 256:512].rearrange("p (g d) -> p g d", d=64)),
                 reads=[BYf], writes=[BV])
            pt, Bpt = ptbank()
            for gp in range(2):
                c.op("pe", lambda e: e.transpose(pt[:, gp * P:(gp + 1) * P], Yb[:, gp * P:(gp + 1) * P], identb[:, :]), reads=[BYb, Bid], writes=[Bpt])
            c.op("pe", lambda e: e.transpose(pt[0:64, 2 * P:3 * P], Yb[:, 512:576], identb[:, :]), reads=[BYb, Bid], writes=[Bpt])
            c.op("act", lambda e: e.copy(out=KT2[:, :, kt * P:(kt + 1) * P], in_=pt[:, 0:2 * P].rearrange("p (a b) -> p a b", b=P)),
                 reads=[Bpt], writes=[BKT])
            c.op("act", lambda e: e.copy(out=kiT[:, kt * P:(kt + 1) * P], in_=pt[0:64, 2 * P:3 * P]), reads=[Bpt], writes=[BkiT])

        def q_front(R, tb):
            rope(Yf, BYf, R, 0, 32, tb)
            for gp in range(2):
                c.op("pool", lambda e: e.tensor_copy(
                    out=Yb[:R, gp * 512:(gp + 1) * 512].rearrange("p (h r d) -> p h r d", h=4, r=2),
                    in_=Yf[:R, gp * 512:(gp + 1) * 512].rearrange("p (r h d) -> p h r d", h=4, r=2)), reads=[BYf], writes=[BYb])
            c.op("pool", lambda e: e.tensor_copy(out=Yb[:R, 1024:2048], in_=Yf[:R, 1024:2048]), reads=[BYf], writes=[BYb])
            c.op("dve", lambda e: e.tensor_scalar(out=wsc[:R, :], in0=Yf[:R, 2048:2064], scalar1=IDX_SCALE, scalar2=None, op0=ALU.mult),
                 reads=[BYf], writes=[Bwsc])

        def q_transposes(R):
            pt, Bpt = ptbank()
            for gp in range(2):
                for hh in range(4):
                    idx = gp * 4 + hh
                    src = Yb[:R, idx * P:(idx + 1) * P]
                    c.op("pe", lambda e: e.transpose(pt[:, idx * P:idx * P + R], src, identb[:R, :R]), reads=[BYb, Bid], writes=[Bpt])
            c.op("act", lambda e: e.copy(out=QT2[:, :, :R], in_=pt[:, :].rearrange("p (a b) -> p a b", b=P)[:, :, :R]), reads=[Bpt], writes=[BQT])
            for h0 in (0, 8):
                pt, Bpt = ptbank()
                for h in range(8):
                    col = 1024 + (h0 + h) * 64
                    c.op("pe", lambda e: e.transpose(pt[0:64, h * P:h * P + R], Yb[:R, col:col + 64], identb[:R, :R]), reads=[BYb, Bid], writes=[Bpt])
                c.op("act", lambda e: e.copy(out=qiT[:, h0:h0 + 8, :R], in_=pt[0:64, :].rearrange("p (a b) -> p a b", b=P)[:, :, :R]),
                     reads=[Bpt], writes=[BqiT])

        import os as _os
        SUB = int(_os.environ.get("DBG_SUB", "99"))
        NQR = int(_os.environ.get("DBG_NQ", str(NQT)))
        def attention(NK, maskfn, Bmask, Od, BOd):
            obanks = [c.banks[0], c.banks[1], c.banks[2]]
            for (ob, Bob) in obanks:
                c.op("pe", lambda e: e.matmul(ob[:, :], lhsT=zerob[:, 0:P], rhs=zerob[:, :], start=True, stop=False),
                     reads=[Bzero], writes=[Bob])
            units = [(kt, g) for kt in range(NK) for g in range(4)]
            LA = 2
            fr = []

            def front_u(ei, kt, g):
                pS, BpS = c.bank((3, 4, 5))
                pb = (g % 2) * 64
                c.op("pe", lambda e: e.matmul(pS[:, :], lhsT=KT2[pb:pb + 64, g // 2, kt * P:(kt + 1) * P],
                                              rhs=QT2[pb:pb + 64, (g // 2) * 4:(g // 2) * 4 + 4, :], start=True, stop=True),
                     reads=[BKT, BQT], writes=[BpS])
                E_, BE_ = Eb[ei % 2]
                Pm_, BPm_ = Pb[ei % 3]
                c.op("act", lambda e: e.activation(out=E_[:, :], in_=pS[:, :], func=AF.Exp, scale=0.125), reads=[BpS], writes=[BE_])
                c.op("pool" if ei % 2 == 0 else "dve", lambda e: e.tensor_tensor(out=Pm_[:, :].rearrange("p (a b) -> p a b", a=4),
                                                       in0=E_[:, :].rearrange("p (a b) -> p a b", a=4),
                                                       in1=bc_mid(maskfn(kt), 4), op=ALU.mult), reads=[BE_, Bmask], writes=[BPm_])
                return (Pm_, BPm_)

            def back_u(ei, kt, g):
                Pm_, BPm_ = fr[ei]
                for hh in range(4):
                    h = 4 * g + hh
                    ob, Bob = obanks[h // 7]
                    oc = (h % 7) * 65
                    c.op("pe", lambda e: e.matmul(ob[:, oc:oc + 65], lhsT=Pm_[:, hh * P:(hh + 1) * P], rhs=Vaug[:, kt, g, :],
                                                  start=False, stop=(kt == NK - 1 and (h % 7 == 6 or h == 15))), reads=[BPm_, BV], writes=[Bob])

            for i in range(len(units) + LA):
                if i < len(units):
                    fr.append(front_u(i, *units[i]))
                if i >= LA:
                    back_u(i - LA, *units[i - LA])
            for bi, (ob, Bob) in enumerate(obanks):
                nh = 7 if bi < 2 else 2
                ov = ob[:, 0:nh * 65].rearrange("p (h d) -> p h d", d=65)
                rec = small[:, 200 + 7 * bi:200 + 7 * bi + nh]
                c.op("dve", lambda e: e.reciprocal(out=rec, in_=ov[:, :, 64]), reads=[Bob], writes=[Bsm])
                c.op("dve", lambda e: e.tensor_tensor(out=Od[:, bi * 7 * 64:(bi * 7 + nh) * 64].rearrange("p (h d) -> p h d", d=64),
                                                      in0=ov[:, :, 0:64], in1=bc_last(rec, 64), op=ALU.mult), reads=[Bob, Bsm], writes=[BOd])

        if stage >= 1:
            for j in range(NQR):
                NK = 16 + j
                N = NK * P
                c.dma("sp", xt[:], xq[j * P:(j + 1) * P, :], Bx, writes=[Bx])
                drain(8)
                front(xt[:, :], Bx, P)
                proj(P, wq_s, Bwq, 2064, 0)
                q_front(P, rq[:, j, :])
                if SUB < 1:
                    continue
                q_transposes(P)
                c.op("dve", lambda e: e.tensor_tensor(out=diag[:, :, :], in0=bc_mid(identb[:, :], 16), in1=bc_last(wsc[:, :], P), op=ALU.mult),
                     reads=[Bid, Bwsc], writes=[Bdiag])
                if SUB < 2:
                    continue
                nch = (N + 511) // 512
                items = [(ci, h) for ci in range(nch) for h in range(16)]
                pIs = {}
                frs = []
                LAI = 2

                def idx_front(ii, ci, h):
                    c0 = ci * 512
                    nn = min(512, N - c0)
                    psc, Bpsc = c.bank((0, 1, 2, 3))
                    c.op("pe", lambda e: e.matmul(psc[:, 0:nn], lhsT=qiT[:, h, :], rhs=kiT[:, c0:c0 + nn], start=True, stop=True),
                         reads=[BqiT, BkiT], writes=[Bpsc])
                    R_, BR_ = Rb[ii % 4]
                    if h % 2 == 0:
                        c.op("act", lambda e: e.activation(out=R_[:, 0:nn], in_=psc[:, 0:nn], func=AF.Relu), reads=[Bpsc], writes=[BR_])
                    else:
                        c.op("dve", lambda e: e.tensor_scalar(out=R_[:, 0:nn], in0=psc[:, 0:nn], scalar1=0.0, scalar2=None, op0=ALU.max),
                             reads=[Bpsc], writes=[BR_])
                    return (R_, BR_)

                def idx_back(ii, ci, h):
                    c0 = ci * 512
                    nn = min(512, N - c0)
                    if h == 0:
                        pIs[ci] = c.bank((4, 5))
                    pI, BpI = pIs[ci]
                    R_, BR_ = frs[ii]
                    c.op("pe", lambda e: e.matmul(pI[:, 0:nn], lhsT=diag[:, h, :], rhs=R_[:, 0:nn], start=(h == 0), stop=(h == 15)),
                         reads=[Bdiag, BR_], writes=[BpI])
                    if h == 15:
                        evac_scores(pI, BpI, P, c0, nn, qpos[:, j:j + 1])

                for ii in range(len(items) + LAI):
                    if ii < len(items):
                        frs.append(idx_front(ii, *items[ii]))
                    if ii >= LAI:
                        idx_back(ii - LAI, *items[ii - LAI])
                if SUB < 3:
                    continue
                bounds(P, nch)
                bisect(P, N)
                if SUB < 4:
                    continue
                mask_transposes(P, NK)
                if SUB < 5:
                    continue
                attention(NK, lambda kt: mskT[:, kt, :], BmT, Obf, BO)
                if SUB < 6:
                    continue
                tail(P, xq[j * P:(j + 1) * P, :], j * P)
                if _os.environ.get("DBG_DUMP") and j == 0:
                    c.dma("sp", o_y[128:256, :], Isb[:, 0:1024], BI, reads=[BI])
                    c.dma("sp", o_y[256:384, 0:256], small[:, :], Bsm, reads=[Bsm])
                    c.dma("sp", o_y[384:512, :], rr[:, :], Brr, reads=[Brr])


        def sample_phase():
            R = TS
            IOA = bass.IndirectOffsetOnAxis
            c.dma("sp", xt[:R, :], xs[:, :], Bx, writes=[Bx])
            c.dma("pool", mselb[:, :], msel_d[:, :], Bmsel, writes=[Bmsel])
            c.dma("sp", ptall[:, :], bass.AP(tensor=pt_d.tensor, offset=0, ap=[[0, P], [1, NSQ * 16]]), Bptall, writes=[Bptall])
            c.op("dve", lambda e: e.tensor_scalar(out=idxall[:, :], in0=ptall[:, :], scalar1=128.0, scalar2=pidx, op0=ALU.mult, op1=ALU.add),
                 reads=[Bptall, Bcst], writes=[Bidx])
            front(xt[:R, :], Bx, R)
            proj(R, wq_s, Bwq, 2064, 0)
            proj(R, wkv_s, Bwkv, 576, 2064)
            tb = rq[0:R, NQT, :]
            q_front(R, tb)
            rope(Yf, BYf, R, 2064, 4, tb)
            rope(Yf, BYf, R, 2064 + 512, 1, tb)
            c.dma("sp", o_ks[:, :], Yf[:R, 2064:2320], BYf, reads=[BYf])
            c.dma("sp", o_vs[:, :], Yf[:R, 2320:2576], BYf, reads=[BYf])
            c.dma("sp", o_iks[:, :], Yf[:R, 2576:2640], BYf, reads=[BYf])
            c.op("pool", lambda e: e.tensor_copy(out=Yb[:R, 2064:2640], in_=Yf[:R, 2064:2640]), reads=[BYf], writes=[BYb])
            q_transposes(R)
            pt, Bpt = ptbank()
            for gp in range(2):
                c.op("pe", lambda e: e.transpose(pt[:, gp * P:gp * P + R], Yb[:R, 2064 + gp * P:2064 + (gp + 1) * P], identb[:R, :R]),
                     reads=[BYb, Bid], writes=[Bpt])
            c.op("pe", lambda e: e.transpose(pt[0:64, 2 * P:2 * P + R], Yb[:R, 2576:2640], identb[:R, :R]), reads=[BYb, Bid], writes=[Bpt])
            c.op("act", lambda e: e.copy(out=KnT2[:, :, :], in_=pt[:, 0:2 * P].rearrange("p (a b) -> p a b", b=P)[:, :, 0:R]), reads=[Bpt], writes=[BKn])
            c.op("act", lambda e: e.copy(out=kinT[:, :], in_=pt[0:64, 2 * P:2 * P + R]), reads=[Bpt], writes=[Bkin])
            Bwsd = Buf("wsd")
            c.dma("sp", wsd_s[:, :], wsc[:R, :], Bwsd, reads=[Bwsc], writes=[Bwsd])
            for h in range(16):
                src = bass.AP(tensor=wsd_s.tensor, offset=h, ap=[[16, 4], [64, NSQ]])
                c.dma("sp", Wht[4 * h:4 * h + 4, :], src, BWht, reads=[Bwsd], writes=[BWht], nowaw=(h > 0), slow=True)
            c.op("dve", lambda e: e.tensor_tensor(out=Wsel[:, :, :], in0=mselb[:, :].rearrange("p (a b) -> p a b", b=64), in1=bc_last(Wht[:, :], 64), op=ALU.mult),
                 reads=[Bmsel, BWht], writes=[BWsel])
            c.op("dve", lambda e: e.memset(kiT[:, PAST:PAST + P], 0.0), writes=[BkiT])
            c.op("dve", lambda e: e.memset(KT2[:, :, PAST:PAST + P], 0.0), writes=[BKT])
            c.op("pool", lambda e: e.memset(Vaug[:, 16, :, 0:64], 0.0), writes=[BV])
            nch = 5
            SS = float(_os.environ.get("DBG_SS", "99"))
            if SS < 1:
                return
            for b in range(NSQ):
                for j in range(16):
                    col = b * 16 + j
                    c.dma("pool", ikp[:, j, :], cik_d[:, :], Bikp, reads=[Bidx], writes=[Bikp], nowaw=(j > 0),
                          indirect=IOA(ap=idxall[:, col:col + 1], axis=0))
                for j0 in (0, 8):
                    pt, Bpt = ptbank()
                    for j in range(8):
                        c.op("pe", lambda e: e.transpose(pt[0:64, j * P:(j + 1) * P], ikp[:, j0 + j, :], identb[:, :]), reads=[Bikp, Bid], writes=[Bpt])
                    c.op("act", lambda e: e.copy(out=kiT[:, j0 * P:(j0 + 8) * P], in_=pt[0:64, :]), reads=[Bpt], writes=[BkiT])
                c.op("dve", lambda e: e.tensor_copy(out=kiT[:, PAST:PAST + 4], in_=kinT[:, 4 * b:4 * b + 4]), reads=[Bkin], writes=[BkiT])
                c.op("dve", lambda e: e.tensor_copy(out=qisb[:, :].rearrange("p (h t) -> p h t", t=4), in_=qiT[:, :, 4 * b:4 * b + 4]), reads=[BqiT], writes=[Bqisb])
                for ci in range(nch):
                    c0 = ci * 512
                    nn = min(512, NKS * P - c0)
                    psc, Bpsc = c.banks[5]
                    pI, BpI = c.banks[ci]
                    c.op("pe", lambda e: e.matmul(psc[0:64, 0:nn], lhsT=qisb[:, :], rhs=kiT[:, c0:c0 + nn], start=True, stop=True),
                         reads=[Bqisb, BkiT], writes=[Bpsc])
                    R_, BR_ = Rb[(b * nch + ci) % 4]
                    c.op("act", lambda e: e.activation(out=R_[0:64, 0:nn], in_=psc[0:64, 0:nn], func=AF.Relu), reads=[Bpsc], writes=[BR_])
                    c.op("pe", lambda e: e.matmul(pI[0:64, 0:nn], lhsT=Wsel[:, b, :], rhs=R_[0:64, 0:nn], start=(b == 0), stop=(b == NSQ - 1)),
                         reads=[BWsel, BR_], writes=[BpI])
            if SS < 2:
                return
            for ci in range(nch):
                c0 = ci * 512
                nn = min(512, NKS * P - c0)
                pI, BpI = c.banks[ci]
                evac_scores(pI, BpI, R, c0, nn, qpos[0:R, NQT:NQT + 1])
            bounds(R, nch)
            bisect(R, NKS * P)
            mask_transposes(R, NKS)
            if SS < 2.5:
                return
            mskS = msk[:, 0:NKS * P].rearrange("p (k t) -> p k t", t=P)
            c.op("pool", lambda e: e.memset(msk[:, 0:NKS * P], 0.0), writes=[Bmsk])
            for b in range(NSQ):
                for j in range(16):
                    col = b * 16 + j
                    c.dma("pool", Kp[:, j, :], ck_d[:, :], BKp, reads=[Bidx], writes=[BKp], nowaw=(j > 0),
                          indirect=IOA(ap=idxall[:, col:col + 1], axis=0))
                for j in range(16):
                    col = b * 16 + j
                    c.dma("pool", Vp[:, j, :], cv_d[:, :], BVp, reads=[Bidx], writes=[BVp], nowaw=(j > 0),
                          indirect=IOA(ap=idxall[:, col:col + 1], axis=0))
                c.op("pool", lambda e: e.tensor_copy(out=Vaug[:, 0:16, :, 0:64], in_=Vp[:, :, :].rearrange("p j (g d) -> p j g d", d=64)),
                     reads=[BVp], writes=[BV])
                c.dma("sp", Vst[:, :], Yf[4 * b:4 * b + 4, 2320:2576], BVst, reads=[BYf], writes=[BVst])
                c.op("pool", lambda e: e.tensor_copy(out=Vaug[0:4, 16, :, 0:64], in_=Vst[:, :].rearrange("p (g d) -> p g d", d=64)), reads=[BVst], writes=[BV])
                for j0 in range(0, 16, 4):
                    pt, Bpt = ptbank()
                    for gp in range(2):
                        for j in range(4):
                            c.op("pe", lambda e: e.transpose(pt[:, (gp * 4 + j) * P:(gp * 4 + j + 1) * P], Kp[:, j0 + j, gp * P:(gp + 1) * P], identb[:, :]),
                                 reads=[BKp, Bid], writes=[Bpt])
                    c.op("act", lambda e: e.copy(out=KT2[:, :, j0 * P:(j0 + 4) * P], in_=pt[:, :].rearrange("p (a b) -> p a b", a=2)), reads=[Bpt], writes=[BKT])
                c.op("dve", lambda e: e.tensor_copy(out=KT2[:, :, PAST:PAST + 4], in_=KnT2[:, :, 4 * b:4 * b + 4]), reads=[BKn], writes=[BKT])
                if SS < 2.7:
                    continue
                if b > 0:
                    c.op("pool", lambda e: e.memset(mskS[:, :, 4 * (b - 1):4 * b], 0.0), writes=[Bmsk])
                c.op("pool", lambda e: e.tensor_copy(out=mskS[:, :, 4 * b:4 * b + 4], in_=mskT[:, 0:NKS, 4 * b:4 * b + 4]), reads=[BmT], writes=[Bmsk])
                attention(NKS, lambda kt: mskS[:, kt, :], Bmsk, xb, Bxb)
                c.dma("sp", Obf[4 * b:4 * b + 4, :], xb[4 * b:4 * b + 4, :], BO, reads=[Bxb], writes=[BO], nowaw=True)
            if SS < 4:
                return
            tail(R, xs[:, :], TQ)

        if stage >= 2:
            sample_phase()
        drain(10000)
        c.barrier()
        es1.__exit__(None, None, None)
        if stage >= 3:
            rest_phase()
        c.finish()
    return nc


def _rope_table(pos):
    half = 8
    inv = (500000.0 ** (-np.arange(half, dtype=np.float32) * np.float32(2.0 / 16))).astype(np.float32)
    ang = pos.astype(np.float32)[:, None] * inv[None, :]
    cs = np.cos(ang).astype(np.float32)
    sn = np.sin(ang).astype(np.float32)
    return np.concatenate([cs, cs, sn], axis=1).astype(np.float32)


_NC_CACHE = {}


def _run(inputs, nphys=None, stage=99, compact=False):
    f = lambda a: np.ascontiguousarray(np.asarray(a))
    x_prompt = f(inputs["x_prompt"]); x_sample = f(inputs["x_sample"])
    cache_k = f(inputs["cache_k"])[0]; cache_v = f(inputs["cache_v"])[0]; cache_ik = f(inputs["cache_idx_k"])[0]
    page_table = f(inputs["page_table"]).astype(np.int32)
    full_nphys = cache_k.shape[0]
    if nphys is None:
        nphys = full_nphys
    key = (nphys, stage)
    if key not in _NC_CACHE:
        _NC_CACHE[key] = build(nphys, stage)
    nc = _NC_CACHE[key]
    consts = np.zeros((P, 512 + 128 + NIT + 1), np.float32)
    consts[:, 0:512] = np.arange(512, dtype=np.float32)[None, :]
    consts[:, 512:640] = np.eye(P, dtype=np.float32)
    consts[:, 640:640 + NIT] = (0.5 ** np.arange(1, NIT + 1, dtype=np.float64)).astype(np.float32)[None, :]
    consts[:, 640 + NIT] = np.arange(P, dtype=np.float32)
    msel = np.zeros((64, NSQ, 64), np.float32)
    for h in range(16):
        for t in range(4):
            for b in range(NSQ):
                msel[h * 4 + t, b, b * 4 + t] = 1.0
    ropek = _rope_table(np.arange(SEQ))
    ropes = _rope_table(PAST + (np.arange(TS) % 4))
    in_maps = []
    for core in range(8):
        b, h = core // 2, core % 2
        pos0 = 0 if h == 0 else POS0
        qp = np.zeros((P, NQT + 1), np.float32)
        qp[:, :NQT] = pos0 + np.arange(P)[:, None] + P * np.arange(NQT)[None, :]
        qp[:TS, NQT] = PAST + (np.arange(TS) % 4)
        sl = slice(core * NSQ, (core + 1) * NSQ)
        pt = page_table[sl]
        if compact:
            pages = np.unique(pt)
            remap = {int(p): i for i, p in enumerate(pages)}
            ck = np.zeros((nphys, P, 256), np.float32); cv = np.zeros((nphys, P, 256), np.float32); ci = np.zeros((nphys, P, 64), np.float32)
            ck[:len(pages)] = cache_k[pages].reshape(-1, P, 256); cv[:len(pages)] = cache_v[pages].reshape(-1, P, 256)
            ci[:len(pages)] = cache_ik[pages]
            pt = np.vectorize(remap.get)(pt).astype(np.int32)
        else:
            ck = cache_k.reshape(-1, P, 256); cv = cache_v.reshape(-1, P, 256); ci = cache_ik
        m = {
            "xk": x_prompt[b], "xq": x_prompt[b, pos0:pos0 + TQ], "xs": x_sample[sl].reshape(TS, D),
            "ropek": ropek, "ropeq": ropek[pos0:pos0 + TQ], "ropes": ropes, "qpos": qp, "consts": consts,
            "msel": msel.reshape(64, NSQ * 64), "pt": pt,
            "cache_k": ck.reshape(nphys * P, 256), "cache_v": cv.reshape(nphys * P, 256), "cache_ik": ci.reshape(nphys * P, 64),
            "st_conv": f(inputs["state_conv"])[0, sl].reshape(NSQ * 2, D),
            "st_ffn": f(inputs["state_ffn"])[:, sl].reshape(2, NSQ * 2, DFF),
            "w_attn_in": f(inputs["w_attn_in"])[0], "w_attn_out": f(inputs["w_attn_out"])[0],
            "w_conv_in": f(inputs["w_conv_in"])[0], "conv_w": f(inputs["conv_w"])[0], "w_conv_out": f(inputs["w_conv_out"])[0],
            "w_ffn_up": f(inputs["w_ffn_up"]), "ffn_conv_w": f(inputs["ffn_conv_w"]), "ffn_conv_b": f(inputs["ffn_conv_b"]),
            "w_ffn_down": f(inputs["w_ffn_down"]), "ln_g": f(inputs["ln_g"]).reshape(4, D), "ln_b": f(inputs["ln_b"]).reshape(4, D),
        }
        in_maps.append({k: np.ascontiguousarray(v) for k, v in m.items()})
    res = run_bass_kernel_spmd(nc, in_maps, core_ids=list(range(8)))
    R = res.results
    B = 4
    y_prompt = np.zeros((B, SEQ, D), np.float32)
    nk = np.zeros((1, B, SEQ, 4, 64), np.float32); nv = np.zeros_like(nk); nik = np.zeros((1, B, SEQ, 64), np.float32)
    cp = np.zeros((1, B, 2, D), np.float32); fp = np.zeros((2, B, 2, DFF), np.float32)
    y_sample = np.zeros((128, 4, D), np.float32)
    nks = np.zeros((1, 128, 4, 4, 64), np.float32); nvs = np.zeros_like(nks); niks = np.zeros((1, 128, 4, 64), np.float32)
    cs = np.zeros((1, 128, 2, D), np.float32); fs = np.zeros((2, 128, 2, DFF), np.float32)
    for core in range(8):
        b, h = core // 2, core % 2
        r = R[core]
        sl = slice(core * NSQ, (core + 1) * NSQ)
        if h == 0:
            y_prompt[b, 0:2048] = r["o_y"][0:2048]
            nk[0, b] = r["o_k"].reshape(SEQ, 4, 64); nv[0, b] = r["o_v"].reshape(SEQ, 4, 64); nik[0, b] = r["o_ik"]
        else:
            y_prompt[b, 2048:4096] = r["o_y"][128:TQ]
            cp[0, b] = r["o_cp"]; fp[:, b] = r["o_fp"]
        y_sample[sl] = r["o_ys"].reshape(NSQ, 4, D)
        nks[0, sl] = r["o_ks"].reshape(NSQ, 4, 4, 64); nvs[0, sl] = r["o_vs"].reshape(NSQ, 4, 4, 64); niks[0, sl] = r["o_iks"].reshape(NSQ, 4, 64)
        cs[0, sl] = r["o_cs"].reshape(NSQ, 2, D); fs[:, sl] = r["o_fs"].reshape(2, NSQ, 2, DFF)
    return (y_prompt, y_sample, nk, nv, nik, nks, nvs, niks, cp, cs, fp, fs), R


def kernel(**inputs):
    outs, _ = _run(inputs)
    return outs
```

```python
import contextlib
import numpy as np
import concourse.bass as bass
import concourse.mybir as mybir
from concourse.bass_utils import run_bass_kernel_spmd

F32 = mybir.dt.float32
BF16 = mybir.dt.bfloat16
I32 = mybir.dt.int32
AF = mybir.ActivationFunctionType
ALU = mybir.AluOpType
AX = mybir.AxisListType

P = 128
D = 1024
DFF = 2816
NCB = 11
SEQ = 4096
NKT = 32
NQT = 17
TQ = NQT * P
POS0 = 1920
NSQ = 16
TS = 64
PAST = 2048
NKS = 17
LS = PAST + 4
TOPK = 256
NIT = 14
ALPHA = 4.0 ** 0.25
IDX_SCALE = 1.0 / 32.0
EPS = 1e-5
NEG = -1.0e30
GROUPS = [(0, 1), (1, 4), (5, 4), (9, 4), (13, 4)]


class Buf:
    __slots__ = ("name", "w", "r", "dsem", "dtot")

    def __init__(self, name):
        self.name = name
        self.w = None
        self.r = {}
        self.dsem = None
        self.dtot = 0


def bc_mid(ap, n):
    a = [list(x) for x in ap.ap]
    return bass.AP(tensor=ap.tensor, offset=ap.offset, ap=[a[0], [0, n]] + a[1:])


def bc_last(ap, n):
    a = [list(x) for x in ap.ap]
    return bass.AP(tensor=ap.tensor, offset=ap.offset, ap=a + [[0, n]])


class Ctx:
    CH = 16000

    def __init__(self, nc, es):
        self.nc = nc
        self.es = es
        self.engs = {"pe": nc.tensor, "dve": nc.vector, "act": nc.scalar, "pool": nc.gpsimd, "sp": nc.sync}
        self.cnt = {e: 0 for e in self.engs}
        self.sems = {e: [] for e in self.engs}
        self.waited = {e: {} for e in self.engs}
        self.dma_bufs = []
        self.nsem = 0
        self.banks = []
        self.bank_i = 0
        self.wbufs = []
        self.wb_i = 0

    def new_sem(self, name):
        self.nsem += 1
        return self.es.enter_context(self.nc.semaphore(name))

    def sb(self, name, shape, dt, es=None):
        return (es or self.es).enter_context(self.nc.sbuf_tensor("sb_" + name, list(shape), dt))

    def ps(self, name, shape, dt):
        return self.es.enter_context(self.nc.psum_tensor("ps_" + name, list(shape), dt))

    def _esem(self, e, tick):
        ch = (tick - 1) // self.CH
        while len(self.sems[e]) <= ch:
            self.sems[e].append(self.new_sem("s_%s_%d" % (e, len(self.sems[e]))))
        return self.sems[e][ch], (tick - 1) % self.CH + 1

    def _wait(self, e, tok):
        if tok[0] == "eng":
            _, f, tick = tok
            key = f
            if self.waited[e].get(key, 0) >= tick:
                return
            sem, val = self._esem(f, tick)
        else:
            _, buf, val = tok
            key = ("d", id(buf))
            tick = val
            if self.waited[e].get(key, 0) >= tick:
                return
            sem = buf.dsem
        self.engs[e].wait_ge(sem, val)
        self.waited[e][key] = tick

    def _deps(self, e, reads, writes, nowaw=False):
        for b in reads:
            t = b.w
            if t is not None:
                if t[0] == "eng" and t[1] == e and e in ("pe", "sp"):
                    continue
                self._wait(e, t)
        if nowaw:
            return
        for b in writes:
            t = b.w
            if t is not None:
                if not (t[0] == "eng" and t[1] == e and e in ("pe", "sp", "dve", "act")):
                    self._wait(e, t)
            for t in b.r.values():
                if t[0] == "eng" and t[1] == e and e != "pool":
                    continue
                self._wait(e, t)

    def _commit(self, tok, reads, writes):
        key = tok[1] if tok[0] == "eng" else ("d", id(tok[1]))
        for b in reads:
            b.r[key] = tok
        for b in writes:
            b.w = tok
            b.r = {}

    def op(self, e, fn, reads=(), writes=()):
        self._deps(e, reads, writes)
        ins = fn(self.engs[e])
        self.cnt[e] += 1
        tick = self.cnt[e]
        sem, _ = self._esem(e, tick)
        ins.then_inc(sem, 1)
        self._commit(("eng", e, tick), reads, writes)
        return ins

    def dma(self, q, out, in_, sbuf, reads=(), writes=(), nowaw=False, indirect=None, slow=False):
        self._deps(q, reads, writes, nowaw)
        if sbuf.dsem is None:
            sbuf.dsem = self.new_sem("d_" + sbuf.name)
            self.dma_bufs.append(sbuf)
        sbuf.dtot += 16
        if indirect is not None:
            ins = self.nc.gpsimd.indirect_dma_start(out=out, out_offset=None, in_=in_, in_offset=indirect)
        else:
            ins = self.engs[q].dma_start(out=out, in_=in_, allow_slow_non_contiguous=True) if slow else self.engs[q].dma_start(out=out, in_=in_)
        ins.then_inc(sbuf.dsem, 16)
        self._commit(("dma", sbuf, sbuf.dtot), reads, writes)

    def barrier(self):
        for e in self.engs:
            for f in self.engs:
                if f != e and self.cnt[f] > 0:
                    self._wait(e, ("eng", f, self.cnt[f]))
            for b in self.dma_bufs:
                if b.dtot > 0:
                    self._wait(e, ("dma", b, b.dtot))

    def finish(self):
        for b in self.dma_bufs:
            self.engs["sp"].wait_ge(b.dsem, b.dtot)

    def bank(self, subset=None):
        if subset is None:
            subset = range(len(self.banks))
        self.bank_i += 1
        return self.banks[subset[self.bank_i % len(subset)]]

    def wbuf(self):
        i = self.wb_i % len(self.wbufs)
        self.wb_i += 1
        return self.wbufs[i]


def build(nphys, stage=99):
    nc = bass.Bass("TRN2", target_bir_lowering=False)

    def din(name, shape, dt=F32):
        return nc.dram_tensor(name, list(shape), dt, kind="ExternalInput").ap()

    def dout(name, shape, dt=F32):
        return nc.dram_tensor(name, list(shape), dt, kind="ExternalOutput").ap()

    def dscr(name, shape, dt):
        return nc.dram_tensor(name, list(shape), dt, kind="Internal").ap()

    NROW = nphys * P
    xk = din("xk", [SEQ, D])
    xq = din("xq", [TQ, D])
    xs = din("xs", [TS, D])
    ropek = din("ropek", [SEQ, 24])
    ropeq = din("ropeq", [TQ, 24])
    ropes = din("ropes", [TS, 24])
    qpos_d = din("qpos", [P, NQT + 1])
    consts = din("consts", [P, 512 + 128 + NIT + 1])
    msel_d = din("msel", [64, NSQ * 64])
    pt_d = din("pt", [NSQ, 16], I32)
    ck_d = din("cache_k", [NROW, 256])
    cv_d = din("cache_v", [NROW, 256])
    cik_d = din("cache_ik", [NROW, 64])
    stc_d = din("st_conv", [NSQ * 2, D])
    stf_d = din("st_ffn", [2, NSQ * 2, DFF])
    w_ai = din("w_attn_in", [D, 2640])
    w_ao = din("w_attn_out", [D, D])
    w_ci = din("w_conv_in", [D, 3 * D])
    cw_d = din("conv_w", [3, D])
    w_co = din("w_conv_out", [D, D])
    w_up = din("w_ffn_up", [2, D, 2 * DFF])
    fcw_d = din("ffn_conv_w", [2, 3, DFF])
    fcb_d = din("ffn_conv_b", [2, DFF])
    w_dn = din("w_ffn_down", [2, DFF, D])
    lng_d = din("ln_g", [4, D])
    lnb_d = din("ln_b", [4, D])

    o_y = dout("o_y", [TQ, D])
    o_ys = dout("o_ys", [TS, D])
    o_k = dout("o_k", [SEQ, 256])
    o_v = dout("o_v", [SEQ, 256])
    o_ik = dout("o_ik", [SEQ, 64])
    o_ks = dout("o_ks", [TS, 256])
    o_vs = dout("o_vs", [TS, 256])
    o_iks = dout("o_iks", [TS, 64])
    o_cp = dout("o_cp", [2, D])
    o_cs = dout("o_cs", [NSQ * 2, D])
    o_fp = dout("o_fp", [2, 2, DFF])
    o_fs = dout("o_fs", [2, NSQ * 2, DFF])

    wq_s = dscr("wq_s", [D, 2064], BF16)
    wkv_s = dscr("wkv_s", [D, 576], BF16)
    wo_s = dscr("wo_s", [D, D], BF16)
    wup_s = dscr("wup_s", [2, NCB, P, 8, 2, 256], BF16)
    wdn_s = dscr("wdn_s", [2, DFF, D], BF16)
    wci_s = dscr("wci_s", [8, P, 8, 3, 128], BF16)
    wco_s = dscr("wco_s", [D, D], BF16)
    y1_s = dscr("y1_s", [TQ + TS, D], F32)
    wsd_s = dscr("wsd_s", [TS, 16], F32)

    es = contextlib.ExitStack()
    with es:
        c = Ctx(nc, es)
        for i in range(6):
            c.banks.append((c.ps("pm%d" % i, [P, 512], F32), Buf("pm%d" % i)))
        ptb = [(c.ps("pt%d" % i, [P, 1024], BF16), Buf("pt%d" % i)) for i in range(2)]
        pt_i = [0]

        def ptbank():
            i = pt_i[0] % 2
            pt_i[0] += 1
            return ptb[i]

        WBN = 4608
        for i in range(4):
            c.wbufs.append((c.sb("wb%d" % i, [P, WBN], BF16), Buf("wb%d" % i)))
        cst = c.sb("cst", [P, 512 + 128 + NIT + 1], F32)
        Bcst = Buf("cst")
        identb = c.sb("identb", [P, P], BF16)
        Bid = Buf("identb")
        zerob = c.sb("zerob", [P, 512], BF16); Bzero = Buf("zerob")
        c.op("pool", lambda e: e.memset(zerob[:, :], 0.0), writes=[Bzero])
        c.dma("sp", cst[:], consts[:, :], Bcst, writes=[Bcst])
        c.dma("pool", identb[:], consts[:, 512:640], Bid, writes=[Bid])
        iota = cst[:, 0:512]
        identf = cst[:, 512:640]
        pow2 = cst[:, 640:640 + NIT]
        pidx = cst[:, 640 + NIT:641 + NIT]

        Bwq, Bwkv, Bwo, Bwup, Bwdn, Bwci, Bwco = (Buf(n) for n in ("wq", "wkv", "wo", "wup", "wdn", "wci", "wco"))

        pending = []

        def conv_now(dst, src, B):
            c.dma("pool", dst, src, B, writes=[B], nowaw=True)

        def conv(dst, src, B):
            if B in (Bwkv, Bwq, Bwo):
                conv_now(dst, src, B)
            else:
                pending.append((dst, src, B))

        def drain(n):
            for _ in range(n):
                if pending:
                    conv_now(*pending.pop(0))

        for r0 in range(0, D, 256):
            rs = slice(r0, r0 + 256)
            conv(wkv_s[rs, 0:512], w_ai[rs, 1024:1536], Bwkv)
            conv(wkv_s[rs, 512:576], w_ai[rs, 2560:2624], Bwkv)
        for r0 in range(0, D, 256):
            rs = slice(r0, r0 + 256)
            conv(wq_s[rs, 0:1024], w_ai[rs, 0:1024], Bwq)
            conv(wq_s[rs, 1024:2048], w_ai[rs, 1536:2560], Bwq)
            conv(wq_s[rs, 2048:2064], w_ai[rs, 2624:2640], Bwq)
            conv(wo_s[rs, :], w_ao[rs, :], Bwo)
        for i in range(2):
            wv = w_up[i].rearrange("(k p) (s c n) -> c k p s n", p=P, s=2, c=NCB, n=256)
            for cb in range(NCB):
                for k in range(8):
                    conv(wup_s[i, cb, :, k, :, :], wv[cb, k], Bwup)
            for r0 in range(0, DFF, 704):
                conv(wdn_s[i, r0:r0 + 704, :], w_dn[i, r0:r0 + 704, :], Bwdn)
            if i == 0:
                wv = w_ci.rearrange("(k p) (s c n) -> c k p s n", p=P, s=3, c=8, n=128)
                for ch in range(8):
                    for k in range(8):
                        conv(wci_s[ch, :, k, :, :], wv[ch, k], Bwci)
                for r0 in range(0, D, 256):
                    conv(wco_s[r0:r0 + 256, :], w_co[r0:r0 + 256, :], Bwco)

        def wload(parts, reads):
            wb, Bw = c.wbuf()
            for (off, a, b, src) in parts:
                dst = wb[:, off:off + a * b].rearrange("p (a b) -> p a b", a=a)
                c.dma("sp", dst, src, Bw, reads=reads, writes=[Bw])
            return wb, Bw

        es1 = contextlib.ExitStack()
        es1.__enter__()
        KT2 = c.sb("KT2", [P, 2, SEQ], BF16, es1); BKT = Buf("KT2")
        kiT = c.sb("kiT", [64, SEQ], BF16, es1); BkiT = Buf("kiT")
        Vaug = c.sb("Vaug", [P, NKT, 4, 65], BF16, es1); BV = Buf("Vaug")
        Isb = c.sb("Isb", [P, SEQ], F32, es1); BI = Buf("Isb")
        msk = c.sb("msk", [P, SEQ], BF16, es1); Bmsk = Buf("msk")
        mskT = c.sb("mskT", [P, NKT, P], BF16, es1); BmT = Buf("mskT")
        QT2 = c.sb("QT2", [P, 8, P], BF16, es1); BQT = Buf("QT2")
        qiT = c.sb("qiT", [64, 16, P], BF16, es1); BqiT = Buf("qiT")
        diag = c.sb("diag", [P, 16, P], BF16, es1); Bdiag = Buf("diag")
        Rb = [(c.sb("R%d" % i, [P, 512], BF16, es1), Buf("R%d" % i)) for i in range(4)]
        Eb = [(c.sb("E%d" % i, [P, 512], BF16, es1), Buf("E%d" % i)) for i in range(2)]
        Pb = [(c.sb("Pm%d" % i, [P, 512], BF16, es1), Buf("Pm%d" % i)) for i in range(3)]
        Yf = c.sb("Yf", [P, 2640], F32, es1); BYf = Buf("Yf")
        Yb = c.sb("Yb", [P, 2640], BF16, es1); BYb = Buf("Yb")
        xt = c.sb("xt", [P, D], F32, es1); Bx = Buf("xt")
        xb = c.sb("xb", [P, D], BF16, es1); Bxb = Buf("xb")
        xT = c.sb("xT", [P, 8, P], BF16, es1); BxT = Buf("xT")
        Obf = c.sb("Obf", [P, D], BF16, es1); BO = Buf("Obf")
        rr = c.sb("rr", [P, D], F32, es1); Brr = Buf("rr")
        yy = rr; Byy = Brr
        gb0 = c.sb("gb0", [P, 2 * D], F32, es1); Bgb0 = Buf("gb0")
        rk = c.sb("rk", [P, NKT, 24], F32, es1); Brk = Buf("rk")
        rq = c.sb("rq", [P, NQT + 1, 24], F32, es1); Brq = Buf("rq")
        qpos = c.sb("qpos", [P, NQT + 1], F32, es1); Bqp = Buf("qpos")
        small = c.sb("small", [P, 256], F32, es1); Bsm = Buf("small")
        tmpr = c.sb("tmpr", [P, 33, 16], F32, es1); Btr = Buf("tmpr")
        wsc = c.sb("wsc", [P, 16], F32, es1); Bwsc = Buf("wsc")
        biasb = c.sb("biasb", [P, 512], F32, es1); Bbias = Buf("biasb")
        ikp = c.sb("ikp", [P, 16, 64], BF16, es1); Bikp = Buf("ikp")
        Kp = c.sb("Kp", [P, 16, 256], BF16, es1); BKp = Buf("Kp")
        Vp = c.sb("Vp", [P, 16, 256], BF16, es1); BVp = Buf("Vp")
        idxall = c.sb("idxall", [P, NSQ * 16], I32, es1); Bidx = Buf("idxall")
        ptall = idxall; Bptall = Bidx
        qisb = c.sb("qisb", [64, 64], BF16, es1); Bqisb = Buf("qisb")
        KnT2 = c.sb("KnT2", [P, 2, 64], BF16, es1); BKn = Buf("KnT2")
        kinT = c.sb("kinT", [64, 64], BF16, es1); Bkin = Buf("kinT")
        Wht = c.sb("Wht", [64, 16], F32, es1); BWht = Buf("Wht")
        Wsel = c.sb("Wsel", [64, NSQ, 64], BF16, es1); BWsel = Buf("Wsel")
        mselb = c.sb("mselb", [64, NSQ * 64], BF16, es1); Bmsel = Buf("mselb")
        Vst = c.sb("Vst", [4, 256], F32, es1); BVst = Buf("Vst")

        for t0 in range(0, NKT, 8):
            c.dma("sp", rk[:, t0:t0 + 8, :], ropek[t0 * P:(t0 + 8) * P, :].rearrange("(t p) c -> p t c", p=P), Brk,
                  writes=[Brk], nowaw=True)
        for t0 in range(0, NQT, 6):
            t1 = min(NQT, t0 + 6)
            c.dma("sp", rq[:, t0:t1, :], ropeq[t0 * P:t1 * P, :].rearrange("(t p) c -> p t c", p=P), Brq,
                  writes=[Brq], nowaw=True)
        c.dma("sp", rq[0:TS, NQT, :], ropes[:, :], Brq, writes=[Brq], nowaw=True)
        c.dma("sp", qpos[:], qpos_d[:, :], Bqp, writes=[Bqp])
        c.dma("sp", gb0[:, 0:D], bass.AP(tensor=lng_d.tensor, offset=0, ap=[[0, P], [1, D]]), Bgb0, writes=[Bgb0], nowaw=True)
        c.dma("sp", gb0[:, D:2 * D], bass.AP(tensor=lnb_d.tensor, offset=0, ap=[[0, P], [1, D]]), Bgb0, writes=[Bgb0], nowaw=True)
        c.op("pool", lambda e: e.memset(Vaug[:, :, :, 64:65], 1.0), writes=[BV])

        def front(src, Bsrc, R):
            c.op("act", lambda e: e.copy(out=xb[:R, :], in_=src), reads=[Bsrc], writes=[Bxb])
            pt, Bpt = ptbank()
            for k in range(8):
                c.op("pe", lambda e: e.transpose(pt[:, k * P:k * P + R], xb[:R, k * P:(k + 1) * P], identb[:R, :R]),
                     reads=[Bxb, Bid], writes=[Bpt])
            c.op("dve", lambda e: e.tensor_copy(out=xT[:, :, :R], in_=pt[:, :].rearrange("p (k t) -> p k t", k=8)[:, :, :R]),
                 reads=[Bpt], writes=[BxT])

        def rope(Y, BY, R, col0, H, tb):
            Yv = Y[:R, col0:col0 + 64 * H].rearrange("p (h d) -> p h d", d=64)
            tA = tmpr[:R, 0:H, 0:8]
            tB = tmpr[:R, 0:H, 8:16]
            sn = bc_mid(tb[:, 16:24], H)
            cs = bc_mid(tb[:, 0:16], H)
            c.op("dve", lambda e: e.tensor_tensor(out=tA, in0=Yv[:, :, 8:16], in1=sn, op=ALU.mult), reads=[BY], writes=[Btr])
            c.op("dve", lambda e: e.tensor_tensor(out=tB, in0=Yv[:, :, 0:8], in1=sn, op=ALU.mult), reads=[BY], writes=[Btr])
            c.op("dve", lambda e: e.tensor_tensor(out=Yv[:, :, 0:16], in0=Yv[:, :, 0:16], in1=cs, op=ALU.mult), reads=[BY], writes=[BY])
            c.op("dve", lambda e: e.tensor_tensor(out=Yv[:, :, 0:8], in0=Yv[:, :, 0:8], in1=tA, op=ALU.subtract), reads=[BY, Btr], writes=[BY])
            c.op("dve", lambda e: e.tensor_tensor(out=Yv[:, :, 8:16], in0=Yv[:, :, 8:16], in1=tB, op=ALU.add), reads=[BY, Btr], writes=[BY])

        def proj(R, wsrc, Bwsrc, ncols, ycol0):
            n0 = 0
            while n0 < ncols:
                nn = min(2048, ncols - n0)
                kper = max(1, min(8, WBN // nn))
                mts = [(m0, min(512, nn - m0)) for m0 in range(0, nn, 512)]
                bks = [c.bank() for _ in mts]
                for k0 in range(0, 8, kper):
                    kk = min(kper, 8 - k0)
                    src = wsrc[k0 * P:(k0 + kk) * P, n0:n0 + nn].rearrange("(k p) n -> p k n", p=P)
                    wb, Bw = wload([(0, kk, nn, src)], [Bwsrc])
                    wv = wb[:, 0:kk * nn].rearrange("p (k n) -> p k n", k=kk)
                    for (m0, mm), (pm, Bpm) in zip(mts, bks):
                        for k in range(kk):
                            c.op("pe", lambda e: e.matmul(pm[:R, 0:mm], lhsT=xT[:, k0 + k, :R], rhs=wv[:, k, m0:m0 + mm],
                                                          start=(k0 + k == 0), stop=(k0 + k == 7)),
                                 reads=[BxT, Bw], writes=[Bpm])
                for (m0, mm), (pm, Bpm) in zip(mts, bks):
                    c.op("act", lambda e: e.copy(out=Yf[:R, ycol0 + n0 + m0:ycol0 + n0 + m0 + mm], in_=pm[:R, 0:mm]),
                         reads=[Bpm], writes=[BYf])
                n0 += nn

        def layer_norm(src, Bsrc, R, gb, Bgb, dst, Bdst):
            st = small[:R, 0:12]
            mv = small[:R, 12:14]
            sd = small[:R, 14:15]
            rs = small[:R, 15:16]
            c.op("dve", lambda e: e.bn_stats(out=st[:, 0:6], in_=src[:, 0:512]), reads=[Bsrc], writes=[Bsm])
            c.op("dve", lambda e: e.bn_stats(out=st[:, 6:12], in_=src[:, 512:1024]), reads=[Bsrc], writes=[Bsm])
            c.op("dve", lambda e: e.bn_aggr(out=mv, in_=st), reads=[Bsm], writes=[Bsm])
            c.op("act", lambda e: e.activation(out=sd, in_=mv[:, 1:2], func=AF.Sqrt, bias=EPS, scale=1.0), reads=[Bsm], writes=[Bsm])
            c.op("dve", lambda e: e.reciprocal(out=rs, in_=sd), reads=[Bsm], writes=[Bsm])
            c.op("dve", lambda e: e.tensor_scalar(out=dst, in0=src, scalar1=mv[:, 0:1], scalar2=rs, op0=ALU.subtract, op1=ALU.mult),
                 reads=[Bsrc, Bsm], writes=[Bdst])
            c.op("pool", lambda e: e.tensor_tensor(out=dst, in0=dst, in1=gb[:R, 0:D], op=ALU.mult), reads=[Bdst, Bgb], writes=[Bdst])
            c.op("pool", lambda e: e.tensor_tensor(out=dst, in0=dst, in1=gb[:R, D:2 * D], op=ALU.add), reads=[Bdst, Bgb], writes=[Bdst])

        def bisect(R, N):
            lo = small[:R, 16:17]
            w0 = small[:R, 17:18]
            mid = small[:R, 18:19]
            cnt = small[:R, 19:20]
            tt = small[:R, 20:21]
            hw = small[:R, 32:32 + NIT]
            c.op("dve", lambda e: e.tensor_scalar(out=hw, in0=pow2[:R, :], scalar1=w0, scalar2=None, op0=ALU.mult), reads=[Bsm, Bcst], writes=[Bsm])
            for k in range(NIT):
                c.op("dve", lambda e: e.tensor_tensor(out=mid, in0=lo, in1=hw[:, k:k + 1], op=ALU.add), reads=[Bsm], writes=[Bsm])
                c.op("dve", lambda e: e.tensor_scalar(out=msk[:R, :N], in0=Isb[:R, :N], scalar1=mid, scalar2=None, op0=ALU.is_ge,
                                                      op1=ALU.add, accum_out=cnt), reads=[BI, Bsm], writes=[Bmsk, Bsm])
                c.op("dve", lambda e: e.tensor_scalar(out=tt, in0=cnt, scalar1=float(TOPK), scalar2=hw[:, k:k + 1], op0=ALU.is_ge, op1=ALU.mult),
                     reads=[Bsm], writes=[Bsm])
                c.op("dve", lambda e: e.tensor_tensor(out=lo, in0=lo, in1=tt, op=ALU.add), reads=[Bsm], writes=[Bsm])
            c.op("dve", lambda e: e.tensor_scalar(out=msk[:R, :N], in0=Isb[:R, :N], scalar1=lo, scalar2=None, op0=ALU.is_ge),
                 reads=[BI, Bsm], writes=[Bmsk])

        def evac_scores(pI, BpI, R, c0, nn, qp):
            ci = c0 // 512
            qrel = small[:R, 21:22]
            c.op("dve", lambda e: e.tensor_scalar(out=qrel, in0=qp, scalar1=float(-c0), scalar2=None, op0=ALU.add), reads=[Bqp, Bsm], writes=[Bsm])
            c.op("dve", lambda e: e.tensor_reduce(out=small[:R, 100 + ci:101 + ci], in_=pI[:R, 0:nn], axis=AX.X, op=ALU.max), reads=[BpI], writes=[Bsm])
            c.op("dve", lambda e: e.tensor_reduce(out=small[:R, 110 + ci:111 + ci], in_=pI[:R, 0:nn], axis=AX.X, op=ALU.min), reads=[BpI], writes=[Bsm])
            bias = biasb[:R, 0:nn]
            c.op("dve", lambda e: e.tensor_scalar(out=bias, in0=iota[:R, 0:nn], scalar1=qrel, scalar2=NEG, op0=ALU.is_gt, op1=ALU.mult),
                 reads=[Bcst, Bsm], writes=[Bbias])
            c.op("dve", lambda e: e.tensor_tensor(out=Isb[:R, c0:c0 + nn], in0=pI[:R, 0:nn], in1=bias, op=ALU.add), reads=[BpI, Bbias], writes=[BI])

        def bounds(R, nch):
            c.op("dve", lambda e: e.tensor_reduce(out=small[:R, 22:23], in_=small[:R, 100:100 + nch], axis=AX.X, op=ALU.max), reads=[Bsm], writes=[Bsm])
            c.op("dve", lambda e: e.tensor_reduce(out=small[:R, 23:24], in_=small[:R, 110:110 + nch], axis=AX.X, op=ALU.min), reads=[Bsm], writes=[Bsm])
            c.op("dve", lambda e: e.tensor_scalar(out=small[:R, 16:17], in0=small[:R, 23:24], scalar1=-1.0, scalar2=None, op0=ALU.add), reads=[Bsm], writes=[Bsm])
            c.op("dve", lambda e: e.tensor_scalar(out=small[:R, 17:18], in0=small[:R, 22:23], scalar1=small[:R, 23:24], scalar2=2.0,
                                                  op0=ALU.subtract, op1=ALU.add), reads=[Bsm], writes=[Bsm])

        def mask_transposes(R, nkt):
            for t0 in range(0, nkt, 8):
                t1 = min(nkt, t0 + 8)
                pt, Bpt = ptbank()
                for t in range(t0, t1):
                    c.op("pe", lambda e: e.transpose(pt[:, (t - t0) * P:(t - t0) * P + R], msk[:R, t * P:(t + 1) * P], identb[:R, :R]),
                         reads=[Bmsk, Bid], writes=[Bpt])
                c.op("act", lambda e: e.copy(out=mskT[:, t0:t1, :R], in_=pt[:, 0:(t1 - t0) * P].rearrange("p (a b) -> p a b", b=P)[:, :, :R]),
                     reads=[Bpt], writes=[BmT])

        def tail(R, xdram, row0):
            c.dma("sp", rr[:R, :], xdram, Brr, writes=[Brr])
            pt, Bpt = ptbank()
            for k in range(8):
                c.op("pe", lambda e: e.transpose(pt[:, k * P:k * P + R], Obf[:R, k * P:(k + 1) * P], identb[:R, :R]),
                     reads=[BO, Bid], writes=[Bpt])
            c.op("dve", lambda e: e.tensor_copy(out=xT[:, :, :R], in_=pt[:, :].rearrange("p (k t) -> p k t", k=8)[:, :, :R]),
                 reads=[Bpt], writes=[BxT])
            tiles = []
            for k0 in (0, 4):
                src = wo_s[k0 * P:(k0 + 4) * P, :].rearrange("(k p) n -> p k n", p=P)
                wb, Bw = wload([(0, 4, D, src)], [Bwo])
                tiles.append((k0, wb, Bw))
            for m0 in (0, 512):
                pm, Bpm = c.bank()
                for (k0, wb, Bw) in tiles:
                    wv = wb[:, 0:4 * D].rearrange("p (k n) -> p k n", k=4)
                    for k in range(4):
                        c.op("pe", lambda e: e.matmul(pm[:R, :], lhsT=xT[:, k0 + k, :R], rhs=wv[:, k, m0:m0 + 512],
                                                      start=(k0 + k == 0), stop=(k0 + k == 7)), reads=[BxT, Bw], writes=[Bpm])
                c.op("dve", lambda e: e.scalar_tensor_tensor(out=rr[:R, m0:m0 + 512], in0=rr[:R, m0:m0 + 512], scalar=ALPHA, in1=pm[:R, :],
                                                             op0=ALU.mult, op1=ALU.add), reads=[Brr, Bpm], writes=[Brr])
            layer_norm(rr[:R, :], Brr, R, gb0, Bgb0, yy[:R, :], Byy)
            c.dma("sp", y1_s[row0:row0 + R, :], yy[:R, :], Byy, reads=[Byy])
            if stage < 3:
                if row0 < TQ:
                    c.dma("sp", o_y[row0:row0 + R, :], yy[:R, :], Byy, reads=[Byy])
                else:
                    c.dma("sp", o_ys[:, :], yy[:R, :], Byy, reads=[Byy])


        def rest_phase():
            es2 = contextlib.ExitStack()
            es2.__enter__()
            ya = c.sb("ya", [P, 4, D], F32, es2); Bya = [Buf("ya%d" % t) for t in range(4)]
            yb_ = c.sb("ybb", [P, 4, D], F32, es2); Byb = [Buf("yb%d" % t) for t in range(4)]
            yT = c.sb("yT", [P, 8, 512], BF16, es2); ByT = Buf("yT")
            hT = c.sb("hT", [P, 22, 512], BF16, es2); BhT = Buf("hT")
            aex = [(c.sb("aex%d" % i, [P, 520], F32, es2), Buf("aex%d" % i)) for i in range(2)]
            uub = [(c.sb("uu%d" % i, [P, 512], F32, es2), Buf("uu%d" % i)) for i in range(2)]
            silb = [(c.sb("sil%d" % i, [P, 512], F32, es2), Buf("sil%d" % i)) for i in range(2)]
            xb2 = c.sb("xb2", [P, D], BF16, es2); Bxb2 = Buf("xb2")
            gbs = [(c.sb("gb%d" % i, [P, 2 * D], F32, es2), Buf("gb%d" % i)) for i in (1, 2, 3)]
            halo_f = c.sb("halo_f", [P, 2, 22, 2], F32, es2); Bhf = Buf("halo_f")
            halo_c = c.sb("halo_c", [P, 8, 2], F32, es2); Bhc = Buf("halo_c")
            prm = c.sb("prm", [P, 22, 11], F32, es2)
            Bprm = Buf("prm")
            sext = c.sb("sext", [P, 22, 32], F32, es2); Bsext = Buf("sext")
            sout = c.sb("sout", [P, 22, 32], F32, es2); Bsout = Buf("sout")
            stg = c.sb("stg", [32, DFF], F32, es2); Bstg = Buf("stg")
            small2 = c.sb("small2", [P, 32], F32, es2); Bsm2 = Buf("small2")

            for li in (1, 2, 3):
                gbt, Bg = gbs[li - 1]
                c.dma("sp", gbt[:, 0:D], bass.AP(tensor=lng_d.tensor, offset=li * D, ap=[[0, P], [1, D]]), Bg, writes=[Bg], nowaw=True)
                c.dma("sp", gbt[:, D:2 * D], bass.AP(tensor=lnb_d.tensor, offset=li * D, ap=[[0, P], [1, D]]), Bg, writes=[Bg], nowaw=True)
            c.op("dve", lambda e: e.memset(stg[:, :], 0.0), writes=[Bstg])
            c.dma("sp", stg[0:6, :], fcw_d.rearrange("i j n -> (i j) n"), Bstg, reads=[Bstg], writes=[Bstg])
            c.dma("sp", stg[6:8, :], fcb_d[:, :], Bstg, reads=[Bstg], writes=[Bstg], nowaw=True)
            c.dma("sp", stg[8:11, 0:D], cw_d[:, :], Bstg, reads=[Bstg], writes=[Bstg], nowaw=True)
            pmp, Bpmp = c.bank()
            for ch in range(22):
                c.op("pe", lambda e: e.transpose(pmp[:, ch * 11:(ch + 1) * 11], stg[0:11, ch * P:(ch + 1) * P], identf[0:11, 0:11]),
                     reads=[Bstg, Bcst], writes=[Bpmp])
            c.op("act", lambda e: e.copy(out=prm[:, :, :], in_=pmp[:, 0:242].rearrange("p (a b) -> p a b", b=11)), reads=[Bpmp], writes=[Bprm])
            c.op("dve", lambda e: e.memset(halo_f[:, :, :, :], 0.0), writes=[Bhf])
            c.op("dve", lambda e: e.memset(halo_c[:, :, :], 0.0), writes=[Bhc])

            def ln2(src, Bsrc, R, gb, Bgb):
                st = small2[:R, 0:12]; mv = small2[:R, 12:14]; sd = small2[:R, 14:15]; rs = small2[:R, 15:16]
                c.op("dve", lambda e: e.bn_stats(out=st[:, 0:6], in_=src[:, 0:512]), reads=[Bsrc], writes=[Bsm2])
                c.op("dve", lambda e: e.bn_stats(out=st[:, 6:12], in_=src[:, 512:1024]), reads=[Bsrc], writes=[Bsm2])
                c.op("dve", lambda e: e.bn_aggr(out=mv, in_=st), reads=[Bsm2], writes=[Bsm2])
                c.op("act", lambda e: e.activation(out=sd, in_=mv[:, 1:2], func=AF.Sqrt, bias=EPS, scale=1.0), reads=[Bsm2], writes=[Bsm2])
                c.op("dve", lambda e: e.reciprocal(out=rs, in_=sd), reads=[Bsm2], writes=[Bsm2])
                c.op("dve", lambda e: e.tensor_scalar(out=src, in0=src, scalar1=mv[:, 0:1], scalar2=rs, op0=ALU.subtract, op1=ALU.mult),
                     reads=[Bsrc, Bsm2], writes=[Bsrc])
                c.op("pool", lambda e: e.tensor_tensor(out=src, in0=src, in1=gb[:R, 0:D], op=ALU.mult), reads=[Bsrc, Bgb], writes=[Bsrc])
                c.op("pool", lambda e: e.tensor_tensor(out=src, in0=src, in1=gb[:R, D:2 * D], op=ALU.add), reads=[Bsrc, Bgb], writes=[Bsrc])

            def to_featT(Y, BY, nt, R):
                for t in range(nt):
                    c.op("act", lambda e: e.copy(out=xb2[:R, :], in_=Y[:R, t, :]), reads=[BY[t]], writes=[Bxb2])
                    pt, Bpt = ptbank()
                    for k in range(8):
                        c.op("pe", lambda e: e.transpose(pt[:, k * P:k * P + R], xb2[:R, k * P:(k + 1) * P], identb[:R, :R]),
                             reads=[Bxb2, Bid], writes=[Bpt])
                    c.op("dve", lambda e: e.tensor_copy(out=yT[:, :, t * P:t * P + R], in_=pt[:, :].rearrange("p (k t) -> p k t", k=8)[:, :, :R]),
                         reads=[Bpt], writes=[ByT])

            def conv3(ae, Bae, N, samp, w0, w1, w2, uu, Buu):
                if samp:
                    av = ae[:, 0:96].rearrange("p (b t) -> p b t", t=6)
                    uv = uu[:, 0:64].rearrange("p (b t) -> p b t", t=4)
                    s0, s1, s2 = av[:, :, 0:4], av[:, :, 1:5], av[:, :, 2:6]
                else:
                    uv = uu[:, 0:N]
                    s0, s1, s2 = ae[:, 0:N], ae[:, 1:N + 1], ae[:, 2:N + 2]
                c.op("dve", lambda e: e.tensor_scalar(out=uv, in0=s0, scalar1=w0, scalar2=None, op0=ALU.mult), reads=[Bae, Bprm], writes=[Buu])
                c.op("dve", lambda e: e.scalar_tensor_tensor(out=uv, in0=s1, scalar=w1, in1=uv, op0=ALU.mult, op1=ALU.add), reads=[Bae, Bprm, Buu], writes=[Buu])
                c.op("dve", lambda e: e.scalar_tensor_tensor(out=uv, in0=s2, scalar=w2, in1=uv, op0=ALU.mult, op1=ALU.add), reads=[Bae, Bprm, Buu], writes=[Buu])

            def load_state_T(src_dram, nch):
                c.dma("sp", stg[:, 0:nch * P], src_dram, Bstg, writes=[Bstg])
                for c0 in range(0, nch, 16):
                    c1 = min(nch, c0 + 16)
                    pm, Bpm = c.bank()
                    for ch in range(c0, c1):
                        c.op("pe", lambda e: e.transpose(pm[:, (ch - c0) * 32:(ch - c0 + 1) * 32], stg[:, ch * P:(ch + 1) * P], identf[0:32, 0:32]),
                             reads=[Bstg, Bcst], writes=[Bpm])
                    c.op("act", lambda e: e.copy(out=sext[:, c0:c1, :], in_=pm[:, 0:(c1 - c0) * 32].rearrange("p (a b) -> p a b", b=32)),
                         reads=[Bpm], writes=[Bsext])

            def store_state_T(src, Bsrc, nch, ncol, dst_dram):
                for c0 in range(0, nch, 4):
                    c1 = min(nch, c0 + 4)
                    pm, Bpm = c.bank()
                    for ch in range(c0, c1):
                        c.op("pe", lambda e: e.transpose(pm[0:ncol, (ch - c0) * P:(ch - c0 + 1) * P], src[:, ch, :], identf[:, :]),
                             reads=[Bsrc, Bcst], writes=[Bpm])
                    c.op("act", lambda e: e.copy(out=stg[0:ncol, c0 * P:c1 * P], in_=pm[0:ncol, 0:(c1 - c0) * P]), reads=[Bpm], writes=[Bstg])
                c.dma("sp", dst_dram, stg[0:ncol, 0:nch * P], Bstg, reads=[Bstg])

            def ffn(i, Yin, BYin, Yout, BYout, nt, R, samp, last, gb, Bgb):
                N = R if samp else nt * P
                if samp:
                    load_state_T(stf_d[i, :, :], 22)
                ri = 0
                for cb in range(NCB):
                    wb, Bw = wload([(0, 8, 512, wup_s[i, cb].rearrange("p k s n -> p k (s n)"))], [Bwup])
                    wv = wb[:, 0:4096].rearrange("p (k s n) -> p k s n", k=8, s=2)
                    for hf in range(2):
                        ch = 2 * cb + hf
                        pa, Bpa = c.bank()
                        pg, Bpg = c.bank()
                        for (pm, Bpm, s_) in ((pa, Bpa, 0), (pg, Bpg, 1)):
                            for k in range(8):
                                c.op("pe", lambda e: e.matmul(pm[:, 0:N], lhsT=wv[:, k, s_, hf * P:(hf + 1) * P], rhs=yT[:, k, 0:N],
                                                              start=(k == 0), stop=(k == 7)), reads=[Bw, ByT], writes=[Bpm])
                        ae, Bae = aex[ri % 2]; uu, Buu = uub[ri % 2]; sl, Bsl = silb[ri % 2]
                        ri += 1
                        if samp:
                            av = ae[:, 0:96].rearrange("p (b t) -> p b t", t=6)
                            c.op("dve", lambda e: e.tensor_copy(out=av[:, :, 0:2], in_=sext[:, ch, :].rearrange("p (b r) -> p b r", r=2)),
                                 reads=[Bsext], writes=[Bae])
                            c.op("act", lambda e: e.copy(out=av[:, :, 2:6], in_=pa[:, 0:64].rearrange("p (b t) -> p b t", t=4)), reads=[Bpa], writes=[Bae])
                            c.op("dve", lambda e: e.tensor_copy(out=sout[:, ch, :].rearrange("p (b r) -> p b r", r=2), in_=av[:, :, 4:6]),
                                 reads=[Bae], writes=[Bsout])
                        else:
                            c.op("dve", lambda e: e.tensor_copy(out=ae[:, 0:2], in_=halo_f[:, i, ch, :]), reads=[Bhf], writes=[Bae])
                            c.op("act", lambda e: e.copy(out=ae[:, 2:2 + N], in_=pa[:, 0:N]), reads=[Bpa], writes=[Bae])
                            c.op("dve", lambda e: e.tensor_copy(out=halo_f[:, i, ch, :], in_=ae[:, N:N + 2]), reads=[Bae], writes=[Bhf])
                        conv3(ae, Bae, N, samp, prm[:, ch, 3 * i:3 * i + 1], prm[:, ch, 3 * i + 1:3 * i + 2], prm[:, ch, 3 * i + 2:3 * i + 3], uu, Buu)
                        c.op("act", lambda e: e.activation(out=sl[:, 0:N], in_=uu[:, 0:N], func=AF.Silu, bias=prm[:, ch, 6 + i:7 + i], scale=1.0),
                             reads=[Buu, Bprm], writes=[Bsl])
                        c.op("dve", lambda e: e.tensor_tensor(out=hT[:, ch, 0:N], in0=sl[:, 0:N], in1=pg[:, 0:N], op=ALU.mult),
                             reads=[Bsl, Bpg], writes=[BhT])
                for m0 in (0, 512):
                    bks = [c.bank() for _ in range(nt)]
                    for c0 in range(0, 22, 4):
                        cc = min(4, 22 - c0)
                        src = wdn_s[i, c0 * P:(c0 + cc) * P, m0:m0 + 512].rearrange("(c p) n -> p c n", p=P)
                        wb, Bw = wload([(0, cc, 512, src)], [Bwdn])
                        wv = wb[:, 0:cc * 512].rearrange("p (c n) -> p c n", c=cc)
                        for t in range(nt):
                            pm, Bpm = bks[t]
                            for cj in range(cc):
                                c.op("pe", lambda e: e.matmul(pm[:R, :], lhsT=hT[:, c0 + cj, t * P:t * P + R], rhs=wv[:, cj, :],
                                                              start=(c0 + cj == 0), stop=(c0 + cj == 21)), reads=[BhT, Bw], writes=[Bpm])
                    for t in range(nt):
                        pm, Bpm = bks[t]
                        c.op("dve", lambda e: e.scalar_tensor_tensor(out=Yout[:R, t, m0:m0 + 512], in0=Yin[:R, t, m0:m0 + 512], scalar=ALPHA,
                                                                     in1=pm[:R, :], op0=ALU.mult, op1=ALU.add), reads=[BYin[t], Bpm], writes=[BYout[t]])
                for t in range(nt):
                    ln2(Yout[:R, t, :], BYout[t], R, gb, Bgb)
                if samp:
                    store_state_T(sout, Bsout, 22, 32, o_fs[i, :, :])
                elif last:
                    store_state_T(halo_f[:, i, :, :], Bhf, 22, 2, o_fp[i, :, :])

            def mixer(Yin, BYin, Yout, BYout, nt, R, samp, last, gb, Bgb):
                N = R if samp else nt * P
                if samp:
                    load_state_T(stc_d[:, :], 8)
                ri = 0
                for ch in range(8):
                    wb, Bw = wload([(0, 8, 384, wci_s[ch].rearrange("p k s n -> p k (s n)"))], [Bwci])
                    wv = wb[:, 0:3072].rearrange("p (k s n) -> p k s n", k=8, s=3)
                    bks = [c.bank() for _ in range(3)]
                    for s_ in range(3):
                        pm, Bpm = bks[s_]
                        for k in range(8):
                            c.op("pe", lambda e: e.matmul(pm[:, 0:N], lhsT=wv[:, k, s_, :], rhs=yT[:, k, 0:N], start=(k == 0), stop=(k == 7)),
                                 reads=[Bw, ByT], writes=[Bpm])
                    (pb_, Bpb_), (pc_, Bpc_), (pu_, Bpu_) = bks
                    ae, Bae = aex[ri % 2]; uu, Buu = uub[ri % 2]
                    ri += 1
                    if samp:
                        av = ae[:, 0:96].rearrange("p (b t) -> p b t", t=6)
                        c.op("dve", lambda e: e.tensor_copy(out=av[:, :, 0:2], in_=sext[:, ch, :].rearrange("p (b r) -> p b r", r=2)), reads=[Bsext], writes=[Bae])
                        c.op("act", lambda e: e.copy(out=av[:, :, 2:6], in_=pc_[:, 0:64].rearrange("p (b t) -> p b t", t=4)), reads=[Bpc_], writes=[Bae])
                        c.op("dve", lambda e: e.tensor_tensor(out=av[:, :, 2:6], in0=av[:, :, 2:6], in1=pu_[:, 0:64].rearrange("p (b t) -> p b t", t=4), op=ALU.mult),
                             reads=[Bae, Bpu_], writes=[Bae])
                        c.op("dve", lambda e: e.tensor_copy(out=sout[:, ch, :].rearrange("p (b r) -> p b r", r=2), in_=av[:, :, 4:6]), reads=[Bae], writes=[Bsout])
                    else:
                        c.op("dve", lambda e: e.tensor_copy(out=ae[:, 0:2], in_=halo_c[:, ch, :]), reads=[Bhc], writes=[Bae])
                        c.op("act", lambda e: e.copy(out=ae[:, 2:2 + N], in_=pc_[:, 0:N]), reads=[Bpc_], writes=[Bae])
                        c.op("dve", lambda e: e.tensor_tensor(out=ae[:, 2:2 + N], in0=ae[:, 2:2 + N], in1=pu_[:, 0:N], op=ALU.mult), reads=[Bae, Bpu_], writes=[Bae])
                        c.op("dve", lambda e: e.tensor_copy(out=halo_c[:, ch, :], in_=ae[:, N:N + 2]), reads=[Bae], writes=[Bhc])
                    conv3(ae, Bae, N, samp, prm[:, ch, 8:9], prm[:, ch, 9:10], prm[:, ch, 10:11], uu, Buu)
                    c.op("dve", lambda e: e.tensor_tensor(out=hT[:, ch, 0:N], in0=uu[:, 0:N], in1=pb_[:, 0:N], op=ALU.mult), reads=[Buu, Bpb_], writes=[BhT])
                tiles = []
                for k0 in (0, 4):
                    src = wco_s[k0 * P:(k0 + 4) * P, :].rearrange("(k p) n -> p k n", p=P)
                    wb, Bw = wload([(0, 4, D, src)], [Bwco])
                    tiles.append((k0, wb, Bw))
                for t in range(nt):
                    for m0 in (0, 512):
                        pm, Bpm = c.bank()
                        for (k0, wb, Bw) in tiles:
                            wv = wb[:, 0:4 * D].rearrange("p (k n) -> p k n", k=4)
                            for k in range(4):
                                c.op("pe", lambda e: e.matmul(pm[:R, :], lhsT=hT[:, k0 + k, t * P:t * P + R], rhs=wv[:, k, m0:m0 + 512],
                                                              start=(k0 + k == 0), stop=(k0 + k == 7)), reads=[BhT, Bw], writes=[Bpm])
                        c.op("dve", lambda e: e.scalar_tensor_tensor(out=Yout[:R, t, m0:m0 + 512], in0=Yin[:R, t, m0:m0 + 512], scalar=ALPHA,
                                                                     in1=pm[:R, :], op0=ALU.mult, op1=ALU.add), reads=[BYin[t], Bpm], writes=[BYout[t]])
                    ln2(Yout[:R, t, :], BYout[t], R, gb, Bgb)
                if samp:
                    store_state_T(sout, Bsout, 8, 32, o_cs[:, :])
                elif last:
                    store_state_T(halo_c[:, :, :], Bhc, 8, 2, o_cp[:, :])

            glist = [(t0 * P, nt, P, False, gi == len(GROUPS) - 1) for gi, (t0, nt) in enumerate(GROUPS)] + [(TQ, 1, TS, True, False)]
            for (row0, nt, R, samp, last) in glist:
                for t in range(nt):
                    c.dma("sp", ya[:R, t, :], y1_s[row0 + t * P:row0 + t * P + R, :], Bya[t], writes=[Bya[t]])
                to_featT(ya, Bya, nt, R)
                ffn(0, ya, Bya, yb_, Byb, nt, R, samp, last, gbs[0][0], gbs[0][1])
                to_featT(yb_, Byb, nt, R)
                mixer(yb_, Byb, ya, Bya, nt, R, samp, last, gbs[1][0], gbs[1][1])
                to_featT(ya, Bya, nt, R)
                ffn(1, ya, Bya, yb_, Byb, nt, R, samp, last, gbs[2][0], gbs[2][1])
                for t in range(nt):
                    dst = o_ys[:, :] if samp else o_y[row0 + t * P:row0 + (t + 1) * P, :]
                    c.dma("sp", dst, yb_[:R, t, :], Byb[t], reads=[Byb[t]])
            c.barrier()
            es2.__exit__(None, None, None)

        for kt in range(NKT):
            c.dma("sp", xt[:], xk[kt * P:(kt + 1) * P, :], Bx, writes=[Bx])
            drain(4)
            front(xt[:, :], Bx, P)
            proj(P, wkv_s, Bwkv, 576, 0)
            rope(Yf, BYf, P, 0, 4, rk[:, kt, :])
            rope(Yf, BYf, P, 512, 1, rk[:, kt, :])
            rows = slice(kt * P, (kt + 1) * P)
            c.dma("sp", o_k[rows, :], Yf[:, 0:256], BYf, reads=[BYf])
            c.dma("sp", o_v[rows, :], Yf[:, 256:512], BYf, reads=[BYf])
            c.dma("sp", o_ik[rows, :], Yf[:, 512:576], BYf, reads=[BYf])
            c.op("pool", lambda e: e.tensor_copy(out=Yb[:, 0:576], in_=Yf[:, 0:576]), reads=[BYf], writes=[BYb])
            c.op("pool", lambda e: e.tensor_copy(out=Vaug[:, kt, :, 0:64], in_=Yf[:, 256:512].rearrange("p (g d) -> p g d", d=64)),
                 reads=[BYf], writes=[BV])
            pt, Bpt = ptbank()
            for gp in range(2):
                c.op("pe", lambda e: e.transpose(pt[:, gp * P:(gp + 1) * P], Yb[:, gp * P:(gp + 1) * P], identb[:, :]), reads=[BYb, Bid], writes=[Bpt])
            c.op("pe", lambda e: e.transpose(pt[0:64, 2 * P:3 * P], Yb[:, 512:576], identb[:, :]), reads=[BYb, Bid], writes=[Bpt])
            c.op("act", lambda e: e.copy(out=KT2[:, :, kt * P:(kt + 1) * P], in_=pt[:, 0:2 * P].rearrange("p (a b) -> p a b", b=P)),
                 reads=[Bpt], writes=[BKT])
            c.op("act", lambda e: e.copy(out=kiT[:, kt * P:(kt + 1) * P], in_=pt[0:64, 2 * P:3 * P]), reads=[Bpt], writes=[BkiT])

        def q_front(R, tb):
            rope(Yf, BYf, R, 0, 32, tb)
            for gp in range(2):
                c.op("pool", lambda e: e.tensor_copy(
                    out=Yb[:R, gp * 512:(gp + 1) * 512].rearrange("p (h r d) -> p h r d", h=4, r=2),
                    in_=Yf[:R, gp * 512:(gp + 1) * 512].rearrange("p (r h d) -> p h r d", h=4, r=2)), reads=[BYf], writes=[BYb])
            c.op("pool", lambda e: e.tensor_copy(out=Yb[:R, 1024:2048], in_=Yf[:R, 1024:2048]), reads=[BYf], writes=[BYb])
            c.op("dve", lambda e: e.tensor_scalar(out=wsc[:R, :], in0=Yf[:R, 2048:2064], scalar1=IDX_SCALE, scalar2=None, op0=ALU.mult),
                 reads=[BYf], writes=[Bwsc])

        def q_transposes(R):
            pt, Bpt = ptbank()
            for gp in range(2):
                for hh in range(4):
                    idx = gp * 4 + hh
                    src = Yb[:R, idx * P:(idx + 1) * P]
                    c.op("pe", lambda e: e.transpose(pt[:, idx * P:idx * P + R], src, identb[:R, :R]), reads=[BYb, Bid], writes=[Bpt])
            c.op("act", lambda e: e.copy(out=QT2[:, :, :R], in_=pt[:, :].rearrange("p (a b) -> p a b", b=P)[:, :, :R]), reads=[Bpt], writes=[BQT])
            for h0 in (0, 8):
                pt, Bpt = ptbank()
                for h in range(8):
                    col = 1024 + (h0 + h) * 64
                    c.op("pe", lambda e: e.transpose(pt[0:64, h * P:h * P + R], Yb[:R, col:col + 64], identb[:R, :R]), reads=[BYb, Bid], writes=[Bpt])
                c.op("act", lambda e: e.copy(out=qiT[:, h0:h0 + 8, :R], in_=pt[0:64, :].rearrange("p (a b) -> p a b", b=P)[:, :, :R]),
                     reads=[Bpt], writes=[BqiT])

        import os as _os
        SUB = int(_os.environ.get("DBG_SUB", "99"))
        NQR = int(_os.environ.get("DBG_NQ", str(NQT)))
        def attention(NK, maskfn, Bmask, Od, BOd, mme=("pool", "dve")):
            obanks = [c.banks[0], c.banks[1], c.banks[2]]
            for (ob, Bob) in obanks:
                c.op("pe", lambda e: e.matmul(ob[:, :], lhsT=zerob[:, 0:P], rhs=zerob[:, :], start=True, stop=False),
                     reads=[Bzero], writes=[Bob])
            units = [(kt, g) for kt in range(NK) for g in range(4)]
            LA = 2
            fr = []

            def front_u(ei, kt, g):
                pS, BpS = c.bank((3, 4, 5))
                pb = (g % 2) * 64
                c.op("pe", lambda e: e.matmul(pS[:, :], lhsT=KT2[pb:pb + 64, g // 2, kt * P:(kt + 1) * P],
                                              rhs=QT2[pb:pb + 64, (g // 2) * 4:(g // 2) * 4 + 4, :], start=True, stop=True),
                     reads=[BKT, BQT], writes=[BpS])
                E_, BE_ = Eb[ei % 2]
                Pm_, BPm_ = Pb[ei % 3]
                c.op("act", lambda e: e.activation(out=E_[:, :], in_=pS[:, :], func=AF.Exp, scale=0.125), reads=[BpS], writes=[BE_])
                c.op(mme[ei % len(mme)], lambda e: e.tensor_tensor(out=Pm_[:, :].rearrange("p (a b) -> p a b", a=4),
                                                       in0=E_[:, :].rearrange("p (a b) -> p a b", a=4),
                                                       in1=bc_mid(maskfn(kt), 4), op=ALU.mult), reads=[BE_, Bmask], writes=[BPm_])
                return (Pm_, BPm_)

            def back_u(ei, kt, g):
                Pm_, BPm_ = fr[ei]
                for hh in range(4):
                    h = 4 * g + hh
                    ob, Bob = obanks[h // 7]
                    oc = (h % 7) * 65
                    c.op("pe", lambda e: e.matmul(ob[:, oc:oc + 65], lhsT=Pm_[:, hh * P:(hh + 1) * P], rhs=Vaug[:, kt, g, :],
                                                  start=False, stop=(kt == NK - 1 and (h % 7 == 6 or h == 15))), reads=[BPm_, BV], writes=[Bob])

            for i in range(len(units) + LA):
                if i < len(units):
                    fr.append(front_u(i, *units[i]))
                if i >= LA:
                    back_u(i - LA, *units[i - LA])
            for bi, (ob, Bob) in enumerate(obanks):
                nh = 7 if bi < 2 else 2
                ov = ob[:, 0:nh * 65].rearrange("p (h d) -> p h d", d=65)
                rec = small[:, 200 + 7 * bi:200 + 7 * bi + nh]
                c.op("dve", lambda e: e.reciprocal(out=rec, in_=ov[:, :, 64]), reads=[Bob], writes=[Bsm])
                c.op("dve", lambda e: e.tensor_tensor(out=Od[:, bi * 7 * 64:(bi * 7 + nh) * 64].rearrange("p (h d) -> p h d", d=64),
                                                      in0=ov[:, :, 0:64], in1=bc_last(rec, 64), op=ALU.mult), reads=[Bob, Bsm], writes=[BOd])

        if stage >= 1:
            for j in range(NQR):
                NK = 16 + j
                N = NK * P
                c.dma("sp", xt[:], xq[j * P:(j + 1) * P, :], Bx, writes=[Bx])
                drain(8)
                front(xt[:, :], Bx, P)
                proj(P, wq_s, Bwq, 2064, 0)
                q_front(P, rq[:, j, :])
                if SUB < 1:
                    continue
                q_transposes(P)
                c.op("dve", lambda e: e.tensor_tensor(out=diag[:, :, :], in0=bc_mid(identb[:, :], 16), in1=bc_last(wsc[:, :], P), op=ALU.mult),
                     reads=[Bid, Bwsc], writes=[Bdiag])
                if SUB < 2:
                    continue
                nch = (N + 511) // 512
                items = [(ci, h) for ci in range(nch) for h in range(16)]
                pIs = {}
                frs = []
                LAI = 3

                def idx_front(ii, ci, h):
                    c0 = ci * 512
                    nn = min(512, N - c0)
                    psc, Bpsc = c.bank((0, 1, 2, 3))
                    c.op("pe", lambda e: e.matmul(psc[:, 0:nn], lhsT=qiT[:, h, :], rhs=kiT[:, c0:c0 + nn], start=True, stop=True),
                         reads=[BqiT, BkiT], writes=[Bpsc])
                    R_, BR_ = Rb[ii % 4]
                    if h % 2 == 0:
                        c.op("act", lambda e: e.activation(out=R_[:, 0:nn], in_=psc[:, 0:nn], func=AF.Relu), reads=[Bpsc], writes=[BR_])
                    else:
                        c.op("dve", lambda e: e.tensor_scalar(out=R_[:, 0:nn], in0=psc[:, 0:nn], scalar1=0.0, scalar2=None, op0=ALU.max),
                             reads=[Bpsc], writes=[BR_])
                    return (R_, BR_)

                def idx_back(ii, ci, h):
                    c0 = ci * 512
                    nn = min(512, N - c0)
                    if h == 0:
                        pIs[ci] = c.bank((4, 5))
                    pI, BpI = pIs[ci]
                    R_, BR_ = frs[ii]
                    c.op("pe", lambda e: e.matmul(pI[:, 0:nn], lhsT=diag[:, h, :], rhs=R_[:, 0:nn], start=(h == 0), stop=(h == 15)),
                         reads=[Bdiag, BR_], writes=[BpI])
                    if h == 15:
                        evac_scores(pI, BpI, P, c0, nn, qpos[:, j:j + 1])

                for ii in range(len(items) + LAI):
                    if ii < len(items):
                        frs.append(idx_front(ii, *items[ii]))
                    if ii >= LAI:
                        idx_back(ii - LAI, *items[ii - LAI])
                if SUB < 3:
                    continue
                bounds(P, nch)
                bisect(P, N)
                if SUB < 4:
                    continue
                mask_transposes(P, NK)
                if SUB < 5:
                    continue
                attention(NK, lambda kt: mskT[:, kt, :], BmT, Obf, BO)
                if SUB < 6:
                    continue
                tail(P, xq[j * P:(j + 1) * P, :], j * P)
                if _os.environ.get("DBG_DUMP") and j == 0:
                    c.dma("sp", o_y[128:256, :], Isb[:, 0:1024], BI, reads=[BI])
                    c.dma("sp", o_y[256:384, 0:256], small[:, :], Bsm, reads=[Bsm])
                    c.dma("sp", o_y[384:512, :], rr[:, :], Brr, reads=[Brr])


        def sample_phase():
            R = TS
            IOA = bass.IndirectOffsetOnAxis
            c.dma("sp", xt[:R, :], xs[:, :], Bx, writes=[Bx])
            c.dma("pool", mselb[:, :], msel_d[:, :], Bmsel, writes=[Bmsel])
            c.dma("sp", ptall[:, :], bass.AP(tensor=pt_d.tensor, offset=0, ap=[[0, P], [1, NSQ * 16]]), Bptall, writes=[Bptall])
            c.op("dve", lambda e: e.tensor_scalar(out=idxall[:, :], in0=ptall[:, :], scalar1=128.0, scalar2=pidx, op0=ALU.mult, op1=ALU.add),
                 reads=[Bptall, Bcst], writes=[Bidx])
            front(xt[:R, :], Bx, R)
            proj(R, wq_s, Bwq, 2064, 0)
            proj(R, wkv_s, Bwkv, 576, 2064)
            tb = rq[0:R, NQT, :]
            q_front(R, tb)
            rope(Yf, BYf, R, 2064, 4, tb)
            rope(Yf, BYf, R, 2064 + 512, 1, tb)
            c.dma("sp", o_ks[:, :], Yf[:R, 2064:2320], BYf, reads=[BYf])
            c.dma("sp", o_vs[:, :], Yf[:R, 2320:2576], BYf, reads=[BYf])
            c.dma("sp", o_iks[:, :], Yf[:R, 2576:2640], BYf, reads=[BYf])
            c.op("pool", lambda e: e.tensor_copy(out=Yb[:R, 2064:2640], in_=Yf[:R, 2064:2640]), reads=[BYf], writes=[BYb])
            q_transposes(R)
            pt, Bpt = ptbank()
            for gp in range(2):
                c.op("pe", lambda e: e.transpose(pt[:, gp * P:gp * P + R], Yb[:R, 2064 + gp * P:2064 + (gp + 1) * P], identb[:R, :R]),
                     reads=[BYb, Bid], writes=[Bpt])
            c.op("pe", lambda e: e.transpose(pt[0:64, 2 * P:2 * P + R], Yb[:R, 2576:2640], identb[:R, :R]), reads=[BYb, Bid], writes=[Bpt])
            c.op("act", lambda e: e.copy(out=KnT2[:, :, :], in_=pt[:, 0:2 * P].rearrange("p (a b) -> p a b", b=P)[:, :, 0:R]), reads=[Bpt], writes=[BKn])
            c.op("act", lambda e: e.copy(out=kinT[:, :], in_=pt[0:64, 2 * P:2 * P + R]), reads=[Bpt], writes=[Bkin])
            Bwsd = Buf("wsd")
            c.dma("sp", wsd_s[:, :], wsc[:R, :], Bwsd, reads=[Bwsc], writes=[Bwsd])
            for h in range(16):
                src = bass.AP(tensor=wsd_s.tensor, offset=h, ap=[[16, 4], [64, NSQ]])
                c.dma("sp", Wht[4 * h:4 * h + 4, :], src, BWht, reads=[Bwsd], writes=[BWht], nowaw=(h > 0), slow=True)
            c.op("dve", lambda e: e.tensor_tensor(out=Wsel[:, :, :], in0=mselb[:, :].rearrange("p (a b) -> p a b", b=64), in1=bc_last(Wht[:, :], 64), op=ALU.mult),
                 reads=[Bmsel, BWht], writes=[BWsel])
            c.op("dve", lambda e: e.memset(kiT[:, PAST:PAST + P], 0.0), writes=[BkiT])
            c.op("dve", lambda e: e.memset(KT2[:, :, PAST:PAST + P], 0.0), writes=[BKT])
            c.op("pool", lambda e: e.memset(Vaug[:, 16, :, 0:64], 0.0), writes=[BV])
            nch = 5
            SS = float(_os.environ.get("DBG_SS", "99"))
            if SS < 1:
                return
            for b in range(NSQ):
                for j in range(16):
                    col = b * 16 + j
                    c.dma("pool", ikp[:, j, :], cik_d[:, :], Bikp, reads=[Bidx], writes=[Bikp], nowaw=(j > 0),
                          indirect=IOA(ap=idxall[:, col:col + 1], axis=0))
                for j0 in (0, 8):
                    pt, Bpt = ptbank()
                    for j in range(8):
                        c.op("pe", lambda e: e.transpose(pt[0:64, j * P:(j + 1) * P], ikp[:, j0 + j, :], identb[:, :]), reads=[Bikp, Bid], writes=[Bpt])
                    c.op("act", lambda e: e.copy(out=kiT[:, j0 * P:(j0 + 8) * P], in_=pt[0:64, :]), reads=[Bpt], writes=[BkiT])
                c.op("dve", lambda e: e.tensor_copy(out=kiT[:, PAST:PAST + 4], in_=kinT[:, 4 * b:4 * b + 4]), reads=[Bkin], writes=[BkiT])
                c.op("dve", lambda e: e.tensor_copy(out=qisb[:, :].rearrange("p (h t) -> p h t", t=4), in_=qiT[:, :, 4 * b:4 * b + 4]), reads=[BqiT], writes=[Bqisb])
                for ci in range(nch):
                    c0 = ci * 512
                    nn = min(512, NKS * P - c0)
                    psc, Bpsc = c.banks[5]
                    pI, BpI = c.banks[ci]
                    c.op("pe", lambda e: e.matmul(psc[0:64, 0:nn], lhsT=qisb[:, :], rhs=kiT[:, c0:c0 + nn], start=True, stop=True),
                         reads=[Bqisb, BkiT], writes=[Bpsc])
                    R_, BR_ = Rb[(b * nch + ci) % 4]
                    c.op("act", lambda e: e.activation(out=R_[0:64, 0:nn], in_=psc[0:64, 0:nn], func=AF.Relu), reads=[Bpsc], writes=[BR_])
                    c.op("pe", lambda e: e.matmul(pI[0:64, 0:nn], lhsT=Wsel[:, b, :], rhs=R_[0:64, 0:nn], start=(b == 0), stop=(b == NSQ - 1)),
                         reads=[BWsel, BR_], writes=[BpI])
            if SS < 2:
                return
            for ci in range(nch):
                c0 = ci * 512
                nn = min(512, NKS * P - c0)
                pI, BpI = c.banks[ci]
                evac_scores(pI, BpI, R, c0, nn, qpos[0:R, NQT:NQT + 1])
            bounds(R, nch)
            bisect(R, NKS * P)
            mask_transposes(R, NKS)
            if SS < 2.5:
                return
            mskS = msk[:, 0:NKS * P].rearrange("p (k t) -> p k t", t=P)
            c.op("pool", lambda e: e.memset(msk[:, 0:NKS * P], 0.0), writes=[Bmsk])
            for b in range(NSQ):
                for j in range(16):
                    col = b * 16 + j
                    c.dma("pool", Kp[:, j, :], ck_d[:, :], BKp, reads=[Bidx], writes=[BKp], nowaw=(j > 0),
                          indirect=IOA(ap=idxall[:, col:col + 1], axis=0))
                for j in range(16):
                    col = b * 16 + j
                    c.dma("pool", Vp[:, j, :], cv_d[:, :], BVp, reads=[Bidx], writes=[BVp], nowaw=(j > 0),
                          indirect=IOA(ap=idxall[:, col:col + 1], axis=0))
                c.op("pool", lambda e: e.tensor_copy(out=Vaug[:, 0:16, :, 0:64], in_=Vp[:, :, :].rearrange("p j (g d) -> p j g d", d=64)),
                     reads=[BVp], writes=[BV])
                c.dma("sp", Vst[:, :], Yf[4 * b:4 * b + 4, 2320:2576], BVst, reads=[BYf], writes=[BVst])
                c.op("pool", lambda e: e.tensor_copy(out=Vaug[0:4, 16, :, 0:64], in_=Vst[:, :].rearrange("p (g d) -> p g d", d=64)), reads=[BVst], writes=[BV])
                for j0 in range(0, 16, 4):
                    pt, Bpt = ptbank()
                    for gp in range(2):
                        for j in range(4):
                            c.op("pe", lambda e: e.transpose(pt[:, (gp * 4 + j) * P:(gp * 4 + j + 1) * P], Kp[:, j0 + j, gp * P:(gp + 1) * P], identb[:, :]),
                                 reads=[BKp, Bid], writes=[Bpt])
                    c.op("act", lambda e: e.copy(out=KT2[:, :, j0 * P:(j0 + 4) * P], in_=pt[:, :].rearrange("p (a b) -> p a b", a=2)), reads=[Bpt], writes=[BKT])
                c.op("dve", lambda e: e.tensor_copy(out=KT2[:, :, PAST:PAST + 4], in_=KnT2[:, :, 4 * b:4 * b + 4]), reads=[BKn], writes=[BKT])
                if SS < 2.7:
                    continue
                if b > 0:
                    c.op("pool", lambda e: e.memset(mskS[:, :, 4 * (b - 1):4 * b], 0.0), writes=[Bmsk])
                c.op("pool", lambda e: e.tensor_copy(out=mskS[:, :, 4 * b:4 * b + 4], in_=mskT[:, 0:NKS, 4 * b:4 * b + 4]), reads=[BmT], writes=[Bmsk])
                attention(NKS, lambda kt: mskS[:, kt, :], Bmsk, xb, Bxb, mme=("dve", "dve", "pool"))
                c.dma("sp", Obf[4 * b:4 * b + 4, :], xb[4 * b:4 * b + 4, :], BO, reads=[Bxb], writes=[BO], nowaw=True)
            if SS < 4:
                return
            tail(R, xs[:, :], TQ)

        if stage >= 2:
            sample_phase()
        drain(10000)
        c.barrier()
        es1.__exit__(None, None, None)
        if stage >= 3:
            rest_phase()
        c.finish()
    return nc


def _rope_table(pos):
    half = 8
    inv = (500000.0 ** (-np.arange(half, dtype=np.float32) * np.float32(2.0 / 16))).astype(np.float32)
    ang = pos.astype(np.float32)[:, None] * inv[None, :]
    cs = np.cos(ang).astype(np.float32)
    sn = np.sin(ang).astype(np.float32)
    return np.concatenate([cs, cs, sn], axis=1).astype(np.float32)


_NC_CACHE = {}


def _run(inputs, nphys=None, stage=99, compact=False):
    f = lambda a: np.ascontiguousarray(np.asarray(a))
    x_prompt = f(inputs["x_prompt"]); x_sample = f(inputs["x_sample"])
    cache_k = f(inputs["cache_k"])[0]; cache_v = f(inputs["cache_v"])[0]; cache_ik = f(inputs["cache_idx_k"])[0]
    page_table = f(inputs["page_table"]).astype(np.int32)
    full_nphys = cache_k.shape[0]
    if nphys is None:
        nphys = full_nphys
    key = (nphys, stage)
    if key not in _NC_CACHE:
        _NC_CACHE[key] = build(nphys, stage)
    nc = _NC_CACHE[key]
    consts = np.zeros((P, 512 + 128 + NIT + 1), np.float32)
    consts[:, 0:512] = np.arange(512, dtype=np.float32)[None, :]
    consts[:, 512:640] = np.eye(P, dtype=np.float32)
    consts[:, 640:640 + NIT] = (0.5 ** np.arange(1, NIT + 1, dtype=np.float64)).astype(np.float32)[None, :]
    consts[:, 640 + NIT] = np.arange(P, dtype=np.float32)
    msel = np.zeros((64, NSQ, 64), np.float32)
    for h in range(16):
        for t in range(4):
            for b in range(NSQ):
                msel[h * 4 + t, b, b * 4 + t] = 1.0
    ropek = _rope_table(np.arange(SEQ))
    ropes = _rope_table(PAST + (np.arange(TS) % 4))
    in_maps = []
    for core in range(8):
        b, h = core // 2, core % 2
        pos0 = 0 if h == 0 else POS0
        qp = np.zeros((P, NQT + 1), np.float32)
        qp[:, :NQT] = pos0 + np.arange(P)[:, None] + P * np.arange(NQT)[None, :]
        qp[:TS, NQT] = PAST + (np.arange(TS) % 4)
        sl = slice(core * NSQ, (core + 1) * NSQ)
        pt = page_table[sl]
        if compact:
            pages = np.unique(pt)
            remap = {int(p): i for i, p in enumerate(pages)}
            ck = np.zeros((nphys, P, 256), np.float32); cv = np.zeros((nphys, P, 256), np.float32); ci = np.zeros((nphys, P, 64), np.float32)
            ck[:len(pages)] = cache_k[pages].reshape(-1, P, 256); cv[:len(pages)] = cache_v[pages].reshape(-1, P, 256)
            ci[:len(pages)] = cache_ik[pages]
            pt = np.vectorize(remap.get)(pt).astype(np.int32)
        else:
            ck = cache_k.reshape(-1, P, 256); cv = cache_v.reshape(-1, P, 256); ci = cache_ik
        m = {
            "xk": x_prompt[b], "xq": x_prompt[b, pos0:pos0 + TQ], "xs": x_sample[sl].reshape(TS, D),
            "ropek": ropek, "ropeq": ropek[pos0:pos0 + TQ], "ropes": ropes, "qpos": qp, "consts": consts,
            "msel": msel.reshape(64, NSQ * 64), "pt": pt,
            "cache_k": ck.reshape(nphys * P, 256), "cache_v": cv.reshape(nphys * P, 256), "cache_ik": ci.reshape(nphys * P, 64),
            "st_conv": f(inputs["state_conv"])[0, sl].reshape(NSQ * 2, D),
            "st_ffn": f(inputs["state_ffn"])[:, sl].reshape(2, NSQ * 2, DFF),
            "w_attn_in": f(inputs["w_attn_in"])[0], "w_attn_out": f(inputs["w_attn_out"])[0],
            "w_conv_in": f(inputs["w_conv_in"])[0], "conv_w": f(inputs["conv_w"])[0], "w_conv_out": f(inputs["w_conv_out"])[0],
            "w_ffn_up": f(inputs["w_ffn_up"]), "ffn_conv_w": f(inputs["ffn_conv_w"]), "ffn_conv_b": f(inputs["ffn_conv_b"]),
            "w_ffn_down": f(inputs["w_ffn_down"]), "ln_g": f(inputs["ln_g"]).reshape(4, D), "ln_b": f(inputs["ln_b"]).reshape(4, D),
        }
        in_maps.append({k: np.ascontiguousarray(v) for k, v in m.items()})
    res = run_bass_kernel_spmd(nc, in_maps, core_ids=list(range(8)))
    R = res.results
    B = 4
    y_prompt = np.zeros((B, SEQ, D), np.float32)
    nk = np.zeros((1, B, SEQ, 4, 64), np.float32); nv = np.zeros_like(nk); nik = np.zeros((1, B, SEQ, 64), np.float32)
    cp = np.zeros((1, B, 2, D), np.float32); fp = np.zeros((2, B, 2, DFF), np.float32)
    y_sample = np.zeros((128, 4, D), np.float32)
    nks = np.zeros((1, 128, 4, 4, 64), np.float32); nvs = np.zeros_like(nks); niks = np.zeros((1, 128, 4, 64), np.float32)
    cs = np.zeros((1, 128, 2, D), np.float32); fs = np.zeros((2, 128, 2, DFF), np.float32)
    for core in range(8):
        b, h = core // 2, core % 2
        r = R[core]
        sl = slice(core * NSQ, (core + 1) * NSQ)
        if h == 0:
            y_prompt[b, 0:2048] = r["o_y"][0:2048]
            nk[0, b] = r["o_k"].reshape(SEQ, 4, 64); nv[0, b] = r["o_v"].reshape(SEQ, 4, 64); nik[0, b] = r["o_ik"]
        else:
            y_prompt[b, 2048:4096] = r["o_y"][128:TQ]
            cp[0, b] = r["o_cp"]; fp[:, b] = r["o_fp"]
        y_sample[sl] = r["o_ys"].reshape(NSQ, 4, D)
        nks[0, sl] = r["o_ks"].reshape(NSQ, 4, 4, 64); nvs[0, sl] = r["o_vs"].reshape(NSQ, 4, 4, 64); niks[0, sl] = r["o_iks"].reshape(NSQ, 4, 64)
        cs[0, sl] = r["o_cs"].reshape(NSQ, 2, D); fs[:, sl] = r["o_fs"].reshape(2, NSQ, 2, DFF)
    return (y_prompt, y_sample, nk, nv, nik, nks, nvs, niks, cp, cs, fp, fs), R


def kernel(**inputs):
    outs, _ = _run(inputs)
    return outs
```

```python
import contextlib
import numpy as np
import concourse.bass as bass
import concourse.mybir as mybir
from concourse.bass_utils import run_bass_kernel_spmd

F32 = mybir.dt.float32
BF16 = mybir.dt.bfloat16
I32 = mybir.dt.int32
AF = mybir.ActivationFunctionType
ALU = mybir.AluOpType
AX = mybir.AxisListType

P = 128
D = 1024
DFF = 2816
NCB = 11
SEQ = 4096
NKT = 32
NQT = 17
TQ = NQT * P
POS0 = 1920
NSQ = 16
TS = 64
PAST = 2048
NKS = 17
LS = PAST + 4
TOPK = 256
NIT = 14
ALPHA = 4.0 ** 0.25
IDX_SCALE = 1.0 / 32.0
EPS = 1e-5
NEG = -1.0e30
GROUPS = [(0, 1), (1, 4), (5, 4), (9, 4), (13, 4)]


class Buf:
    __slots__ = ("name", "w", "r", "dsem", "dtot")

    def __init__(self, name):
        self.name = name
        self.w = None
        self.r = {}
        self.dsem = None
        self.dtot = 0


def bc_mid(ap, n):
    a = [list(x) for x in ap.ap]
    return bass.AP(tensor=ap.tensor, offset=ap.offset, ap=[a[0], [0, n]] + a[1:])


def bc_last(ap, n):
    a = [list(x) for x in ap.ap]
    return bass.AP(tensor=ap.tensor, offset=ap.offset, ap=a + [[0, n]])


class Ctx:
    CH = 16000

    def __init__(self, nc, es):
        self.nc = nc
        self.es = es
        self.engs = {"pe": nc.tensor, "dve": nc.vector, "act": nc.scalar, "pool": nc.gpsimd, "sp": nc.sync}
        self.cnt = {e: 0 for e in self.engs}
        self.sems = {e: [] for e in self.engs}
        self.waited = {e: {} for e in self.engs}
        self.dma_bufs = []
        self.nsem = 0
        self.banks = []
        self.bank_i = 0
        self.wbufs = []
        self.wb_i = 0

    def new_sem(self, name):
        self.nsem += 1
        return self.es.enter_context(self.nc.semaphore(name))

    def sb(self, name, shape, dt, es=None):
        return (es or self.es).enter_context(self.nc.sbuf_tensor("sb_" + name, list(shape), dt))

    def ps(self, name, shape, dt):
        return self.es.enter_context(self.nc.psum_tensor("ps_" + name, list(shape), dt))

    def _esem(self, e, tick):
        ch = (tick - 1) // self.CH
        while len(self.sems[e]) <= ch:
            self.sems[e].append(self.new_sem("s_%s_%d" % (e, len(self.sems[e]))))
        return self.sems[e][ch], (tick - 1) % self.CH + 1

    def _wait(self, e, tok):
        if tok[0] == "eng":
            _, f, tick = tok
            key = f
            if self.waited[e].get(key, 0) >= tick:
                return
            sem, val = self._esem(f, tick)
        else:
            _, buf, val = tok
            key = ("d", id(buf))
            tick = val
            if self.waited[e].get(key, 0) >= tick:
                return
            sem = buf.dsem
        self.engs[e].wait_ge(sem, val)
        self.waited[e][key] = tick

    def _deps(self, e, reads, writes, nowaw=False):
        for b in reads:
            t = b.w
            if t is not None:
                if t[0] == "eng" and t[1] == e and e in ("pe", "sp"):
                    continue
                self._wait(e, t)
        if nowaw:
            return
        for b in writes:
            t = b.w
            if t is not None:
                if not (t[0] == "eng" and t[1] == e and e in ("pe", "sp", "dve", "act")):
                    self._wait(e, t)
            for t in b.r.values():
                if t[0] == "eng" and t[1] == e and e != "pool":
                    continue
                self._wait(e, t)

    def _commit(self, tok, reads, writes):
        key = tok[1] if tok[0] == "eng" else ("d", id(tok[1]))
        for b in reads:
            b.r[key] = tok
        for b in writes:
            b.w = tok
            b.r = {}

    def op(self, e, fn, reads=(), writes=()):
        self._deps(e, reads, writes)
        ins = fn(self.engs[e])
        self.cnt[e] += 1
        tick = self.cnt[e]
        sem, _ = self._esem(e, tick)
        ins.then_inc(sem, 1)
        self._commit(("eng", e, tick), reads, writes)
        return ins

    def dma(self, q, out, in_, sbuf, reads=(), writes=(), nowaw=False, indirect=None, slow=False):
        self._deps(q, reads, writes, nowaw)
        if sbuf.dsem is None:
            sbuf.dsem = self.new_sem("d_" + sbuf.name)
            self.dma_bufs.append(sbuf)
        sbuf.dtot += 16
        if indirect is not None:
            ins = self.nc.gpsimd.indirect_dma_start(out=out, out_offset=None, in_=in_, in_offset=indirect)
        else:
            ins = self.engs[q].dma_start(out=out, in_=in_, allow_slow_non_contiguous=True) if slow else self.engs[q].dma_start(out=out, in_=in_)
        ins.then_inc(sbuf.dsem, 16)
        self._commit(("dma", sbuf, sbuf.dtot), reads, writes)

    def barrier(self):
        for e in self.engs:
            for f in self.engs:
                if f != e and self.cnt[f] > 0:
                    self._wait(e, ("eng", f, self.cnt[f]))
            for b in self.dma_bufs:
                if b.dtot > 0:
                    self._wait(e, ("dma", b, b.dtot))

    def finish(self):
        for b in self.dma_bufs:
            self.engs["sp"].wait_ge(b.dsem, b.dtot)

    def bank(self, subset=None):
        if subset is None:
            subset = range(len(self.banks))
        self.bank_i += 1
        return self.banks[subset[self.bank_i % len(subset)]]

    def wbuf(self):
        i = self.wb_i % len(self.wbufs)
        self.wb_i += 1
        return self.wbufs[i]


def build(nphys, stage=99):
    nc = bass.Bass("TRN2", target_bir_lowering=False)

    def din(name, shape, dt=F32):
        return nc.dram_tensor(name, list(shape), dt, kind="ExternalInput").ap()

    def dout(name, shape, dt=F32):
        return nc.dram_tensor(name, list(shape), dt, kind="ExternalOutput").ap()

    def dscr(name, shape, dt):
        return nc.dram_tensor(name, list(shape), dt, kind="Internal").ap()

    NROW = nphys * P
    xk = din("xk", [SEQ, D])
    xq = din("xq", [TQ, D])
    xs = din("xs", [TS, D])
    ropek = din("ropek", [SEQ, 24])
    ropeq = din("ropeq", [TQ, 24])
    ropes = din("ropes", [TS, 24])
    qpos_d = din("qpos", [P, NQT + 1])
    consts = din("consts", [P, 512 + 128 + NIT + 2])
    msel_d = din("msel", [64, NSQ * 64])
    pt_d = din("pt", [NSQ, 16], I32)
    ck_d = din("cache_k", [NROW, 256])
    cv_d = din("cache_v", [NROW, 256])
    cik_d = din("cache_ik", [NROW, 64])
    stc_d = din("st_conv", [NSQ * 2, D])
    stf_d = din("st_ffn", [2, NSQ * 2, DFF])
    w_ai = din("w_attn_in", [D, 2640])
    w_ao = din("w_attn_out", [D, D])
    w_ci = din("w_conv_in", [D, 3 * D])
    cw_d = din("conv_w", [3, D])
    w_co = din("w_conv_out", [D, D])
    w_up = din("w_ffn_up", [2, D, 2 * DFF])
    fcw_d = din("ffn_conv_w", [2, 3, DFF])
    fcb_d = din("ffn_conv_b", [2, DFF])
    w_dn = din("w_ffn_down", [2, DFF, D])
    lng_d = din("ln_g", [4, D])
    lnb_d = din("ln_b", [4, D])

    o_y = dout("o_y", [TQ, D])
    o_ys = dout("o_ys", [TS, D])
    o_k = dout("o_k", [SEQ, 256])
    o_v = dout("o_v", [SEQ, 256])
    o_ik = dout("o_ik", [SEQ, 64])
    o_ks = dout("o_ks", [TS, 256])
    o_vs = dout("o_vs", [TS, 256])
    o_iks = dout("o_iks", [TS, 64])
    o_cp = dout("o_cp", [2, D])
    o_cs = dout("o_cs", [NSQ * 2, D])
    o_fp = dout("o_fp", [2, 2, DFF])
    o_fs = dout("o_fs", [2, NSQ * 2, DFF])

    wq_s = dscr("wq_s", [D, 2064], BF16)
    wkv_s = dscr("wkv_s", [D, 576], BF16)
    wo_s = dscr("wo_s", [D, D], BF16)
    wup_s = dscr("wup_s", [2, NCB, P, 8, 2, 256], BF16)
    wdn_s = dscr("wdn_s", [2, DFF, D], BF16)
    wci_s = dscr("wci_s", [8, P, 8, 3, 128], BF16)
    wco_s = dscr("wco_s", [D, D], BF16)
    y1_s = dscr("y1_s", [TQ + TS, D], F32)
    wsd_s = dscr("wsd_s", [TS, 16], F32)

    es = contextlib.ExitStack()
    with es:
        c = Ctx(nc, es)
        for i in range(6):
            c.banks.append((c.ps("pm%d" % i, [P, 512], F32), Buf("pm%d" % i)))
        ptb = [(c.ps("pt%d" % i, [P, 1024], BF16), Buf("pt%d" % i)) for i in range(2)]
        pt_i = [0]

        def ptbank():
            i = pt_i[0] % 2
            pt_i[0] += 1
            return ptb[i]

        WBN = 4608
        for i in range(4):
            c.wbufs.append((c.sb("wb%d" % i, [P, WBN], BF16), Buf("wb%d" % i)))
        cst = c.sb("cst", [P, 512 + 128 + NIT + 2], F32)
        Bcst = Buf("cst")
        identb = c.sb("identb", [P, P], BF16)
        Bid = Buf("identb")
        zerob = c.sb("zerob", [P, 512], BF16); Bzero = Buf("zerob")
        c.op("pool", lambda e: e.memset(zerob[:, :], 0.0), writes=[Bzero])
        c.dma("sp", cst[:], consts[:, :], Bcst, writes=[Bcst])
        c.dma("pool", identb[:], consts[:, 512:640], Bid, writes=[Bid])
        iota = cst[:, 0:512]
        identf = cst[:, 512:640]
        pow2 = cst[:, 640:640 + NIT]
        pidx = cst[:, 640 + NIT:641 + NIT]
        pm16 = cst[:, 641 + NIT:642 + NIT]

        Bwq, Bwkv, Bwo, Bwup, Bwdn, Bwci, Bwco = (Buf(n) for n in ("wq", "wkv", "wo", "wup", "wdn", "wci", "wco"))

        pending = []

        def conv_now(dst, src, B):
            c.dma("pool", dst, src, B, writes=[B], nowaw=True)

        def conv(dst, src, B):
            if B in (Bwkv, Bwq, Bwo):
                conv_now(dst, src, B)
            else:
                pending.append((dst, src, B))

        def drain(n):
            for _ in range(n):
                if pending:
                    conv_now(*pending.pop(0))

        for r0 in range(0, D, 256):
            rs = slice(r0, r0 + 256)
            conv(wkv_s[rs, 0:512], w_ai[rs, 1024:1536], Bwkv)
            conv(wkv_s[rs, 512:576], w_ai[rs, 2560:2624], Bwkv)
        for r0 in range(0, D, 256):
            rs = slice(r0, r0 + 256)
            conv(wq_s[rs, 0:1024], w_ai[rs, 0:1024], Bwq)
            conv(wq_s[rs, 1024:2048], w_ai[rs, 1536:2560], Bwq)
            conv(wq_s[rs, 2048:2064], w_ai[rs, 2624:2640], Bwq)
            conv(wo_s[rs, :], w_ao[rs, :], Bwo)
        for i in range(2):
            wv = w_up[i].rearrange("(k p) (s c n) -> c k p s n", p=P, s=2, c=NCB, n=256)
            for cb in range(NCB):
                for k in range(8):
                    conv(wup_s[i, cb, :, k, :, :], wv[cb, k], Bwup)
            for r0 in range(0, DFF, 704):
                conv(wdn_s[i, r0:r0 + 704, :], w_dn[i, r0:r0 + 704, :], Bwdn)
            if i == 0:
                wv = w_ci.rearrange("(k p) (s c n) -> c k p s n", p=P, s=3, c=8, n=128)
                for ch in range(8):
                    for k in range(8):
                        conv(wci_s[ch, :, k, :, :], wv[ch, k], Bwci)
                for r0 in range(0, D, 256):
                    conv(wco_s[r0:r0 + 256, :], w_co[r0:r0 + 256, :], Bwco)

        def wload(parts, reads):
            wb, Bw = c.wbuf()
            for (off, a, b, src) in parts:
                dst = wb[:, off:off + a * b].rearrange("p (a b) -> p a b", a=a)
                c.dma("sp", dst, src, Bw, reads=reads, writes=[Bw])
            return wb, Bw

        es1 = contextlib.ExitStack()
        es1.__enter__()
        KT2 = c.sb("KT2", [P, 2, SEQ], BF16, es1); BKT = Buf("KT2")
        kiT = c.sb("kiT", [64, SEQ], BF16, es1); BkiT = Buf("kiT")
        Vaug = c.sb("Vaug", [P, NKT, 4, 65], BF16, es1); BV = Buf("Vaug")
        Isb = c.sb("Isb", [P, SEQ], F32, es1); BI = Buf("Isb")
        msk = c.sb("msk", [P, SEQ], BF16, es1); Bmsk = Buf("msk")
        mskT = c.sb("mskT", [P, NKT, P], BF16, es1); BmT = Buf("mskT")
        QT2 = c.sb("QT2", [P, 8, P], BF16, es1); BQT = Buf("QT2")
        qiT = c.sb("qiT", [64, 16, P], BF16, es1); BqiT = Buf("qiT")
        diag = c.sb("diag", [P, 16, P], BF16, es1); Bdiag = Buf("diag")
        Rb = [(c.sb("R%d" % i, [P, 512], BF16, es1), Buf("R%d" % i)) for i in range(4)]
        Eb = [(c.sb("E%d" % i, [P, 512], BF16, es1), Buf("E%d" % i)) for i in range(2)]
        Pb = [(c.sb("Pm%d" % i, [P, 512], BF16, es1), Buf("Pm%d" % i)) for i in range(3)]
        Yf = c.sb("Yf", [P, 2640], F32, es1); BYf = Buf("Yf")
        Yb = c.sb("Yb", [P, 2640], BF16, es1); BYb = Buf("Yb")
        xt = c.sb("xt", [P, D], F32, es1); Bx = Buf("xt")
        xb = c.sb("xb", [P, D], BF16, es1); Bxb = Buf("xb")
        xT = c.sb("xT", [P, 8, P], BF16, es1); BxT = Buf("xT")
        Obf = c.sb("Obf", [P, D], BF16, es1); BO = Buf("Obf")
        rr = c.sb("rr", [P, D], F32, es1); Brr = Buf("rr")
        yy = rr; Byy = Brr
        gb0 = c.sb("gb0", [P, 2 * D], F32, es1); Bgb0 = Buf("gb0")
        rk = c.sb("rk", [P, NKT, 24], F32, es1); Brk = Buf("rk")
        rq = c.sb("rq", [P, NQT + 1, 24], F32, es1); Brq = Buf("rq")
        qpos = c.sb("qpos", [P, NQT + 1], F32, es1); Bqp = Buf("qpos")
        small = c.sb("small", [P, 256], F32, es1); Bsm = Buf("small")
        tmpr = c.sb("tmpr", [P, 33, 16], F32, es1); Btr = Buf("tmpr")
        wsc = c.sb("wsc", [P, 16], F32, es1); Bwsc = Buf("wsc")
        biasb = c.sb("biasb", [P, 512], F32, es1); Bbias = Buf("biasb")
        ikp = c.sb("ikp", [P, 16, 64], BF16, es1); Bikp = Buf("ikp")
        Kp = c.sb("Kp", [P, 16, 256], BF16, es1); BKp = Buf("Kp")
        Vp = c.sb("Vp", [P, 16, 256], BF16, es1); BVp = Buf("Vp")
        idxall = c.sb("idxall", [P, NSQ * 16], I32, es1); Bidx = Buf("idxall")
        ptall = idxall; Bptall = Bidx
        qisb = c.sb("qisb", [64, 64], BF16, es1); Bqisb = Buf("qisb")
        KnT2 = c.sb("KnT2", [P, 2, 64], BF16, es1); BKn = Buf("KnT2")
        kinT = c.sb("kinT", [64, 64], BF16, es1); Bkin = Buf("kinT")
        Wht = c.sb("Wht", [64, 16], F32, es1); BWht = Buf("Wht")
        Wsel = c.sb("Wsel", [64, NSQ, 64], BF16, es1); BWsel = Buf("Wsel")
        mselb = c.sb("mselb", [64, NSQ * 64], BF16, es1); Bmsel = Buf("mselb")
        Vst = c.sb("Vst", [4, 256], F32, es1); BVst = Buf("Vst")

        for t0 in range(0, NKT, 8):
            c.dma("sp", rk[:, t0:t0 + 8, :], ropek[t0 * P:(t0 + 8) * P, :].rearrange("(t p) c -> p t c", p=P), Brk,
                  writes=[Brk], nowaw=True)
        for t0 in range(0, NQT, 6):
            t1 = min(NQT, t0 + 6)
            c.dma("sp", rq[:, t0:t1, :], ropeq[t0 * P:t1 * P, :].rearrange("(t p) c -> p t c", p=P), Brq,
                  writes=[Brq], nowaw=True)
        c.dma("sp", rq[0:TS, NQT, :], ropes[:, :], Brq, writes=[Brq], nowaw=True)
        c.dma("sp", qpos[:], qpos_d[:, :], Bqp, writes=[Bqp])
        c.dma("sp", gb0[:, 0:D], bass.AP(tensor=lng_d.tensor, offset=0, ap=[[0, P], [1, D]]), Bgb0, writes=[Bgb0], nowaw=True)
        c.dma("sp", gb0[:, D:2 * D], bass.AP(tensor=lnb_d.tensor, offset=0, ap=[[0, P], [1, D]]), Bgb0, writes=[Bgb0], nowaw=True)
        c.op("pool", lambda e: e.memset(Vaug[:, :, :, 64:65], 1.0), writes=[BV])

        def front(src, Bsrc, R):
            c.op("act", lambda e: e.copy(out=xb[:R, :], in_=src), reads=[Bsrc], writes=[Bxb])
            pt, Bpt = ptbank()
            for k in range(8):
                c.op("pe", lambda e: e.transpose(pt[:, k * P:k * P + R], xb[:R, k * P:(k + 1) * P], identb[:R, :R]),
                     reads=[Bxb, Bid], writes=[Bpt])
            c.op("dve", lambda e: e.tensor_copy(out=xT[:, :, :R], in_=pt[:, :].rearrange("p (k t) -> p k t", k=8)[:, :, :R]),
                 reads=[Bpt], writes=[BxT])

        def rope(Y, BY, R, col0, H, tb):
            Yv = Y[:R, col0:col0 + 64 * H].rearrange("p (h d) -> p h d", d=64)
            tA = tmpr[:R, 0:H, 0:8]
            tB = tmpr[:R, 0:H, 8:16]
            sn = bc_mid(tb[:, 16:24], H)
            cs = bc_mid(tb[:, 0:16], H)
            c.op("dve", lambda e: e.tensor_tensor(out=tA, in0=Yv[:, :, 8:16], in1=sn, op=ALU.mult), reads=[BY], writes=[Btr])
            c.op("dve", lambda e: e.tensor_tensor(out=tB, in0=Yv[:, :, 0:8], in1=sn, op=ALU.mult), reads=[BY], writes=[Btr])
            c.op("dve", lambda e: e.tensor_tensor(out=Yv[:, :, 0:16], in0=Yv[:, :, 0:16], in1=cs, op=ALU.mult), reads=[BY], writes=[BY])
            c.op("dve", lambda e: e.tensor_tensor(out=Yv[:, :, 0:8], in0=Yv[:, :, 0:8], in1=tA, op=ALU.subtract), reads=[BY, Btr], writes=[BY])
            c.op("dve", lambda e: e.tensor_tensor(out=Yv[:, :, 8:16], in0=Yv[:, :, 8:16], in1=tB, op=ALU.add), reads=[BY, Btr], writes=[BY])

        def proj(R, wsrc, Bwsrc, ncols, ycol0):
            n0 = 0
            while n0 < ncols:
                nn = min(2048, ncols - n0)
                kper = max(1, min(8, WBN // nn))
                mts = [(m0, min(512, nn - m0)) for m0 in range(0, nn, 512)]
                bks = [c.bank() for _ in mts]
                for k0 in range(0, 8, kper):
                    kk = min(kper, 8 - k0)
                    src = wsrc[k0 * P:(k0 + kk) * P, n0:n0 + nn].rearrange("(k p) n -> p k n", p=P)
                    wb, Bw = wload([(0, kk, nn, src)], [Bwsrc])
                    wv = wb[:, 0:kk * nn].rearrange("p (k n) -> p k n", k=kk)
                    for (m0, mm), (pm, Bpm) in zip(mts, bks):
                        for k in range(kk):
                            c.op("pe", lambda e: e.matmul(pm[:R, 0:mm], lhsT=xT[:, k0 + k, :R], rhs=wv[:, k, m0:m0 + mm],
                                                          start=(k0 + k == 0), stop=(k0 + k == 7)),
                                 reads=[BxT, Bw], writes=[Bpm])
                for (m0, mm), (pm, Bpm) in zip(mts, bks):
                    c.op("act", lambda e: e.copy(out=Yf[:R, ycol0 + n0 + m0:ycol0 + n0 + m0 + mm], in_=pm[:R, 0:mm]),
                         reads=[Bpm], writes=[BYf])
                n0 += nn

        def layer_norm(src, Bsrc, R, gb, Bgb, dst, Bdst):
            st = small[:R, 0:12]
            mv = small[:R, 12:14]
            sd = small[:R, 14:15]
            rs = small[:R, 15:16]
            c.op("dve", lambda e: e.bn_stats(out=st[:, 0:6], in_=src[:, 0:512]), reads=[Bsrc], writes=[Bsm])
            c.op("dve", lambda e: e.bn_stats(out=st[:, 6:12], in_=src[:, 512:1024]), reads=[Bsrc], writes=[Bsm])
            c.op("dve", lambda e: e.bn_aggr(out=mv, in_=st), reads=[Bsm], writes=[Bsm])
            c.op("act", lambda e: e.activation(out=sd, in_=mv[:, 1:2], func=AF.Sqrt, bias=EPS, scale=1.0), reads=[Bsm], writes=[Bsm])
            c.op("dve", lambda e: e.reciprocal(out=rs, in_=sd), reads=[Bsm], writes=[Bsm])
            c.op("dve", lambda e: e.tensor_scalar(out=dst, in0=src, scalar1=mv[:, 0:1], scalar2=rs, op0=ALU.subtract, op1=ALU.mult),
                 reads=[Bsrc, Bsm], writes=[Bdst])
            c.op("pool", lambda e: e.tensor_tensor(out=dst, in0=dst, in1=gb[:R, 0:D], op=ALU.mult), reads=[Bdst, Bgb], writes=[Bdst])
            c.op("pool", lambda e: e.tensor_tensor(out=dst, in0=dst, in1=gb[:R, D:2 * D], op=ALU.add), reads=[Bdst, Bgb], writes=[Bdst])

        def bisect(R, N):
            lo = small[:R, 16:17]
            w0 = small[:R, 17:18]
            mid = small[:R, 18:19]
            cnt = small[:R, 19:20]
            tt = small[:R, 20:21]
            hw = small[:R, 32:32 + NIT]
            c.op("dve", lambda e: e.tensor_scalar(out=hw, in0=pow2[:R, :], scalar1=w0, scalar2=None, op0=ALU.mult), reads=[Bsm, Bcst], writes=[Bsm])
            for k in range(NIT):
                c.op("dve", lambda e: e.tensor_tensor(out=mid, in0=lo, in1=hw[:, k:k + 1], op=ALU.add), reads=[Bsm], writes=[Bsm])
                c.op("dve", lambda e: e.tensor_scalar(out=msk[:R, :N], in0=Isb[:R, :N], scalar1=mid, scalar2=None, op0=ALU.is_ge,
                                                      op1=ALU.add, accum_out=cnt), reads=[BI, Bsm], writes=[Bmsk, Bsm])
                c.op("dve", lambda e: e.tensor_scalar(out=tt, in0=cnt, scalar1=float(TOPK), scalar2=hw[:, k:k + 1], op0=ALU.is_ge, op1=ALU.mult),
                     reads=[Bsm], writes=[Bsm])
                c.op("dve", lambda e: e.tensor_tensor(out=lo, in0=lo, in1=tt, op=ALU.add), reads=[Bsm], writes=[Bsm])
            c.op("dve", lambda e: e.tensor_scalar(out=msk[:R, :N], in0=Isb[:R, :N], scalar1=lo, scalar2=None, op0=ALU.is_ge),
                 reads=[BI, Bsm], writes=[Bmsk])

        def evac_scores(pI, BpI, R, c0, nn, qp):
            ci = c0 // 512
            qrel = small[:R, 21:22]
            c.op("dve", lambda e: e.tensor_scalar(out=qrel, in0=qp, scalar1=float(-c0), scalar2=None, op0=ALU.add), reads=[Bqp, Bsm], writes=[Bsm])
            c.op("dve", lambda e: e.tensor_reduce(out=small[:R, 100 + ci:101 + ci], in_=pI[:R, 0:nn], axis=AX.X, op=ALU.max), reads=[BpI], writes=[Bsm])
            c.op("dve", lambda e: e.tensor_reduce(out=small[:R, 110 + ci:111 + ci], in_=pI[:R, 0:nn], axis=AX.X, op=ALU.min), reads=[BpI], writes=[Bsm])
            bias = biasb[:R, 0:nn]
            c.op("dve", lambda e: e.tensor_scalar(out=bias, in0=iota[:R, 0:nn], scalar1=qrel, scalar2=NEG, op0=ALU.is_gt, op1=ALU.mult),
                 reads=[Bcst, Bsm], writes=[Bbias])
            c.op("dve", lambda e: e.tensor_tensor(out=Isb[:R, c0:c0 + nn], in0=pI[:R, 0:nn], in1=bias, op=ALU.add), reads=[BpI, Bbias], writes=[BI])

        def bounds(R, nch):
            c.op("dve", lambda e: e.tensor_reduce(out=small[:R, 22:23], in_=small[:R, 100:100 + nch], axis=AX.X, op=ALU.max), reads=[Bsm], writes=[Bsm])
            c.op("dve", lambda e: e.tensor_reduce(out=small[:R, 23:24], in_=small[:R, 110:110 + nch], axis=AX.X, op=ALU.min), reads=[Bsm], writes=[Bsm])
            c.op("dve", lambda e: e.tensor_scalar(out=small[:R, 16:17], in0=small[:R, 23:24], scalar1=-1.0, scalar2=None, op0=ALU.add), reads=[Bsm], writes=[Bsm])
            c.op("dve", lambda e: e.tensor_scalar(out=small[:R, 17:18], in0=small[:R, 22:23], scalar1=small[:R, 23:24], scalar2=2.0,
                                                  op0=ALU.subtract, op1=ALU.add), reads=[Bsm], writes=[Bsm])

        def mask_transposes(R, nkt):
            for t0 in range(0, nkt, 8):
                t1 = min(nkt, t0 + 8)
                pt, Bpt = ptbank()
                for t in range(t0, t1):
                    c.op("pe", lambda e: e.transpose(pt[:, (t - t0) * P:(t - t0) * P + R], msk[:R, t * P:(t + 1) * P], identb[:R, :R]),
                         reads=[Bmsk, Bid], writes=[Bpt])
                c.op("act", lambda e: e.copy(out=mskT[:, t0:t1, :R], in_=pt[:, 0:(t1 - t0) * P].rearrange("p (a b) -> p a b", b=P)[:, :, :R]),
                     reads=[Bpt], writes=[BmT])

        def tail(R, xdram, row0):
            c.dma("sp", rr[:R, :], xdram, Brr, writes=[Brr])
            pt, Bpt = ptbank()
            for k in range(8):
                c.op("pe", lambda e: e.transpose(pt[:, k * P:k * P + R], Obf[:R, k * P:(k + 1) * P], identb[:R, :R]),
                     reads=[BO, Bid], writes=[Bpt])
            c.op("dve", lambda e: e.tensor_copy(out=xT[:, :, :R], in_=pt[:, :].rearrange("p (k t) -> p k t", k=8)[:, :, :R]),
                 reads=[Bpt], writes=[BxT])
            tiles = []
            for k0 in (0, 4):
                src = wo_s[k0 * P:(k0 + 4) * P, :].rearrange("(k p) n -> p k n", p=P)
                wb, Bw = wload([(0, 4, D, src)], [Bwo])
                tiles.append((k0, wb, Bw))
            for m0 in (0, 512):
                pm, Bpm = c.bank()
                for (k0, wb, Bw) in tiles:
                    wv = wb[:, 0:4 * D].rearrange("p (k n) -> p k n", k=4)
                    for k in range(4):
                        c.op("pe", lambda e: e.matmul(pm[:R, :], lhsT=xT[:, k0 + k, :R], rhs=wv[:, k, m0:m0 + 512],
                                                      start=(k0 + k == 0), stop=(k0 + k == 7)), reads=[BxT, Bw], writes=[Bpm])
                c.op("dve", lambda e: e.scalar_tensor_tensor(out=rr[:R, m0:m0 + 512], in0=rr[:R, m0:m0 + 512], scalar=ALPHA, in1=pm[:R, :],
                                                             op0=ALU.mult, op1=ALU.add), reads=[Brr, Bpm], writes=[Brr])
            layer_norm(rr[:R, :], Brr, R, gb0, Bgb0, yy[:R, :], Byy)
            c.dma("sp", y1_s[row0:row0 + R, :], yy[:R, :], Byy, reads=[Byy])
            if stage < 3:
                if row0 < TQ:
                    c.dma("sp", o_y[row0:row0 + R, :], yy[:R, :], Byy, reads=[Byy])
                else:
                    c.dma("sp", o_ys[:, :], yy[:R, :], Byy, reads=[Byy])


        def rest_phase():
            es2 = contextlib.ExitStack()
            es2.__enter__()
            ya = c.sb("ya", [P, 4, D], F32, es2); Bya = [Buf("ya%d" % t) for t in range(4)]
            yb_ = c.sb("ybb", [P, 4, D], F32, es2); Byb = [Buf("yb%d" % t) for t in range(4)]
            yT = c.sb("yT", [P, 8, 512], BF16, es2); ByT = Buf("yT")
            hT = c.sb("hT", [P, 22, 512], BF16, es2); BhT = Buf("hT")
            aex = [(c.sb("aex%d" % i, [P, 520], F32, es2), Buf("aex%d" % i)) for i in range(2)]
            uub = [(c.sb("uu%d" % i, [P, 512], F32, es2), Buf("uu%d" % i)) for i in range(2)]
            silb = [(c.sb("sil%d" % i, [P, 512], F32, es2), Buf("sil%d" % i)) for i in range(2)]
            xb2 = c.sb("xb2", [P, D], BF16, es2); Bxb2 = Buf("xb2")
            gbs = [(c.sb("gb%d" % i, [P, 2 * D], F32, es2), Buf("gb%d" % i)) for i in (1, 2, 3)]
            halo_f = c.sb("halo_f", [P, 2, 22, 2], F32, es2); Bhf = Buf("halo_f")
            halo_c = c.sb("halo_c", [P, 8, 2], F32, es2); Bhc = Buf("halo_c")
            prm = c.sb("prm", [P, 22, 11], F32, es2)
            Bprm = Buf("prm")
            sext = c.sb("sext", [P, 22, 32], F32, es2); Bsext = Buf("sext")
            sout = c.sb("sout", [P, 22, 32], F32, es2); Bsout = Buf("sout")
            stg = c.sb("stg", [32, DFF], F32, es2); Bstg = Buf("stg")
            small2 = c.sb("small2", [P, 32], F32, es2); Bsm2 = Buf("small2")

            for li in (1, 2, 3):
                gbt, Bg = gbs[li - 1]
                c.dma("sp", gbt[:, 0:D], bass.AP(tensor=lng_d.tensor, offset=li * D, ap=[[0, P], [1, D]]), Bg, writes=[Bg], nowaw=True)
                c.dma("sp", gbt[:, D:2 * D], bass.AP(tensor=lnb_d.tensor, offset=li * D, ap=[[0, P], [1, D]]), Bg, writes=[Bg], nowaw=True)
            c.op("dve", lambda e: e.memset(stg[:, :], 0.0), writes=[Bstg])
            c.dma("sp", stg[0:6, :], fcw_d.rearrange("i j n -> (i j) n"), Bstg, reads=[Bstg], writes=[Bstg])
            c.dma("sp", stg[6:8, :], fcb_d[:, :], Bstg, reads=[Bstg], writes=[Bstg], nowaw=True)
            c.dma("sp", stg[8:11, 0:D], cw_d[:, :], Bstg, reads=[Bstg], writes=[Bstg], nowaw=True)
            pmp, Bpmp = c.bank()
            for ch in range(22):
                c.op("pe", lambda e: e.transpose(pmp[:, ch * 11:(ch + 1) * 11], stg[0:11, ch * P:(ch + 1) * P], identf[0:11, 0:11]),
                     reads=[Bstg, Bcst], writes=[Bpmp])
            c.op("act", lambda e: e.copy(out=prm[:, :, :], in_=pmp[:, 0:242].rearrange("p (a b) -> p a b", b=11)), reads=[Bpmp], writes=[Bprm])
            c.op("dve", lambda e: e.memset(halo_f[:, :, :, :], 0.0), writes=[Bhf])
            c.op("dve", lambda e: e.memset(halo_c[:, :, :], 0.0), writes=[Bhc])

            def ln2(src, Bsrc, R, gb, Bgb):
                st = small2[:R, 0:12]; mv = small2[:R, 12:14]; sd = small2[:R, 14:15]; rs = small2[:R, 15:16]
                c.op("dve", lambda e: e.bn_stats(out=st[:, 0:6], in_=src[:, 0:512]), reads=[Bsrc], writes=[Bsm2])
                c.op("dve", lambda e: e.bn_stats(out=st[:, 6:12], in_=src[:, 512:1024]), reads=[Bsrc], writes=[Bsm2])
                c.op("dve", lambda e: e.bn_aggr(out=mv, in_=st), reads=[Bsm2], writes=[Bsm2])
                c.op("act", lambda e: e.activation(out=sd, in_=mv[:, 1:2], func=AF.Sqrt, bias=EPS, scale=1.0), reads=[Bsm2], writes=[Bsm2])
                c.op("dve", lambda e: e.reciprocal(out=rs, in_=sd), reads=[Bsm2], writes=[Bsm2])
                c.op("dve", lambda e: e.tensor_scalar(out=src, in0=src, scalar1=mv[:, 0:1], scalar2=rs, op0=ALU.subtract, op1=ALU.mult),
                     reads=[Bsrc, Bsm2], writes=[Bsrc])
                c.op("pool", lambda e: e.tensor_tensor(out=src, in0=src, in1=gb[:R, 0:D], op=ALU.mult), reads=[Bsrc, Bgb], writes=[Bsrc])
                c.op("pool", lambda e: e.tensor_tensor(out=src, in0=src, in1=gb[:R, D:2 * D], op=ALU.add), reads=[Bsrc, Bgb], writes=[Bsrc])

            def to_featT(Y, BY, nt, R):
                for t in range(nt):
                    c.op("act", lambda e: e.copy(out=xb2[:R, :], in_=Y[:R, t, :]), reads=[BY[t]], writes=[Bxb2])
                    pt, Bpt = ptbank()
                    for k in range(8):
                        c.op("pe", lambda e: e.transpose(pt[:, k * P:k * P + R], xb2[:R, k * P:(k + 1) * P], identb[:R, :R]),
                             reads=[Bxb2, Bid], writes=[Bpt])
                    c.op("dve", lambda e: e.tensor_copy(out=yT[:, :, t * P:t * P + R], in_=pt[:, :].rearrange("p (k t) -> p k t", k=8)[:, :, :R]),
                         reads=[Bpt], writes=[ByT])

            def conv3(ae, Bae, N, samp, w0, w1, w2, uu, Buu):
                if samp:
                    av = ae[:, 0:96].rearrange("p (b t) -> p b t", t=6)
                    uv = uu[:, 0:64].rearrange("p (b t) -> p b t", t=4)
                    s0, s1, s2 = av[:, :, 0:4], av[:, :, 1:5], av[:, :, 2:6]
                else:
                    uv = uu[:, 0:N]
                    s0, s1, s2 = ae[:, 0:N], ae[:, 1:N + 1], ae[:, 2:N + 2]
                c.op("dve", lambda e: e.tensor_scalar(out=uv, in0=s0, scalar1=w0, scalar2=None, op0=ALU.mult), reads=[Bae, Bprm], writes=[Buu])
                c.op("dve", lambda e: e.scalar_tensor_tensor(out=uv, in0=s1, scalar=w1, in1=uv, op0=ALU.mult, op1=ALU.add), reads=[Bae, Bprm, Buu], writes=[Buu])
                c.op("dve", lambda e: e.scalar_tensor_tensor(out=uv, in0=s2, scalar=w2, in1=uv, op0=ALU.mult, op1=ALU.add), reads=[Bae, Bprm, Buu], writes=[Buu])

            def load_state_T(src_dram, nch):
                c.dma("sp", stg[:, 0:nch * P], src_dram, Bstg, writes=[Bstg])
                for c0 in range(0, nch, 16):
                    c1 = min(nch, c0 + 16)
                    pm, Bpm = c.bank()
                    for ch in range(c0, c1):
                        c.op("pe", lambda e: e.transpose(pm[:, (ch - c0) * 32:(ch - c0 + 1) * 32], stg[:, ch * P:(ch + 1) * P], identf[0:32, 0:32]),
                             reads=[Bstg, Bcst], writes=[Bpm])
                    c.op("act", lambda e: e.copy(out=sext[:, c0:c1, :], in_=pm[:, 0:(c1 - c0) * 32].rearrange("p (a b) -> p a b", b=32)),
                         reads=[Bpm], writes=[Bsext])

            def store_state_T(src, Bsrc, nch, ncol, dst_dram):
                for c0 in range(0, nch, 4):
                    c1 = min(nch, c0 + 4)
                    pm, Bpm = c.bank()
                    for ch in range(c0, c1):
                        c.op("pe", lambda e: e.transpose(pm[0:ncol, (ch - c0) * P:(ch - c0 + 1) * P], src[:, ch, :], identf[:, :]),
                             reads=[Bsrc, Bcst], writes=[Bpm])
                    c.op("act", lambda e: e.copy(out=stg[0:ncol, c0 * P:c1 * P], in_=pm[0:ncol, 0:(c1 - c0) * P]), reads=[Bpm], writes=[Bstg])
                c.dma("sp", dst_dram, stg[0:ncol, 0:nch * P], Bstg, reads=[Bstg])

            def ffn(i, Yin, BYin, Yout, BYout, nt, R, samp, last, gb, Bgb):
                N = R if samp else nt * P
                if samp:
                    load_state_T(stf_d[i, :, :], 22)
                ri = 0
                for cb in range(NCB):
                    wb, Bw = wload([(0, 8, 512, wup_s[i, cb].rearrange("p k s n -> p k (s n)"))], [Bwup])
                    wv = wb[:, 0:4096].rearrange("p (k s n) -> p k s n", k=8, s=2)
                    for hf in range(2):
                        ch = 2 * cb + hf
                        pa, Bpa = c.bank()
                        pg, Bpg = c.bank()
                        for (pm, Bpm, s_) in ((pa, Bpa, 0), (pg, Bpg, 1)):
                            for k in range(8):
                                c.op("pe", lambda e: e.matmul(pm[:, 0:N], lhsT=wv[:, k, s_, hf * P:(hf + 1) * P], rhs=yT[:, k, 0:N],
                                                              start=(k == 0), stop=(k == 7)), reads=[Bw, ByT], writes=[Bpm])
                        ae, Bae = aex[ri % 2]; uu, Buu = uub[ri % 2]; sl, Bsl = silb[ri % 2]
                        ri += 1
                        if samp:
                            av = ae[:, 0:96].rearrange("p (b t) -> p b t", t=6)
                            c.op("dve", lambda e: e.tensor_copy(out=av[:, :, 0:2], in_=sext[:, ch, :].rearrange("p (b r) -> p b r", r=2)),
                                 reads=[Bsext], writes=[Bae])
                            c.op("act", lambda e: e.copy(out=av[:, :, 2:6], in_=pa[:, 0:64].rearrange("p (b t) -> p b t", t=4)), reads=[Bpa], writes=[Bae])
                            c.op("dve", lambda e: e.tensor_copy(out=sout[:, ch, :].rearrange("p (b r) -> p b r", r=2), in_=av[:, :, 4:6]),
                                 reads=[Bae], writes=[Bsout])
                        else:
                            c.op("dve", lambda e: e.tensor_copy(out=ae[:, 0:2], in_=halo_f[:, i, ch, :]), reads=[Bhf], writes=[Bae])
                            c.op("act", lambda e: e.copy(out=ae[:, 2:2 + N], in_=pa[:, 0:N]), reads=[Bpa], writes=[Bae])
                            c.op("dve", lambda e: e.tensor_copy(out=halo_f[:, i, ch, :], in_=ae[:, N:N + 2]), reads=[Bae], writes=[Bhf])
                        conv3(ae, Bae, N, samp, prm[:, ch, 3 * i:3 * i + 1], prm[:, ch, 3 * i + 1:3 * i + 2], prm[:, ch, 3 * i + 2:3 * i + 3], uu, Buu)
                        c.op("act", lambda e: e.activation(out=sl[:, 0:N], in_=uu[:, 0:N], func=AF.Silu, bias=prm[:, ch, 6 + i:7 + i], scale=1.0),
                             reads=[Buu, Bprm], writes=[Bsl])
                        c.op("dve", lambda e: e.tensor_tensor(out=hT[:, ch, 0:N], in0=sl[:, 0:N], in1=pg[:, 0:N], op=ALU.mult),
                             reads=[Bsl, Bpg], writes=[BhT])
                for m0 in (0, 512):
                    bks = [c.bank() for _ in range(nt)]
                    for c0 in range(0, 22, 4):
                        cc = min(4, 22 - c0)
                        src = wdn_s[i, c0 * P:(c0 + cc) * P, m0:m0 + 512].rearrange("(c p) n -> p c n", p=P)
                        wb, Bw = wload([(0, cc, 512, src)], [Bwdn])
                        wv = wb[:, 0:cc * 512].rearrange("p (c n) -> p c n", c=cc)
                        for t in range(nt):
                            pm, Bpm = bks[t]
                            for cj in range(cc):
                                c.op("pe", lambda e: e.matmul(pm[:R, :], lhsT=hT[:, c0 + cj, t * P:t * P + R], rhs=wv[:, cj, :],
                                                              start=(c0 + cj == 0), stop=(c0 + cj == 21)), reads=[BhT, Bw], writes=[Bpm])
                    for t in range(nt):
                        pm, Bpm = bks[t]
                        c.op("dve", lambda e: e.scalar_tensor_tensor(out=Yout[:R, t, m0:m0 + 512], in0=Yin[:R, t, m0:m0 + 512], scalar=ALPHA,
                                                                     in1=pm[:R, :], op0=ALU.mult, op1=ALU.add), reads=[BYin[t], Bpm], writes=[BYout[t]])
                for t in range(nt):
                    ln2(Yout[:R, t, :], BYout[t], R, gb, Bgb)
                if samp:
                    store_state_T(sout, Bsout, 22, 32, o_fs[i, :, :])
                elif last:
                    store_state_T(halo_f[:, i, :, :], Bhf, 22, 2, o_fp[i, :, :])

            def mixer(Yin, BYin, Yout, BYout, nt, R, samp, last, gb, Bgb):
                N = R if samp else nt * P
                if samp:
                    load_state_T(stc_d[:, :], 8)
                ri = 0
                for ch in range(8):
                    wb, Bw = wload([(0, 8, 384, wci_s[ch].rearrange("p k s n -> p k (s n)"))], [Bwci])
                    wv = wb[:, 0:3072].rearrange("p (k s n) -> p k s n", k=8, s=3)
                    bks = [c.bank() for _ in range(3)]
                    for s_ in range(3):
                        pm, Bpm = bks[s_]
                        for k in range(8):
                            c.op("pe", lambda e: e.matmul(pm[:, 0:N], lhsT=wv[:, k, s_, :], rhs=yT[:, k, 0:N], start=(k == 0), stop=(k == 7)),
                                 reads=[Bw, ByT], writes=[Bpm])
                    (pb_, Bpb_), (pc_, Bpc_), (pu_, Bpu_) = bks
                    ae, Bae = aex[ri % 2]; uu, Buu = uub[ri % 2]
                    ri += 1
                    if samp:
                        av = ae[:, 0:96].rearrange("p (b t) -> p b t", t=6)
                        c.op("dve", lambda e: e.tensor_copy(out=av[:, :, 0:2], in_=sext[:, ch, :].rearrange("p (b r) -> p b r", r=2)), reads=[Bsext], writes=[Bae])
                        c.op("act", lambda e: e.copy(out=av[:, :, 2:6], in_=pc_[:, 0:64].rearrange("p (b t) -> p b t", t=4)), reads=[Bpc_], writes=[Bae])
                        c.op("dve", lambda e: e.tensor_tensor(out=av[:, :, 2:6], in0=av[:, :, 2:6], in1=pu_[:, 0:64].rearrange("p (b t) -> p b t", t=4), op=ALU.mult),
                             reads=[Bae, Bpu_], writes=[Bae])
                        c.op("dve", lambda e: e.tensor_copy(out=sout[:, ch, :].rearrange("p (b r) -> p b r", r=2), in_=av[:, :, 4:6]), reads=[Bae], writes=[Bsout])
                    else:
                        c.op("dve", lambda e: e.tensor_copy(out=ae[:, 0:2], in_=halo_c[:, ch, :]), reads=[Bhc], writes=[Bae])
                        c.op("act", lambda e: e.copy(out=ae[:, 2:2 + N], in_=pc_[:, 0:N]), reads=[Bpc_], writes=[Bae])
                        c.op("dve", lambda e: e.tensor_tensor(out=ae[:, 2:2 + N], in0=ae[:, 2:2 + N], in1=pu_[:, 0:N], op=ALU.mult), reads=[Bae, Bpu_], writes=[Bae])
                        c.op("dve", lambda e: e.tensor_copy(out=halo_c[:, ch, :], in_=ae[:, N:N + 2]), reads=[Bae], writes=[Bhc])
                    conv3(ae, Bae, N, samp, prm[:, ch, 8:9], prm[:, ch, 9:10], prm[:, ch, 10:11], uu, Buu)
                    c.op("dve", lambda e: e.tensor_tensor(out=hT[:, ch, 0:N], in0=uu[:, 0:N], in1=pb_[:, 0:N], op=ALU.mult), reads=[Buu, Bpb_], writes=[BhT])
                tiles = []
                for k0 in (0, 4):
                    src = wco_s[k0 * P:(k0 + 4) * P, :].rearrange("(k p) n -> p k n", p=P)
                    wb, Bw = wload([(0, 4, D, src)], [Bwco])
                    tiles.append((k0, wb, Bw))
                for t in range(nt):
                    for m0 in (0, 512):
                        pm, Bpm = c.bank()
                        for (k0, wb, Bw) in tiles:
                            wv = wb[:, 0:4 * D].rearrange("p (k n) -> p k n", k=4)
                            for k in range(4):
                                c.op("pe", lambda e: e.matmul(pm[:R, :], lhsT=hT[:, k0 + k, t * P:t * P + R], rhs=wv[:, k, m0:m0 + 512],
                                                              start=(k0 + k == 0), stop=(k0 + k == 7)), reads=[BhT, Bw], writes=[Bpm])
                        c.op("dve", lambda e: e.scalar_tensor_tensor(out=Yout[:R, t, m0:m0 + 512], in0=Yin[:R, t, m0:m0 + 512], scalar=ALPHA,
                                                                     in1=pm[:R, :], op0=ALU.mult, op1=ALU.add), reads=[BYin[t], Bpm], writes=[BYout[t]])
                    ln2(Yout[:R, t, :], BYout[t], R, gb, Bgb)
                if samp:
                    store_state_T(sout, Bsout, 8, 32, o_cs[:, :])
                elif last:
                    store_state_T(halo_c[:, :, :], Bhc, 8, 2, o_cp[:, :])

            glist = [(t0 * P, nt, P, False, gi == len(GROUPS) - 1) for gi, (t0, nt) in enumerate(GROUPS)] + [(TQ, 1, TS, True, False)]
            for (row0, nt, R, samp, last) in glist:
                for t in range(nt):
                    c.dma("sp", ya[:R, t, :], y1_s[row0 + t * P:row0 + t * P + R, :], Bya[t], writes=[Bya[t]])
                to_featT(ya, Bya, nt, R)
                ffn(0, ya, Bya, yb_, Byb, nt, R, samp, last, gbs[0][0], gbs[0][1])
                to_featT(yb_, Byb, nt, R)
                mixer(yb_, Byb, ya, Bya, nt, R, samp, last, gbs[1][0], gbs[1][1])
                to_featT(ya, Bya, nt, R)
                ffn(1, ya, Bya, yb_, Byb, nt, R, samp, last, gbs[2][0], gbs[2][1])
                for t in range(nt):
                    dst = o_ys[:, :] if samp else o_y[row0 + t * P:row0 + (t + 1) * P, :]
                    c.dma("sp", dst, yb_[:R, t, :], Byb[t], reads=[Byb[t]])
            c.barrier()
            es2.__exit__(None, None, None)

        for kt in range(NKT):
            c.dma("sp", xt[:], xk[kt * P:(kt + 1) * P, :], Bx, writes=[Bx])
            drain(4)
            front(xt[:, :], Bx, P)
            proj(P, wkv_s, Bwkv, 576, 0)
            rope(Yf, BYf, P, 0, 4, rk[:, kt, :])
            rope(Yf, BYf, P, 512, 1, rk[:, kt, :])
            rows = slice(kt * P, (kt + 1) * P)
            c.dma("sp", o_k[rows, :], Yf[:, 0:256], BYf, reads=[BYf])
            c.dma("sp", o_v[rows, :], Yf[:, 256:512], BYf, reads=[BYf])
            c.dma("sp", o_ik[rows, :], Yf[:, 512:576], BYf, reads=[BYf])
            c.op("pool", lambda e: e.tensor_copy(out=Yb[:, 0:576], in_=Yf[:, 0:576]), reads=[BYf], writes=[BYb])
            c.op("pool", lambda e: e.tensor_copy(out=Vaug[:, kt, :, 0:64], in_=Yf[:, 256:512].rearrange("p (g d) -> p g d", d=64)),
                 reads=[BYf], writes=[BV])
            pt, Bpt = ptbank()
            for gp in range(2):
                c.op("pe", lambda e: e.transpose(pt[:, gp * P:(gp + 1) * P], Yb[:, gp * P:(gp + 1) * P], identb[:, :]), reads=[BYb, Bid], writes=[Bpt])
            c.op("pe", lambda e: e.transpose(pt[0:64, 2 * P:3 * P], Yb[:, 512:576], identb[:, :]), reads=[BYb, Bid], writes=[Bpt])
            c.op("act", lambda e: e.copy(out=KT2[:, :, kt * P:(kt + 1) * P], in_=pt[:, 0:2 * P].rearrange("p (a b) -> p a b", b=P)),
                 reads=[Bpt], writes=[BKT])
            c.op("act", lambda e: e.copy(out=kiT[:, kt * P:(kt + 1) * P], in_=pt[0:64, 2 * P:3 * P]), reads=[Bpt], writes=[BkiT])

        def q_front(R, tb):
            rope(Yf, BYf, R, 0, 32, tb)
            for gp in range(2):
                c.op("pool", lambda e: e.tensor_copy(
                    out=Yb[:R, gp * 512:(gp + 1) * 512].rearrange("p (h r d) -> p h r d", h=4, r=2),
                    in_=Yf[:R, gp * 512:(gp + 1) * 512].rearrange("p (r h d) -> p h r d", h=4, r=2)), reads=[BYf], writes=[BYb])
            c.op("pool", lambda e: e.tensor_copy(out=Yb[:R, 1024:2048], in_=Yf[:R, 1024:2048]), reads=[BYf], writes=[BYb])
            c.op("dve", lambda e: e.tensor_scalar(out=wsc[:R, :], in0=Yf[:R, 2048:2064], scalar1=IDX_SCALE, scalar2=None, op0=ALU.mult),
                 reads=[BYf], writes=[Bwsc])

        def q_transposes(R):
            pt, Bpt = ptbank()
            for gp in range(2):
                for hh in range(4):
                    idx = gp * 4 + hh
                    src = Yb[:R, idx * P:(idx + 1) * P]
                    c.op("pe", lambda e: e.transpose(pt[:, idx * P:idx * P + R], src, identb[:R, :R]), reads=[BYb, Bid], writes=[Bpt])
            c.op("act", lambda e: e.copy(out=QT2[:, :, :R], in_=pt[:, :].rearrange("p (a b) -> p a b", b=P)[:, :, :R]), reads=[Bpt], writes=[BQT])
            for h0 in (0, 8):
                pt, Bpt = ptbank()
                for h in range(8):
                    col = 1024 + (h0 + h) * 64
                    c.op("pe", lambda e: e.transpose(pt[0:64, h * P:h * P + R], Yb[:R, col:col + 64], identb[:R, :R]), reads=[BYb, Bid], writes=[Bpt])
                c.op("act", lambda e: e.copy(out=qiT[:, h0:h0 + 8, :R], in_=pt[0:64, :].rearrange("p (a b) -> p a b", b=P)[:, :, :R]),
                     reads=[Bpt], writes=[BqiT])

        import os as _os
        SUB = int(_os.environ.get("DBG_SUB", "99"))
        NQR = int(_os.environ.get("DBG_NQ", str(NQT)))
        def attention(NK, maskfn, Bmask, Od, BOd, mme=("pool", "dve")):
            obanks = [c.banks[0], c.banks[1], c.banks[2]]
            for (ob, Bob) in obanks:
                c.op("pe", lambda e: e.matmul(ob[:, :], lhsT=zerob[:, 0:P], rhs=zerob[:, :], start=True, stop=False),
                     reads=[Bzero], writes=[Bob])
            units = [(kt, g) for kt in range(NK) for g in range(4)]
            LA = 2
            fr = []

            def front_u(ei, kt, g):
                pS, BpS = c.bank((3, 4, 5))
                pb = (g % 2) * 64
                c.op("pe", lambda e: e.matmul(pS[:, :], lhsT=KT2[pb:pb + 64, g // 2, kt * P:(kt + 1) * P],
                                              rhs=QT2[pb:pb + 64, (g // 2) * 4:(g // 2) * 4 + 4, :], start=True, stop=True),
                     reads=[BKT, BQT], writes=[BpS])
                E_, BE_ = Eb[ei % 2]
                Pm_, BPm_ = Pb[ei % 3]
                c.op("act", lambda e: e.activation(out=E_[:, :], in_=pS[:, :], func=AF.Exp, scale=0.125), reads=[BpS], writes=[BE_])
                c.op(mme[ei % len(mme)], lambda e: e.tensor_tensor(out=Pm_[:, :].rearrange("p (a b) -> p a b", a=4),
                                                       in0=E_[:, :].rearrange("p (a b) -> p a b", a=4),
                                                       in1=bc_mid(maskfn(kt), 4), op=ALU.mult), reads=[BE_, Bmask], writes=[BPm_])
                return (Pm_, BPm_)

            def back_u(ei, kt, g):
                Pm_, BPm_ = fr[ei]
                for hh in range(4):
                    h = 4 * g + hh
                    ob, Bob = obanks[h // 7]
                    oc = (h % 7) * 65
                    c.op("pe", lambda e: e.matmul(ob[:, oc:oc + 65], lhsT=Pm_[:, hh * P:(hh + 1) * P], rhs=Vaug[:, kt, g, :],
                                                  start=False, stop=(kt == NK - 1 and (h % 7 == 6 or h == 15))), reads=[BPm_, BV], writes=[Bob])

            for i in range(len(units) + LA):
                if i < len(units):
                    fr.append(front_u(i, *units[i]))
                if i >= LA:
                    back_u(i - LA, *units[i - LA])
            for bi, (ob, Bob) in enumerate(obanks):
                nh = 7 if bi < 2 else 2
                ov = ob[:, 0:nh * 65].rearrange("p (h d) -> p h d", d=65)
                rec = small[:, 200 + 7 * bi:200 + 7 * bi + nh]
                c.op("dve", lambda e: e.reciprocal(out=rec, in_=ov[:, :, 64]), reads=[Bob], writes=[Bsm])
                c.op("dve", lambda e: e.tensor_tensor(out=Od[:, bi * 7 * 64:(bi * 7 + nh) * 64].rearrange("p (h d) -> p h d", d=64),
                                                      in0=ov[:, :, 0:64], in1=bc_last(rec, 64), op=ALU.mult), reads=[Bob, Bsm], writes=[BOd])

        if stage >= 1:
            for j in range(NQR):
                NK = 16 + j
                N = NK * P
                c.dma("sp", xt[:], xq[j * P:(j + 1) * P, :], Bx, writes=[Bx])
                drain(8)
                front(xt[:, :], Bx, P)
                proj(P, wq_s, Bwq, 2064, 0)
                q_front(P, rq[:, j, :])
                if SUB < 1:
                    continue
                q_transposes(P)
                c.op("dve", lambda e: e.tensor_tensor(out=diag[:, :, :], in0=bc_mid(identb[:, :], 16), in1=bc_last(wsc[:, :], P), op=ALU.mult),
                     reads=[Bid, Bwsc], writes=[Bdiag])
                if SUB < 2:
                    continue
                nch = (N + 511) // 512
                items = [(ci, h) for ci in range(nch) for h in range(16)]
                pIs = {}
                frs = []
                LAI = 3

                def idx_front(ii, ci, h):
                    c0 = ci * 512
                    nn = min(512, N - c0)
                    psc, Bpsc = c.bank((0, 1, 2, 3))
                    c.op("pe", lambda e: e.matmul(psc[:, 0:nn], lhsT=qiT[:, h, :], rhs=kiT[:, c0:c0 + nn], start=True, stop=True),
                         reads=[BqiT, BkiT], writes=[Bpsc])
                    R_, BR_ = Rb[ii % 4]
                    if h % 2 == 0:
                        c.op("act", lambda e: e.activation(out=R_[:, 0:nn], in_=psc[:, 0:nn], func=AF.Relu), reads=[Bpsc], writes=[BR_])
                    else:
                        c.op("dve", lambda e: e.tensor_scalar(out=R_[:, 0:nn], in0=psc[:, 0:nn], scalar1=0.0, scalar2=None, op0=ALU.max),
                             reads=[Bpsc], writes=[BR_])
                    return (R_, BR_)

                def idx_back(ii, ci, h):
                    c0 = ci * 512
                    nn = min(512, N - c0)
                    if h == 0:
                        pIs[ci] = c.bank((4, 5))
                    pI, BpI = pIs[ci]
                    R_, BR_ = frs[ii]
                    c.op("pe", lambda e: e.matmul(pI[:, 0:nn], lhsT=diag[:, h, :], rhs=R_[:, 0:nn], start=(h == 0), stop=(h == 15)),
                         reads=[Bdiag, BR_], writes=[BpI])
                    if h == 15:
                        evac_scores(pI, BpI, P, c0, nn, qpos[:, j:j + 1])

                for ii in range(len(items) + LAI):
                    if ii < len(items):
                        frs.append(idx_front(ii, *items[ii]))
                    if ii >= LAI:
                        idx_back(ii - LAI, *items[ii - LAI])
                if SUB < 3:
                    continue
                bounds(P, nch)
                bisect(P, N)
                if SUB < 4:
                    continue
                mask_transposes(P, NK)
                if SUB < 5:
                    continue
                attention(NK, lambda kt: mskT[:, kt, :], BmT, Obf, BO)
                if SUB < 6:
                    continue
                tail(P, xq[j * P:(j + 1) * P, :], j * P)
                if _os.environ.get("DBG_DUMP") and j == 0:
                    c.dma("sp", o_y[128:256, :], Isb[:, 0:1024], BI, reads=[BI])
                    c.dma("sp", o_y[256:384, 0:256], small[:, :], Bsm, reads=[Bsm])
                    c.dma("sp", o_y[384:512, :], rr[:, :], Brr, reads=[Brr])


        def sample_phase():
            R = TS
            IOA = bass.IndirectOffsetOnAxis
            c.dma("sp", xt[:R, :], xs[:, :], Bx, writes=[Bx])
            c.dma("pool", mselb[:, :], msel_d[:, :], Bmsel, writes=[Bmsel])
            for jj in range(8):
                c.dma("sp", idxall[16 * jj:16 * jj + 16, 0:2 * NSQ], bass.AP(tensor=pt_d.tensor, offset=jj, ap=[[0, 16], [8, 2 * NSQ]]), Bidx,
                      writes=[Bidx], nowaw=(jj > 0), slow=True)
            c.op("dve", lambda e: e.tensor_scalar(out=idxall[:, 0:2 * NSQ], in0=idxall[:, 0:2 * NSQ], scalar1=16.0, scalar2=pm16, op0=ALU.mult, op1=ALU.add),
                 reads=[Bidx, Bcst], writes=[Bidx])
            cik8 = cik_d.rearrange("(r k) d -> r (k d)", k=8)
            ck8 = ck_d.rearrange("(r k) d -> r (k d)", k=8)
            cv8 = cv_d.rearrange("(r k) d -> r (k d)", k=8)
            front(xt[:R, :], Bx, R)
            proj(R, wq_s, Bwq, 2064, 0)
            proj(R, wkv_s, Bwkv, 576, 2064)
            tb = rq[0:R, NQT, :]
            q_front(R, tb)
            rope(Yf, BYf, R, 2064, 4, tb)
            rope(Yf, BYf, R, 2064 + 512, 1, tb)
            c.dma("sp", o_ks[:, :], Yf[:R, 2064:2320], BYf, reads=[BYf])
            c.dma("sp", o_vs[:, :], Yf[:R, 2320:2576], BYf, reads=[BYf])
            c.dma("sp", o_iks[:, :], Yf[:R, 2576:2640], BYf, reads=[BYf])
            c.op("pool", lambda e: e.tensor_copy(out=Yb[:R, 2064:2640], in_=Yf[:R, 2064:2640]), reads=[BYf], writes=[BYb])
            q_transposes(R)
            pt, Bpt = ptbank()
            for gp in range(2):
                c.op("pe", lambda e: e.transpose(pt[:, gp * P:gp * P + R], Yb[:R, 2064 + gp * P:2064 + (gp + 1) * P], identb[:R, :R]),
                     reads=[BYb, Bid], writes=[Bpt])
            c.op("pe", lambda e: e.transpose(pt[0:64, 2 * P:2 * P + R], Yb[:R, 2576:2640], identb[:R, :R]), reads=[BYb, Bid], writes=[Bpt])
            c.op("act", lambda e: e.copy(out=KnT2[:, :, :], in_=pt[:, 0:2 * P].rearrange("p (a b) -> p a b", b=P)[:, :, 0:R]), reads=[Bpt], writes=[BKn])
            c.op("act", lambda e: e.copy(out=kinT[:, :], in_=pt[0:64, 2 * P:2 * P + R]), reads=[Bpt], writes=[Bkin])
            Bwsd = Buf("wsd")
            c.dma("sp", wsd_s[:, :], wsc[:R, :], Bwsd, reads=[Bwsc], writes=[Bwsd])
            for h in range(16):
                src = bass.AP(tensor=wsd_s.tensor, offset=h, ap=[[16, 4], [64, NSQ]])
                c.dma("sp", Wht[4 * h:4 * h + 4, :], src, BWht, reads=[Bwsd], writes=[BWht], nowaw=(h > 0), slow=True)
            c.op("dve", lambda e: e.tensor_tensor(out=Wsel[:, :, :], in0=mselb[:, :].rearrange("p (a b) -> p a b", b=64), in1=bc_last(Wht[:, :], 64), op=ALU.mult),
                 reads=[Bmsel, BWht], writes=[BWsel])
            c.op("dve", lambda e: e.memset(kiT[:, PAST:PAST + P], 0.0), writes=[BkiT])
            c.op("dve", lambda e: e.memset(KT2[:, :, PAST:PAST + P], 0.0), writes=[BKT])
            c.op("pool", lambda e: e.memset(Vaug[:, 16, :, 0:64], 0.0), writes=[BV])
            nch = 5
            SS = float(_os.environ.get("DBG_SS", "99"))
            if SS < 1:
                return
            for b in range(NSQ):
                for hf in range(2):
                    col = b * 2 + hf
                    c.dma("pool", ikp[:, hf * 8:(hf + 1) * 8, :].rearrange("p a d -> p (a d)"), cik8, Bikp, reads=[Bidx], writes=[Bikp], nowaw=(hf > 0),
                          indirect=IOA(ap=idxall[:, col:col + 1], axis=0))
                for j0 in (0, 8):
                    pt, Bpt = ptbank()
                    for j in range(8):
                        c.op("pe", lambda e: e.transpose(pt[0:64, j * P:(j + 1) * P], ikp[:, j0 + j, :], identb[:, :]), reads=[Bikp, Bid], writes=[Bpt])
                    c.op("act", lambda e: e.copy(out=kiT[:, j0 * P:(j0 + 8) * P], in_=pt[0:64, :]), reads=[Bpt], writes=[BkiT])
                c.op("dve", lambda e: e.tensor_copy(out=kiT[:, PAST:PAST + 4], in_=kinT[:, 4 * b:4 * b + 4]), reads=[Bkin], writes=[BkiT])
                c.op("dve", lambda e: e.tensor_copy(out=qisb[:, :].rearrange("p (h t) -> p h t", t=4), in_=qiT[:, :, 4 * b:4 * b + 4]), reads=[BqiT], writes=[Bqisb])
                for ci in range(nch):
                    c0 = ci * 512
                    nn = min(512, NKS * P - c0)
                    psc, Bpsc = c.banks[5]
                    pI, BpI = c.banks[ci]
                    c.op("pe", lambda e: e.matmul(psc[0:64, 0:nn], lhsT=qisb[:, :], rhs=kiT[:, c0:c0 + nn], start=True, stop=True),
                         reads=[Bqisb, BkiT], writes=[Bpsc])
                    R_, BR_ = Rb[(b * nch + ci) % 4]
                    c.op("act", lambda e: e.activation(out=R_[0:64, 0:nn], in_=psc[0:64, 0:nn], func=AF.Relu), reads=[Bpsc], writes=[BR_])
                    c.op("pe", lambda e: e.matmul(pI[0:64, 0:nn], lhsT=Wsel[:, b, :], rhs=R_[0:64, 0:nn], start=(b == 0), stop=(b == NSQ - 1)),
                         reads=[BWsel, BR_], writes=[BpI])
            if SS < 2:
                return
            for ci in range(nch):
                c0 = ci * 512
                nn = min(512, NKS * P - c0)
                pI, BpI = c.banks[ci]
                evac_scores(pI, BpI, R, c0, nn, qpos[0:R, NQT:NQT + 1])
            bounds(R, nch)
            bisect(R, NKS * P)
            mask_transposes(R, NKS)
            if SS < 2.5:
                return
            mskS = msk[:, 0:NKS * P].rearrange("p (k t) -> p k t", t=P)
            c.op("pool", lambda e: e.memset(msk[:, 0:NKS * P], 0.0), writes=[Bmsk])
            for b in range(NSQ):
                for hf in range(2):
                    col = b * 2 + hf
                    c.dma("pool", Kp[:, hf * 8:(hf + 1) * 8, :].rearrange("p a d -> p (a d)"), ck8, BKp, reads=[Bidx], writes=[BKp], nowaw=(hf > 0),
                          indirect=IOA(ap=idxall[:, col:col + 1], axis=0))
                for hf in range(2):
                    col = b * 2 + hf
                    c.dma("pool", Vp[:, hf * 8:(hf + 1) * 8, :].rearrange("p a d -> p (a d)"), cv8, BVp, reads=[Bidx], writes=[BVp], nowaw=(hf > 0),
                          indirect=IOA(ap=idxall[:, col:col + 1], axis=0))
                c.op("pool", lambda e: e.tensor_copy(out=Vaug[:, 0:16, :, 0:64], in_=Vp[:, :, :].rearrange("p j (g d) -> p j g d", d=64)),
                     reads=[BVp], writes=[BV])
                c.dma("sp", Vst[:, :], Yf[4 * b:4 * b + 4, 2320:2576], BVst, reads=[BYf], writes=[BVst])
                c.op("pool", lambda e: e.tensor_copy(out=Vaug[0:4, 16, :, 0:64], in_=Vst[:, :].rearrange("p (g d) -> p g d", d=64)), reads=[BVst], writes=[BV])
                for j0 in range(0, 16, 4):
                    pt, Bpt = ptbank()
                    for gp in range(2):
                        for j in range(4):
                            c.op("pe", lambda e: e.transpose(pt[:, (gp * 4 + j) * P:(gp * 4 + j + 1) * P], Kp[:, j0 + j, gp * P:(gp + 1) * P], identb[:, :]),
                                 reads=[BKp, Bid], writes=[Bpt])
                    c.op("act", lambda e: e.copy(out=KT2[:, :, j0 * P:(j0 + 4) * P], in_=pt[:, :].rearrange("p (a b) -> p a b", a=2)), reads=[Bpt], writes=[BKT])
                c.op("dve", lambda e: e.tensor_copy(out=KT2[:, :, PAST:PAST + 4], in_=KnT2[:, :, 4 * b:4 * b + 4]), reads=[BKn], writes=[BKT])
                if SS < 2.7:
                    continue
                if b > 0:
                    c.op("pool", lambda e: e.memset(mskS[:, :, 4 * (b - 1):4 * b], 0.0), writes=[Bmsk])
                c.op("pool", lambda e: e.tensor_copy(out=mskS[:, :, 4 * b:4 * b + 4], in_=mskT[:, 0:NKS, 4 * b:4 * b + 4]), reads=[BmT], writes=[Bmsk])
                attention(NKS, lambda kt: mskS[:, kt, :], Bmsk, xb, Bxb, mme=("dve", "dve", "pool"))
                c.dma("sp", Obf[4 * b:4 * b + 4, :], xb[4 * b:4 * b + 4, :], BO, reads=[Bxb], writes=[BO], nowaw=True)
            if SS < 4:
                return
            tail(R, xs[:, :], TQ)

        if stage >= 2:
            sample_phase()
        drain(10000)
        c.barrier()
        es1.__exit__(None, None, None)
        if stage >= 3:
            rest_phase()
        c.finish()
    return nc


def _rope_table(pos):
    half = 8
    inv = (500000.0 ** (-np.arange(half, dtype=np.float32) * np.float32(2.0 / 16))).astype(np.float32)
    ang = pos.astype(np.float32)[:, None] * inv[None, :]
    cs = np.cos(ang).astype(np.float32)
    sn = np.sin(ang).astype(np.float32)
    return np.concatenate([cs, cs, sn], axis=1).astype(np.float32)


_NC_CACHE = {}


def _run(inputs, nphys=None, stage=99, compact=False):
    f = lambda a: np.ascontiguousarray(np.asarray(a))
    x_prompt = f(inputs["x_prompt"]); x_sample = f(inputs["x_sample"])
    cache_k = f(inputs["cache_k"])[0]; cache_v = f(inputs["cache_v"])[0]; cache_ik = f(inputs["cache_idx_k"])[0]
    page_table = f(inputs["page_table"]).astype(np.int32)
    full_nphys = cache_k.shape[0]
    if nphys is None:
        nphys = full_nphys
    key = (nphys, stage)
    if key not in _NC_CACHE:
        _NC_CACHE[key] = build(nphys, stage)
    nc = _NC_CACHE[key]
    consts = np.zeros((P, 512 + 128 + NIT + 2), np.float32)
    consts[:, 0:512] = np.arange(512, dtype=np.float32)[None, :]
    consts[:, 512:640] = np.eye(P, dtype=np.float32)
    consts[:, 640:640 + NIT] = (0.5 ** np.arange(1, NIT + 1, dtype=np.float64)).astype(np.float32)[None, :]
    consts[:, 640 + NIT] = np.arange(P, dtype=np.float32)
    consts[:, 641 + NIT] = (np.arange(P) % 16).astype(np.float32)
    msel = np.zeros((64, NSQ, 64), np.float32)
    for h in range(16):
        for t in range(4):
            for b in range(NSQ):
                msel[h * 4 + t, b, b * 4 + t] = 1.0
    ropek = _rope_table(np.arange(SEQ))
    ropes = _rope_table(PAST + (np.arange(TS) % 4))
    in_maps = []
    for core in range(8):
        b, h = core // 2, core % 2
        pos0 = 0 if h == 0 else POS0
        qp = np.zeros((P, NQT + 1), np.float32)
        qp[:, :NQT] = pos0 + np.arange(P)[:, None] + P * np.arange(NQT)[None, :]
        qp[:TS, NQT] = PAST + (np.arange(TS) % 4)
        sl = slice(core * NSQ, (core + 1) * NSQ)
        pt = page_table[sl]
        if compact:
            pages = np.unique(pt)
            remap = {int(p): i for i, p in enumerate(pages)}
            ck = np.zeros((nphys, P, 256), np.float32); cv = np.zeros((nphys, P, 256), np.float32); ci = np.zeros((nphys, P, 64), np.float32)
            ck[:len(pages)] = cache_k[pages].reshape(-1, P, 256); cv[:len(pages)] = cache_v[pages].reshape(-1, P, 256)
            ci[:len(pages)] = cache_ik[pages]
            pt = np.vectorize(remap.get)(pt).astype(np.int32)
        else:
            ck = cache_k.reshape(-1, P, 256); cv = cache_v.reshape(-1, P, 256); ci = cache_ik
        m = {
            "xk": x_prompt[b], "xq": x_prompt[b, pos0:pos0 + TQ], "xs": x_sample[sl].reshape(TS, D),
            "ropek": ropek, "ropeq": ropek[pos0:pos0 + TQ], "ropes": ropes, "qpos": qp, "consts": consts,
            "msel": msel.reshape(64, NSQ * 64), "pt": pt,
            "cache_k": ck.reshape(nphys * P, 256), "cache_v": cv.reshape(nphys * P, 256), "cache_ik": ci.reshape(nphys * P, 64),
            "st_conv": f(inputs["state_conv"])[0, sl].reshape(NSQ * 2, D),
            "st_ffn": f(inputs["state_ffn"])[:, sl].reshape(2, NSQ * 2, DFF),
            "w_attn_in": f(inputs["w_attn_in"])[0], "w_attn_out": f(inputs["w_attn_out"])[0],
            "w_conv_in": f(inputs["w_conv_in"])[0], "conv_w": f(inputs["conv_w"])[0], "w_conv_out": f(inputs["w_conv_out"])[0],
            "w_ffn_up": f(inputs["w_ffn_up"]), "ffn_conv_w": f(inputs["ffn_conv_w"]), "ffn_conv_b": f(inputs["ffn_conv_b"]),
            "w_ffn_down": f(inputs["w_ffn_down"]), "ln_g": f(inputs["ln_g"]).reshape(4, D), "ln_b": f(inputs["ln_b"]).reshape(4, D),
        }
        in_maps.append({k: np.ascontiguousarray(v) for k, v in m.items()})
    res = run_bass_kernel_spmd(nc, in_maps, core_ids=list(range(8)))
    R = res.results
    B = 4
    y_prompt = np.zeros((B, SEQ, D), np.float32)
    nk = np.zeros((1, B, SEQ, 4, 64), np.float32); nv = np.zeros_like(nk); nik = np.zeros((1, B, SEQ, 64), np.float32)
    cp = np.zeros((1, B, 2, D), np.float32); fp = np.zeros((2, B, 2, DFF), np.float32)
    y_sample = np.zeros((128, 4, D), np.float32)
    nks = np.zeros((1, 128, 4, 4, 64), np.float32); nvs = np.zeros_like(nks); niks = np.zeros((1, 128, 4, 64), np.float32)
    cs = np.zeros((1, 128, 2, D), np.float32); fs = np.zeros((2, 128, 2, DFF), np.float32)
    for core in range(8):
        b, h = core // 2, core % 2
        r = R[core]
        sl = slice(core * NSQ, (core + 1) * NSQ)
        if h == 0:
            y_prompt[b, 0:2048] = r["o_y"][0:2048]
            nk[0, b] = r["o_k"].reshape(SEQ, 4, 64); nv[0, b] = r["o_v"].reshape(SEQ, 4, 64); nik[0, b] = r["o_ik"]
        else:
            y_prompt[b, 2048:4096] = r["o_y"][128:TQ]
            cp[0, b] = r["o_cp"]; fp[:, b] = r["o_fp"]
        y_sample[sl] = r["o_ys"].reshape(NSQ, 4, D)
        nks[0, sl] = r["o_ks"].reshape(NSQ, 4, 4, 64); nvs[0, sl] = r["o_vs"].reshape(NSQ, 4, 4, 64); niks[0, sl] = r["o_iks"].reshape(NSQ, 4, 64)
        cs[0, sl] = r["o_cs"].reshape(NSQ, 2, D); fs[:, sl] = r["o_fs"].reshape(2, NSQ, 2, DFF)
    return (y_prompt, y_sample, nk, nv, nik, nks, nvs, niks, cp, cs, fp, fs), R


def kernel(**inputs):
    outs, _ = _run(inputs)
    return outs
```

```python
import contextlib
import numpy as np
import concourse.bass as bass
import concourse.mybir as mybir
from concourse.bass_utils import run_bass_kernel_spmd

F32 = mybir.dt.float32
BF16 = mybir.dt.bfloat16
I32 = mybir.dt.int32
AF = mybir.ActivationFunctionType
ALU = mybir.AluOpType
AX = mybir.AxisListType

P = 128
D = 1024
DFF = 2816
NCB = 11
SEQ = 4096
NKT = 32
NQT = 17
TQ = NQT * P
POS0 = 1920
NSQ = 16
TS = 64
PAST = 2048
NKS = 17
LS = PAST + 4
TOPK = 256
NIT = 14
ALPHA = 4.0 ** 0.25
IDX_SCALE = 1.0 / 32.0
EPS = 1e-5
NEG = -1.0e30
GROUPS = [(0, 1), (1, 4), (5, 4), (9, 4), (13, 4)]


class Buf:
    __slots__ = ("name", "w", "r", "dsem", "dtot")

    def __init__(self, name):
        self.name = name
        self.w = None
        self.r = {}
        self.dsem = None
        self.dtot = 0


def bc_mid(ap, n):
    a = [list(x) for x in ap.ap]
    return bass.AP(tensor=ap.tensor, offset=ap.offset, ap=[a[0], [0, n]] + a[1:])


def bc_last(ap, n):
    a = [list(x) for x in ap.ap]
    return bass.AP(tensor=ap.tensor, offset=ap.offset, ap=a + [[0, n]])


class Ctx:
    CH = 16000

    def __init__(self, nc, es):
        self.nc = nc
        self.es = es
        self.engs = {"pe": nc.tensor, "dve": nc.vector, "act": nc.scalar, "pool": nc.gpsimd, "sp": nc.sync}
        self.cnt = {e: 0 for e in self.engs}
        self.sems = {e: [] for e in self.engs}
        self.waited = {e: {} for e in self.engs}
        self.dma_bufs = []
        self.nsem = 0
        self.banks = []
        self.bank_i = 0
        self.wbufs = []
        self.wb_i = 0

    def new_sem(self, name):
        self.nsem += 1
        return self.es.enter_context(self.nc.semaphore(name))

    def sb(self, name, shape, dt, es=None):
        return (es or self.es).enter_context(self.nc.sbuf_tensor("sb_" + name, list(shape), dt))

    def ps(self, name, shape, dt):
        return self.es.enter_context(self.nc.psum_tensor("ps_" + name, list(shape), dt))

    def _esem(self, e, tick):
        ch = (tick - 1) // self.CH
        while len(self.sems[e]) <= ch:
            self.sems[e].append(self.new_sem("s_%s_%d" % (e, len(self.sems[e]))))
        return self.sems[e][ch], (tick - 1) % self.CH + 1

    def _wait(self, e, tok):
        if tok[0] == "eng":
            _, f, tick = tok
            key = f
            if self.waited[e].get(key, 0) >= tick:
                return
            sem, val = self._esem(f, tick)
        else:
            _, buf, val = tok
            key = ("d", id(buf))
            tick = val
            if self.waited[e].get(key, 0) >= tick:
                return
            sem = buf.dsem
        self.engs[e].wait_ge(sem, val)
        self.waited[e][key] = tick

    def _deps(self, e, reads, writes, nowaw=False):
        for b in reads:
            t = b.w
            if t is not None:
                if t[0] == "eng" and t[1] == e and e in ("pe", "sp"):
                    continue
                self._wait(e, t)
        if nowaw:
            return
        for b in writes:
            t = b.w
            if t is not None:
                if not (t[0] == "eng" and t[1] == e and e in ("pe", "sp", "dve", "act")):
                    self._wait(e, t)
            for t in b.r.values():
                if t[0] == "eng" and t[1] == e and e != "pool":
                    continue
                self._wait(e, t)

    def _commit(self, tok, reads, writes):
        key = tok[1] if tok[0] == "eng" else ("d", id(tok[1]))
        for b in reads:
            b.r[key] = tok
        for b in writes:
            b.w = tok
            b.r = {}

    def op(self, e, fn, reads=(), writes=()):
        self._deps(e, reads, writes)
        ins = fn(self.engs[e])
        self.cnt[e] += 1
        tick = self.cnt[e]
        sem, _ = self._esem(e, tick)
        ins.then_inc(sem, 1)
        self._commit(("eng", e, tick), reads, writes)
        return ins

    def dma(self, q, out, in_, sbuf, reads=(), writes=(), nowaw=False, indirect=None, slow=False):
        self._deps(q, reads, writes, nowaw)
        if sbuf.dsem is None:
            sbuf.dsem = self.new_sem("d_" + sbuf.name)
            self.dma_bufs.append(sbuf)
        sbuf.dtot += 16
        if indirect is not None:
            ins = self.nc.gpsimd.indirect_dma_start(out=out, out_offset=None, in_=in_, in_offset=indirect)
        else:
            ins = self.engs[q].dma_start(out=out, in_=in_, allow_slow_non_contiguous=True) if slow else self.engs[q].dma_start(out=out, in_=in_)
        ins.then_inc(sbuf.dsem, 16)
        self._commit(("dma", sbuf, sbuf.dtot), reads, writes)

    def barrier(self):
        for e in self.engs:
            for f in self.engs:
                if f != e and self.cnt[f] > 0:
                    self._wait(e, ("eng", f, self.cnt[f]))
            for b in self.dma_bufs:
                if b.dtot > 0:
                    self._wait(e, ("dma", b, b.dtot))

    def finish(self):
        for b in self.dma_bufs:
            self.engs["sp"].wait_ge(b.dsem, b.dtot)

    def bank(self, subset=None):
        if subset is None:
            subset = range(len(self.banks))
        self.bank_i += 1
        return self.banks[subset[self.bank_i % len(subset)]]

    def wbuf(self):
        i = self.wb_i % len(self.wbufs)
        self.wb_i += 1
        return self.wbufs[i]


def build(nphys, stage=99):
    nc = bass.Bass("TRN2", target_bir_lowering=False)

    def din(name, shape, dt=F32):
        return nc.dram_tensor(name, list(shape), dt, kind="ExternalInput").ap()

    def dout(name, shape, dt=F32):
        return nc.dram_tensor(name, list(shape), dt, kind="ExternalOutput").ap()

    def dscr(name, shape, dt):
        return nc.dram_tensor(name, list(shape), dt, kind="Internal").ap()

    NROW = nphys * P
    xk = din("xk", [SEQ, D])
    xq = din("xq", [TQ, D])
    xs = din("xs", [TS, D])
    ropek = din("ropek", [SEQ, 24])
    ropeq = din("ropeq", [TQ, 24])
    ropes = din("ropes", [TS, 24])
    qpos_d = din("qpos", [P, NQT + 1])
    consts = din("consts", [P, 512 + 128 + NIT + 2])
    msel_d = din("msel", [64, NSQ * 64])
    pt_d = din("pt", [NSQ, 16], I32)
    ck_d = din("cache_k", [NROW, 256])
    cv_d = din("cache_v", [NROW, 256])
    cik_d = din("cache_ik", [NROW, 64])
    stc_d = din("st_conv", [NSQ * 2, D])
    stf_d = din("st_ffn", [2, NSQ * 2, DFF])
    w_ai = din("w_attn_in", [D, 2640])
    w_ao = din("w_attn_out", [D, D])
    w_ci = din("w_conv_in", [D, 3 * D])
    cw_d = din("conv_w", [3, D])
    w_co = din("w_conv_out", [D, D])
    w_up = din("w_ffn_up", [2, D, 2 * DFF])
    fcw_d = din("ffn_conv_w", [2, 3, DFF])
    fcb_d = din("ffn_conv_b", [2, DFF])
    w_dn = din("w_ffn_down", [2, DFF, D])
    lng_d = din("ln_g", [4, D])
    lnb_d = din("ln_b", [4, D])

    o_y = dout("o_y", [TQ, D])
    o_ys = dout("o_ys", [TS, D])
    o_k = dout("o_k", [SEQ, 256])
    o_v = dout("o_v", [SEQ, 256])
    o_ik = dout("o_ik", [SEQ, 64])
    o_ks = dout("o_ks", [TS, 256])
    o_vs = dout("o_vs", [TS, 256])
    o_iks = dout("o_iks", [TS, 64])
    o_cp = dout("o_cp", [2, D])
    o_cs = dout("o_cs", [NSQ * 2, D])
    o_fp = dout("o_fp", [2, 2, DFF])
    o_fs = dout("o_fs", [2, NSQ * 2, DFF])

    wq_s = dscr("wq_s", [D, 2064], BF16)
    wkv_s = dscr("wkv_s", [D, 576], BF16)
    wo_s = dscr("wo_s", [D, D], BF16)
    wup_s = dscr("wup_s", [2, NCB, P, 8, 2, 256], BF16)
    wdn_s = dscr("wdn_s", [2, DFF, D], BF16)
    wci_s = dscr("wci_s", [8, P, 8, 3, 128], BF16)
    wco_s = dscr("wco_s", [D, D], BF16)
    y1_s = dscr("y1_s", [TQ + TS, D], F32)
    wsd_s = dscr("wsd_s", [TS, 16], F32)

    es = contextlib.ExitStack()
    with es:
        c = Ctx(nc, es)
        for i in range(6):
            c.banks.append((c.ps("pm%d" % i, [P, 512], F32), Buf("pm%d" % i)))
        ptb = [(c.ps("pt%d" % i, [P, 1024], BF16), Buf("pt%d" % i)) for i in range(2)]
        pt_i = [0]

        def ptbank():
            i = pt_i[0] % 2
            pt_i[0] += 1
            return ptb[i]

        WBN = 4608
        for i in range(4):
            c.wbufs.append((c.sb("wb%d" % i, [P, WBN], BF16), Buf("wb%d" % i)))
        cst = c.sb("cst", [P, 512 + 128 + NIT + 2], F32)
        Bcst = Buf("cst")
        identb = c.sb("identb", [P, P], BF16)
        Bid = Buf("identb")
        zerob = c.sb("zerob", [P, 512], BF16); Bzero = Buf("zerob")
        c.op("pool", lambda e: e.memset(zerob[:, :], 0.0), writes=[Bzero])
        c.dma("sp", cst[:], consts[:, :], Bcst, writes=[Bcst])
        c.dma("pool", identb[:], consts[:, 512:640], Bid, writes=[Bid])
        iota = cst[:, 0:512]
        identf = cst[:, 512:640]
        pow2 = cst[:, 640:640 + NIT]
        pidx = cst[:, 640 + NIT:641 + NIT]
        pm16 = cst[:, 641 + NIT:642 + NIT]

        Bwq, Bwkv, Bwo, Bwup, Bwdn, Bwci, Bwco = (Buf(n) for n in ("wq", "wkv", "wo", "wup", "wdn", "wci", "wco"))

        pending = []

        def conv_now(dst, src, B):
            c.dma("pool", dst, src, B, writes=[B], nowaw=True)

        def conv(dst, src, B):
            if B in (Bwkv, Bwq, Bwo):
                conv_now(dst, src, B)
            else:
                pending.append((dst, src, B))

        def drain(n):
            for _ in range(n):
                if pending:
                    conv_now(*pending.pop(0))

        for r0 in range(0, D, 256):
            rs = slice(r0, r0 + 256)
            conv(wkv_s[rs, 0:512], w_ai[rs, 1024:1536], Bwkv)
            conv(wkv_s[rs, 512:576], w_ai[rs, 2560:2624], Bwkv)
        for r0 in range(0, D, 256):
            rs = slice(r0, r0 + 256)
            conv(wq_s[rs, 0:1024], w_ai[rs, 0:1024], Bwq)
            conv(wq_s[rs, 1024:2048], w_ai[rs, 1536:2560], Bwq)
            conv(wq_s[rs, 2048:2064], w_ai[rs, 2624:2640], Bwq)
            conv(wo_s[rs, :], w_ao[rs, :], Bwo)
        for i in range(2):
            wv = w_up[i].rearrange("(k p) (s c n) -> c k p s n", p=P, s=2, c=NCB, n=256)
            for cb in range(NCB):
                for k in range(8):
                    conv(wup_s[i, cb, :, k, :, :], wv[cb, k], Bwup)
            for r0 in range(0, DFF, 704):
                conv(wdn_s[i, r0:r0 + 704, :], w_dn[i, r0:r0 + 704, :], Bwdn)
            if i == 0:
                wv = w_ci.rearrange("(k p) (s c n) -> c k p s n", p=P, s=3, c=8, n=128)
                for ch in range(8):
                    for k in range(8):
                        conv(wci_s[ch, :, k, :, :], wv[ch, k], Bwci)
                for r0 in range(0, D, 256):
                    conv(wco_s[r0:r0 + 256, :], w_co[r0:r0 + 256, :], Bwco)

        def wload(parts, reads):
            wb, Bw = c.wbuf()
            for (off, a, b, src) in parts:
                dst = wb[:, off:off + a * b].rearrange("p (a b) -> p a b", a=a)
                c.dma("sp", dst, src, Bw, reads=reads, writes=[Bw])
            return wb, Bw

        es1 = contextlib.ExitStack()
        es1.__enter__()
        KT2 = c.sb("KT2", [P, 2, SEQ], BF16, es1); BKT = Buf("KT2")
        kiT = c.sb("kiT", [64, SEQ], BF16, es1); BkiT = Buf("kiT")
        Vaug = c.sb("Vaug", [P, NKT, 4, 65], BF16, es1); BV = Buf("Vaug")
        Isb = c.sb("Isb", [P, SEQ], F32, es1); BI = Buf("Isb")
        msk = c.sb("msk", [P, SEQ], BF16, es1); Bmsk = Buf("msk")
        mskT = c.sb("mskT", [P, NKT, P], BF16, es1); BmT = Buf("mskT")
        QT2 = c.sb("QT2", [P, 8, P], BF16, es1); BQT = Buf("QT2")
        qiT = c.sb("qiT", [64, 16, P], BF16, es1); BqiT = Buf("qiT")
        diag = c.sb("diag", [P, 16, P], BF16, es1); Bdiag = Buf("diag")
        Rb = [(c.sb("R%d" % i, [P, 512], BF16, es1), Buf("R%d" % i)) for i in range(4)]
        Eb = [(c.sb("E%d" % i, [P, 512], BF16, es1), Buf("E%d" % i)) for i in range(2)]
        Pb = [(c.sb("Pm%d" % i, [P, 512], BF16, es1), Buf("Pm%d" % i)) for i in range(4)]
        Yf = c.sb("Yf", [P, 2640], F32, es1); BYf = Buf("Yf")
        Yb = c.sb("Yb", [P, 2640], BF16, es1); BYb = Buf("Yb")
        xt = c.sb("xt", [P, D], F32, es1); Bx = Buf("xt")
        xb = c.sb("xb", [P, D], BF16, es1); Bxb = Buf("xb")
        xT = c.sb("xT", [P, 8, P], BF16, es1); BxT = Buf("xT")
        Obf = c.sb("Obf", [P, D], BF16, es1); BO = Buf("Obf")
        rr = c.sb("rr", [P, D], F32, es1); Brr = Buf("rr")
        yy = rr; Byy = Brr
        gb0 = c.sb("gb0", [P, 2 * D], F32, es1); Bgb0 = Buf("gb0")
        rk = c.sb("rk", [P, NKT, 24], F32, es1); Brk = Buf("rk")
        rq = c.sb("rq", [P, NQT + 1, 24], F32, es1); Brq = Buf("rq")
        qpos = c.sb("qpos", [P, NQT + 1], F32, es1); Bqp = Buf("qpos")
        small = c.sb("small", [P, 256], F32, es1); Bsm = Buf("small")
        tmpr = c.sb("tmpr", [P, 33, 16], F32, es1); Btr = Buf("tmpr")
        wsc = c.sb("wsc", [P, 16], F32, es1); Bwsc = Buf("wsc")
        biasb = c.sb("biasb", [P, 512], F32, es1); Bbias = Buf("biasb")
        ikp = c.sb("ikp", [P, 16, 64], BF16, es1); Bikp = Buf("ikp")
        Kp = c.sb("Kp", [P, 16, 256], BF16, es1); BKp = Buf("Kp")
        Vp = c.sb("Vp", [P, 16, 256], BF16, es1); BVp = Buf("Vp")
        idxall = c.sb("idxall", [P, NSQ * 2], I32, es1); Bidx = Buf("idxall")
        ptall = idxall; Bptall = Bidx
        qisb = c.sb("qisb", [64, 64], BF16, es1); Bqisb = Buf("qisb")
        KnT2 = c.sb("KnT2", [P, 2, 64], BF16, es1); BKn = Buf("KnT2")
        kinT = c.sb("kinT", [64, 64], BF16, es1); Bkin = Buf("kinT")
        Wht = c.sb("Wht", [64, 16], F32, es1); BWht = Buf("Wht")
        Wsel = c.sb("Wsel", [64, NSQ, 64], BF16, es1); BWsel = Buf("Wsel")
        mselb = c.sb("mselb", [64, NSQ * 64], BF16, es1); Bmsel = Buf("mselb")
        Vst = c.sb("Vst", [4, 256], F32, es1); BVst = Buf("Vst")

        for t0 in range(0, NKT, 8):
            c.dma("sp", rk[:, t0:t0 + 8, :], ropek[t0 * P:(t0 + 8) * P, :].rearrange("(t p) c -> p t c", p=P), Brk,
                  writes=[Brk], nowaw=True)
        for t0 in range(0, NQT, 6):
            t1 = min(NQT, t0 + 6)
            c.dma("sp", rq[:, t0:t1, :], ropeq[t0 * P:t1 * P, :].rearrange("(t p) c -> p t c", p=P), Brq,
                  writes=[Brq], nowaw=True)
        c.dma("sp", rq[0:TS, NQT, :], ropes[:, :], Brq, writes=[Brq], nowaw=True)
        c.dma("sp", qpos[:], qpos_d[:, :], Bqp, writes=[Bqp])
        c.dma("sp", gb0[:, 0:D], bass.AP(tensor=lng_d.tensor, offset=0, ap=[[0, P], [1, D]]), Bgb0, writes=[Bgb0], nowaw=True)
        c.dma("sp", gb0[:, D:2 * D], bass.AP(tensor=lnb_d.tensor, offset=0, ap=[[0, P], [1, D]]), Bgb0, writes=[Bgb0], nowaw=True)
        c.op("pool", lambda e: e.memset(Vaug[:, :, :, 64:65], 1.0), writes=[BV])

        def front(src, Bsrc, R):
            c.op("act", lambda e: e.copy(out=xb[:R, :], in_=src), reads=[Bsrc], writes=[Bxb])
            pt, Bpt = ptbank()
            for k in range(8):
                c.op("pe", lambda e: e.transpose(pt[:, k * P:k * P + R], xb[:R, k * P:(k + 1) * P], identb[:R, :R]),
                     reads=[Bxb, Bid], writes=[Bpt])
            c.op("dve", lambda e: e.tensor_copy(out=xT[:, :, :R], in_=pt[:, :].rearrange("p (k t) -> p k t", k=8)[:, :, :R]),
                 reads=[Bpt], writes=[BxT])

        def rope(Y, BY, R, col0, H, tb):
            Yv = Y[:R, col0:col0 + 64 * H].rearrange("p (h d) -> p h d", d=64)
            tA = tmpr[:R, 0:H, 0:8]
            tB = tmpr[:R, 0:H, 8:16]
            sn = bc_mid(tb[:, 16:24], H)
            cs = bc_mid(tb[:, 0:16], H)
            c.op("dve", lambda e: e.tensor_tensor(out=tA, in0=Yv[:, :, 8:16], in1=sn, op=ALU.mult), reads=[BY], writes=[Btr])
            c.op("dve", lambda e: e.tensor_tensor(out=tB, in0=Yv[:, :, 0:8], in1=sn, op=ALU.mult), reads=[BY], writes=[Btr])
            c.op("dve", lambda e: e.tensor_tensor(out=Yv[:, :, 0:16], in0=Yv[:, :, 0:16], in1=cs, op=ALU.mult), reads=[BY], writes=[BY])
            c.op("dve", lambda e: e.tensor_tensor(out=Yv[:, :, 0:8], in0=Yv[:, :, 0:8], in1=tA, op=ALU.subtract), reads=[BY, Btr], writes=[BY])
            c.op("dve", lambda e: e.tensor_tensor(out=Yv[:, :, 8:16], in0=Yv[:, :, 8:16], in1=tB, op=ALU.add), reads=[BY, Btr], writes=[BY])

        def proj(R, wsrc, Bwsrc, ncols, ycol0):
            n0 = 0
            while n0 < ncols:
                nn = min(2048, ncols - n0)
                kper = max(1, min(8, WBN // nn))
                mts = [(m0, min(512, nn - m0)) for m0 in range(0, nn, 512)]
                bks = [c.bank() for _ in mts]
                for k0 in range(0, 8, kper):
                    kk = min(kper, 8 - k0)
                    src = wsrc[k0 * P:(k0 + kk) * P, n0:n0 + nn].rearrange("(k p) n -> p k n", p=P)
                    wb, Bw = wload([(0, kk, nn, src)], [Bwsrc])
                    wv = wb[:, 0:kk * nn].rearrange("p (k n) -> p k n", k=kk)
                    for (m0, mm), (pm, Bpm) in zip(mts, bks):
                        for k in range(kk):
                            c.op("pe", lambda e: e.matmul(pm[:R, 0:mm], lhsT=xT[:, k0 + k, :R], rhs=wv[:, k, m0:m0 + mm],
                                                          start=(k0 + k == 0), stop=(k0 + k == 7)),
                                 reads=[BxT, Bw], writes=[Bpm])
                for (m0, mm), (pm, Bpm) in zip(mts, bks):
                    c.op("act", lambda e: e.copy(out=Yf[:R, ycol0 + n0 + m0:ycol0 + n0 + m0 + mm], in_=pm[:R, 0:mm]),
                         reads=[Bpm], writes=[BYf])
                n0 += nn

        def layer_norm(src, Bsrc, R, gb, Bgb, dst, Bdst):
            st = small[:R, 0:12]
            mv = small[:R, 12:14]
            sd = small[:R, 14:15]
            rs = small[:R, 15:16]
            c.op("dve", lambda e: e.bn_stats(out=st[:, 0:6], in_=src[:, 0:512]), reads=[Bsrc], writes=[Bsm])
            c.op("dve", lambda e: e.bn_stats(out=st[:, 6:12], in_=src[:, 512:1024]), reads=[Bsrc], writes=[Bsm])
            c.op("dve", lambda e: e.bn_aggr(out=mv, in_=st), reads=[Bsm], writes=[Bsm])
            c.op("act", lambda e: e.activation(out=sd, in_=mv[:, 1:2], func=AF.Sqrt, bias=EPS, scale=1.0), reads=[Bsm], writes=[Bsm])
            c.op("dve", lambda e: e.reciprocal(out=rs, in_=sd), reads=[Bsm], writes=[Bsm])
            c.op("dve", lambda e: e.tensor_scalar(out=dst, in0=src, scalar1=mv[:, 0:1], scalar2=rs, op0=ALU.subtract, op1=ALU.mult),
                 reads=[Bsrc, Bsm], writes=[Bdst])
            c.op("pool", lambda e: e.tensor_tensor(out=dst, in0=dst, in1=gb[:R, 0:D], op=ALU.mult), reads=[Bdst, Bgb], writes=[Bdst])
            c.op("pool", lambda e: e.tensor_tensor(out=dst, in0=dst, in1=gb[:R, D:2 * D], op=ALU.add), reads=[Bdst, Bgb], writes=[Bdst])

        def bisect(R, N):
            lo = small[:R, 16:17]
            w0 = small[:R, 17:18]
            mid = small[:R, 18:19]
            cnt = small[:R, 19:20]
            tt = small[:R, 20:21]
            hw = small[:R, 32:32 + NIT]
            c.op("dve", lambda e: e.tensor_scalar(out=hw, in0=pow2[:R, :], scalar1=w0, scalar2=None, op0=ALU.mult), reads=[Bsm, Bcst], writes=[Bsm])
            for k in range(NIT):
                c.op("dve", lambda e: e.tensor_tensor(out=mid, in0=lo, in1=hw[:, k:k + 1], op=ALU.add), reads=[Bsm], writes=[Bsm])
                c.op("dve", lambda e: e.tensor_scalar(out=msk[:R, :N], in0=Isb[:R, :N], scalar1=mid, scalar2=None, op0=ALU.is_ge,
                                                      op1=ALU.add, accum_out=cnt), reads=[BI, Bsm], writes=[Bmsk, Bsm])
                c.op("dve", lambda e: e.tensor_scalar(out=tt, in0=cnt, scalar1=float(TOPK), scalar2=hw[:, k:k + 1], op0=ALU.is_ge, op1=ALU.mult),
                     reads=[Bsm], writes=[Bsm])
                c.op("dve", lambda e: e.tensor_tensor(out=lo, in0=lo, in1=tt, op=ALU.add), reads=[Bsm], writes=[Bsm])
            c.op("dve", lambda e: e.tensor_scalar(out=msk[:R, :N], in0=Isb[:R, :N], scalar1=lo, scalar2=None, op0=ALU.is_ge),
                 reads=[BI, Bsm], writes=[Bmsk])

        def evac_scores(pI, BpI, R, c0, nn, qp):
            ci = c0 // 512
            qrel = small[:R, 21:22]
            c.op("dve", lambda e: e.tensor_scalar(out=qrel, in0=qp, scalar1=float(-c0), scalar2=None, op0=ALU.add), reads=[Bqp, Bsm], writes=[Bsm])
            c.op("dve", lambda e: e.tensor_reduce(out=small[:R, 100 + ci:101 + ci], in_=pI[:R, 0:nn], axis=AX.X, op=ALU.max), reads=[BpI], writes=[Bsm])
            c.op("dve", lambda e: e.tensor_reduce(out=small[:R, 110 + ci:111 + ci], in_=pI[:R, 0:nn], axis=AX.X, op=ALU.min), reads=[BpI], writes=[Bsm])
            bias = biasb[:R, 0:nn]
            c.op("dve", lambda e: e.tensor_scalar(out=bias, in0=iota[:R, 0:nn], scalar1=qrel, scalar2=NEG, op0=ALU.is_gt, op1=ALU.mult),
                 reads=[Bcst, Bsm], writes=[Bbias])
            c.op("dve", lambda e: e.tensor_tensor(out=Isb[:R, c0:c0 + nn], in0=pI[:R, 0:nn], in1=bias, op=ALU.add), reads=[BpI, Bbias], writes=[BI])

        def bounds(R, nch):
            c.op("dve", lambda e: e.tensor_reduce(out=small[:R, 22:23], in_=small[:R, 100:100 + nch], axis=AX.X, op=ALU.max), reads=[Bsm], writes=[Bsm])
            c.op("dve", lambda e: e.tensor_reduce(out=small[:R, 23:24], in_=small[:R, 110:110 + nch], axis=AX.X, op=ALU.min), reads=[Bsm], writes=[Bsm])
            c.op("dve", lambda e: e.tensor_scalar(out=small[:R, 16:17], in0=small[:R, 23:24], scalar1=-1.0, scalar2=None, op0=ALU.add), reads=[Bsm], writes=[Bsm])
            c.op("dve", lambda e: e.tensor_scalar(out=small[:R, 17:18], in0=small[:R, 22:23], scalar1=small[:R, 23:24], scalar2=2.0,
                                                  op0=ALU.subtract, op1=ALU.add), reads=[Bsm], writes=[Bsm])

        def mask_transposes(R, nkt):
            for t0 in range(0, nkt, 8):
                t1 = min(nkt, t0 + 8)
                pt, Bpt = ptbank()
                for t in range(t0, t1):
                    c.op("pe", lambda e: e.transpose(pt[:, (t - t0) * P:(t - t0) * P + R], msk[:R, t * P:(t + 1) * P], identb[:R, :R]),
                         reads=[Bmsk, Bid], writes=[Bpt])
                c.op("act", lambda e: e.copy(out=mskT[:, t0:t1, :R], in_=pt[:, 0:(t1 - t0) * P].rearrange("p (a b) -> p a b", b=P)[:, :, :R]),
                     reads=[Bpt], writes=[BmT])

        def tail(R, xdram, row0):
            c.dma("sp", rr[:R, :], xdram, Brr, writes=[Brr])
            pt, Bpt = ptbank()
            for k in range(8):
                c.op("pe", lambda e: e.transpose(pt[:, k * P:k * P + R], Obf[:R, k * P:(k + 1) * P], identb[:R, :R]),
                     reads=[BO, Bid], writes=[Bpt])
            c.op("dve", lambda e: e.tensor_copy(out=xT[:, :, :R], in_=pt[:, :].rearrange("p (k t) -> p k t", k=8)[:, :, :R]),
                 reads=[Bpt], writes=[BxT])
            tiles = []
            for k0 in (0, 4):
                src = wo_s[k0 * P:(k0 + 4) * P, :].rearrange("(k p) n -> p k n", p=P)
                wb, Bw = wload([(0, 4, D, src)], [Bwo])
                tiles.append((k0, wb, Bw))
            for m0 in (0, 512):
                pm, Bpm = c.bank()
                for (k0, wb, Bw) in tiles:
                    wv = wb[:, 0:4 * D].rearrange("p (k n) -> p k n", k=4)
                    for k in range(4):
                        c.op("pe", lambda e: e.matmul(pm[:R, :], lhsT=xT[:, k0 + k, :R], rhs=wv[:, k, m0:m0 + 512],
                                                      start=(k0 + k == 0), stop=(k0 + k == 7)), reads=[BxT, Bw], writes=[Bpm])
                c.op("dve", lambda e: e.scalar_tensor_tensor(out=rr[:R, m0:m0 + 512], in0=rr[:R, m0:m0 + 512], scalar=ALPHA, in1=pm[:R, :],
                                                             op0=ALU.mult, op1=ALU.add), reads=[Brr, Bpm], writes=[Brr])
            layer_norm(rr[:R, :], Brr, R, gb0, Bgb0, yy[:R, :], Byy)
            c.dma("sp", y1_s[row0:row0 + R, :], yy[:R, :], Byy, reads=[Byy])
            if stage < 3:
                if row0 < TQ:
                    c.dma("sp", o_y[row0:row0 + R, :], yy[:R, :], Byy, reads=[Byy])
                else:
                    c.dma("sp", o_ys[:, :], yy[:R, :], Byy, reads=[Byy])


        def rest_phase():
            es2 = contextlib.ExitStack()
            es2.__enter__()
            ya = c.sb("ya", [P, 4, D], F32, es2); Bya = [Buf("ya%d" % t) for t in range(4)]
            yb_ = c.sb("ybb", [P, 4, D], F32, es2); Byb = [Buf("yb%d" % t) for t in range(4)]
            yT = c.sb("yT", [P, 8, 512], BF16, es2); ByT = Buf("yT")
            hT = c.sb("hT", [P, 22, 512], BF16, es2); BhT = Buf("hT")
            aex = [(c.sb("aex%d" % i, [P, 520], F32, es2), Buf("aex%d" % i)) for i in range(2)]
            uub = [(c.sb("uu%d" % i, [P, 512], F32, es2), Buf("uu%d" % i)) for i in range(2)]
            silb = [(c.sb("sil%d" % i, [P, 512], F32, es2), Buf("sil%d" % i)) for i in range(2)]
            xb2 = c.sb("xb2", [P, D], BF16, es2); Bxb2 = Buf("xb2")
            gbs = [(c.sb("gb%d" % i, [P, 2 * D], F32, es2), Buf("gb%d" % i)) for i in (1, 2, 3)]
            halo_f = c.sb("halo_f", [P, 2, 22, 2], F32, es2); Bhf = Buf("halo_f")
            halo_c = c.sb("halo_c", [P, 8, 2], F32, es2); Bhc = Buf("halo_c")
            prm = c.sb("prm", [P, 22, 11], F32, es2)
            Bprm = Buf("prm")
            sext = c.sb("sext", [P, 22, 32], F32, es2); Bsext = Buf("sext")
            sout = c.sb("sout", [P, 22, 32], F32, es2); Bsout = Buf("sout")
            stg = c.sb("stg", [32, DFF], F32, es2); Bstg = Buf("stg")
            small2 = c.sb("small2", [P, 32], F32, es2); Bsm2 = Buf("small2")

            for li in (1, 2, 3):
                gbt, Bg = gbs[li - 1]
                c.dma("sp", gbt[:, 0:D], bass.AP(tensor=lng_d.tensor, offset=li * D, ap=[[0, P], [1, D]]), Bg, writes=[Bg], nowaw=True)
                c.dma("sp", gbt[:, D:2 * D], bass.AP(tensor=lnb_d.tensor, offset=li * D, ap=[[0, P], [1, D]]), Bg, writes=[Bg], nowaw=True)
            c.op("dve", lambda e: e.memset(stg[:, :], 0.0), writes=[Bstg])
            c.dma("sp", stg[0:6, :], fcw_d.rearrange("i j n -> (i j) n"), Bstg, reads=[Bstg], writes=[Bstg])
            c.dma("sp", stg[6:8, :], fcb_d[:, :], Bstg, reads=[Bstg], writes=[Bstg], nowaw=True)
            c.dma("sp", stg[8:11, 0:D], cw_d[:, :], Bstg, reads=[Bstg], writes=[Bstg], nowaw=True)
            pmp, Bpmp = c.bank()
            for ch in range(22):
                c.op("pe", lambda e: e.transpose(pmp[:, ch * 11:(ch + 1) * 11], stg[0:11, ch * P:(ch + 1) * P], identf[0:11, 0:11]),
                     reads=[Bstg, Bcst], writes=[Bpmp])
            c.op("act", lambda e: e.copy(out=prm[:, :, :], in_=pmp[:, 0:242].rearrange("p (a b) -> p a b", b=11)), reads=[Bpmp], writes=[Bprm])
            c.op("dve", lambda e: e.memset(halo_f[:, :, :, :], 0.0), writes=[Bhf])
            c.op("dve", lambda e: e.memset(halo_c[:, :, :], 0.0), writes=[Bhc])

            def ln2(src, Bsrc, R, gb, Bgb):
                st = small2[:R, 0:12]; mv = small2[:R, 12:14]; sd = small2[:R, 14:15]; rs = small2[:R, 15:16]
                c.op("dve", lambda e: e.bn_stats(out=st[:, 0:6], in_=src[:, 0:512]), reads=[Bsrc], writes=[Bsm2])
                c.op("dve", lambda e: e.bn_stats(out=st[:, 6:12], in_=src[:, 512:1024]), reads=[Bsrc], writes=[Bsm2])
                c.op("dve", lambda e: e.bn_aggr(out=mv, in_=st), reads=[Bsm2], writes=[Bsm2])
                c.op("act", lambda e: e.activation(out=sd, in_=mv[:, 1:2], func=AF.Sqrt, bias=EPS, scale=1.0), reads=[Bsm2], writes=[Bsm2])
                c.op("dve", lambda e: e.reciprocal(out=rs, in_=sd), reads=[Bsm2], writes=[Bsm2])
                c.op("dve", lambda e: e.tensor_scalar(out=src, in0=src, scalar1=mv[:, 0:1], scalar2=rs, op0=ALU.subtract, op1=ALU.mult),
                     reads=[Bsrc, Bsm2], writes=[Bsrc])
                c.op("pool", lambda e: e.tensor_tensor(out=src, in0=src, in1=gb[:R, 0:D], op=ALU.mult), reads=[Bsrc, Bgb], writes=[Bsrc])
                c.op("pool", lambda e: e.tensor_tensor(out=src, in0=src, in1=gb[:R, D:2 * D], op=ALU.add), reads=[Bsrc, Bgb], writes=[Bsrc])

            def to_featT(Y, BY, nt, R):
                for t in range(nt):
                    c.op("act", lambda e: e.copy(out=xb2[:R, :], in_=Y[:R, t, :]), reads=[BY[t]], writes=[Bxb2])
                    pt, Bpt = ptbank()
                    for k in range(8):
                        c.op("pe", lambda e: e.transpose(pt[:, k * P:k * P + R], xb2[:R, k * P:(k + 1) * P], identb[:R, :R]),
                             reads=[Bxb2, Bid], writes=[Bpt])
                    c.op("dve", lambda e: e.tensor_copy(out=yT[:, :, t * P:t * P + R], in_=pt[:, :].rearrange("p (k t) -> p k t", k=8)[:, :, :R]),
                         reads=[Bpt], writes=[ByT])

            def conv3(ae, Bae, N, samp, w0, w1, w2, uu, Buu):
                if samp:
                    av = ae[:, 0:96].rearrange("p (b t) -> p b t", t=6)
                    uv = uu[:, 0:64].rearrange("p (b t) -> p b t", t=4)
                    s0, s1, s2 = av[:, :, 0:4], av[:, :, 1:5], av[:, :, 2:6]
                else:
                    uv = uu[:, 0:N]
                    s0, s1, s2 = ae[:, 0:N], ae[:, 1:N + 1], ae[:, 2:N + 2]
                c.op("dve", lambda e: e.tensor_scalar(out=uv, in0=s0, scalar1=w0, scalar2=None, op0=ALU.mult), reads=[Bae, Bprm], writes=[Buu])
                c.op("dve", lambda e: e.scalar_tensor_tensor(out=uv, in0=s1, scalar=w1, in1=uv, op0=ALU.mult, op1=ALU.add), reads=[Bae, Bprm, Buu], writes=[Buu])
                c.op("dve", lambda e: e.scalar_tensor_tensor(out=uv, in0=s2, scalar=w2, in1=uv, op0=ALU.mult, op1=ALU.add), reads=[Bae, Bprm, Buu], writes=[Buu])

            def load_state_T(src_dram, nch):
                c.dma("sp", stg[:, 0:nch * P], src_dram, Bstg, writes=[Bstg])
                for c0 in range(0, nch, 16):
                    c1 = min(nch, c0 + 16)
                    pm, Bpm = c.bank()
                    for ch in range(c0, c1):
                        c.op("pe", lambda e: e.transpose(pm[:, (ch - c0) * 32:(ch - c0 + 1) * 32], stg[:, ch * P:(ch + 1) * P], identf[0:32, 0:32]),
                             reads=[Bstg, Bcst], writes=[Bpm])
                    c.op("act", lambda e: e.copy(out=sext[:, c0:c1, :], in_=pm[:, 0:(c1 - c0) * 32].rearrange("p (a b) -> p a b", b=32)),
                         reads=[Bpm], writes=[Bsext])

            def store_state_T(src, Bsrc, nch, ncol, dst_dram):
                for c0 in range(0, nch, 4):
                    c1 = min(nch, c0 + 4)
                    pm, Bpm = c.bank()
                    for ch in range(c0, c1):
                        c.op("pe", lambda e: e.transpose(pm[0:ncol, (ch - c0) * P:(ch - c0 + 1) * P], src[:, ch, :], identf[:, :]),
                             reads=[Bsrc, Bcst], writes=[Bpm])
                    c.op("act", lambda e: e.copy(out=stg[0:ncol, c0 * P:c1 * P], in_=pm[0:ncol, 0:(c1 - c0) * P]), reads=[Bpm], writes=[Bstg])
                c.dma("sp", dst_dram, stg[0:ncol, 0:nch * P], Bstg, reads=[Bstg])

            def ffn(i, Yin, BYin, Yout, BYout, nt, R, samp, last, gb, Bgb):
                N = R if samp else nt * P
                if samp:
                    load_state_T(stf_d[i, :, :], 22)
                ri = 0
                for cb in range(NCB):
                    wb, Bw = wload([(0, 8, 512, wup_s[i, cb].rearrange("p k s n -> p k (s n)"))], [Bwup])
                    wv = wb[:, 0:4096].rearrange("p (k s n) -> p k s n", k=8, s=2)
                    for hf in range(2):
                        ch = 2 * cb + hf
                        pa, Bpa = c.bank()
                        pg, Bpg = c.bank()
                        for (pm, Bpm, s_) in ((pa, Bpa, 0), (pg, Bpg, 1)):
                            for k in range(8):
                                c.op("pe", lambda e: e.matmul(pm[:, 0:N], lhsT=wv[:, k, s_, hf * P:(hf + 1) * P], rhs=yT[:, k, 0:N],
                                                              start=(k == 0), stop=(k == 7)), reads=[Bw, ByT], writes=[Bpm])
                        ae, Bae = aex[ri % 2]; uu, Buu = uub[ri % 2]; sl, Bsl = silb[ri % 2]
                        ri += 1
                        if samp:
                            av = ae[:, 0:96].rearrange("p (b t) -> p b t", t=6)
                            c.op("dve", lambda e: e.tensor_copy(out=av[:, :, 0:2], in_=sext[:, ch, :].rearrange("p (b r) -> p b r", r=2)),
                                 reads=[Bsext], writes=[Bae])
                            c.op("act", lambda e: e.copy(out=av[:, :, 2:6], in_=pa[:, 0:64].rearrange("p (b t) -> p b t", t=4)), reads=[Bpa], writes=[Bae])
                            c.op("dve", lambda e: e.tensor_copy(out=sout[:, ch, :].rearrange("p (b r) -> p b r", r=2), in_=av[:, :, 4:6]),
                                 reads=[Bae], writes=[Bsout])
                        else:
                            c.op("dve", lambda e: e.tensor_copy(out=ae[:, 0:2], in_=halo_f[:, i, ch, :]), reads=[Bhf], writes=[Bae])
                            c.op("act", lambda e: e.copy(out=ae[:, 2:2 + N], in_=pa[:, 0:N]), reads=[Bpa], writes=[Bae])
                            c.op("dve", lambda e: e.tensor_copy(out=halo_f[:, i, ch, :], in_=ae[:, N:N + 2]), reads=[Bae], writes=[Bhf])
                        conv3(ae, Bae, N, samp, prm[:, ch, 3 * i:3 * i + 1], prm[:, ch, 3 * i + 1:3 * i + 2], prm[:, ch, 3 * i + 2:3 * i + 3], uu, Buu)
                        c.op("act", lambda e: e.activation(out=sl[:, 0:N], in_=uu[:, 0:N], func=AF.Silu, bias=prm[:, ch, 6 + i:7 + i], scale=1.0),
                             reads=[Buu, Bprm], writes=[Bsl])
                        c.op("dve", lambda e: e.tensor_tensor(out=hT[:, ch, 0:N], in0=sl[:, 0:N], in1=pg[:, 0:N], op=ALU.mult),
                             reads=[Bsl, Bpg], writes=[BhT])
                for m0 in (0, 512):
                    bks = [c.bank() for _ in range(nt)]
                    for c0 in range(0, 22, 4):
                        cc = min(4, 22 - c0)
                        src = wdn_s[i, c0 * P:(c0 + cc) * P, m0:m0 + 512].rearrange("(c p) n -> p c n", p=P)
                        wb, Bw = wload([(0, cc, 512, src)], [Bwdn])
                        wv = wb[:, 0:cc * 512].rearrange("p (c n) -> p c n", c=cc)
                        for t in range(nt):
                            pm, Bpm = bks[t]
                            for cj in range(cc):
                                c.op("pe", lambda e: e.matmul(pm[:R, :], lhsT=hT[:, c0 + cj, t * P:t * P + R], rhs=wv[:, cj, :],
                                                              start=(c0 + cj == 0), stop=(c0 + cj == 21)), reads=[BhT, Bw], writes=[Bpm])
                    for t in range(nt):
                        pm, Bpm = bks[t]
                        c.op("dve", lambda e: e.scalar_tensor_tensor(out=Yout[:R, t, m0:m0 + 512], in0=Yin[:R, t, m0:m0 + 512], scalar=ALPHA,
                                                                     in1=pm[:R, :], op0=ALU.mult, op1=ALU.add), reads=[BYin[t], Bpm], writes=[BYout[t]])
                for t in range(nt):
                    ln2(Yout[:R, t, :], BYout[t], R, gb, Bgb)
                if samp:
                    store_state_T(sout, Bsout, 22, 32, o_fs[i, :, :])
                elif last:
                    store_state_T(halo_f[:, i, :, :], Bhf, 22, 2, o_fp[i, :, :])

            def mixer(Yin, BYin, Yout, BYout, nt, R, samp, last, gb, Bgb):
                N = R if samp else nt * P
                if samp:
                    load_state_T(stc_d[:, :], 8)
                ri = 0
                for ch in range(8):
                    wb, Bw = wload([(0, 8, 384, wci_s[ch].rearrange("p k s n -> p k (s n)"))], [Bwci])
                    wv = wb[:, 0:3072].rearrange("p (k s n) -> p k s n", k=8, s=3)
                    bks = [c.bank() for _ in range(3)]
                    for s_ in range(3):
                        pm, Bpm = bks[s_]
                        for k in range(8):
                            c.op("pe", lambda e: e.matmul(pm[:, 0:N], lhsT=wv[:, k, s_, :], rhs=yT[:, k, 0:N], start=(k == 0), stop=(k == 7)),
                                 reads=[Bw, ByT], writes=[Bpm])
                    (pb_, Bpb_), (pc_, Bpc_), (pu_, Bpu_) = bks
                    ae, Bae = aex[ri % 2]; uu, Buu = uub[ri % 2]
                    ri += 1
                    if samp:
                        av = ae[:, 0:96].rearrange("p (b t) -> p b t", t=6)
                        c.op("dve", lambda e: e.tensor_copy(out=av[:, :, 0:2], in_=sext[:, ch, :].rearrange("p (b r) -> p b r", r=2)), reads=[Bsext], writes=[Bae])
                        c.op("act", lambda e: e.copy(out=av[:, :, 2:6], in_=pc_[:, 0:64].rearrange("p (b t) -> p b t", t=4)), reads=[Bpc_], writes=[Bae])
                        c.op("dve", lambda e: e.tensor_tensor(out=av[:, :, 2:6], in0=av[:, :, 2:6], in1=pu_[:, 0:64].rearrange("p (b t) -> p b t", t=4), op=ALU.mult),
                             reads=[Bae, Bpu_], writes=[Bae])
                        c.op("dve", lambda e: e.tensor_copy(out=sout[:, ch, :].rearrange("p (b r) -> p b r", r=2), in_=av[:, :, 4:6]), reads=[Bae], writes=[Bsout])
                    else:
                        c.op("dve", lambda e: e.tensor_copy(out=ae[:, 0:2], in_=halo_c[:, ch, :]), reads=[Bhc], writes=[Bae])
                        c.op("act", lambda e: e.copy(out=ae[:, 2:2 + N], in_=pc_[:, 0:N]), reads=[Bpc_], writes=[Bae])
                        c.op("dve", lambda e: e.tensor_tensor(out=ae[:, 2:2 + N], in0=ae[:, 2:2 + N], in1=pu_[:, 0:N], op=ALU.mult), reads=[Bae, Bpu_], writes=[Bae])
                        c.op("dve", lambda e: e.tensor_copy(out=halo_c[:, ch, :], in_=ae[:, N:N + 2]), reads=[Bae], writes=[Bhc])
                    conv3(ae, Bae, N, samp, prm[:, ch, 8:9], prm[:, ch, 9:10], prm[:, ch, 10:11], uu, Buu)
                    c.op("dve", lambda e: e.tensor_tensor(out=hT[:, ch, 0:N], in0=uu[:, 0:N], in1=pb_[:, 0:N], op=ALU.mult), reads=[Buu, Bpb_], writes=[BhT])
                tiles = []
                for k0 in (0, 4):
                    src = wco_s[k0 * P:(k0 + 4) * P, :].rearrange("(k p) n -> p k n", p=P)
                    wb, Bw = wload([(0, 4, D, src)], [Bwco])
                    tiles.append((k0, wb, Bw))
                for t in range(nt):
                    for m0 in (0, 512):
                        pm, Bpm = c.bank()
                        for (k0, wb, Bw) in tiles:
                            wv = wb[:, 0:4 * D].rearrange("p (k n) -> p k n", k=4)
                            for k in range(4):
                                c.op("pe", lambda e: e.matmul(pm[:R, :], lhsT=hT[:, k0 + k, t * P:t * P + R], rhs=wv[:, k, m0:m0 + 512],
                                                              start=(k0 + k == 0), stop=(k0 + k == 7)), reads=[BhT, Bw], writes=[Bpm])
                        c.op("dve", lambda e: e.scalar_tensor_tensor(out=Yout[:R, t, m0:m0 + 512], in0=Yin[:R, t, m0:m0 + 512], scalar=ALPHA,
                                                                     in1=pm[:R, :], op0=ALU.mult, op1=ALU.add), reads=[BYin[t], Bpm], writes=[BYout[t]])
                    ln2(Yout[:R, t, :], BYout[t], R, gb, Bgb)
                if samp:
                    store_state_T(sout, Bsout, 8, 32, o_cs[:, :])
                elif last:
                    store_state_T(halo_c[:, :, :], Bhc, 8, 2, o_cp[:, :])

            glist = [(t0 * P, nt, P, False, gi == len(GROUPS) - 1) for gi, (t0, nt) in enumerate(GROUPS)] + [(TQ, 1, TS, True, False)]
            for (row0, nt, R, samp, last) in glist:
                for t in range(nt):
                    c.dma("sp", ya[:R, t, :], y1_s[row0 + t * P:row0 + t * P + R, :], Bya[t], writes=[Bya[t]])
                to_featT(ya, Bya, nt, R)
                ffn(0, ya, Bya, yb_, Byb, nt, R, samp, last, gbs[0][0], gbs[0][1])
                to_featT(yb_, Byb, nt, R)
                mixer(yb_, Byb, ya, Bya, nt, R, samp, last, gbs[1][0], gbs[1][1])
                to_featT(ya, Bya, nt, R)
                ffn(1, ya, Bya, yb_, Byb, nt, R, samp, last, gbs[2][0], gbs[2][1])
                for t in range(nt):
                    dst = o_ys[:, :] if samp else o_y[row0 + t * P:row0 + (t + 1) * P, :]
                    c.dma("sp", dst, yb_[:R, t, :], Byb[t], reads=[Byb[t]])
            c.barrier()
            es2.__exit__(None, None, None)

        for kt in range(NKT):
            c.dma("sp", xt[:], xk[kt * P:(kt + 1) * P, :], Bx, writes=[Bx])
            drain(4)
            front(xt[:, :], Bx, P)
            proj(P, wkv_s, Bwkv, 576, 0)
            rope(Yf, BYf, P, 0, 4, rk[:, kt, :])
            rope(Yf, BYf, P, 512, 1, rk[:, kt, :])
            rows = slice(kt * P, (kt + 1) * P)
            c.dma("sp", o_k[rows, :], Yf[:, 0:256], BYf, reads=[BYf])
            c.dma("sp", o_v[rows, :], Yf[:, 256:512], BYf, reads=[BYf])
            c.dma("sp", o_ik[rows, :], Yf[:, 512:576], BYf, reads=[BYf])
            c.op("pool", lambda e: e.tensor_copy(out=Yb[:, 0:576], in_=Yf[:, 0:576]), reads=[BYf], writes=[BYb])
            c.op("pool", lambda e: e.tensor_copy(out=Vaug[:, kt, :, 0:64], in_=Yf[:, 256:512].rearrange("p (g d) -> p g d", d=64)),
                 reads=[BYf], writes=[BV])
            pt, Bpt = ptbank()
            for gp in range(2):
                c.op("pe", lambda e: e.transpose(pt[:, gp * P:(gp + 1) * P], Yb[:, gp * P:(gp + 1) * P], identb[:, :]), reads=[BYb, Bid], writes=[Bpt])
            c.op("pe", lambda e: e.transpose(pt[0:64, 2 * P:3 * P], Yb[:, 512:576], identb[:, :]), reads=[BYb, Bid], writes=[Bpt])
            c.op("act", lambda e: e.copy(out=KT2[:, :, kt * P:(kt + 1) * P], in_=pt[:, 0:2 * P].rearrange("p (a b) -> p a b", b=P)),
                 reads=[Bpt], writes=[BKT])
            c.op("act", lambda e: e.copy(out=kiT[:, kt * P:(kt + 1) * P], in_=pt[0:64, 2 * P:3 * P]), reads=[Bpt], writes=[BkiT])

        def q_front(R, tb):
            rope(Yf, BYf, R, 0, 32, tb)
            for gp in range(2):
                c.op("pool", lambda e: e.tensor_copy(
                    out=Yb[:R, gp * 512:(gp + 1) * 512].rearrange("p (h r d) -> p h r d", h=4, r=2),
                    in_=Yf[:R, gp * 512:(gp + 1) * 512].rearrange("p (r h d) -> p h r d", h=4, r=2)), reads=[BYf], writes=[BYb])
            c.op("pool", lambda e: e.tensor_copy(out=Yb[:R, 1024:2048], in_=Yf[:R, 1024:2048]), reads=[BYf], writes=[BYb])
            c.op("dve", lambda e: e.tensor_scalar(out=wsc[:R, :], in0=Yf[:R, 2048:2064], scalar1=IDX_SCALE, scalar2=None, op0=ALU.mult),
                 reads=[BYf], writes=[Bwsc])

        def q_transposes(R):
            pt, Bpt = ptbank()
            for gp in range(2):
                for hh in range(4):
                    idx = gp * 4 + hh
                    src = Yb[:R, idx * P:(idx + 1) * P]
                    c.op("pe", lambda e: e.transpose(pt[:, idx * P:idx * P + R], src, identb[:R, :R]), reads=[BYb, Bid], writes=[Bpt])
            c.op("act", lambda e: e.copy(out=QT2[:, :, :R], in_=pt[:, :].rearrange("p (a b) -> p a b", b=P)[:, :, :R]), reads=[Bpt], writes=[BQT])
            for h0 in (0, 8):
                pt, Bpt = ptbank()
                for h in range(8):
                    col = 1024 + (h0 + h) * 64
                    c.op("pe", lambda e: e.transpose(pt[0:64, h * P:h * P + R], Yb[:R, col:col + 64], identb[:R, :R]), reads=[BYb, Bid], writes=[Bpt])
                c.op("act", lambda e: e.copy(out=qiT[:, h0:h0 + 8, :R], in_=pt[0:64, :].rearrange("p (a b) -> p a b", b=P)[:, :, :R]),
                     reads=[Bpt], writes=[BqiT])

        import os as _os
        SUB = int(_os.environ.get("DBG_SUB", "99"))
        NQR = int(_os.environ.get("DBG_NQ", str(NQT)))
        def attention(NK, maskfn, Bmask, Od, BOd, mme=("pool", "dve")):
            obanks = [c.banks[0], c.banks[1], c.banks[2]]
            for (ob, Bob) in obanks:
                c.op("pe", lambda e: e.matmul(ob[:, :], lhsT=zerob[:, 0:P], rhs=zerob[:, :], start=True, stop=False),
                     reads=[Bzero], writes=[Bob])
            units = [(kt, g) for kt in range(NK) for g in range(4)]
            LA = 3
            fr = []

            def front_u(ei, kt, g):
                pS, BpS = c.bank((3, 4, 5))
                pb = (g % 2) * 64
                c.op("pe", lambda e: e.matmul(pS[:, :], lhsT=KT2[pb:pb + 64, g // 2, kt * P:(kt + 1) * P],
                                              rhs=QT2[pb:pb + 64, (g // 2) * 4:(g // 2) * 4 + 4, :], start=True, stop=True),
                     reads=[BKT, BQT], writes=[BpS])
                E_, BE_ = Eb[ei % 2]
                Pm_, BPm_ = Pb[ei % 4]
                c.op("act", lambda e: e.activation(out=E_[:, :], in_=pS[:, :], func=AF.Exp, scale=0.125), reads=[BpS], writes=[BE_])
                c.op(mme[ei % len(mme)], lambda e: e.tensor_tensor(out=Pm_[:, :].rearrange("p (a b) -> p a b", a=4),
                                                       in0=E_[:, :].rearrange("p (a b) -> p a b", a=4),
                                                       in1=bc_mid(maskfn(kt), 4), op=ALU.mult), reads=[BE_, Bmask], writes=[BPm_])
                return (Pm_, BPm_)

            def back_u(ei, kt, g):
                Pm_, BPm_ = fr[ei]
                for hh in range(4):
                    h = 4 * g + hh
                    ob, Bob = obanks[h // 7]
                    oc = (h % 7) * 65
                    c.op("pe", lambda e: e.matmul(ob[:, oc:oc + 65], lhsT=Pm_[:, hh * P:(hh + 1) * P], rhs=Vaug[:, kt, g, :],
                                                  start=False, stop=(kt == NK - 1 and (h % 7 == 6 or h == 15))), reads=[BPm_, BV], writes=[Bob])

            for i in range(len(units) + LA):
                if i < len(units):
                    fr.append(front_u(i, *units[i]))
                if i >= LA:
                    back_u(i - LA, *units[i - LA])
            for bi, (ob, Bob) in enumerate(obanks):
                nh = 7 if bi < 2 else 2
                ov = ob[:, 0:nh * 65].rearrange("p (h d) -> p h d", d=65)
                rec = small[:, 200 + 7 * bi:200 + 7 * bi + nh]
                c.op("dve", lambda e: e.reciprocal(out=rec, in_=ov[:, :, 64]), reads=[Bob], writes=[Bsm])
                c.op("dve", lambda e: e.tensor_tensor(out=Od[:, bi * 7 * 64:(bi * 7 + nh) * 64].rearrange("p (h d) -> p h d", d=64),
                                                      in0=ov[:, :, 0:64], in1=bc_last(rec, 64), op=ALU.mult), reads=[Bob, Bsm], writes=[BOd])

        if stage >= 1:
            for j in range(NQR):
                NK = 16 + j
                N = NK * P
                c.dma("sp", xt[:], xq[j * P:(j + 1) * P, :], Bx, writes=[Bx])
                drain(8)
                front(xt[:, :], Bx, P)
                proj(P, wq_s, Bwq, 2064, 0)
                q_front(P, rq[:, j, :])
                if SUB < 1:
                    continue
                q_transposes(P)
                c.op("dve", lambda e: e.tensor_tensor(out=diag[:, :, :], in0=bc_mid(identb[:, :], 16), in1=bc_last(wsc[:, :], P), op=ALU.mult),
                     reads=[Bid, Bwsc], writes=[Bdiag])
                if SUB < 2:
                    continue
                nch = (N + 511) // 512
                items = [(ci, h) for ci in range(nch) for h in range(16)]
                pIs = {}
                frs = []
                LAI = 3

                def idx_front(ii, ci, h):
                    c0 = ci * 512
                    nn = min(512, N - c0)
                    psc, Bpsc = c.bank((0, 1, 2, 3))
                    c.op("pe", lambda e: e.matmul(psc[:, 0:nn], lhsT=qiT[:, h, :], rhs=kiT[:, c0:c0 + nn], start=True, stop=True),
                         reads=[BqiT, BkiT], writes=[Bpsc])
                    R_, BR_ = Rb[ii % 4]
                    if h % 2 == 0:
                        c.op("act", lambda e: e.activation(out=R_[:, 0:nn], in_=psc[:, 0:nn], func=AF.Relu), reads=[Bpsc], writes=[BR_])
                    else:
                        c.op("dve", lambda e: e.tensor_scalar(out=R_[:, 0:nn], in0=psc[:, 0:nn], scalar1=0.0, scalar2=None, op0=ALU.max),
                             reads=[Bpsc], writes=[BR_])
                    return (R_, BR_)

                def idx_back(ii, ci, h):
                    c0 = ci * 512
                    nn = min(512, N - c0)
                    if h == 0:
                        pIs[ci] = c.bank((4, 5))
                    pI, BpI = pIs[ci]
                    R_, BR_ = frs[ii]
                    c.op("pe", lambda e: e.matmul(pI[:, 0:nn], lhsT=diag[:, h, :], rhs=R_[:, 0:nn], start=(h == 0), stop=(h == 15)),
                         reads=[Bdiag, BR_], writes=[BpI])
                    if h == 15:
                        evac_scores(pI, BpI, P, c0, nn, qpos[:, j:j + 1])

                for ii in range(len(items) + LAI):
                    if ii < len(items):
                        frs.append(idx_front(ii, *items[ii]))
                    if ii >= LAI:
                        idx_back(ii - LAI, *items[ii - LAI])
                if SUB < 3:
                    continue
                bounds(P, nch)
                bisect(P, N)
                if SUB < 4:
                    continue
                mask_transposes(P, NK)
                if SUB < 5:
                    continue
                attention(NK, lambda kt: mskT[:, kt, :], BmT, Obf, BO)
                if SUB < 6:
                    continue
                tail(P, xq[j * P:(j + 1) * P, :], j * P)
                if _os.environ.get("DBG_DUMP") and j == 0:
                    c.dma("sp", o_y[128:256, :], Isb[:, 0:1024], BI, reads=[BI])
                    c.dma("sp", o_y[256:384, 0:256], small[:, :], Bsm, reads=[Bsm])
                    c.dma("sp", o_y[384:512, :], rr[:, :], Brr, reads=[Brr])


        def sample_phase():
            R = TS
            IOA = bass.IndirectOffsetOnAxis
            c.dma("sp", xt[:R, :], xs[:, :], Bx, writes=[Bx])
            c.dma("pool", mselb[:, :], msel_d[:, :], Bmsel, writes=[Bmsel])
            for jj in range(8):
                c.dma("sp", idxall[16 * jj:16 * jj + 16, 0:2 * NSQ], bass.AP(tensor=pt_d.tensor, offset=jj, ap=[[0, 16], [8, 2 * NSQ]]), Bidx,
                      writes=[Bidx], nowaw=(jj > 0), slow=True)
            c.op("dve", lambda e: e.tensor_scalar(out=idxall[:, 0:2 * NSQ], in0=idxall[:, 0:2 * NSQ], scalar1=16.0, scalar2=pm16, op0=ALU.mult, op1=ALU.add),
                 reads=[Bidx, Bcst], writes=[Bidx])
            cik8 = cik_d.rearrange("(r k) d -> r (k d)", k=8)
            ck8 = ck_d.rearrange("(r k) d -> r (k d)", k=8)
            cv8 = cv_d.rearrange("(r k) d -> r (k d)", k=8)
            front(xt[:R, :], Bx, R)
            proj(R, wq_s, Bwq, 2064, 0)
            proj(R, wkv_s, Bwkv, 576, 2064)
            tb = rq[0:R, NQT, :]
            q_front(R, tb)
            rope(Yf, BYf, R, 2064, 4, tb)
            rope(Yf, BYf, R, 2064 + 512, 1, tb)
            c.dma("sp", o_ks[:, :], Yf[:R, 2064:2320], BYf, reads=[BYf])
            c.dma("sp", o_vs[:, :], Yf[:R, 2320:2576], BYf, reads=[BYf])
            c.dma("sp", o_iks[:, :], Yf[:R, 2576:2640], BYf, reads=[BYf])
            c.op("pool", lambda e: e.tensor_copy(out=Yb[:R, 2064:2640], in_=Yf[:R, 2064:2640]), reads=[BYf], writes=[BYb])
            q_transposes(R)
            pt, Bpt = ptbank()
            for gp in range(2):
                c.op("pe", lambda e: e.transpose(pt[:, gp * P:gp * P + R], Yb[:R, 2064 + gp * P:2064 + (gp + 1) * P], identb[:R, :R]),
                     reads=[BYb, Bid], writes=[Bpt])
            c.op("pe", lambda e: e.transpose(pt[0:64, 2 * P:2 * P + R], Yb[:R, 2576:2640], identb[:R, :R]), reads=[BYb, Bid], writes=[Bpt])
            c.op("act", lambda e: e.copy(out=KnT2[:, :, :], in_=pt[:, 0:2 * P].rearrange("p (a b) -> p a b", b=P)[:, :, 0:R]), reads=[Bpt], writes=[BKn])
            c.op("act", lambda e: e.copy(out=kinT[:, :], in_=pt[0:64, 2 * P:2 * P + R]), reads=[Bpt], writes=[Bkin])
            Bwsd = Buf("wsd")
            c.dma("sp", wsd_s[:, :], wsc[:R, :], Bwsd, reads=[Bwsc], writes=[Bwsd])
            for h in range(16):
                src = bass.AP(tensor=wsd_s.tensor, offset=h, ap=[[16, 4], [64, NSQ]])
                c.dma("sp", Wht[4 * h:4 * h + 4, :], src, BWht, reads=[Bwsd], writes=[BWht], nowaw=(h > 0), slow=True)
            c.op("dve", lambda e: e.tensor_tensor(out=Wsel[:, :, :], in0=mselb[:, :].rearrange("p (a b) -> p a b", b=64), in1=bc_last(Wht[:, :], 64), op=ALU.mult),
                 reads=[Bmsel, BWht], writes=[BWsel])
            c.op("dve", lambda e: e.memset(kiT[:, PAST:PAST + P], 0.0), writes=[BkiT])
            c.op("dve", lambda e: e.memset(KT2[:, :, PAST:PAST + P], 0.0), writes=[BKT])
            c.op("pool", lambda e: e.memset(Vaug[:, 16, :, 0:64], 0.0), writes=[BV])
            nch = 5
            SS = float(_os.environ.get("DBG_SS", "99"))
            if SS < 1:
                return
            for b in range(NSQ):
                for hf in range(2):
                    col = b * 2 + hf
                    c.dma("pool", ikp[:, hf * 8:(hf + 1) * 8, :].rearrange("p a d -> p (a d)"), cik8, Bikp, reads=[Bidx], writes=[Bikp], nowaw=(hf > 0),
                          indirect=IOA(ap=idxall[:, col:col + 1], axis=0))
                for j0 in (0, 8):
                    pt, Bpt = ptbank()
                    for j in range(8):
                        c.op("pe", lambda e: e.transpose(pt[0:64, j * P:(j + 1) * P], ikp[:, j0 + j, :], identb[:, :]), reads=[Bikp, Bid], writes=[Bpt])
                    c.op("act", lambda e: e.copy(out=kiT[:, j0 * P:(j0 + 8) * P], in_=pt[0:64, :]), reads=[Bpt], writes=[BkiT])
                c.op("dve", lambda e: e.tensor_copy(out=kiT[:, PAST:PAST + 4], in_=kinT[:, 4 * b:4 * b + 4]), reads=[Bkin], writes=[BkiT])
                c.op("dve", lambda e: e.tensor_copy(out=qisb[:, :].rearrange("p (h t) -> p h t", t=4), in_=qiT[:, :, 4 * b:4 * b + 4]), reads=[BqiT], writes=[Bqisb])
                for ci in range(nch):
                    c0 = ci * 512
                    nn = min(512, NKS * P - c0)
                    psc, Bpsc = c.banks[5]
                    pI, BpI = c.banks[ci]
                    c.op("pe", lambda e: e.matmul(psc[0:64, 0:nn], lhsT=qisb[:, :], rhs=kiT[:, c0:c0 + nn], start=True, stop=True),
                         reads=[Bqisb, BkiT], writes=[Bpsc])
                    R_, BR_ = Rb[(b * nch + ci) % 4]
                    c.op("act", lambda e: e.activation(out=R_[0:64, 0:nn], in_=psc[0:64, 0:nn], func=AF.Relu), reads=[Bpsc], writes=[BR_])
                    c.op("pe", lambda e: e.matmul(pI[0:64, 0:nn], lhsT=Wsel[:, b, :], rhs=R_[0:64, 0:nn], start=(b == 0), stop=(b == NSQ - 1)),
                         reads=[BWsel, BR_], writes=[BpI])
            if SS < 2:
                return
            for ci in range(nch):
                c0 = ci * 512
                nn = min(512, NKS * P - c0)
                pI, BpI = c.banks[ci]
                evac_scores(pI, BpI, R, c0, nn, qpos[0:R, NQT:NQT + 1])
            bounds(R, nch)
            bisect(R, NKS * P)
            mask_transposes(R, NKS)
            if SS < 2.5:
                return
            mskS = msk[:, 0:NKS * P].rearrange("p (k t) -> p k t", t=P)
            c.op("pool", lambda e: e.memset(msk[:, 0:NKS * P], 0.0), writes=[Bmsk])
            for b in range(NSQ):
                for hf in range(2):
                    col = b * 2 + hf
                    c.dma("pool", Kp[:, hf * 8:(hf + 1) * 8, :].rearrange("p a d -> p (a d)"), ck8, BKp, reads=[Bidx], writes=[BKp], nowaw=(hf > 0),
                          indirect=IOA(ap=idxall[:, col:col + 1], axis=0))
                for hf in range(2):
                    col = b * 2 + hf
                    c.dma("pool", Vp[:, hf * 8:(hf + 1) * 8, :].rearrange("p a d -> p (a d)"), cv8, BVp, reads=[Bidx], writes=[BVp], nowaw=(hf > 0),
                          indirect=IOA(ap=idxall[:, col:col + 1], axis=0))
                c.op("pool", lambda e: e.tensor_copy(out=Vaug[:, 0:16, :, 0:64], in_=Vp[:, :, :].rearrange("p j (g d) -> p j g d", d=64)),
                     reads=[BVp], writes=[BV])
                c.dma("sp", Vst[:, :], Yf[4 * b:4 * b + 4, 2320:2576], BVst, reads=[BYf], writes=[BVst])
                c.op("pool", lambda e: e.tensor_copy(out=Vaug[0:4, 16, :, 0:64], in_=Vst[:, :].rearrange("p (g d) -> p g d", d=64)), reads=[BVst], writes=[BV])
                for j0 in range(0, 16, 4):
                    pt, Bpt = ptbank()
                    for gp in range(2):
                        for j in range(4):
                            c.op("pe", lambda e: e.transpose(pt[:, (gp * 4 + j) * P:(gp * 4 + j + 1) * P], Kp[:, j0 + j, gp * P:(gp + 1) * P], identb[:, :]),
                                 reads=[BKp, Bid], writes=[Bpt])
                    c.op("act", lambda e: e.copy(out=KT2[:, :, j0 * P:(j0 + 4) * P], in_=pt[:, :].rearrange("p (a b) -> p a b", a=2)), reads=[Bpt], writes=[BKT])
                c.op("dve", lambda e: e.tensor_copy(out=KT2[:, :, PAST:PAST + 4], in_=KnT2[:, :, 4 * b:4 * b + 4]), reads=[BKn], writes=[BKT])
                if SS < 2.7:
                    continue
                if b > 0:
                    c.op("pool", lambda e: e.memset(mskS[:, :, 4 * (b - 1):4 * b], 0.0), writes=[Bmsk])
                c.op("pool", lambda e: e.tensor_copy(out=mskS[:, :, 4 * b:4 * b + 4], in_=mskT[:, 0:NKS, 4 * b:4 * b + 4]), reads=[BmT], writes=[Bmsk])
                attention(NKS, lambda kt: mskS[:, kt, :], Bmsk, xb, Bxb, mme=("dve", "dve", "pool"))
                c.dma("sp", Obf[4 * b:4 * b + 4, :], xb[4 * b:4 * b + 4, :], BO, reads=[Bxb], writes=[BO], nowaw=True)
            if SS < 4:
                return
            tail(R, xs[:, :], TQ)

        if stage >= 2:
            sample_phase()
        drain(10000)
        c.barrier()
        es1.__exit__(None, None, None)
        if stage >= 3:
            rest_phase()
        c.finish()
    return nc


def _rope_table(pos):
    half = 8
    inv = (500000.0 ** (-np.arange(half, dtype=np.float32) * np.float32(2.0 / 16))).astype(np.float32)
    ang = pos.astype(np.float32)[:, None] * inv[None, :]
    cs = np.cos(ang).astype(np.float32)
    sn = np.sin(ang).astype(np.float32)
    return np.concatenate([cs, cs, sn], axis=1).astype(np.float32)


_NC_CACHE = {}


def _run(inputs, nphys=None, stage=99, compact=False):
    f = lambda a: np.ascontiguousarray(np.asarray(a))
    x_prompt = f(inputs["x_prompt"]); x_sample = f(inputs["x_sample"])
    cache_k = f(inputs["cache_k"])[0]; cache_v = f(inputs["cache_v"])[0]; cache_ik = f(inputs["cache_idx_k"])[0]
    page_table = f(inputs["page_table"]).astype(np.int32)
    full_nphys = cache_k.shape[0]
    if nphys is None:
        nphys = full_nphys
    key = (nphys, stage)
    if key not in _NC_CACHE:
        _NC_CACHE[key] = build(nphys, stage)
    nc = _NC_CACHE[key]
    consts = np.zeros((P, 512 + 128 + NIT + 2), np.float32)
    consts[:, 0:512] = np.arange(512, dtype=np.float32)[None, :]
    consts[:, 512:640] = np.eye(P, dtype=np.float32)
    consts[:, 640:640 + NIT] = (0.5 ** np.arange(1, NIT + 1, dtype=np.float64)).astype(np.float32)[None, :]
    consts[:, 640 + NIT] = np.arange(P, dtype=np.float32)
    consts[:, 641 + NIT] = (np.arange(P) % 16).astype(np.float32)
    msel = np.zeros((64, NSQ, 64), np.float32)
    for h in range(16):
        for t in range(4):
            for b in range(NSQ):
                msel[h * 4 + t, b, b * 4 + t] = 1.0
    ropek = _rope_table(np.arange(SEQ))
    ropes = _rope_table(PAST + (np.arange(TS) % 4))
    in_maps = []
    for core in range(8):
        b, h = core // 2, core % 2
        pos0 = 0 if h == 0 else POS0
        qp = np.zeros((P, NQT + 1), np.float32)
        qp[:, :NQT] = pos0 + np.arange(P)[:, None] + P * np.arange(NQT)[None, :]
        qp[:TS, NQT] = PAST + (np.arange(TS) % 4)
        sl = slice(core * NSQ, (core + 1) * NSQ)
        pt = page_table[sl]
        if compact:
            pages = np.unique(pt)
            remap = {int(p): i for i, p in enumerate(pages)}
            ck = np.zeros((nphys, P, 256), np.float32); cv = np.zeros((nphys, P, 256), np.float32); ci = np.zeros((nphys, P, 64), np.float32)
            ck[:len(pages)] = cache_k[pages].reshape(-1, P, 256); cv[:len(pages)] = cache_v[pages].reshape(-1, P, 256)
            ci[:len(pages)] = cache_ik[pages]
            pt = np.vectorize(remap.get)(pt).astype(np.int32)
        else:
            ck = cache_k.reshape(-1, P, 256); cv = cache_v.reshape(-1, P, 256); ci = cache_ik
        m = {
            "xk": x_prompt[b], "xq": x_prompt[b, pos0:pos0 + TQ], "xs": x_sample[sl].reshape(TS, D),
            "ropek": ropek, "ropeq": ropek[pos0:pos0 + TQ], "ropes": ropes, "qpos": qp, "consts": consts,
            "msel": msel.reshape(64, NSQ * 64), "pt": pt,
            "cache_k": ck.reshape(nphys * P, 256), "cache_v": cv.reshape(nphys * P, 256), "cache_ik": ci.reshape(nphys * P, 64),
            "st_conv": f(inputs["state_conv"])[0, sl].reshape(NSQ * 2, D),
            "st_ffn": f(inputs["state_ffn"])[:, sl].reshape(2, NSQ * 2, DFF),
            "w_attn_in": f(inputs["w_attn_in"])[0], "w_attn_out": f(inputs["w_attn_out"])[0],
            "w_conv_in": f(inputs["w_conv_in"])[0], "conv_w": f(inputs["conv_w"])[0], "w_conv_out": f(inputs["w_conv_out"])[0],
            "w_ffn_up": f(inputs["w_ffn_up"]), "ffn_conv_w": f(inputs["ffn_conv_w"]), "ffn_conv_b": f(inputs["ffn_conv_b"]),
            "w_ffn_down": f(inputs["w_ffn_down"]), "ln_g": f(inputs["ln_g"]).reshape(4, D), "ln_b": f(inputs["ln_b"]).reshape(4, D),
        }
        in_maps.append({k: np.ascontiguousarray(v) for k, v in m.items()})
    res = run_bass_kernel_spmd(nc, in_maps, core_ids=list(range(8)))
    R = res.results
    B = 4
    y_prompt = np.zeros((B, SEQ, D), np.float32)
    nk = np.zeros((1, B, SEQ, 4, 64), np.float32); nv = np.zeros_like(nk); nik = np.zeros((1, B, SEQ, 64), np.float32)
    cp = np.zeros((1, B, 2, D), np.float32); fp = np.zeros((2, B, 2, DFF), np.float32)
    y_sample = np.zeros((128, 4, D), np.float32)
    nks = np.zeros((1, 128, 4, 4, 64), np.float32); nvs = np.zeros_like(nks); niks = np.zeros((1, 128, 4, 64), np.float32)
    cs = np.zeros((1, 128, 2, D), np.float32); fs = np.zeros((2, 128, 2, DFF), np.float32)
    for core in range(8):
        b, h = core // 2, core % 2
        r = R[core]
        sl = slice(core * NSQ, (core + 1) * NSQ)
        if h == 0:
            y_prompt[b, 0:2048] = r["o_y"][0:2048]
            nk[0, b] = r["o_k"].reshape(SEQ, 4, 64); nv[0, b] = r["o_v"].reshape(SEQ, 4, 64); nik[0, b] = r["o_ik"]
        else:
            y_prompt[b, 2048:4096] = r["o_y"][128:TQ]
            cp[0, b] = r["o_cp"]; fp[:, b] = r["o_fp"]
        y_sample[sl] = r["o_ys"].reshape(NSQ, 4, D)
        nks[0, sl] = r["o_ks"].reshape(NSQ, 4, 4, 64); nvs[0, sl] = r["o_vs"].reshape(NSQ, 4, 4, 64); niks[0, sl] = r["o_iks"].reshape(NSQ, 4, 64)
        cs[0, sl] = r["o_cs"].reshape(NSQ, 2, D); fs[:, sl] = r["o_fs"].reshape(2, NSQ, 2, DFF)
    return (y_prompt, y_sample, nk, nv, nik, nks, nvs, niks, cp, cs, fp, fs), R


def kernel(**inputs):
    outs, _ = _run(inputs)
    return outs
```

```python
import contextlib
import numpy as np
import concourse.bass as bass
import concourse.mybir as mybir
from concourse.bass_utils import run_bass_kernel_spmd

F32 = mybir.dt.float32
BF16 = mybir.dt.bfloat16
I32 = mybir.dt.int32
AF = mybir.ActivationFunctionType
ALU = mybir.AluOpType
AX = mybir.AxisListType

P = 128
D = 1024
DFF = 2816
NCB = 11
SEQ = 4096
NKT = 32
NQT = 17
TQ = NQT * P
POS0 = 1920
NSQ = 16
TS = 64
PAST = 2048
NKS = 17
LS = PAST + 4
TOPK = 256
NIT = 14
ALPHA = 4.0 ** 0.25
IDX_SCALE = 1.0 / 32.0
EPS = 1e-5
NEG = -1.0e30
GROUPS = [(0, 1), (1, 4), (5, 4), (9, 4), (13, 4)]


class Buf:
    __slots__ = ("name", "w", "r", "dsem", "dtot")

    def __init__(self, name):
        self.name = name
        self.w = None
        self.r = {}
        self.dsem = None
        self.dtot = 0


def bc_mid(ap, n):
    a = [list(x) for x in ap.ap]
    return bass.AP(tensor=ap.tensor, offset=ap.offset, ap=[a[0], [0, n]] + a[1:])


def bc_last(ap, n):
    a = [list(x) for x in ap.ap]
    return bass.AP(tensor=ap.tensor, offset=ap.offset, ap=a + [[0, n]])


class Ctx:
    CH = 16000

    def __init__(self, nc, es):
        self.nc = nc
        self.es = es
        self.engs = {"pe": nc.tensor, "dve": nc.vector, "act": nc.scalar, "pool": nc.gpsimd, "sp": nc.sync}
        self.cnt = {e: 0 for e in self.engs}
        self.sems = {e: [] for e in self.engs}
        self.waited = {e: {} for e in self.engs}
        self.dma_bufs = []
        self.nsem = 0
        self.banks = []
        self.bank_i = 0
        self.wbufs = []
        self.wb_i = 0

    def new_sem(self, name):
        self.nsem += 1
        return self.es.enter_context(self.nc.semaphore(name))

    def sb(self, name, shape, dt, es=None):
        return (es or self.es).enter_context(self.nc.sbuf_tensor("sb_" + name, list(shape), dt))

    def ps(self, name, shape, dt):
        return self.es.enter_context(self.nc.psum_tensor("ps_" + name, list(shape), dt))

    def _esem(self, e, tick):
        ch = (tick - 1) // self.CH
        while len(self.sems[e]) <= ch:
            self.sems[e].append(self.new_sem("s_%s_%d" % (e, len(self.sems[e]))))
        return self.sems[e][ch], (tick - 1) % self.CH + 1

    def _wait(self, e, tok):
        if tok[0] == "eng":
            _, f, tick = tok
            key = f
            if self.waited[e].get(key, 0) >= tick:
                return
            sem, val = self._esem(f, tick)
        else:
            _, buf, val = tok
            key = ("d", id(buf))
            tick = val
            if self.waited[e].get(key, 0) >= tick:
                return
            sem = buf.dsem
        self.engs[e].wait_ge(sem, val)
        self.waited[e][key] = tick

    def _deps(self, e, reads, writes, nowaw=False):
        for b in reads:
            t = b.w
            if t is not None:
                if t[0] == "eng" and t[1] == e and e in ("pe", "sp"):
                    continue
                self._wait(e, t)
        if nowaw:
            return
        for b in writes:
            t = b.w
            if t is not None:
                if not (t[0] == "eng" and t[1] == e and e in ("pe", "sp", "dve", "act")):
                    self._wait(e, t)
            for t in b.r.values():
                if t[0] == "eng" and t[1] == e and e != "pool":
                    continue
                self._wait(e, t)

    def _commit(self, tok, reads, writes):
        key = tok[1] if tok[0] == "eng" else ("d", id(tok[1]))
        for b in reads:
            b.r[key] = tok
        for b in writes:
            b.w = tok
            b.r = {}

    def op(self, e, fn, reads=(), writes=()):
        self._deps(e, reads, writes)
        ins = fn(self.engs[e])
        self.cnt[e] += 1
        tick = self.cnt[e]
        sem, _ = self._esem(e, tick)
        ins.then_inc(sem, 1)
        self._commit(("eng", e, tick), reads, writes)
        return ins

    def dma(self, q, out, in_, sbuf, reads=(), writes=(), nowaw=False, indirect=None, slow=False):
        self._deps(q, reads, writes, nowaw)
        if sbuf.dsem is None:
            sbuf.dsem = self.new_sem("d_" + sbuf.name)
            self.dma_bufs.append(sbuf)
        sbuf.dtot += 16
        if indirect is not None:
            ins = self.nc.gpsimd.indirect_dma_start(out=out, out_offset=None, in_=in_, in_offset=indirect)
        else:
            ins = self.engs[q].dma_start(out=out, in_=in_, allow_slow_non_contiguous=True) if slow else self.engs[q].dma_start(out=out, in_=in_)
        ins.then_inc(sbuf.dsem, 16)
        self._commit(("dma", sbuf, sbuf.dtot), reads, writes)

    def barrier(self):
        for e in self.engs:
            for f in self.engs:
                if f != e and self.cnt[f] > 0:
                    self._wait(e, ("eng", f, self.cnt[f]))
            for b in self.dma_bufs:
                if b.dtot > 0:
                    self._wait(e, ("dma", b, b.dtot))

    def finish(self):
        for b in self.dma_bufs:
            self.engs["sp"].wait_ge(b.dsem, b.dtot)

    def bank(self, subset=None):
        if subset is None:
            subset = range(len(self.banks))
        self.bank_i += 1
        return self.banks[subset[self.bank_i % len(subset)]]

    def wbuf(self):
        i = self.wb_i % len(self.wbufs)
        self.wb_i += 1
        return self.wbufs[i]


def build(nphys, stage=99):
    nc = bass.Bass("TRN2", target_bir_lowering=False)

    def din(name, shape, dt=F32):
        return nc.dram_tensor(name, list(shape), dt, kind="ExternalInput").ap()

    def dout(name, shape, dt=F32):
        return nc.dram_tensor(name, list(shape), dt, kind="ExternalOutput").ap()

    def dscr(name, shape, dt):
        return nc.dram_tensor(name, list(shape), dt, kind="Internal").ap()

    NROW = nphys * P
    xk = din("xk", [SEQ, D])
    xq = din("xq", [TQ, D])
    xs = din("xs", [TS, D])
    ropek = din("ropek", [SEQ, 24])
    ropeq = din("ropeq", [TQ, 24])
    ropes = din("ropes", [TS, 24])
    qpos_d = din("qpos", [P, NQT + 1])
    consts = din("consts", [P, 512 + 128 + NIT + 2])
    msel_d = din("msel", [64, NSQ * 64])
    pt_d = din("pt", [NSQ, 16], I32)
    ck_d = din("cache_k", [NROW, 256])
    cv_d = din("cache_v", [NROW, 256])
    cik_d = din("cache_ik", [NROW, 64])
    stc_d = din("st_conv", [NSQ * 2, D])
    stf_d = din("st_ffn", [2, NSQ * 2, DFF])
    w_ai = din("w_attn_in", [D, 2640])
    w_ao = din("w_attn_out", [D, D])
    w_ci = din("w_conv_in", [D, 3 * D])
    cw_d = din("conv_w", [3, D])
    w_co = din("w_conv_out", [D, D])
    w_up = din("w_ffn_up", [2, D, 2 * DFF])
    fcw_d = din("ffn_conv_w", [2, 3, DFF])
    fcb_d = din("ffn_conv_b", [2, DFF])
    w_dn = din("w_ffn_down", [2, DFF, D])
    lng_d = din("ln_g", [4, D])
    lnb_d = din("ln_b", [4, D])

    o_y = dout("o_y", [TQ, D])
    o_ys = dout("o_ys", [TS, D])
    o_k = dout("o_k", [SEQ, 256])
    o_v = dout("o_v", [SEQ, 256])
    o_ik = dout("o_ik", [SEQ, 64])
    o_ks = dout("o_ks", [TS, 256])
    o_vs = dout("o_vs", [TS, 256])
    o_iks = dout("o_iks", [TS, 64])
    o_cp = dout("o_cp", [2, D])
    o_cs = dout("o_cs", [NSQ * 2, D])
    o_fp = dout("o_fp", [2, 2, DFF])
    o_fs = dout("o_fs", [2, NSQ * 2, DFF])

    wq_s = dscr("wq_s", [D, 2064], BF16)
    wkv_s = dscr("wkv_s", [D, 576], BF16)
    wo_s = dscr("wo_s", [D, D], BF16)
    wup_s = dscr("wup_s", [2, NCB, P, 8, 2, 256], BF16)
    wdn_s = dscr("wdn_s", [2, DFF, D], BF16)
    wci_s = dscr("wci_s", [8, P, 8, 3, 128], BF16)
    wco_s = dscr("wco_s", [D, D], BF16)
    y1_s = dscr("y1_s", [TQ + TS, D], F32)
    wsd_s = dscr("wsd_s", [TS, 16], F32)

    es = contextlib.ExitStack()
    with es:
        c = Ctx(nc, es)
        for i in range(6):
            c.banks.append((c.ps("pm%d" % i, [P, 512], F32), Buf("pm%d" % i)))
        ptb = [(c.ps("pt%d" % i, [P, 1024], BF16), Buf("pt%d" % i)) for i in range(2)]
        pt_i = [0]

        def ptbank():
            i = pt_i[0] % 2
            pt_i[0] += 1
            return ptb[i]

        WBN = 4608
        for i in range(4):
            c.wbufs.append((c.sb("wb%d" % i, [P, WBN], BF16), Buf("wb%d" % i)))
        cst = c.sb("cst", [P, 512 + 128 + NIT + 2], F32)
        Bcst = Buf("cst")
        identb = c.sb("identb", [P, P], BF16)
        Bid = Buf("identb")
        zerob = c.sb("zerob", [P, 512], BF16); Bzero = Buf("zerob")
        c.op("pool", lambda e: e.memset(zerob[:, :], 0.0), writes=[Bzero])
        c.dma("sp", cst[:], consts[:, :], Bcst, writes=[Bcst])
        c.dma("pool", identb[:], consts[:, 512:640], Bid, writes=[Bid])
        iota = cst[:, 0:512]
        identf = cst[:, 512:640]
        pow2 = cst[:, 640:640 + NIT]
        pidx = cst[:, 640 + NIT:641 + NIT]
        pm16 = cst[:, 641 + NIT:642 + NIT]

        Bwq, Bwkv, Bwo, Bwup, Bwdn, Bwci, Bwco = (Buf(n) for n in ("wq", "wkv", "wo", "wup", "wdn", "wci", "wco"))

        pending = []

        def conv_now(dst, src, B):
            c.dma("pool", dst, src, B, writes=[B], nowaw=True)

        def conv(dst, src, B):
            if B in (Bwkv, Bwq, Bwo):
                conv_now(dst, src, B)
            else:
                pending.append((dst, src, B))

        def drain(n):
            for _ in range(n):
                if pending:
                    conv_now(*pending.pop(0))

        for r0 in range(0, D, 256):
            rs = slice(r0, r0 + 256)
            conv(wkv_s[rs, 0:512], w_ai[rs, 1024:1536], Bwkv)
            conv(wkv_s[rs, 512:576], w_ai[rs, 2560:2624], Bwkv)
        for r0 in range(0, D, 256):
            rs = slice(r0, r0 + 256)
            conv(wq_s[rs, 0:1024], w_ai[rs, 0:1024], Bwq)
            conv(wq_s[rs, 1024:2048], w_ai[rs, 1536:2560], Bwq)
            conv(wq_s[rs, 2048:2064], w_ai[rs, 2624:2640], Bwq)
            conv(wo_s[rs, :], w_ao[rs, :], Bwo)
        for i in range(2):
            wv = w_up[i].rearrange("(k p) (s c n) -> c k p s n", p=P, s=2, c=NCB, n=256)
            for cb in range(NCB):
                for k in range(8):
                    conv(wup_s[i, cb, :, k, :, :], wv[cb, k], Bwup)
            for r0 in range(0, DFF, 704):
                conv(wdn_s[i, r0:r0 + 704, :], w_dn[i, r0:r0 + 704, :], Bwdn)
            if i == 0:
                wv = w_ci.rearrange("(k p) (s c n) -> c k p s n", p=P, s=3, c=8, n=128)
                for ch in range(8):
                    for k in range(8):
                        conv(wci_s[ch, :, k, :, :], wv[ch, k], Bwci)
                for r0 in range(0, D, 256):
                    conv(wco_s[r0:r0 + 256, :], w_co[r0:r0 + 256, :], Bwco)

        def wload(parts, reads):
            wb, Bw = c.wbuf()
            for (off, a, b, src) in parts:
                dst = wb[:, off:off + a * b].rearrange("p (a b) -> p a b", a=a)
                c.dma("sp", dst, src, Bw, reads=reads, writes=[Bw])
            return wb, Bw

        es1 = contextlib.ExitStack()
        es1.__enter__()
        KT2 = c.sb("KT2", [P, 2, SEQ], BF16, es1); BKT = Buf("KT2")
        kiT = c.sb("kiT", [64, SEQ], BF16, es1); BkiT = Buf("kiT")
        Vaug = c.sb("Vaug", [P, NKT, 4, 65], BF16, es1); BV = Buf("Vaug")
        Isb = c.sb("Isb", [P, SEQ], F32, es1); BI = Buf("Isb")
        msk = c.sb("msk", [P, SEQ], BF16, es1); Bmsk = Buf("msk")
        mskT = c.sb("mskT", [P, NKT, P], BF16, es1); BmT = Buf("mskT")
        QT2 = c.sb("QT2", [P, 8, P], BF16, es1); BQT = Buf("QT2")
        qiT = c.sb("qiT", [64, 16, P], BF16, es1); BqiT = Buf("qiT")
        diag = c.sb("diag", [P, 16, P], BF16, es1); Bdiag = Buf("diag")
        Rb = [(c.sb("R%d" % i, [P, 512], BF16, es1), Buf("R%d" % i)) for i in range(4)]
        Eb = [(c.sb("E%d" % i, [P, 512], BF16, es1), Buf("E%d" % i)) for i in range(2)]
        Pb = [(c.sb("Pm%d" % i, [P, 512], BF16, es1), Buf("Pm%d" % i)) for i in range(4)]
        Yf = c.sb("Yf", [P, 2640], F32, es1); BYf = Buf("Yf")
        Yb = c.sb("Yb", [P, 2640], BF16, es1); BYb = Buf("Yb")
        xt = c.sb("xt", [P, D], F32, es1); Bx = Buf("xt")
        xb = c.sb("xb", [P, D], BF16, es1); Bxb = Buf("xb")
        xT = c.sb("xT", [P, 8, P], BF16, es1); BxT = Buf("xT")
        Obf = c.sb("Obf", [P, D], BF16, es1); BO = Buf("Obf")
        rr = c.sb("rr", [P, D], F32, es1); Brr = Buf("rr")
        yy = rr; Byy = Brr
        gb0 = c.sb("gb0", [P, 2 * D], F32, es1); Bgb0 = Buf("gb0")
        rk = c.sb("rk", [P, NKT, 24], F32, es1); Brk = Buf("rk")
        rq = c.sb("rq", [P, NQT + 1, 24], F32, es1); Brq = Buf("rq")
        qpos = c.sb("qpos", [P, NQT + 1], F32, es1); Bqp = Buf("qpos")
        small = c.sb("small", [P, 256], F32, es1); Bsm = Buf("small")
        tmpr = c.sb("tmpr", [P, 33, 16], F32, es1); Btr = Buf("tmpr")
        wsc = c.sb("wsc", [P, 16], F32, es1); Bwsc = Buf("wsc")
        biasb = c.sb("biasb", [P, 512], F32, es1); Bbias = Buf("biasb")
        ikp = c.sb("ikp", [P, 16, 64], BF16, es1); Bikp = Buf("ikp")
        Kp = c.sb("Kp", [P, 16, 256], BF16, es1); BKp = Buf("Kp")
        Vp = c.sb("Vp", [P, 16, 256], BF16, es1); BVp = Buf("Vp")
        idxall = c.sb("idxall", [P, NSQ * 2], I32, es1); Bidx = Buf("idxall")
        ptall = idxall; Bptall = Bidx
        qisb = c.sb("qisb", [64, 64], BF16, es1); Bqisb = Buf("qisb")
        KnT2 = c.sb("KnT2", [P, 2, 64], BF16, es1); BKn = Buf("KnT2")
        kinT = c.sb("kinT", [64, 64], BF16, es1); Bkin = Buf("kinT")
        Wht = c.sb("Wht", [64, 16], F32, es1); BWht = Buf("Wht")
        Wsel = c.sb("Wsel", [64, NSQ, 64], BF16, es1); BWsel = Buf("Wsel")
        mselb = c.sb("mselb", [64, NSQ * 64], BF16, es1); Bmsel = Buf("mselb")
        Vst = c.sb("Vst", [4, 256], F32, es1); BVst = Buf("Vst")

        for t0 in range(0, NKT, 8):
            c.dma("sp", rk[:, t0:t0 + 8, :], ropek[t0 * P:(t0 + 8) * P, :].rearrange("(t p) c -> p t c", p=P), Brk,
                  writes=[Brk], nowaw=True)
        for t0 in range(0, NQT, 6):
            t1 = min(NQT, t0 + 6)
            c.dma("sp", rq[:, t0:t1, :], ropeq[t0 * P:t1 * P, :].rearrange("(t p) c -> p t c", p=P), Brq,
                  writes=[Brq], nowaw=True)
        c.dma("sp", rq[0:TS, NQT, :], ropes[:, :], Brq, writes=[Brq], nowaw=True)
        c.dma("sp", qpos[:], qpos_d[:, :], Bqp, writes=[Bqp])
        c.dma("sp", gb0[:, 0:D], bass.AP(tensor=lng_d.tensor, offset=0, ap=[[0, P], [1, D]]), Bgb0, writes=[Bgb0], nowaw=True)
        c.dma("sp", gb0[:, D:2 * D], bass.AP(tensor=lnb_d.tensor, offset=0, ap=[[0, P], [1, D]]), Bgb0, writes=[Bgb0], nowaw=True)
        c.op("pool", lambda e: e.memset(Vaug[:, :, :, 64:65], 1.0), writes=[BV])

        def front(src, Bsrc, R):
            c.op("act", lambda e: e.copy(out=xb[:R, :], in_=src), reads=[Bsrc], writes=[Bxb])
            pt, Bpt = ptbank()
            for k in range(8):
                c.op("pe", lambda e: e.transpose(pt[:, k * P:k * P + R], xb[:R, k * P:(k + 1) * P], identb[:R, :R]),
                     reads=[Bxb, Bid], writes=[Bpt])
            c.op("dve", lambda e: e.tensor_copy(out=xT[:, :, :R], in_=pt[:, :].rearrange("p (k t) -> p k t", k=8)[:, :, :R]),
                 reads=[Bpt], writes=[BxT])

        def rope(Y, BY, R, col0, H, tb):
            Yv = Y[:R, col0:col0 + 64 * H].rearrange("p (h d) -> p h d", d=64)
            tA = tmpr[:R, 0:H, 0:8]
            tB = tmpr[:R, 0:H, 8:16]
            sn = bc_mid(tb[:, 16:24], H)
            cs = bc_mid(tb[:, 0:16], H)
            c.op("dve", lambda e: e.tensor_tensor(out=tA, in0=Yv[:, :, 8:16], in1=sn, op=ALU.mult), reads=[BY, Brk, Brq], writes=[Btr])
            c.op("dve", lambda e: e.tensor_tensor(out=tB, in0=Yv[:, :, 0:8], in1=sn, op=ALU.mult), reads=[BY, Brk, Brq], writes=[Btr])
            c.op("dve", lambda e: e.tensor_tensor(out=Yv[:, :, 0:16], in0=Yv[:, :, 0:16], in1=cs, op=ALU.mult), reads=[BY, Brk, Brq], writes=[BY])
            c.op("dve", lambda e: e.tensor_tensor(out=Yv[:, :, 0:8], in0=Yv[:, :, 0:8], in1=tA, op=ALU.subtract), reads=[BY, Btr], writes=[BY])
            c.op("dve", lambda e: e.tensor_tensor(out=Yv[:, :, 8:16], in0=Yv[:, :, 8:16], in1=tB, op=ALU.add), reads=[BY, Btr], writes=[BY])

        def proj(R, wsrc, Bwsrc, ncols, ycol0):
            n0 = 0
            while n0 < ncols:
                nn = min(2048, ncols - n0)
                kper = max(1, min(8, WBN // nn))
                mts = [(m0, min(512, nn - m0)) for m0 in range(0, nn, 512)]
                bks = [c.bank() for _ in mts]
                for k0 in range(0, 8, kper):
                    kk = min(kper, 8 - k0)
                    src = wsrc[k0 * P:(k0 + kk) * P, n0:n0 + nn].rearrange("(k p) n -> p k n", p=P)
                    wb, Bw = wload([(0, kk, nn, src)], [Bwsrc])
                    wv = wb[:, 0:kk * nn].rearrange("p (k n) -> p k n", k=kk)
                    for (m0, mm), (pm, Bpm) in zip(mts, bks):
                        for k in range(kk):
                            c.op("pe", lambda e: e.matmul(pm[:R, 0:mm], lhsT=xT[:, k0 + k, :R], rhs=wv[:, k, m0:m0 + mm],
                                                          start=(k0 + k == 0), stop=(k0 + k == 7)),
                                 reads=[BxT, Bw], writes=[Bpm])
                for (m0, mm), (pm, Bpm) in zip(mts, bks):
                    c.op("act", lambda e: e.copy(out=Yf[:R, ycol0 + n0 + m0:ycol0 + n0 + m0 + mm], in_=pm[:R, 0:mm]),
                         reads=[Bpm], writes=[BYf])
                n0 += nn

        def layer_norm(src, Bsrc, R, gb, Bgb, dst, Bdst):
            st = small[:R, 0:12]
            mv = small[:R, 12:14]
            sd = small[:R, 14:15]
            rs = small[:R, 15:16]
            c.op("dve", lambda e: e.bn_stats(out=st[:, 0:6], in_=src[:, 0:512]), reads=[Bsrc], writes=[Bsm])
            c.op("dve", lambda e: e.bn_stats(out=st[:, 6:12], in_=src[:, 512:1024]), reads=[Bsrc], writes=[Bsm])
            c.op("dve", lambda e: e.bn_aggr(out=mv, in_=st), reads=[Bsm], writes=[Bsm])
            c.op("act", lambda e: e.activation(out=sd, in_=mv[:, 1:2], func=AF.Sqrt, bias=EPS, scale=1.0), reads=[Bsm], writes=[Bsm])
            c.op("dve", lambda e: e.reciprocal(out=rs, in_=sd), reads=[Bsm], writes=[Bsm])
            c.op("dve", lambda e: e.tensor_scalar(out=dst, in0=src, scalar1=mv[:, 0:1], scalar2=rs, op0=ALU.subtract, op1=ALU.mult),
                 reads=[Bsrc, Bsm], writes=[Bdst])
            c.op("pool", lambda e: e.tensor_tensor(out=dst, in0=dst, in1=gb[:R, 0:D], op=ALU.mult), reads=[Bdst, Bgb], writes=[Bdst])
            c.op("pool", lambda e: e.tensor_tensor(out=dst, in0=dst, in1=gb[:R, D:2 * D], op=ALU.add), reads=[Bdst, Bgb], writes=[Bdst])

        def bisect(R, N):
            lo = small[:R, 16:17]
            w0 = small[:R, 17:18]
            mid = small[:R, 18:19]
            cnt = small[:R, 19:20]
            tt = small[:R, 20:21]
            hw = small[:R, 32:32 + NIT]
            c.op("dve", lambda e: e.tensor_scalar(out=hw, in0=pow2[:R, :], scalar1=w0, scalar2=None, op0=ALU.mult), reads=[Bsm, Bcst], writes=[Bsm])
            for k in range(NIT):
                c.op("dve", lambda e: e.tensor_tensor(out=mid, in0=lo, in1=hw[:, k:k + 1], op=ALU.add), reads=[Bsm], writes=[Bsm])
                c.op("dve", lambda e: e.tensor_scalar(out=msk[:R, :N], in0=Isb[:R, :N], scalar1=mid, scalar2=None, op0=ALU.is_ge,
                                                      op1=ALU.add, accum_out=cnt), reads=[BI, Bsm], writes=[Bmsk, Bsm])
                c.op("dve", lambda e: e.tensor_scalar(out=tt, in0=cnt, scalar1=float(TOPK), scalar2=hw[:, k:k + 1], op0=ALU.is_ge, op1=ALU.mult),
                     reads=[Bsm], writes=[Bsm])
                c.op("dve", lambda e: e.tensor_tensor(out=lo, in0=lo, in1=tt, op=ALU.add), reads=[Bsm], writes=[Bsm])
            c.op("dve", lambda e: e.tensor_scalar(out=msk[:R, :N], in0=Isb[:R, :N], scalar1=lo, scalar2=None, op0=ALU.is_ge),
                 reads=[BI, Bsm], writes=[Bmsk])

        def evac_scores(pI, BpI, R, c0, nn, qp):
            ci = c0 // 512
            qrel = small[:R, 21:22]
            c.op("dve", lambda e: e.tensor_scalar(out=qrel, in0=qp, scalar1=float(-c0), scalar2=None, op0=ALU.add), reads=[Bqp, Bsm], writes=[Bsm])
            c.op("dve", lambda e: e.tensor_reduce(out=small[:R, 100 + ci:101 + ci], in_=pI[:R, 0:nn], axis=AX.X, op=ALU.max), reads=[BpI], writes=[Bsm])
            c.op("dve", lambda e: e.tensor_reduce(out=small[:R, 110 + ci:111 + ci], in_=pI[:R, 0:nn], axis=AX.X, op=ALU.min), reads=[BpI], writes=[Bsm])
            bias = biasb[:R, 0:nn]
            c.op("dve", lambda e: e.tensor_scalar(out=bias, in0=iota[:R, 0:nn], scalar1=qrel, scalar2=NEG, op0=ALU.is_gt, op1=ALU.mult),
                 reads=[Bcst, Bsm], writes=[Bbias])
            c.op("dve", lambda e: e.tensor_tensor(out=Isb[:R, c0:c0 + nn], in0=pI[:R, 0:nn], in1=bias, op=ALU.add), reads=[BpI, Bbias], writes=[BI])

        def bounds(R, nch):
            c.op("dve", lambda e: e.tensor_reduce(out=small[:R, 22:23], in_=small[:R, 100:100 + nch], axis=AX.X, op=ALU.max), reads=[Bsm], writes=[Bsm])
            c.op("dve", lambda e: e.tensor_reduce(out=small[:R, 23:24], in_=small[:R, 110:110 + nch], axis=AX.X, op=ALU.min), reads=[Bsm], writes=[Bsm])
            c.op("dve", lambda e: e.tensor_scalar(out=small[:R, 16:17], in0=small[:R, 23:24], scalar1=-1.0, scalar2=None, op0=ALU.add), reads=[Bsm], writes=[Bsm])
            c.op("dve", lambda e: e.tensor_scalar(out=small[:R, 17:18], in0=small[:R, 22:23], scalar1=small[:R, 23:24], scalar2=2.0,
                                                  op0=ALU.subtract, op1=ALU.add), reads=[Bsm], writes=[Bsm])

        def mask_transposes(R, nkt):
            for t0 in range(0, nkt, 8):
                t1 = min(nkt, t0 + 8)
                pt, Bpt = ptbank()
                for t in range(t0, t1):
                    c.op("pe", lambda e: e.transpose(pt[:, (t - t0) * P:(t - t0) * P + R], msk[:R, t * P:(t + 1) * P], identb[:R, :R]),
                         reads=[Bmsk, Bid], writes=[Bpt])
                c.op("act", lambda e: e.copy(out=mskT[:, t0:t1, :R], in_=pt[:, 0:(t1 - t0) * P].rearrange("p (a b) -> p a b", b=P)[:, :, :R]),
                     reads=[Bpt], writes=[BmT])

        def tail(R, xdram, row0):
            c.dma("sp", rr[:R, :], xdram, Brr, writes=[Brr])
            pt, Bpt = ptbank()
            for k in range(8):
                c.op("pe", lambda e: e.transpose(pt[:, k * P:k * P + R], Obf[:R, k * P:(k + 1) * P], identb[:R, :R]),
                     reads=[BO, Bid], writes=[Bpt])
            c.op("dve", lambda e: e.tensor_copy(out=xT[:, :, :R], in_=pt[:, :].rearrange("p (k t) -> p k t", k=8)[:, :, :R]),
                 reads=[Bpt], writes=[BxT])
            tiles = []
            for k0 in (0, 4):
                src = wo_s[k0 * P:(k0 + 4) * P, :].rearrange("(k p) n -> p k n", p=P)
                wb, Bw = wload([(0, 4, D, src)], [Bwo])
                tiles.append((k0, wb, Bw))
            for m0 in (0, 512):
                pm, Bpm = c.bank()
                for (k0, wb, Bw) in tiles:
                    wv = wb[:, 0:4 * D].rearrange("p (k n) -> p k n", k=4)
                    for k in range(4):
                        c.op("pe", lambda e: e.matmul(pm[:R, :], lhsT=xT[:, k0 + k, :R], rhs=wv[:, k, m0:m0 + 512],
                                                      start=(k0 + k == 0), stop=(k0 + k == 7)), reads=[BxT, Bw], writes=[Bpm])
                c.op("dve", lambda e: e.scalar_tensor_tensor(out=rr[:R, m0:m0 + 512], in0=rr[:R, m0:m0 + 512], scalar=ALPHA, in1=pm[:R, :],
                                                             op0=ALU.mult, op1=ALU.add), reads=[Brr, Bpm], writes=[Brr])
            layer_norm(rr[:R, :], Brr, R, gb0, Bgb0, yy[:R, :], Byy)
            c.dma("sp", y1_s[row0:row0 + R, :], yy[:R, :], Byy, reads=[Byy])
            if stage < 3:
                if row0 < TQ:
                    c.dma("sp", o_y[row0:row0 + R, :], yy[:R, :], Byy, reads=[Byy])
                else:
                    c.dma("sp", o_ys[:, :], yy[:R, :], Byy, reads=[Byy])


        def rest_phase():
            es2 = contextlib.ExitStack()
            es2.__enter__()
            ya = c.sb("ya", [P, 4, D], F32, es2); Bya = [Buf("ya%d" % t) for t in range(4)]
            yb_ = c.sb("ybb", [P, 4, D], F32, es2); Byb = [Buf("yb%d" % t) for t in range(4)]
            yT = c.sb("yT", [P, 8, 512], BF16, es2); ByT = Buf("yT")
            hT = c.sb("hT", [P, 22, 512], BF16, es2); BhT = Buf("hT")
            aex = [(c.sb("aex%d" % i, [P, 520], F32, es2), Buf("aex%d" % i)) for i in range(2)]
            uub = [(c.sb("uu%d" % i, [P, 512], F32, es2), Buf("uu%d" % i)) for i in range(2)]
            silb = [(c.sb("sil%d" % i, [P, 512], F32, es2), Buf("sil%d" % i)) for i in range(2)]
            xb2 = c.sb("xb2", [P, D], BF16, es2); Bxb2 = Buf("xb2")
            gbs = [(c.sb("gb%d" % i, [P, 2 * D], F32, es2), Buf("gb%d" % i)) for i in (1, 2, 3)]
            halo_f = c.sb("halo_f", [P, 2, 22, 2], F32, es2); Bhf = Buf("halo_f")
            halo_c = c.sb("halo_c", [P, 8, 2], F32, es2); Bhc = Buf("halo_c")
            prm = c.sb("prm", [P, 22, 11], F32, es2)
            Bprm = Buf("prm")
            sext = c.sb("sext", [P, 22, 32], F32, es2); Bsext = Buf("sext")
            sout = c.sb("sout", [P, 22, 32], F32, es2); Bsout = Buf("sout")
            stg = c.sb("stg", [32, DFF], F32, es2); Bstg = Buf("stg")
            small2 = c.sb("small2", [P, 32], F32, es2); Bsm2 = Buf("small2")

            for li in (1, 2, 3):
                gbt, Bg = gbs[li - 1]
                c.dma("sp", gbt[:, 0:D], bass.AP(tensor=lng_d.tensor, offset=li * D, ap=[[0, P], [1, D]]), Bg, writes=[Bg], nowaw=True)
                c.dma("sp", gbt[:, D:2 * D], bass.AP(tensor=lnb_d.tensor, offset=li * D, ap=[[0, P], [1, D]]), Bg, writes=[Bg], nowaw=True)
            c.op("dve", lambda e: e.memset(stg[:, :], 0.0), writes=[Bstg])
            c.dma("sp", stg[0:6, :], fcw_d.rearrange("i j n -> (i j) n"), Bstg, reads=[Bstg], writes=[Bstg])
            c.dma("sp", stg[6:8, :], fcb_d[:, :], Bstg, reads=[Bstg], writes=[Bstg], nowaw=True)
            c.dma("sp", stg[8:11, 0:D], cw_d[:, :], Bstg, reads=[Bstg], writes=[Bstg], nowaw=True)
            pmp, Bpmp = c.bank()
            for ch in range(22):
                c.op("pe", lambda e: e.transpose(pmp[:, ch * 11:(ch + 1) * 11], stg[0:11, ch * P:(ch + 1) * P], identf[0:11, 0:11]),
                     reads=[Bstg, Bcst], writes=[Bpmp])
            c.op("act", lambda e: e.copy(out=prm[:, :, :], in_=pmp[:, 0:242].rearrange("p (a b) -> p a b", b=11)), reads=[Bpmp], writes=[Bprm])
            c.op("dve", lambda e: e.memset(halo_f[:, :, :, :], 0.0), writes=[Bhf])
            c.op("dve", lambda e: e.memset(halo_c[:, :, :], 0.0), writes=[Bhc])

            def ln2(src, Bsrc, R, gb, Bgb):
                st = small2[:R, 0:12]; mv = small2[:R, 12:14]; sd = small2[:R, 14:15]; rs = small2[:R, 15:16]
                c.op("dve", lambda e: e.bn_stats(out=st[:, 0:6], in_=src[:, 0:512]), reads=[Bsrc], writes=[Bsm2])
                c.op("dve", lambda e: e.bn_stats(out=st[:, 6:12], in_=src[:, 512:1024]), reads=[Bsrc], writes=[Bsm2])
                c.op("dve", lambda e: e.bn_aggr(out=mv, in_=st), reads=[Bsm2], writes=[Bsm2])
                c.op("act", lambda e: e.activation(out=sd, in_=mv[:, 1:2], func=AF.Sqrt, bias=EPS, scale=1.0), reads=[Bsm2], writes=[Bsm2])
                c.op("dve", lambda e: e.reciprocal(out=rs, in_=sd), reads=[Bsm2], writes=[Bsm2])
                c.op("dve", lambda e: e.tensor_scalar(out=src, in0=src, scalar1=mv[:, 0:1], scalar2=rs, op0=ALU.subtract, op1=ALU.mult),
                     reads=[Bsrc, Bsm2], writes=[Bsrc])
                c.op("pool", lambda e: e.tensor_tensor(out=src, in0=src, in1=gb[:R, 0:D], op=ALU.mult), reads=[Bsrc, Bgb], writes=[Bsrc])
                c.op("pool", lambda e: e.tensor_tensor(out=src, in0=src, in1=gb[:R, D:2 * D], op=ALU.add), reads=[Bsrc, Bgb], writes=[Bsrc])

            def to_featT(Y, BY, nt, R):
                for t in range(nt):
                    c.op("act", lambda e: e.copy(out=xb2[:R, :], in_=Y[:R, t, :]), reads=[BY[t]], writes=[Bxb2])
                    pt, Bpt = ptbank()
                    for k in range(8):
                        c.op("pe", lambda e: e.transpose(pt[:, k * P:k * P + R], xb2[:R, k * P:(k + 1) * P], identb[:R, :R]),
                             reads=[Bxb2, Bid], writes=[Bpt])
                    c.op("dve", lambda e: e.tensor_copy(out=yT[:, :, t * P:t * P + R], in_=pt[:, :].rearrange("p (k t) -> p k t", k=8)[:, :, :R]),
                         reads=[Bpt], writes=[ByT])

            def conv3(ae, Bae, N, samp, w0, w1, w2, uu, Buu):
                if samp:
                    av = ae[:, 0:96].rearrange("p (b t) -> p b t", t=6)
                    uv = uu[:, 0:64].rearrange("p (b t) -> p b t", t=4)
                    s0, s1, s2 = av[:, :, 0:4], av[:, :, 1:5], av[:, :, 2:6]
                else:
                    uv = uu[:, 0:N]
                    s0, s1, s2 = ae[:, 0:N], ae[:, 1:N + 1], ae[:, 2:N + 2]
                c.op("dve", lambda e: e.tensor_scalar(out=uv, in0=s0, scalar1=w0, scalar2=None, op0=ALU.mult), reads=[Bae, Bprm], writes=[Buu])
                c.op("dve", lambda e: e.scalar_tensor_tensor(out=uv, in0=s1, scalar=w1, in1=uv, op0=ALU.mult, op1=ALU.add), reads=[Bae, Bprm, Buu], writes=[Buu])
                c.op("dve", lambda e: e.scalar_tensor_tensor(out=uv, in0=s2, scalar=w2, in1=uv, op0=ALU.mult, op1=ALU.add), reads=[Bae, Bprm, Buu], writes=[Buu])

            def load_state_T(src_dram, nch):
                c.dma("sp", stg[:, 0:nch * P], src_dram, Bstg, writes=[Bstg])
                for c0 in range(0, nch, 16):
                    c1 = min(nch, c0 + 16)
                    pm, Bpm = c.bank()
                    for ch in range(c0, c1):
                        c.op("pe", lambda e: e.transpose(pm[:, (ch - c0) * 32:(ch - c0 + 1) * 32], stg[:, ch * P:(ch + 1) * P], identf[0:32, 0:32]),
                             reads=[Bstg, Bcst], writes=[Bpm])
                    c.op("act", lambda e: e.copy(out=sext[:, c0:c1, :], in_=pm[:, 0:(c1 - c0) * 32].rearrange("p (a b) -> p a b", b=32)),
                         reads=[Bpm], writes=[Bsext])

            def store_state_T(src, Bsrc, nch, ncol, dst_dram):
                for c0 in range(0, nch, 4):
                    c1 = min(nch, c0 + 4)
                    pm, Bpm = c.bank()
                    for ch in range(c0, c1):
                        c.op("pe", lambda e: e.transpose(pm[0:ncol, (ch - c0) * P:(ch - c0 + 1) * P], src[:, ch, :], identf[:, :]),
                             reads=[Bsrc, Bcst], writes=[Bpm])
                    c.op("act", lambda e: e.copy(out=stg[0:ncol, c0 * P:c1 * P], in_=pm[0:ncol, 0:(c1 - c0) * P]), reads=[Bpm], writes=[Bstg])
                c.dma("sp", dst_dram, stg[0:ncol, 0:nch * P], Bstg, reads=[Bstg])

            def ffn(i, Yin, BYin, Yout, BYout, nt, R, samp, last, gb, Bgb):
                N = R if samp else nt * P
                if samp:
                    load_state_T(stf_d[i, :, :], 22)
                ri = 0
                for cb in range(NCB):
                    wb, Bw = wload([(0, 8, 512, wup_s[i, cb].rearrange("p k s n -> p k (s n)"))], [Bwup])
                    wv = wb[:, 0:4096].rearrange("p (k s n) -> p k s n", k=8, s=2)
                    for hf in range(2):
                        ch = 2 * cb + hf
                        pa, Bpa = c.bank()
                        pg, Bpg = c.bank()
                        for (pm, Bpm, s_) in ((pa, Bpa, 0), (pg, Bpg, 1)):
                            for k in range(8):
                                c.op("pe", lambda e: e.matmul(pm[:, 0:N], lhsT=wv[:, k, s_, hf * P:(hf + 1) * P], rhs=yT[:, k, 0:N],
                                                              start=(k == 0), stop=(k == 7)), reads=[Bw, ByT], writes=[Bpm])
                        ae, Bae = aex[ri % 2]; uu, Buu = uub[ri % 2]; sl, Bsl = silb[ri % 2]
                        ri += 1
                        if samp:
                            av = ae[:, 0:96].rearrange("p (b t) -> p b t", t=6)
                            c.op("dve", lambda e: e.tensor_copy(out=av[:, :, 0:2], in_=sext[:, ch, :].rearrange("p (b r) -> p b r", r=2)),
                                 reads=[Bsext], writes=[Bae])
                            c.op("act", lambda e: e.copy(out=av[:, :, 2:6], in_=pa[:, 0:64].rearrange("p (b t) -> p b t", t=4)), reads=[Bpa], writes=[Bae])
                            c.op("dve", lambda e: e.tensor_copy(out=sout[:, ch, :].rearrange("p (b r) -> p b r", r=2), in_=av[:, :, 4:6]),
                                 reads=[Bae], writes=[Bsout])
                        else:
                            c.op("dve", lambda e: e.tensor_copy(out=ae[:, 0:2], in_=halo_f[:, i, ch, :]), reads=[Bhf], writes=[Bae])
                            c.op("act", lambda e: e.copy(out=ae[:, 2:2 + N], in_=pa[:, 0:N]), reads=[Bpa], writes=[Bae])
                            c.op("dve", lambda e: e.tensor_copy(out=halo_f[:, i, ch, :], in_=ae[:, N:N + 2]), reads=[Bae], writes=[Bhf])
                        conv3(ae, Bae, N, samp, prm[:, ch, 3 * i:3 * i + 1], prm[:, ch, 3 * i + 1:3 * i + 2], prm[:, ch, 3 * i + 2:3 * i + 3], uu, Buu)
                        c.op("act", lambda e: e.activation(out=sl[:, 0:N], in_=uu[:, 0:N], func=AF.Silu, bias=prm[:, ch, 6 + i:7 + i], scale=1.0),
                             reads=[Buu, Bprm], writes=[Bsl])
                        c.op("dve", lambda e: e.tensor_tensor(out=hT[:, ch, 0:N], in0=sl[:, 0:N], in1=pg[:, 0:N], op=ALU.mult),
                             reads=[Bsl, Bpg], writes=[BhT])
                for m0 in (0, 512):
                    bks = [c.bank() for _ in range(nt)]
                    for c0 in range(0, 22, 4):
                        cc = min(4, 22 - c0)
                        src = wdn_s[i, c0 * P:(c0 + cc) * P, m0:m0 + 512].rearrange("(c p) n -> p c n", p=P)
                        wb, Bw = wload([(0, cc, 512, src)], [Bwdn])
                        wv = wb[:, 0:cc * 512].rearrange("p (c n) -> p c n", c=cc)
                        for t in range(nt):
                            pm, Bpm = bks[t]
                            for cj in range(cc):
                                c.op("pe", lambda e: e.matmul(pm[:R, :], lhsT=hT[:, c0 + cj, t * P:t * P + R], rhs=wv[:, cj, :],
                                                              start=(c0 + cj == 0), stop=(c0 + cj == 21)), reads=[BhT, Bw], writes=[Bpm])
                    for t in range(nt):
                        pm, Bpm = bks[t]
                        c.op("dve", lambda e: e.scalar_tensor_tensor(out=Yout[:R, t, m0:m0 + 512], in0=Yin[:R, t, m0:m0 + 512], scalar=ALPHA,
                                                                     in1=pm[:R, :], op0=ALU.mult, op1=ALU.add), reads=[BYin[t], Bpm], writes=[BYout[t]])
                for t in range(nt):
                    ln2(Yout[:R, t, :], BYout[t], R, gb, Bgb)
                if samp:
                    store_state_T(sout, Bsout, 22, 32, o_fs[i, :, :])
                elif last:
                    store_state_T(halo_f[:, i, :, :], Bhf, 22, 2, o_fp[i, :, :])

            def mixer(Yin, BYin, Yout, BYout, nt, R, samp, last, gb, Bgb):
                N = R if samp else nt * P
                if samp:
                    load_state_T(stc_d[:, :], 8)
                ri = 0
                for ch in range(8):
                    wb, Bw = wload([(0, 8, 384, wci_s[ch].rearrange("p k s n -> p k (s n)"))], [Bwci])
                    wv = wb[:, 0:3072].rearrange("p (k s n) -> p k s n", k=8, s=3)
                    bks = [c.bank() for _ in range(3)]
                    for s_ in range(3):
                        pm, Bpm = bks[s_]
                        for k in range(8):
                            c.op("pe", lambda e: e.matmul(pm[:, 0:N], lhsT=wv[:, k, s_, :], rhs=yT[:, k, 0:N], start=(k == 0), stop=(k == 7)),
                                 reads=[Bw, ByT], writes=[Bpm])
                    (pb_, Bpb_), (pc_, Bpc_), (pu_, Bpu_) = bks
                    ae, Bae = aex[ri % 2]; uu, Buu = uub[ri % 2]
                    ri += 1
                    if samp:
                        av = ae[:, 0:96].rearrange("p (b t) -> p b t", t=6)
                        c.op("dve", lambda e: e.tensor_copy(out=av[:, :, 0:2], in_=sext[:, ch, :].rearrange("p (b r) -> p b r", r=2)), reads=[Bsext], writes=[Bae])
                        c.op("act", lambda e: e.copy(out=av[:, :, 2:6], in_=pc_[:, 0:64].rearrange("p (b t) -> p b t", t=4)), reads=[Bpc_], writes=[Bae])
                        c.op("dve", lambda e: e.tensor_tensor(out=av[:, :, 2:6], in0=av[:, :, 2:6], in1=pu_[:, 0:64].rearrange("p (b t) -> p b t", t=4), op=ALU.mult),
                             reads=[Bae, Bpu_], writes=[Bae])
                        c.op("dve", lambda e: e.tensor_copy(out=sout[:, ch, :].rearrange("p (b r) -> p b r", r=2), in_=av[:, :, 4:6]), reads=[Bae], writes=[Bsout])
                    else:
                        c.op("dve", lambda e: e.tensor_copy(out=ae[:, 0:2], in_=halo_c[:, ch, :]), reads=[Bhc], writes=[Bae])
                        c.op("act", lambda e: e.copy(out=ae[:, 2:2 + N], in_=pc_[:, 0:N]), reads=[Bpc_], writes=[Bae])
                        c.op("dve", lambda e: e.tensor_tensor(out=ae[:, 2:2 + N], in0=ae[:, 2:2 + N], in1=pu_[:, 0:N], op=ALU.mult), reads=[Bae, Bpu_], writes=[Bae])
                        c.op("dve", lambda e: e.tensor_copy(out=halo_c[:, ch, :], in_=ae[:, N:N + 2]), reads=[Bae], writes=[Bhc])
                    conv3(ae, Bae, N, samp, prm[:, ch, 8:9], prm[:, ch, 9:10], prm[:, ch, 10:11], uu, Buu)
                    c.op("dve", lambda e: e.tensor_tensor(out=hT[:, ch, 0:N], in0=uu[:, 0:N], in1=pb_[:, 0:N], op=ALU.mult), reads=[Buu, Bpb_], writes=[BhT])
                tiles = []
                for k0 in (0, 4):
                    src = wco_s[k0 * P:(k0 + 4) * P, :].rearrange("(k p) n -> p k n", p=P)
                    wb, Bw = wload([(0, 4, D, src)], [Bwco])
                    tiles.append((k0, wb, Bw))
                for t in range(nt):
                    for m0 in (0, 512):
                        pm, Bpm = c.bank()
                        for (k0, wb, Bw) in tiles:
                            wv = wb[:, 0:4 * D].rearrange("p (k n) -> p k n", k=4)
                            for k in range(4):
                                c.op("pe", lambda e: e.matmul(pm[:R, :], lhsT=hT[:, k0 + k, t * P:t * P + R], rhs=wv[:, k, m0:m0 + 512],
                                                              start=(k0 + k == 0), stop=(k0 + k == 7)), reads=[BhT, Bw], writes=[Bpm])
                        c.op("dve", lambda e: e.scalar_tensor_tensor(out=Yout[:R, t, m0:m0 + 512], in0=Yin[:R, t, m0:m0 + 512], scalar=ALPHA,
                                                                     in1=pm[:R, :], op0=ALU.mult, op1=ALU.add), reads=[BYin[t], Bpm], writes=[BYout[t]])
                    ln2(Yout[:R, t, :], BYout[t], R, gb, Bgb)
                if samp:
                    store_state_T(sout, Bsout, 8, 32, o_cs[:, :])
                elif last:
                    store_state_T(halo_c[:, :, :], Bhc, 8, 2, o_cp[:, :])

            glist = [(t0 * P, nt, P, False, gi == len(GROUPS) - 1) for gi, (t0, nt) in enumerate(GROUPS)] + [(TQ, 1, TS, True, False)]
            for (row0, nt, R, samp, last) in glist:
                for t in range(nt):
                    c.dma("sp", ya[:R, t, :], y1_s[row0 + t * P:row0 + t * P + R, :], Bya[t], writes=[Bya[t]])
                to_featT(ya, Bya, nt, R)
                ffn(0, ya, Bya, yb_, Byb, nt, R, samp, last, gbs[0][0], gbs[0][1])
                to_featT(yb_, Byb, nt, R)
                mixer(yb_, Byb, ya, Bya, nt, R, samp, last, gbs[1][0], gbs[1][1])
                to_featT(ya, Bya, nt, R)
                ffn(1, ya, Bya, yb_, Byb, nt, R, samp, last, gbs[2][0], gbs[2][1])
                for t in range(nt):
                    dst = o_ys[:, :] if samp else o_y[row0 + t * P:row0 + (t + 1) * P, :]
                    c.dma("sp", dst, yb_[:R, t, :], Byb[t], reads=[Byb[t]])
            c.barrier()
            es2.__exit__(None, None, None)

        for kt in range(NKT):
            c.dma("sp", xt[:], xk[kt * P:(kt + 1) * P, :], Bx, writes=[Bx])
            drain(4)
            front(xt[:, :], Bx, P)
            proj(P, wkv_s, Bwkv, 576, 0)
            rope(Yf, BYf, P, 0, 4, rk[:, kt, :])
            rope(Yf, BYf, P, 512, 1, rk[:, kt, :])
            rows = slice(kt * P, (kt + 1) * P)
            c.dma("sp", o_k[rows, :], Yf[:, 0:256], BYf, reads=[BYf])
            c.dma("sp", o_v[rows, :], Yf[:, 256:512], BYf, reads=[BYf])
            c.dma("sp", o_ik[rows, :], Yf[:, 512:576], BYf, reads=[BYf])
            c.op("pool", lambda e: e.tensor_copy(out=Yb[:, 0:576], in_=Yf[:, 0:576]), reads=[BYf], writes=[BYb])
            c.op("pool", lambda e: e.tensor_copy(out=Vaug[:, kt, :, 0:64], in_=Yf[:, 256:512].rearrange("p (g d) -> p g d", d=64)),
                 reads=[BYf], writes=[BV])
            pt, Bpt = ptbank()
            for gp in range(2):
                c.op("pe", lambda e: e.transpose(pt[:, gp * P:(gp + 1) * P], Yb[:, gp * P:(gp + 1) * P], identb[:, :]), reads=[BYb, Bid], writes=[Bpt])
            c.op("pe", lambda e: e.transpose(pt[0:64, 2 * P:3 * P], Yb[:, 512:576], identb[:, :]), reads=[BYb, Bid], writes=[Bpt])
            c.op("act", lambda e: e.copy(out=KT2[:, :, kt * P:(kt + 1) * P], in_=pt[:, 0:2 * P].rearrange("p (a b) -> p a b", b=P)),
                 reads=[Bpt], writes=[BKT])
            c.op("act", lambda e: e.copy(out=kiT[:, kt * P:(kt + 1) * P], in_=pt[0:64, 2 * P:3 * P]), reads=[Bpt], writes=[BkiT])

        def q_front(R, tb):
            rope(Yf, BYf, R, 0, 32, tb)
            for gp in range(2):
                c.op("pool", lambda e: e.tensor_copy(
                    out=Yb[:R, gp * 512:(gp + 1) * 512].rearrange("p (h r d) -> p h r d", h=4, r=2),
                    in_=Yf[:R, gp * 512:(gp + 1) * 512].rearrange("p (r h d) -> p h r d", h=4, r=2)), reads=[BYf], writes=[BYb])
            c.op("pool", lambda e: e.tensor_copy(out=Yb[:R, 1024:2048], in_=Yf[:R, 1024:2048]), reads=[BYf], writes=[BYb])
            c.op("dve", lambda e: e.tensor_scalar(out=wsc[:R, :], in0=Yf[:R, 2048:2064], scalar1=IDX_SCALE, scalar2=None, op0=ALU.mult),
                 reads=[BYf], writes=[Bwsc])

        def q_transposes(R):
            pt, Bpt = ptbank()
            for gp in range(2):
                for hh in range(4):
                    idx = gp * 4 + hh
                    src = Yb[:R, idx * P:(idx + 1) * P]
                    c.op("pe", lambda e: e.transpose(pt[:, idx * P:idx * P + R], src, identb[:R, :R]), reads=[BYb, Bid], writes=[Bpt])
            c.op("act", lambda e: e.copy(out=QT2[:, :, :R], in_=pt[:, :].rearrange("p (a b) -> p a b", b=P)[:, :, :R]), reads=[Bpt], writes=[BQT])
            for h0 in (0, 8):
                pt, Bpt = ptbank()
                for h in range(8):
                    col = 1024 + (h0 + h) * 64
                    c.op("pe", lambda e: e.transpose(pt[0:64, h * P:h * P + R], Yb[:R, col:col + 64], identb[:R, :R]), reads=[BYb, Bid], writes=[Bpt])
                c.op("act", lambda e: e.copy(out=qiT[:, h0:h0 + 8, :R], in_=pt[0:64, :].rearrange("p (a b) -> p a b", b=P)[:, :, :R]),
                     reads=[Bpt], writes=[BqiT])

        import os as _os
        SUB = int(_os.environ.get("DBG_SUB", "99"))
        NQR = int(_os.environ.get("DBG_NQ", str(NQT)))
        def attention(NK, maskfn, Bmask, Od, BOd, mme=("pool", "dve")):
            obanks = [c.banks[0], c.banks[1], c.banks[2]]
            for (ob, Bob) in obanks:
                c.op("pe", lambda e: e.matmul(ob[:, :], lhsT=zerob[:, 0:P], rhs=zerob[:, :], start=True, stop=False),
                     reads=[Bzero], writes=[Bob])
            units = [(kt, g) for kt in range(NK) for g in range(4)]
            LA = 3
            fr = []

            def front_u(ei, kt, g):
                pS, BpS = c.bank((3, 4, 5))
                pb = (g % 2) * 64
                c.op("pe", lambda e: e.matmul(pS[:, :], lhsT=KT2[pb:pb + 64, g // 2, kt * P:(kt + 1) * P],
                                              rhs=QT2[pb:pb + 64, (g // 2) * 4:(g // 2) * 4 + 4, :], start=True, stop=True),
                     reads=[BKT, BQT], writes=[BpS])
                E_, BE_ = Eb[ei % 2]
                Pm_, BPm_ = Pb[ei % 4]
                c.op("act", lambda e: e.activation(out=E_[:, :], in_=pS[:, :], func=AF.Exp, scale=0.125), reads=[BpS], writes=[BE_])
                c.op(mme[ei % len(mme)], lambda e: e.tensor_tensor(out=Pm_[:, :].rearrange("p (a b) -> p a b", a=4),
                                                       in0=E_[:, :].rearrange("p (a b) -> p a b", a=4),
                                                       in1=bc_mid(maskfn(kt), 4), op=ALU.mult), reads=[BE_, Bmask], writes=[BPm_])
                return (Pm_, BPm_)

            def back_u(ei, kt, g):
                Pm_, BPm_ = fr[ei]
                for hh in range(4):
                    h = 4 * g + hh
                    ob, Bob = obanks[h // 7]
                    oc = (h % 7) * 65
                    c.op("pe", lambda e: e.matmul(ob[:, oc:oc + 65], lhsT=Pm_[:, hh * P:(hh + 1) * P], rhs=Vaug[:, kt, g, :],
                                                  start=False, stop=(kt == NK - 1 and (h % 7 == 6 or h == 15))), reads=[BPm_, BV], writes=[Bob])

            for i in range(len(units) + LA):
                if i < len(units):
                    fr.append(front_u(i, *units[i]))
                if i >= LA:
                    back_u(i - LA, *units[i - LA])
            for bi, (ob, Bob) in enumerate(obanks):
                nh = 7 if bi < 2 else 2
                ov = ob[:, 0:nh * 65].rearrange("p (h d) -> p h d", d=65)
                rec = small[:, 200 + 7 * bi:200 + 7 * bi + nh]
                c.op("dve", lambda e: e.reciprocal(out=rec, in_=ov[:, :, 64]), reads=[Bob], writes=[Bsm])
                c.op("dve", lambda e: e.tensor_tensor(out=Od[:, bi * 7 * 64:(bi * 7 + nh) * 64].rearrange("p (h d) -> p h d", d=64),
                                                      in0=ov[:, :, 0:64], in1=bc_last(rec, 64), op=ALU.mult), reads=[Bob, Bsm], writes=[BOd])

        if stage >= 1:
            for j in range(NQR):
                NK = 16 + j
                N = NK * P
                c.dma("sp", xt[:], xq[j * P:(j + 1) * P, :], Bx, writes=[Bx])
                drain(8)
                front(xt[:, :], Bx, P)
                proj(P, wq_s, Bwq, 2064, 0)
                q_front(P, rq[:, j, :])
                if SUB < 1:
                    continue
                q_transposes(P)
                c.op("dve", lambda e: e.tensor_tensor(out=diag[:, :, :], in0=bc_mid(identb[:, :], 16), in1=bc_last(wsc[:, :], P), op=ALU.mult),
                     reads=[Bid, Bwsc], writes=[Bdiag])
                if SUB < 2:
                    continue
                nch = (N + 511) // 512
                items = [(ci, h) for ci in range(nch) for h in range(16)]
                pIs = {}
                frs = []
                LAI = 3

                def idx_front(ii, ci, h):
                    c0 = ci * 512
                    nn = min(512, N - c0)
                    psc, Bpsc = c.bank((0, 1, 2, 3))
                    c.op("pe", lambda e: e.matmul(psc[:, 0:nn], lhsT=qiT[:, h, :], rhs=kiT[:, c0:c0 + nn], start=True, stop=True),
                         reads=[BqiT, BkiT], writes=[Bpsc])
                    R_, BR_ = Rb[ii % 4]
                    if h % 2 == 0:
                        c.op("act", lambda e: e.activation(out=R_[:, 0:nn], in_=psc[:, 0:nn], func=AF.Relu), reads=[Bpsc], writes=[BR_])
                    else:
                        c.op("dve", lambda e: e.tensor_scalar(out=R_[:, 0:nn], in0=psc[:, 0:nn], scalar1=0.0, scalar2=None, op0=ALU.max),
                             reads=[Bpsc], writes=[BR_])
                    return (R_, BR_)

                def idx_back(ii, ci, h):
                    c0 = ci * 512
                    nn = min(512, N - c0)
                    if h == 0:
                        pIs[ci] = c.bank((4, 5))
                    pI, BpI = pIs[ci]
                    R_, BR_ = frs[ii]
                    c.op("pe", lambda e: e.matmul(pI[:, 0:nn], lhsT=diag[:, h, :], rhs=R_[:, 0:nn], start=(h == 0), stop=(h == 15)),
                         reads=[Bdiag, BR_], writes=[BpI])
                    if h == 15:
                        evac_scores(pI, BpI, P, c0, nn, qpos[:, j:j + 1])

                for ii in range(len(items) + LAI):
                    if ii < len(items):
                        frs.append(idx_front(ii, *items[ii]))
                    if ii >= LAI:
                        idx_back(ii - LAI, *items[ii - LAI])
                if SUB < 3:
                    continue
                bounds(P, nch)
                bisect(P, N)
                if SUB < 4:
                    continue
                mask_transposes(P, NK)
                if SUB < 5:
                    continue
                attention(NK, lambda kt: mskT[:, kt, :], BmT, Obf, BO)
                if SUB < 6:
                    continue
                tail(P, xq[j * P:(j + 1) * P, :], j * P)
                if _os.environ.get("DBG_DUMP") and j == 0:
                    c.dma("sp", o_y[128:256, :], Isb[:, 0:1024], BI, reads=[BI])
                    c.dma("sp", o_y[256:384, 0:256], small[:, :], Bsm, reads=[Bsm])
                    c.dma("sp", o_y[384:512, :], rr[:, :], Brr, reads=[Brr])


        def sample_phase():
            R = TS
            IOA = bass.IndirectOffsetOnAxis
            c.dma("sp", xt[:R, :], xs[:, :], Bx, writes=[Bx])
            c.dma("pool", mselb[:, :], msel_d[:, :], Bmsel, writes=[Bmsel])
            for jj in range(8):
                c.dma("sp", idxall[16 * jj:16 * jj + 16, 0:2 * NSQ], bass.AP(tensor=pt_d.tensor, offset=jj, ap=[[0, 16], [8, 2 * NSQ]]), Bidx,
                      writes=[Bidx], nowaw=(jj > 0), slow=True)
            c.op("dve", lambda e: e.tensor_scalar(out=idxall[:, 0:2 * NSQ], in0=idxall[:, 0:2 * NSQ], scalar1=16.0, scalar2=pm16, op0=ALU.mult, op1=ALU.add),
                 reads=[Bidx, Bcst], writes=[Bidx])
            cik8 = cik_d.rearrange("(r k) d -> r (k d)", k=8)
            ck8 = ck_d.rearrange("(r k) d -> r (k d)", k=8)
            cv8 = cv_d.rearrange("(r k) d -> r (k d)", k=8)
            front(xt[:R, :], Bx, R)
            proj(R, wq_s, Bwq, 2064, 0)
            proj(R, wkv_s, Bwkv, 576, 2064)
            tb = rq[0:R, NQT, :]
            q_front(R, tb)
            rope(Yf, BYf, R, 2064, 4, tb)
            rope(Yf, BYf, R, 2064 + 512, 1, tb)
            c.dma("sp", o_ks[:, :], Yf[:R, 2064:2320], BYf, reads=[BYf])
            c.dma("sp", o_vs[:, :], Yf[:R, 2320:2576], BYf, reads=[BYf])
            c.dma("sp", o_iks[:, :], Yf[:R, 2576:2640], BYf, reads=[BYf])
            c.op("pool", lambda e: e.tensor_copy(out=Yb[:R, 2064:2640], in_=Yf[:R, 2064:2640]), reads=[BYf], writes=[BYb])
            q_transposes(R)
            pt, Bpt = ptbank()
            for gp in range(2):
                c.op("pe", lambda e: e.transpose(pt[:, gp * P:gp * P + R], Yb[:R, 2064 + gp * P:2064 + (gp + 1) * P], identb[:R, :R]),
                     reads=[BYb, Bid], writes=[Bpt])
            c.op("pe", lambda e: e.transpose(pt[0:64, 2 * P:2 * P + R], Yb[:R, 2576:2640], identb[:R, :R]), reads=[BYb, Bid], writes=[Bpt])
            c.op("act", lambda e: e.copy(out=KnT2[:, :, :], in_=pt[:, 0:2 * P].rearrange("p (a b) -> p a b", b=P)[:, :, 0:R]), reads=[Bpt], writes=[BKn])
            c.op("act", lambda e: e.copy(out=kinT[:, :], in_=pt[0:64, 2 * P:2 * P + R]), reads=[Bpt], writes=[Bkin])
            Bwsd = Buf("wsd")
            c.dma("sp", wsd_s[:, :], wsc[:R, :], Bwsd, reads=[Bwsc], writes=[Bwsd])
            for h in range(16):
                src = bass.AP(tensor=wsd_s.tensor, offset=h, ap=[[16, 4], [64, NSQ]])
                c.dma("sp", Wht[4 * h:4 * h + 4, :], src, BWht, reads=[Bwsd], writes=[BWht], nowaw=(h > 0), slow=True)
            c.op("dve", lambda e: e.tensor_tensor(out=Wsel[:, :, :], in0=mselb[:, :].rearrange("p (a b) -> p a b", b=64), in1=bc_last(Wht[:, :], 64), op=ALU.mult),
                 reads=[Bmsel, BWht], writes=[BWsel])
            c.op("dve", lambda e: e.memset(kiT[:, PAST:PAST + P], 0.0), writes=[BkiT])
            c.op("dve", lambda e: e.memset(KT2[:, :, PAST:PAST + P], 0.0), writes=[BKT])
            c.op("pool", lambda e: e.memset(Vaug[:, 16, :, 0:64], 0.0), writes=[BV])
            nch = 5
            SS = float(_os.environ.get("DBG_SS", "99"))
            if SS < 1:
                return
            for b in range(NSQ):
                for hf in range(2):
                    col = b * 2 + hf
                    c.dma("pool", ikp[:, hf * 8:(hf + 1) * 8, :].rearrange("p a d -> p (a d)"), cik8, Bikp, reads=[Bidx], writes=[Bikp], nowaw=(hf > 0),
                          indirect=IOA(ap=idxall[:, col:col + 1], axis=0))
                for j0 in (0, 8):
                    pt, Bpt = ptbank()
                    for j in range(8):
                        c.op("pe", lambda e: e.transpose(pt[0:64, j * P:(j + 1) * P], ikp[:, j0 + j, :], identb[:, :]), reads=[Bikp, Bid], writes=[Bpt])
                    c.op("act", lambda e: e.copy(out=kiT[:, j0 * P:(j0 + 8) * P], in_=pt[0:64, :]), reads=[Bpt], writes=[BkiT])
                c.op("dve", lambda e: e.tensor_copy(out=kiT[:, PAST:PAST + 4], in_=kinT[:, 4 * b:4 * b + 4]), reads=[Bkin], writes=[BkiT])
                c.op("dve", lambda e: e.tensor_copy(out=qisb[:, :].rearrange("p (h t) -> p h t", t=4), in_=qiT[:, :, 4 * b:4 * b + 4]), reads=[BqiT], writes=[Bqisb])
                for ci in range(nch):
                    c0 = ci * 512
                    nn = min(512, NKS * P - c0)
                    psc, Bpsc = c.banks[5]
                    pI, BpI = c.banks[ci]
                    c.op("pe", lambda e: e.matmul(psc[0:64, 0:nn], lhsT=qisb[:, :], rhs=kiT[:, c0:c0 + nn], start=True, stop=True),
                         reads=[Bqisb, BkiT], writes=[Bpsc])
                    R_, BR_ = Rb[(b * nch + ci) % 4]
                    c.op("act", lambda e: e.activation(out=R_[0:64, 0:nn], in_=psc[0:64, 0:nn], func=AF.Relu), reads=[Bpsc], writes=[BR_])
                    c.op("pe", lambda e: e.matmul(pI[0:64, 0:nn], lhsT=Wsel[:, b, :], rhs=R_[0:64, 0:nn], start=(b == 0), stop=(b == NSQ - 1)),
                         reads=[BWsel, BR_], writes=[BpI])
            if SS < 2:
                return
            for ci in range(nch):
                c0 = ci * 512
                nn = min(512, NKS * P - c0)
                pI, BpI = c.banks[ci]
                evac_scores(pI, BpI, R, c0, nn, qpos[0:R, NQT:NQT + 1])
            bounds(R, nch)
            bisect(R, NKS * P)
            mask_transposes(R, NKS)
            if SS < 2.5:
                return
            mskS = msk[:, 0:NKS * P].rearrange("p (k t) -> p k t", t=P)
            c.op("pool", lambda e: e.memset(msk[:, 0:NKS * P], 0.0), writes=[Bmsk])
            for b in range(NSQ):
                for hf in range(2):
                    col = b * 2 + hf
                    c.dma("pool", Kp[:, hf * 8:(hf + 1) * 8, :].rearrange("p a d -> p (a d)"), ck8, BKp, reads=[Bidx], writes=[BKp], nowaw=(hf > 0),
                          indirect=IOA(ap=idxall[:, col:col + 1], axis=0))
                for hf in range(2):
                    col = b * 2 + hf
                    c.dma("pool", Vp[:, hf * 8:(hf + 1) * 8, :].rearrange("p a d -> p (a d)"), cv8, BVp, reads=[Bidx], writes=[BVp], nowaw=(hf > 0),
                          indirect=IOA(ap=idxall[:, col:col + 1], axis=0))
                c.op("pool", lambda e: e.tensor_copy(out=Vaug[:, 0:16, :, 0:64], in_=Vp[:, :, :].rearrange("p j (g d) -> p j g d", d=64)),
                     reads=[BVp], writes=[BV])
                c.dma("sp", Vst[:, :], Yf[4 * b:4 * b + 4, 2320:2576], BVst, reads=[BYf], writes=[BVst])
                c.op("pool", lambda e: e.tensor_copy(out=Vaug[0:4, 16, :, 0:64], in_=Vst[:, :].rearrange("p (g d) -> p g d", d=64)), reads=[BVst], writes=[BV])
                for j0 in range(0, 16, 4):
                    pt, Bpt = ptbank()
                    for gp in range(2):
                        for j in range(4):
                            c.op("pe", lambda e: e.transpose(pt[:, (gp * 4 + j) * P:(gp * 4 + j + 1) * P], Kp[:, j0 + j, gp * P:(gp + 1) * P], identb[:, :]),
                                 reads=[BKp, Bid], writes=[Bpt])
                    c.op("act", lambda e: e.copy(out=KT2[:, :, j0 * P:(j0 + 4) * P], in_=pt[:, :].rearrange("p (a b) -> p a b", a=2)), reads=[Bpt], writes=[BKT])
                c.op("dve", lambda e: e.tensor_copy(out=KT2[:, :, PAST:PAST + 4], in_=KnT2[:, :, 4 * b:4 * b + 4]), reads=[BKn], writes=[BKT])
                if SS < 2.7:
                    continue
                if b > 0:
                    c.op("pool", lambda e: e.memset(mskS[:, :, 4 * (b - 1):4 * b], 0.0), writes=[Bmsk])
                c.op("pool", lambda e: e.tensor_copy(out=mskS[:, :, 4 * b:4 * b + 4], in_=mskT[:, 0:NKS, 4 * b:4 * b + 4]), reads=[BmT], writes=[Bmsk])
                attention(NKS, lambda kt: mskS[:, kt, :], Bmsk, xb, Bxb, mme=("dve", "dve", "pool"))
                c.dma("sp", Obf[4 * b:4 * b + 4, :], xb[4 * b:4 * b + 4, :], BO, reads=[Bxb], writes=[BO], nowaw=True)
            if SS < 4:
                return
            tail(R, xs[:, :], TQ)

        if stage >= 2:
            sample_phase()
        drain(10000)
        c.barrier()
        es1.__exit__(None, None, None)
        if stage >= 3:
            rest_phase()
        c.finish()
    return nc


def _rope_table(pos):
    half = 8
    inv = (500000.0 ** (-np.arange(half, dtype=np.float32) * np.float32(2.0 / 16))).astype(np.float32)
    ang = pos.astype(np.float32)[:, None] * inv[None, :]
    cs = np.cos(ang).astype(np.float32)
    sn = np.sin(ang).astype(np.float32)
    return np.concatenate([cs, cs, sn], axis=1).astype(np.float32)


_NC_CACHE = {}


def _run(inputs, nphys=None, stage=99, compact=False):
    f = lambda a: np.ascontiguousarray(np.asarray(a))
    x_prompt = f(inputs["x_prompt"]); x_sample = f(inputs["x_sample"])
    cache_k = f(inputs["cache_k"])[0]; cache_v = f(inputs["cache_v"])[0]; cache_ik = f(inputs["cache_idx_k"])[0]
    page_table = f(inputs["page_table"]).astype(np.int32)
    full_nphys = cache_k.shape[0]
    if nphys is None:
        nphys = full_nphys
    key = (nphys, stage)
    if key not in _NC_CACHE:
        _NC_CACHE[key] = build(nphys, stage)
    nc = _NC_CACHE[key]
    consts = np.zeros((P, 512 + 128 + NIT + 2), np.float32)
    consts[:, 0:512] = np.arange(512, dtype=np.float32)[None, :]
    consts[:, 512:640] = np.eye(P, dtype=np.float32)
    consts[:, 640:640 + NIT] = (0.5 ** np.arange(1, NIT + 1, dtype=np.float64)).astype(np.float32)[None, :]
    consts[:, 640 + NIT] = np.arange(P, dtype=np.float32)
    consts[:, 641 + NIT] = (np.arange(P) % 16).astype(np.float32)
    msel = np.zeros((64, NSQ, 64), np.float32)
    for h in range(16):
        for t in range(4):
            for b in range(NSQ):
                msel[h * 4 + t, b, b * 4 + t] = 1.0
    ropek = _rope_table(np.arange(SEQ))
    ropes = _rope_table(PAST + (np.arange(TS) % 4))
    in_maps = []
    for core in range(8):
        b, h = core // 2, core % 2
        pos0 = 0 if h == 0 else POS0
        qp = np.zeros((P, NQT + 1), np.float32)
        qp[:, :NQT] = pos0 + np.arange(P)[:, None] + P * np.arange(NQT)[None, :]
        qp[:TS, NQT] = PAST + (np.arange(TS) % 4)
        sl = slice(core * NSQ, (core + 1) * NSQ)
        pt = page_table[sl]
        if compact:
            pages = np.unique(pt)
            remap = {int(p): i for i, p in enumerate(pages)}
            ck = np.zeros((nphys, P, 256), np.float32); cv = np.zeros((nphys, P, 256), np.float32); ci = np.zeros((nphys, P, 64), np.float32)
            ck[:len(pages)] = cache_k[pages].reshape(-1, P, 256); cv[:len(pages)] = cache_v[pages].reshape(-1, P, 256)
            ci[:len(pages)] = cache_ik[pages]
            pt = np.vectorize(remap.get)(pt).astype(np.int32)
        else:
            ck = cache_k.reshape(-1, P, 256); cv = cache_v.reshape(-1, P, 256); ci = cache_ik
        m = {
            "xk": x_prompt[b], "xq": x_prompt[b, pos0:pos0 + TQ], "xs": x_sample[sl].reshape(TS, D),
            "ropek": ropek, "ropeq": ropek[pos0:pos0 + TQ], "ropes": ropes, "qpos": qp, "consts": consts,
            "msel": msel.reshape(64, NSQ * 64), "pt": pt,
            "cache_k": ck.reshape(nphys * P, 256), "cache_v": cv.reshape(nphys * P, 256), "cache_ik": ci.reshape(nphys * P, 64),
            "st_conv": f(inputs["state_conv"])[0, sl].reshape(NSQ * 2, D),
            "st_ffn": f(inputs["state_ffn"])[:, sl].reshape(2, NSQ * 2, DFF),
            "w_attn_in": f(inputs["w_attn_in"])[0], "w_attn_out": f(inputs["w_attn_out"])[0],
            "w_conv_in": f(inputs["w_conv_in"])[0], "conv_w": f(inputs["conv_w"])[0], "w_conv_out": f(inputs["w_conv_out"])[0],
            "w_ffn_up": f(inputs["w_ffn_up"]), "ffn_conv_w": f(inputs["ffn_conv_w"]), "ffn_conv_b": f(inputs["ffn_conv_b"]),
            "w_ffn_down": f(inputs["w_ffn_down"]), "ln_g": f(inputs["ln_g"]).reshape(4, D), "ln_b": f(inputs["ln_b"]).reshape(4, D),
        }
        in_maps.append({k: np.ascontiguousarray(v) for k, v in m.items()})
    res = run_bass_kernel_spmd(nc, in_maps, core_ids=list(range(8)))
    R = res.results
    B = 4
    y_prompt = np.zeros((B, SEQ, D), np.float32)
    nk = np.zeros((1, B, SEQ, 4, 64), np.float32); nv = np.zeros_like(nk); nik = np.zeros((1, B, SEQ, 64), np.float32)
    cp = np.zeros((1, B, 2, D), np.float32); fp = np.zeros((2, B, 2, DFF), np.float32)
    y_sample = np.zeros((128, 4, D), np.float32)
    nks = np.zeros((1, 128, 4, 4, 64), np.float32); nvs = np.zeros_like(nks); niks = np.zeros((1, 128, 4, 64), np.float32)
    cs = np.zeros((1, 128, 2, D), np.float32); fs = np.zeros((2, 128, 2, DFF), np.float32)
    for core in range(8):
        b, h = core // 2, core % 2
        r = R[core]
        sl = slice(core * NSQ, (core + 1) * NSQ)
        if h == 0:
            y_prompt[b, 0:2048] = r["o_y"][0:2048]
            nk[0, b] = r["o_k"].reshape(SEQ, 4, 64); nv[0, b] = r["o_v"].reshape(SEQ, 4, 64); nik[0, b] = r["o_ik"]
        else:
            y_prompt[b, 2048:4096] = r["o_y"][128:TQ]
            cp[0, b] = r["o_cp"]; fp[:, b] = r["o_fp"]
        y_sample[sl] = r["o_ys"].reshape(NSQ, 4, D)
        nks[0, sl] = r["o_ks"].reshape(NSQ, 4, 4, 64); nvs[0, sl] = r["o_vs"].reshape(NSQ, 4, 4, 64); niks[0, sl] = r["o_iks"].reshape(NSQ, 4, 64)
        cs[0, sl] = r["o_cs"].reshape(NSQ, 2, D); fs[:, sl] = r["o_fs"].reshape(2, NSQ, 2, DFF)
    return (y_prompt, y_sample, nk, nv, nik, nks, nvs, niks, cp, cs, fp, fs), R


def kernel(**inputs):
    outs, _ = _run(inputs)
    return outs
```
